# Optimizing a Trainium2 kernel written in Bass

```python
import math
import jax
import jax.numpy as jnp
from jax import lax
import numpy as np

D_MODEL = 1024
BATCH = 8
SEQ = 4096
DEPTH = 4

GRID_W = 64
CTX_LEN = 256
N_DIR = 2
N_BRANCH = 2
EPS = 1e-6
LRU_WIDTH = 1024
LRU_HEADS = 16
LRU_HEAD_DIM = LRU_WIDTH // LRU_HEADS
LRU_CONV = 4
LRU_C = 8.0
SSD_INNER = 2 * D_MODEL
SSD_HEAD_DIM = 64
SSD_HEADS = SSD_INNER // SSD_HEAD_DIM
SSD_GROUPS = 4
SSD_STATE = 128
SSD_CONV = 4
SSD_CHUNK = 128
SSD_CONV_DIM = SSD_INNER + 2 * SSD_GROUPS * SSD_STATE
FFN_DIM = 2816
FFN_CONV = 3
IN_SIZES = (LRU_WIDTH, LRU_WIDTH, SSD_INNER, SSD_CONV_DIM, N_DIR * SSD_HEADS, N_BRANCH * D_MODEL)
IN_DIM = 2 * LRU_WIDTH + SSD_INNER + SSD_CONV_DIM + N_DIR * SSD_HEADS + N_BRANCH * D_MODEL

kernel_name = "hybrid_rglru_ssd_convffn_dit"


def rmsnorm(x, w):
    xf = x.astype(jnp.float32)
    y = xf * lax.rsqrt(jnp.mean(xf * xf, axis=-1, keepdims=True) + EPS)
    return (y * w.astype(jnp.float32)).astype(x.dtype)


def modulate(h, shift, scale):
    return h * (1 + scale) + shift


def split_cols(t, sizes):
    offs, acc = [], 0
    for s in sizes[:-1]:
        acc += s
        offs.append(acc)
    return jnp.split(t, offs, axis=-1)


def rev(t, d):
    return t[:, ::-1] if d == 1 else t


def dwconv1d(x, w, b):
    k, ch = w.shape
    y = lax.conv_general_dilated(x, w[:, None, :].astype(x.dtype), window_strides=(1,),
                                 padding=((k // 2, k - 1 - k // 2),),
                                 dimension_numbers=('NWC', 'WIO', 'NWC'), feature_group_count=ch)
    return y + b


def dwconv2d(x, w, b, rows, cols):
    bsz, seqlen, ch = x.shape
    g = x.reshape(bsz, rows, cols, ch)
    y = lax.conv_general_dilated(g, w[:, :, None, :].astype(x.dtype), window_strides=(1, 1),
                                 padding=((1, 1), (1, 1)),
                                 dimension_numbers=('NHWC', 'HWIO', 'NHWC'), feature_group_count=ch)
    return y.reshape(bsz, seqlen, ch) + b


def linear_scan(a, b, h0):
    def comb(left, right):
        al, bl = left
        ar, br = right
        return ar * al, ar * bl + br
    a_cum, h = lax.associative_scan(comb, (a, b), axis=1)
    return h + a_cum * h0[:, None, :]


def rglru(u, wa, ba, wx, bx, lam, h0):
    bsz, seqlen, width = u.shape
    ub = u.reshape(bsz, seqlen, LRU_HEADS, LRU_HEAD_DIM)
    r = jax.nn.sigmoid(jnp.einsum('blhi,hij->blhj', ub, wa.astype(jnp.float32)).reshape(bsz, seqlen, width)
                       + ba.astype(jnp.float32))
    i = jax.nn.sigmoid(jnp.einsum('blhi,hij->blhj', ub, wx.astype(jnp.float32)).reshape(bsz, seqlen, width)
                       + bx.astype(jnp.float32))
    log_a = LRU_C * r * jax.nn.log_sigmoid(lam.astype(jnp.float32))
    a = jnp.exp(log_a)
    b = jnp.sqrt(-jnp.expm1(2.0 * log_a)) * (i * u)
    return linear_scan(a, b, h0)


def segsum(x):
    t = x.shape[-1]
    cs = jnp.cumsum(x, axis=-1)
    s = cs[..., :, None] - cs[..., None, :]
    mask = jnp.tril(jnp.ones((t, t), dtype=bool))
    return jnp.where(mask, s, -jnp.inf)


def ssd_chunked(xs, dt, a, bm, cm, h0):
    bsz, seqlen, nh, hp = xs.shape
    ng, ns = bm.shape[2], bm.shape[3]
    rep = nh // ng
    q = SSD_CHUNK
    nc = seqlen // q
    xdt = (xs * dt[..., None]).reshape(bsz, nc, q, ng, rep, hp)
    da = (dt * a).reshape(bsz, nc, q, ng, rep).transpose(0, 3, 4, 1, 2)
    da_cs = jnp.cumsum(da, axis=-1)
    bc = bm.reshape(bsz, nc, q, ng, ns)
    cc = cm.reshape(bsz, nc, q, ng, ns)
    decay_in = jnp.exp(segsum(da))
    cb = jnp.einsum('bclgn,bcsgn->bgcls', cc, bc)
    y_diag = jnp.einsum('bgcls,bgrcls,bcsgrp->bclgrp', cb, decay_in, xdt)
    decay_st = jnp.exp(da_cs[..., -1:] - da_cs)
    states = jnp.einsum('bcsgn,bgrcs,bcsgrp->bcgrpn', bc, decay_st, xdt)
    states = jnp.concatenate([h0.reshape(bsz, 1, ng, rep, hp, ns), states], axis=1)
    chunk_tot = jnp.pad(da_cs[..., -1], ((0, 0), (0, 0), (0, 0), (1, 0)))
    decay_ch = jnp.exp(segsum(chunk_tot))
    states = jnp.einsum('bgrzc,bcgrpn->bzgrpn', decay_ch, states)
    prev, final = states[:, :-1], states[:, -1]
    y_off = jnp.einsum('bclgn,bcgrpn,bgrcl->bclgrp', cc, prev, jnp.exp(da_cs))
    y = (y_diag + y_off).reshape(bsz, seqlen, nh, hp)
    return y, final.reshape(bsz, nh, hp, ns)


def ssd_final_state(xs, dt, a, bm):
    bsz, seqlen, nh, hp = xs.shape
    ng, ns = bm.shape[2], bm.shape[3]
    rep = nh // ng
    da_cs = jnp.cumsum(dt * a, axis=1)
    w = (jnp.exp(da_cs[:, -1:] - da_cs) * dt).reshape(bsz, seqlen, ng, rep)
    st = jnp.einsum('blgn,blgr,blgrp->bgrpn', bm, w, xs.reshape(bsz, seqlen, ng, rep, hp))
    return st.reshape(bsz, nh, hp, ns)


def ssd_heads(v):
    bsz, seqlen, _ = v.shape
    xs, bm, cm = split_cols(v.astype(jnp.float32), (SSD_INNER, SSD_GROUPS * SSD_STATE, SSD_GROUPS * SSD_STATE))
    return (xs.reshape(bsz, seqlen, SSD_HEADS, SSD_HEAD_DIM),
            bm.reshape(bsz, seqlen, SSD_GROUPS, SSD_STATE),
            cm.reshape(bsz, seqlen, SSD_GROUPS, SSD_STATE))


def ssd_output(y, xs, z, ssd_d, norm_w):
    bsz, seqlen = z.shape[:2]
    y = (y + ssd_d.astype(jnp.float32)[:, None] * xs).reshape(bsz, seqlen, SSD_INNER)
    g = (y * jax.nn.silu(z.astype(jnp.float32))).reshape(bsz, seqlen, SSD_GROUPS, SSD_INNER // SSD_GROUPS)
    g = g * lax.rsqrt(jnp.mean(g * g, axis=-1, keepdims=True) + EPS)
    return (g.reshape(bsz, seqlen, SSD_INNER) * norm_w.astype(jnp.float32)).astype(z.dtype)


def mixer(hc, hx, w_in, lru_conv_w, lru_conv_b, lru_wa, lru_ba, lru_wx, lru_bx, lru_lambda, lru_proj,
          ssd_conv_w, ssd_conv_b, ssd_dt_bias, ssd_a_log, ssd_d, ssd_norm_w, ssd_proj, w_out, ctx_out):
    lx_c, lg_c, z_c, xbc_c, dt_c, gt_c = split_cols(hc @ w_in, IN_SIZES)
    lx_x, lg_x, z_x, xbc_x, dt_x, gt_x = split_cols(hx @ w_in, IN_SIZES)
    bsz = hx.shape[0]

    u_c = dwconv1d(lx_c, lru_conv_w, lru_conv_b).astype(jnp.float32)
    u_x = dwconv1d(lx_x, lru_conv_w, lru_conv_b).astype(jnp.float32)
    h0 = jnp.zeros((bsz, LRU_WIDTH), jnp.float32)
    ya_c, ya_x = [], []
    for d in range(N_DIR):
        hs_c = rglru(rev(u_c, d), lru_wa[d], lru_ba[d], lru_wx[d], lru_bx[d], lru_lambda[d], h0)
        hs_x = rglru(rev(u_x, d), lru_wa[d], lru_ba[d], lru_wx[d], lru_bx[d], lru_lambda[d], hs_c[:, -1])
        if ctx_out:
            ya_c.append(rev(hs_c, d))
        ya_x.append(rev(hs_x, d))

    xs_c, b_c, c_c = ssd_heads(jax.nn.silu(dwconv1d(xbc_c, ssd_conv_w, ssd_conv_b)))
    xs_x, b_x, c_x = ssd_heads(jax.nn.silu(dwconv1d(xbc_x, ssd_conv_w, ssd_conv_b)))
    s0 = jnp.zeros((bsz, SSD_HEADS, SSD_HEAD_DIM, SSD_STATE), jnp.float32)
    yb_c, yb_x = [], []
    for d in range(N_DIR):
        a = -jnp.exp(ssd_a_log[d].astype(jnp.float32))
        bias = ssd_dt_bias[d].astype(jnp.float32)
        dtd_c = jax.nn.softplus(dt_c[..., d * SSD_HEADS:(d + 1) * SSD_HEADS].astype(jnp.float32) + bias)
        dtd_x = jax.nn.softplus(dt_x[..., d * SSD_HEADS:(d + 1) * SSD_HEADS].astype(jnp.float32) + bias)
        if ctx_out:
            y_c, st_c = ssd_chunked(rev(xs_c, d), rev(dtd_c, d), a, rev(b_c, d), rev(c_c, d), s0)
            yb_c.append(rev(y_c, d))
        else:
            st_c = ssd_final_state(rev(xs_c, d), rev(dtd_c, d), a, rev(b_c, d))
        y_x, _ = ssd_chunked(rev(xs_x, d), rev(dtd_x, d), a, rev(b_x, d), rev(c_x, d), st_c)
        yb_x.append(rev(y_x, d))

    def merge(ya, lg, yb, xs, z, gt):
        ra = ((ya[0] + ya[1]).astype(lg.dtype) * jax.nn.gelu(lg)) @ lru_proj
        rb = ssd_output(yb[0] + yb[1], xs, z, ssd_d, ssd_norm_w) @ ssd_proj
        ga, gb = jnp.split(jax.nn.sigmoid(gt), 2, axis=-1)
        return (ga * ra + gb * rb) @ w_out

    out_x = merge(ya_x, lg_x, yb_x, xs_x, z_x, gt_x)
    out_c = merge(ya_c, lg_c, yb_c, xs_c, z_c, gt_c) if ctx_out else None
    return out_c, out_x


def conv_ffn(h, w_up, conv_w, conv_b, w_down, rows, cols):
    u, v = jnp.split(h @ w_up, 2, axis=-1)
    u = dwconv2d(u, conv_w, conv_b, rows, cols)
    return (jax.nn.gelu(u) * v) @ w_down


def setup_inputs(seed: int = 0) -> dict:
    key = jax.random.key(seed)
    ks = jax.random.split(key, 32)

    def nrm(k, shape, scale):
        return jax.random.normal(k, shape, jnp.float32) * scale

    L = DEPTH
    a0 = jax.random.uniform(ks[14], (L, N_DIR, LRU_WIDTH), jnp.float32, 0.9, 0.999)
    s = a0 ** (1.0 / LRU_C)
    lam = jnp.log(s) - jnp.log1p(-s)
    dt0 = jnp.exp(jax.random.uniform(ks[18], (L, N_DIR, SSD_HEADS), jnp.float32, math.log(1e-3), math.log(1e-1)))
    dt_bias = dt0 + jnp.log(-jnp.expm1(-dt0))
    a_log = jnp.log(jax.random.uniform(ks[19], (L, N_DIR, SSD_HEADS), jnp.float32, 1.0, 16.0))
    return {
        "x": nrm(ks[0], (BATCH, SEQ, D_MODEL), 1.0),
        "c": nrm(ks[1], (BATCH, D_MODEL), 1.0),
        "ctx": nrm(ks[2], (BATCH, CTX_LEN, D_MODEL), 1.0),
        "c_ctx": nrm(ks[3], (D_MODEL,), 1.0),
        "ada_w": nrm(ks[4], (L, D_MODEL, 6 * D_MODEL), 0.5 * D_MODEL ** -0.5),
        "ada_b": nrm(ks[5], (L, 6 * D_MODEL), 0.02),
        "norm_mix_w": 1.0 + nrm(ks[6], (L, D_MODEL), 0.05),
        "norm_ffn_w": 1.0 + nrm(ks[7], (L, D_MODEL), 0.05),
        "w_in": nrm(ks[8], (L, D_MODEL, IN_DIM), D_MODEL ** -0.5),
        "lru_conv_w": nrm(ks[9], (L, LRU_CONV, LRU_WIDTH), LRU_CONV ** -0.5),
        "lru_conv_b": nrm(ks[10], (L, LRU_WIDTH), 0.02),
        "lru_wa": nrm(ks[11], (L, N_DIR, LRU_HEADS, LRU_HEAD_DIM, LRU_HEAD_DIM), LRU_HEAD_DIM ** -0.5),
        "lru_ba": nrm(ks[12], (L, N_DIR, LRU_WIDTH), 0.02),
        "lru_wx": nrm(ks[13], (L, N_DIR, LRU_HEADS, LRU_HEAD_DIM, LRU_HEAD_DIM), LRU_HEAD_DIM ** -0.5),
        "lru_bx": nrm(ks[15], (L, N_DIR, LRU_WIDTH), 0.02),
        "lru_lambda": lam,
        "lru_proj": nrm(ks[16], (L, LRU_WIDTH, D_MODEL), LRU_WIDTH ** -0.5),
        "ssd_conv_w": nrm(ks[17], (L, SSD_CONV, SSD_CONV_DIM), SSD_CONV ** -0.5),
        "ssd_conv_b": nrm(ks[20], (L, SSD_CONV_DIM), 0.02),
        "ssd_dt_bias": dt_bias,
        "ssd_a_log": a_log,
        "ssd_d": 1.0 + nrm(ks[21], (L, SSD_HEADS), 0.1),
        "ssd_norm_w": 1.0 + nrm(ks[22], (L, SSD_INNER), 0.05),
        "ssd_proj": nrm(ks[23], (L, SSD_INNER, D_MODEL), SSD_INNER ** -0.5),
        "w_out": nrm(ks[24], (L, D_MODEL, D_MODEL), D_MODEL ** -0.5),
        "ffn_w_up": nrm(ks[25], (L, D_MODEL, 2 * FFN_DIM), D_MODEL ** -0.5),
        "ffn_conv_w": nrm(ks[26], (L, FFN_CONV, FFN_CONV, FFN_DIM), 1.0 / FFN_CONV),
        "ffn_conv_b": nrm(ks[27], (L, FFN_DIM), 0.02),
        "ffn_w_down": nrm(ks[28], (L, FFN_DIM, D_MODEL), FFN_DIM ** -0.5),
        "final_norm_w": 1.0 + nrm(ks[29], (D_MODEL,), 0.05),
    }


def reference(x, c, ctx, c_ctx, ada_w, ada_b, norm_mix_w, norm_ffn_w, w_in, lru_conv_w, lru_conv_b,
              lru_wa, lru_ba, lru_wx, lru_bx, lru_lambda, lru_proj, ssd_conv_w, ssd_conv_b, ssd_dt_bias,
              ssd_a_log, ssd_d, ssd_norm_w, ssd_proj, w_out, ffn_w_up, ffn_conv_w, ffn_conv_b, ffn_w_down,
              final_norm_w):
    rows = x.shape[1] // GRID_W
    ctx_len = ctx.shape[1]
    sc = jax.nn.silu(c)
    scc = jax.nn.silu(c_ctx)
    for l in range(DEPTH):
        last = l == DEPTH - 1
        mx = jnp.split((sc @ ada_w[l] + ada_b[l])[:, None, :], 6, axis=-1)
        mc = jnp.split((scc @ ada_w[l] + ada_b[l])[None, None, :], 6, axis=-1)
        hx = modulate(rmsnorm(x, norm_mix_w[l]), mx[0], mx[1])
        hc = modulate(rmsnorm(ctx, norm_mix_w[l]), mc[0], mc[1])
        oc, ox = mixer(hc, hx, w_in[l], lru_conv_w[l], lru_conv_b[l], lru_wa[l], lru_ba[l], lru_wx[l],
                       lru_bx[l], lru_lambda[l], lru_proj[l], ssd_conv_w[l], ssd_conv_b[l], ssd_dt_bias[l],
                       ssd_a_log[l], ssd_d[l], ssd_norm_w[l], ssd_proj[l], w_out[l], not last)
        x = x + mx[2] * ox
        hx2 = modulate(rmsnorm(x, norm_ffn_w[l]), mx[3], mx[4])
        x = x + mx[5] * conv_ffn(hx2, ffn_w_up[l], ffn_conv_w[l], ffn_conv_b[l], ffn_w_down[l], rows, GRID_W)
        if not last:
            ctx = ctx + mc[2] * oc
            hc2 = modulate(rmsnorm(ctx, norm_ffn_w[l]), mc[3], mc[4])
            ctx = ctx + mc[5] * conv_ffn(hc2, ffn_w_up[l], ffn_conv_w[l], ffn_conv_b[l], ffn_w_down[l], 1, ctx_len)
    return rmsnorm(x, final_norm_w)
```

```python
import numpy as np
from contextlib import ExitStack
import concourse.bass as bass
import concourse.mybir as mybir

F32 = mybir.dt.float32
BF16 = mybir.dt.bfloat16
AF = mybir.ActivationFunctionType
ALU = mybir.AluOpType

ENGS = ("pe", "act", "dve", "pool", "sp")


class R:
    __slots__ = ("name", "w", "rs", "dsem", "dcnt")

    def __init__(self, name):
        self.name = name
        self.w = None
        self.rs = {}
        self.dsem = None
        self.dcnt = 0


class Sync:
    def __init__(self, nc, stack):
        self.nc = nc
        self.stack = stack
        self.ops = {e: [] for e in ENGS}
        self.sems = {}
        for e in ENGS:
            self.sems[("E", e)] = stack.enter_context(nc.semaphore("S_" + e))
        self.cnt = {e: 0 for e in ENGS}
        self.seen = {e: {} for e in ENGS}
        self.ndsem = 0
        self.dcount = {}
        self.dsem_by_name = {}

    def _waits(self, eng, reads, writes, extra=()):
        w = {}

        def add(ev):
            if ev is None:
                return
            k, v = ev
            if w.get(k, 0) < v:
                w[k] = v
        for r in reads:
            add(r.w)
        for r in writes:
            add(r.w)
            for k, v in r.rs.items():
                add((k, v))
        for ev in extra:
            add(ev)
        out = []
        seen = self.seen[eng]
        for k, v in w.items():
            if eng == "pe" and k == ("E", "pe"):
                continue
            if seen.get(k, 0) >= v:
                continue
            seen[k] = v
            out.append((k, v))
        return out

    def _mark(self, ev, reads, writes):
        k, v = ev
        for r in reads:
            if r.rs.get(k, 0) < v:
                r.rs[k] = v
        for r in writes:
            r.w = ev
            r.rs = {}

    def op(self, eng, fn, reads=(), writes=()):
        waits = self._waits(eng, reads, writes)
        self.cnt[eng] += 1
        ev = (("E", eng), self.cnt[eng])
        self.ops[eng].append((waits, fn, ("E", eng), 1))
        self._mark(ev, reads, writes)
        return ev

    def dma(self, out_ap, in_ap, sres, reads=(), writes=(), q="sp"):
        key = self.dsem_by_name.get(sres.name)
        if key is None:
            key = ("D", self.ndsem)
            self.ndsem += 1
            self.sems[key] = self.stack.enter_context(self.nc.semaphore("D%d" % key[1]))
            self.dsem_by_name[sres.name] = key
            self.dcount[key] = 0
        extra = []
        if self.dcount[key]:
            extra.append((key, self.dcount[key]))
        waits = self._waits(q, reads, writes, extra)
        self.dcount[key] += 16
        ev = (key, self.dcount[key])

        def fn(e, out_ap=out_ap, in_ap=in_ap):
            return e.dma_start(out=out_ap, in_=in_ap)
        self.ops[q].append((waits, fn, key, 16))
        self._mark(ev, reads, writes)
        return ev

    def barrier(self):
        evs = [(("E", e), self.cnt[e]) for e in ENGS if self.cnt[e]]
        evs += [(k, None) for k in self.sems if k[0] == "D"]
        for e in ENGS:
            waits = []
            for k, v in evs:
                if v is None:
                    v = self.dcount.get(k, 0)
                if v and self.seen[e].get(k, 0) < v and not (k == ("E", e)):
                    self.seen[e][k] = v
                    waits.append((k, v))
            if waits:
                self.ops[e].append((waits, None, None, 0))

    def fence(self, eng, resources):
        waits = self._waits(eng, [], resources)
        if waits:
            self.ops[eng].append((waits, None, None, 0))

    def emit(self):
        nc = self.nc
        engmap = {"pe": "tensor", "act": "scalar", "dve": "vector", "pool": "gpsimd", "sp": "sync"}
        with nc.Block() as block:
            for e in ENGS:
                ops = self.ops[e]

                def body(eng, ops=ops):
                    for waits, fn, semkey, inc in ops:
                        for k, v in waits:
                            eng.wait_ge(self.sems[k], v)
                        if fn is not None:
                            ins = fn(eng)
                            ins.then_inc(self.sems[semkey], inc)
                getattr(block, engmap[e])(body)
        self.ops = {e: [] for e in ENGS}


class Cfg:
    def __init__(self, D=1024, SEQ=4096, CTX=256, GW=64, LW=1024, SI=2048, NG=4, FF=2816, DEPTH=4, TS=512):
        self.D, self.SEQ, self.CTX, self.GW, self.LW, self.SI, self.NG, self.FF, self.DEPTH, self.TS = \
            D, SEQ, CTX, GW, LW, SI, NG, FF, DEPTH, TS
        self.T = CTX + SEQ
        self.KD = D // 128
        self.LWC = LW // 128
        self.SIC = SI // 128
        self.NH = SI // 64
        self.HPG = self.NH // NG
        self.CD = SI + 2 * NG * 128
        self.CDC = self.CD // 128
        self.FFC = FF // 128
        self.IN_DIM = 2 * LW + SI + self.CD + 2 * self.NH + 2 * D
        self.o_lx, self.o_lg = 0, LW
        self.o_z = 2 * LW
        self.o_xbc = 2 * LW + SI
        self.o_dt = self.o_xbc + self.CD
        self.o_gt = self.o_dt + 2 * self.NH
        self.tiles = []
        for t0 in range(0, CTX, TS):
            self.tiles.append((t0, min(TS, CTX - t0), 1))
        for t0 in range(0, SEQ, TS):
            self.tiles.append((CTX + t0, min(TS, SEQ - t0), 0))
        self.NCH = self.T // 128
        self.TS2 = min(256, TS)
        self.tiles2 = []
        for t0 in range(0, CTX, self.TS2):
            self.tiles2.append((t0, min(self.TS2, CTX - t0), 1))
        for t0 in range(0, SEQ, self.TS2):
            self.tiles2.append((CTX + t0, min(self.TS2, SEQ - t0), 0))
        off = {}
        o = 0
        for name, n in (("nmw", self.KD), ("nfw", self.KD), ("adab", 6 * self.KD), ("lcw", 4 * self.LWC),
                        ("lcb", self.LWC), ("lba", 2 * self.LWC), ("lbx", 2 * self.LWC), ("llam", 2 * self.LWC),
                        ("scw", 4 * self.CDC), ("scb", self.CDC), ("sdd", self.SIC), ("snw", self.SIC),
                        ("fcw", 9 * self.FFC), ("fcb", self.FFC), ("fnw", self.KD), ("dtb", 1), ("alog", 1)):
            off[name] = o
            o += n
        self.voff = off
        self.NV = o


def colmajor(v):
    v = np.asarray(v, np.float32)
    return np.ascontiguousarray(v.reshape(-1, 128).T)


def host_prep(cfg, inp, b):
    L = cfg.DEPTH
    m = {}
    m["xr0"] = np.ascontiguousarray(np.concatenate([inp["ctx"][b].T, inp["x"][b].T], axis=1).astype(np.float32))
    cv = np.zeros((128, cfg.KD, 2), np.float32)
    cv[:, :, 0] = colmajor(inp["c"][b])
    cv[:, :, 1] = colmajor(inp["c_ctx"])
    m["cvec"] = cv
    vecs = np.zeros((L, 128, cfg.NV), np.float32)
    vo = cfg.voff
    for l in range(L):
        def put(name, arr2d):
            vecs[l, :arr2d.shape[0], vo[name]:vo[name] + arr2d.shape[1]] = arr2d
        put("nmw", colmajor(inp["norm_mix_w"][l]))
        put("nfw", colmajor(inp["norm_ffn_w"][l]))
        put("adab", colmajor(inp["ada_b"][l]))
        put("lcw", np.concatenate([colmajor(inp["lru_conv_w"][l][k]) for k in range(4)], axis=1))
        put("lcb", colmajor(inp["lru_conv_b"][l]))
        put("lba", np.concatenate([colmajor(inp["lru_ba"][l][d]) for d in range(2)], axis=1))
        put("lbx", np.concatenate([colmajor(inp["lru_bx"][l][d]) for d in range(2)], axis=1))
        put("llam", np.concatenate([colmajor(inp["lru_lambda"][l][d]) for d in range(2)], axis=1))
        put("scw", np.concatenate([colmajor(inp["ssd_conv_w"][l][k]) for k in range(4)], axis=1))
        put("scb", colmajor(inp["ssd_conv_b"][l]))
        put("sdd", colmajor(np.repeat(np.asarray(inp["ssd_d"][l]), 64)))
        put("snw", colmajor(inp["ssd_norm_w"][l]))
        fw_ = np.asarray(inp["ffn_conv_w"][l]).reshape(9, cfg.FF)
        put("fcw", np.concatenate([colmajor(fw_[k]) for k in range(9)], axis=1))
        put("fcb", colmajor(inp["ffn_conv_b"][l]))
        put("fnw", colmajor(inp["final_norm_w"]))
        dtb = np.zeros((64, 1), np.float32)
        alog = np.zeros((64, 1), np.float32)
        for d in range(2):
            dtb[d * 32:d * 32 + cfg.NH, 0] = inp["ssd_dt_bias"][l][d]
            alog[d * 32:d * 32 + cfg.NH, 0] = inp["ssd_a_log"][l][d]
        put("dtb", dtb)
        put("alog", alog)
    m["vecs"] = vecs
    for k in ("ada_w", "w_in", "lru_wa", "lru_wx", "lru_proj", "ssd_proj", "w_out", "ffn_w_up", "ffn_w_down"):
        m[k] = np.ascontiguousarray(np.asarray(inp[k], np.float32))
    ident = np.eye(128, dtype=np.float32)
    m["c_ident"] = ident
    idx = np.arange(128)
    mk = np.zeros((2, 128, 128), np.float32)
    mk[0] = (idx[None, :] < idx[:, None])
    mk[1] = (idx[None, :] > idx[:, None])
    m["c_mask"] = mk
    sel = np.zeros((64, 64, 128), np.float32)
    for k in range(64):
        sel[k, k, :] = 1.0
    m["c_sel"] = sel
    rm = np.ones((2, 64, cfg.T), np.float32)
    rm[0, :, 0::128] = 0.0
    rm[1, :, 127::128] = 0.0
    m["c_rmask"] = rm
    return m


GELU_C = 1.5957691216057308
_uid = [0]


class Scope:
    def __init__(self, prog):
        self.p = prog
        self.st = ExitStack()

    def __enter__(self):
        return self

    def sb(self, name, shape, dt):
        _uid[0] += 1
        return self.st.enter_context(self.p.nc.sbuf_tensor("%s_%d" % (name, _uid[0]), list(shape), dt))

    def __exit__(self, *a):
        if a[0] is None:
            self.p.sy.barrier()
            self.p.sy.emit()
        self.st.close()
        return False


class Prog:
    def __init__(self, cfg, layers, debug=(), final=True, stop_after=None, stages=None, emit_xr=None):
        self.cfg = cfg
        self.layers = list(layers)
        self.debug = set(debug)
        self.final = final
        self.stop_after = stop_after
        self.stages = stages
        self.emit_xr = (not final) if emit_xr is None else emit_xr

    def build(self):
        nc = bass.Bass("TRN2", target_bir_lowering=False)
        self.nc = nc
        with ExitStack() as st:
            self.st = st
            self.sy = Sync(nc, st)
            self.declare_io()
            with Scope(self) as top:
                self.setup(top)
                for li, l in enumerate(self.layers):
                    self.layer(l, first=(li == 0))
                    self.sy.barrier()
                    self.sy.emit()
                if self.final:
                    self.final_norm()
                if self.emit_xr:
                    self.copy_xr_out()
        return nc

    def din(self, name, shape, dt=F32):
        return self.nc.dram_tensor(name, list(shape), dt, kind="ExternalInput").ap()

    def dscr(self, name, shape, dt, out=False):
        dbg = name in self.debug
        kind = "ExternalOutput" if (out or dbg) else "Internal"
        return self.nc.dram_tensor(name if out else ("d_" + name), list(shape), dt, kind=kind).ap()

    def declare_io(self):
        cfg = self.cfg
        L = cfg.DEPTH
        self.i = {}
        self.i["xr0"] = self.din("xr0", [cfg.D, cfg.T])
        self.i["cvec"] = self.din("cvec", [128, cfg.KD, 2])
        self.i["vecs"] = self.din("vecs", [L, 128, cfg.NV])
        self.i["ada_w"] = self.din("ada_w", [L, cfg.D, 6 * cfg.D])
        self.i["w_in"] = self.din("w_in", [L, cfg.D, cfg.IN_DIM])
        self.i["lru_wa"] = self.din("lru_wa", [L, 2, cfg.LW // 64, 64, 64])
        self.i["lru_wx"] = self.din("lru_wx", [L, 2, cfg.LW // 64, 64, 64])
        self.i["lru_proj"] = self.din("lru_proj", [L, cfg.LW, cfg.D])
        self.i["ssd_proj"] = self.din("ssd_proj", [L, cfg.SI, cfg.D])
        self.i["w_out"] = self.din("w_out", [L, cfg.D, cfg.D])
        self.i["ffn_w_up"] = self.din("ffn_w_up", [L, cfg.D, 2 * cfg.FF])
        self.i["ffn_w_down"] = self.din("ffn_w_down", [L, cfg.FF, cfg.D])
        self.i["c_ident"] = self.din("c_ident", [128, 128])
        self.i["c_mask"] = self.din("c_mask", [2, 128, 128])
        self.i["c_sel"] = self.din("c_sel", [64, 64, 128])
        self.i["c_rmask"] = self.din("c_rmask", [2, 64, cfg.T])
        if self.final:
            self.outf = self.dscr("out" if not self.emit_xr else "outf", [cfg.D, cfg.SEQ], F32, out=True)
        if self.emit_xr:
            self.out = self.dscr("out" if not self.final else "outx", [cfg.D, cfg.T], F32, out=True)
        self.XR = self.dscr("XR", [cfg.D, cfg.T], F32)
        self.YAG = self.dscr("YAG", [cfg.LW, cfg.T], BF16)
        self.Zs = self.dscr("Zs", [cfg.SI, cfg.T], BF16)
        self.GT = self.dscr("GT", [2 * cfg.D, cfg.T], BF16)
        self.XBC = self.dscr("XBC", [cfg.CD, cfg.T], BF16)
        self.Y = self.dscr("Y", [cfg.SI, cfg.T], F32)
        self.GN = self.dscr("GN", [cfg.SI, cfg.T], BF16)
        self.ACTV = self.dscr("ACTV", [cfg.FF, cfg.T], BF16)
        self.DTS = self.dscr("DTS", [3, 64, cfg.T], F32)
        self.TOTS = self.dscr("TOTS", [64, cfg.NCH], F32)
        self.rDTS = R("DTS")
        self.rXR, self.rYAG, self.rZs, self.rGT, self.rXBC, self.rY, self.rGN, self.rACTV = [R(n) for n in (
            "XR", "YAG", "Zs", "GT", "XBC", "Y", "GN", "ACTV")]

    def setup(self, top):
        cfg, sy = self.cfg, self.sy
        self.ident_f = top.sb("ident_f", [128, 128], F32)
        self.ident_b = top.sb("ident_b", [128, 128], BF16)
        self.ones_f = top.sb("ones_f", [128, 128], F32)
        self.ones_b = top.sb("ones_b", [128, 128], BF16)
        self.negi = top.sb("negi", [128, 128], BF16)
        self.maskf = top.sb("maskf", [128, 2, 128], F32)
        self.maskb = top.sb("maskb", [128, 2, 128], BF16)
        self.rC = R("consts")
        self.rCm = R("constsm")
        rC = self.rC
        sy.dma(self.ident_f[:], self.i["c_ident"], rC, writes=[rC])
        sy.dma(self.maskf[:], self.i["c_mask"].rearrange("d s l -> s d l"), self.rCm, writes=[self.rCm])
        sy.op("dve", lambda e: e.tensor_copy(out=self.ident_b[:], in_=self.ident_f[:]), reads=[rC], writes=[rC])
        sy.op("dve", lambda e: e.tensor_copy(out=self.maskb[:], in_=self.maskf[:]), reads=[self.rCm], writes=[self.rCm])
        sy.op("dve", lambda e: e.tensor_scalar(out=self.negi[:], in0=self.ident_f[:], scalar1=-30000.0, scalar2=None, op0=ALU.mult),
              reads=[rC], writes=[rC])
        sy.op("dve", lambda e: e.memset(self.ones_f[:], 1.0), writes=[rC])
        sy.op("dve", lambda e: e.memset(self.ones_b[:], 1.0), writes=[rC])
        self.sc = top.sb("sc", [128, cfg.KD, 2], F32)
        self.rSC = R("sc")
        sy.dma(self.sc[:], self.i["cvec"], self.rSC, writes=[self.rSC])
        sy.op("act", lambda e: e.activation(out=self.sc[:], in_=self.sc[:], func=AF.Silu), reads=[self.rSC], writes=[self.rSC])
        self.vec = top.sb("vec", [128, cfg.NV], F32)
        self.rVEC = R("vec")
        self.mod = top.sb("mod", [128, 6 * cfg.KD, 2], F32)
        self.rMOD = R("mod")
        self.g1 = top.sb("g1", [128, cfg.KD, 2], F32)
        self.g2 = top.sb("g2", [128, cfg.KD, 2], F32)
        self.c8 = top.sb("c8", [128, 2 * cfg.LWC], F32)
        self.aneg = top.sb("aneg", [64, 1], F32)
        self.PS = [top.st.enter_context(self.nc.psum_tensor("ps%d" % i, [128, 512], F32)) for i in range(7)]
        self.rPS = [R("ps%d" % i) for i in range(7)]
        self.PSB = top.st.enter_context(self.nc.psum_tensor("psb", [128, 1024], BF16))
        self.rPSB = R("psb")
        self.ps_i = 0
        self.wst = [top.sb("wst%d" % i, [128, 8, 256], F32) for i in range(2)]
        self.rWST = [R("wst%d" % i) for i in range(2)]
        self.wst_i = 0
        sy.barrier()
        sy.emit()

    def vcol(self, name, j=0, n=1, rows=128):
        o = self.cfg.voff[name] + j
        return self.vec[:rows, o:o + n]

    def nextps(self, k=4):
        i = self.ps_i % k
        self.ps_i += 1
        return self.PS[i], self.rPS[i]

    def load_w(self, dst, rdst, src2d, K, C, rows_last=128):
        sy = self.sy
        for k0 in range(0, K, 8):
            kn = min(8, K - k0)
            for c0 in range(0, C, 256):
                cn = min(256, C - c0)
                bi = self.wst_i
                self.wst_i ^= 1
                ws, rw = self.wst[bi], self.rWST[bi]
                src = src2d[k0 * 128:(k0 + kn) * 128, c0:c0 + cn].rearrange("(kc p) c -> p kc c", p=128)
                sy.dma(ws[:, :kn, :cn], src, rw, writes=[rw])
                sy.op("pool", lambda e, ws=ws, kn=kn, cn=cn, k0=k0, c0=c0: e.tensor_copy(
                    out=dst[:, k0:k0 + kn, c0:c0 + cn], in_=ws[:, :kn, :cn]), reads=[rw], writes=[rdst])

    def mm_group(self, ps_ap, pairs, reads, rps):
        def fn(e):
            ins = None
            n = len(pairs)
            for i, (a, b) in enumerate(pairs):
                ins = e.matmul(ps_ap, a, b, start=(i == 0), stop=(i == n - 1))
            return ins
        self.sy.op("pe", fn, reads=reads, writes=[rps])

    def adaln(self, l):
        cfg, sy = self.cfg, self.sy
        KD = cfg.KD
        with Scope(self) as ss:
            adaw = [ss.sb("adaw", [128, KD, 512], F32) for i in range(2)]
            rA = [R("adaw%d" % i) for i in range(2)]
            sy.dma(self.vec[:], self.i["vecs"][l], self.rVEC, writes=[self.rVEC])
            psb, rps = self.PS[6], self.rPS[6]
            ncol = 6 * cfg.D
            bi = 0
            for g0 in range(0, ncol, 512):
                gw = min(512, ncol - g0)
                wt, rw = adaw[bi], rA[bi]
                bi ^= 1
                src = self.i["ada_w"][l][:, g0:g0 + gw].rearrange("(kc p) c -> p kc c", p=128)
                sy.dma(wt[:, :, :gw], src, rw, writes=[rw])
                for oc in range(gw // 128):
                    o = g0 // 128 + oc
                    self.mm_group(psb[:, 2 * o:2 * o + 2],
                                  [(wt[:, kc, oc * 128:(oc + 1) * 128], self.sc[:, kc, :]) for kc in range(KD)],
                                  [rw, self.rSC], rps)
            ab = self.vcol("adab", 0, 6 * KD)
            sy.op("dve", lambda e: e.tensor_tensor(out=self.mod[:], in0=psb[:, 0:12 * KD].rearrange("p (o s) -> p o s", s=2),
                                                   in1=ab.unsqueeze(2).to_broadcast([128, 6 * KD, 2]), op=ALU.add),
                  reads=[rps, self.rVEC], writes=[self.rMOD])
            for g, wname, m in ((self.g1, "nmw", 1), (self.g2, "nfw", 4)):
                def fn(e, g=g, wname=wname, m=m):
                    return e.scalar_tensor_tensor(out=g[:], in0=self.mod[:, m * KD:(m + 1) * KD, :], scalar=1.0,
                                                  in1=self.vcol(wname, 0, KD).unsqueeze(2).to_broadcast([128, KD, 2]),
                                                  op0=ALU.add, op1=ALU.mult)
                sy.op("dve", fn, reads=[self.rMOD, self.rVEC], writes=[self.rMOD])
            n8 = 2 * cfg.LWC
            sy.op("act", lambda e: e.activation(out=self.c8[:], in_=self.vcol("llam", 0, n8), func=AF.Exp, scale=-1.0),
                  reads=[self.rVEC], writes=[self.rMOD])
            sy.op("act", lambda e: e.activation(out=self.c8[:], in_=self.c8[:], func=AF.Ln, bias=1.0), reads=[self.rMOD], writes=[self.rMOD])
            sy.op("dve", lambda e: e.tensor_scalar(out=self.c8[:], in0=self.c8[:], scalar1=-8.0, scalar2=None, op0=ALU.mult),
                  reads=[self.rMOD], writes=[self.rMOD])
            sy.op("act", lambda e: e.activation(out=self.aneg[:], in_=self.vcol("alog", 0, 1, rows=64), func=AF.Exp),
                  reads=[self.rVEC], writes=[self.rMOD])
            sy.op("dve", lambda e: e.tensor_scalar(out=self.aneg[:], in0=self.aneg[:], scalar1=-1.0, scalar2=None, op0=ALU.mult),
                  reads=[self.rMOD], writes=[self.rMOD])

    def norm_mod(self, first, g, shift_m, H, rH):
        cfg, sy = self.cfg, self.sy
        KD = cfg.KD
        with Scope(self) as ss:
            xts = [ss.sb("xt", [128, KD, cfg.TS2], F32) for i in range(2)]
            rxs = [R("xt%d" % i) for i in range(2)]
            sq = ss.sb("sq", [128, KD, cfg.TS2], F32)
            rsq = R("sq")
            rstd = ss.sb("rstd", [128, cfg.TS2], F32)
            rrs = R("rstd")
            for ti, (t0, n, s) in enumerate(cfg.tiles2):
                xt, rx = xts[ti % 2], rxs[ti % 2]
                src = (self.i["xr0"] if first else self.XR)[:, t0:t0 + n].rearrange("(kc p) t -> p kc t", p=128)
                sy.dma(xt[:, :, :n], src, rx, reads=([] if first else [self.rXR]), writes=[rx])
                self.rstd_of(xt, rx, n, sq, rsq, rstd, rrs, cfg.D)
                sy.op("dve", lambda e, xt=xt, n=n: e.tensor_tensor(out=xt[:, :, :n], in0=xt[:, :, :n],
                                                                   in1=rstd[:, :n].unsqueeze(1).to_broadcast([128, KD, n]), op=ALU.mult),
                      reads=[rx, rrs], writes=[rx])
                for kc in range(KD):
                    sy.op("dve", lambda e, xt=xt, n=n, kc=kc, t0=t0, s=s: e.tensor_scalar(
                        out=H[:, kc, t0:t0 + n], in0=xt[:, kc, :n], scalar1=g[:, kc, s:s + 1],
                        scalar2=self.mod[:, shift_m * KD + kc, s:s + 1], op0=ALU.mult, op1=ALU.add),
                        reads=[rx, self.rMOD], writes=[rH])

    def rstd_of(self, xt, rx, n, sq, rsq, rstd, rrs, nchan):
        cfg, sy = self.cfg, self.sy
        K = nchan // 128
        sy.op("act", lambda e: e.activation(out=sq[:, :K, :n], in_=xt[:, :K, :n], func=AF.Square), reads=[rx], writes=[rsq])
        psb, rps = self.PS[5], self.rPS[5]
        self.mm_group(psb[:, :n], [(self.ones_f[:], sq[:, kc, :n]) for kc in range(K)], [rsq, self.rC], rps)
        sy.op("act", lambda e: e.activation(out=rstd[:, :n], in_=psb[:, :n], func=AF.Sqrt, scale=1.0 / nchan, bias=1e-6),
              reads=[rps], writes=[rrs])
        sy.op("dve", lambda e: e.reciprocal(out=rstd[:, :n], in_=rstd[:, :n]), reads=[rrs], writes=[rrs])

    def proj_tile(self, W, rW, H, rH, t0, n, M=128):
        KD = self.cfg.KD
        ps, rps = self.nextps()
        self.mm_group(ps[:M, :n], [(W[:, kc, :M], H[:, kc, t0:t0 + n]) for kc in range(KD)], [rW, rH], rps)
        return ps, rps

    def seqpos(self, t):
        return t + 2 if t < self.cfg.CTX else t + 5

    def make_diag(self, DG, rDG, name, ntap, nch, j):
        for k in range(ntap):
            self.sy.op("dve", lambda e, k=k: e.tensor_scalar(out=DG[:, k, :], in0=self.ident_f[:], scalar1=self.vcol(name, k * nch + j, 1),
                                                             scalar2=None, op0=ALU.mult), reads=[self.rC, self.rVEC], writes=[rDG])

    def conv1d_tile(self, DG, rDG, SEQB, rSEQB, t0, n):
        ps, rps = self.nextps()
        p0 = self.seqpos(t0)
        self.mm_group(ps[:, :n], [(DG[:, k, :], SEQB[:, p0 + k - 2:p0 + k - 2 + n]) for k in range(4)], [rDG, rSEQB], rps)
        return ps, rps

    def gelu_from(self, ss_tmp, rtmp, src_ap, rsrc, out_ap, rout, n, bias=None, mul=None, rmul=None):
        sy = self.sy
        x, x2, u = ss_tmp[:, 0, :n], ss_tmp[:, 1, :n], ss_tmp[:, 2, :n]
        if bias is not None:
            sy.op("act", lambda e: e.activation(out=x, in_=src_ap, func=AF.Identity, bias=bias), reads=[rsrc, self.rVEC], writes=[rtmp])
        else:
            sy.op("act", lambda e: e.activation(out=x, in_=src_ap, func=AF.Identity), reads=[rsrc], writes=[rtmp])
        sy.op("pool", lambda e: e.tensor_tensor(out=x2, in0=x, in1=x, op=ALU.mult), reads=[rtmp], writes=[rtmp])
        sy.op("dve", lambda e: e.tensor_scalar(out=x2, in0=x2, scalar1=0.044715, scalar2=1.0, op0=ALU.mult, op1=ALU.add), reads=[rtmp], writes=[rtmp])
        sy.op("pool", lambda e: e.tensor_tensor(out=u, in0=x2, in1=x, op=ALU.mult), reads=[rtmp], writes=[rtmp])
        sy.op("act", lambda e: e.activation(out=u, in_=u, func=AF.Sigmoid, scale=GELU_C), reads=[rtmp], writes=[rtmp])
        if mul is None:
            sy.op("dve", lambda e: e.tensor_tensor(out=out_ap, in0=u, in1=x, op=ALU.mult), reads=[rtmp], writes=[rout])
        else:
            sy.op("dve", lambda e: e.tensor_tensor(out=u, in0=u, in1=x, op=ALU.mult), reads=[rtmp], writes=[rtmp])
            sy.op("dve", lambda e: e.tensor_tensor(out=out_ap, in0=u, in1=mul, op=ALU.mult), reads=[rtmp, rmul], writes=[rout])

    def lru(self, l, H, rH):
        cfg, sy = self.cfg, self.sy
        KD, T, CTX = cfg.KD, cfg.T, cfg.CTX
        with Scope(self) as ss:
            W = ss.sb("wl", [128, KD, 256], BF16)
            rW = R("wl")
            SEQB = ss.sb("seqb", [128, T + 6], BF16)
            rSEQB = R("seqb")
            Ub = ss.sb("ub", [128, T], BF16)
            rU = R("u")
            A = ss.sb("a", [128, T], F32)
            B0 = ss.sb("b0", [128, T], F32)
            B1 = ss.sb("b1", [128, T], F32)
            rA, rB0, rB1 = R("a"), R("b0"), R("b1")
            YG = ss.sb("yg", [128, T], BF16)
            rYG = R("yg")
            DG = ss.sb("dg", [128, 4, 128], BF16)
            rDG = R("dg")
            GST = ss.sb("gst", [128, 4, 128], F32)
            rGST = R("gst")
            GW = ss.sb("gw", [128, 4, 128], BF16)
            rGW = R("gw")
            TMP = ss.sb("tmp", [128, 3, cfg.TS], F32)
            rTMP = R("tmp")
            RT = ss.sb("rt", [128, 4, cfg.TS], F32)
            rRT = R("rt")
            sy.op("dve", lambda e: e.memset(SEQB[:], 0.0), writes=[rSEQB])
            for j in range(cfg.LWC):
                wsrc = self.i["w_in"][l]
                self.load_w(W[:, :, 0:128], rW, wsrc[:, cfg.o_lx + j * 128: cfg.o_lx + (j + 1) * 128], KD, 128)
                self.load_w(W[:, :, 128:256], rW, wsrc[:, cfg.o_lg + j * 128: cfg.o_lg + (j + 1) * 128], KD, 128)
                self.make_diag(DG, rDG, "lcw", 4, cfg.LWC, j)
                sy.op("pool", lambda e: e.memset(GST[:], 0.0), writes=[rGST])
                for d in range(2):
                    for wi, wname in enumerate(("lru_wa", "lru_wx")):
                        for hh in range(2):
                            sy.dma(GST[hh * 64:(hh + 1) * 64, d * 2 + wi, hh * 64:(hh + 1) * 64],
                                   self.i[wname][l, d, 2 * j + hh], rGST, writes=[rGST])
                sy.op("pool", lambda e: e.tensor_copy(out=GW[:], in_=GST[:]), reads=[rGST], writes=[rGW])
                for (t0, n, s) in cfg.tiles:
                    ps, rps = self.proj_tile(W[:, :, 0:128], rW, H, rH, t0, n)
                    p0 = self.seqpos(t0)
                    sy.op("act", lambda e, ps=ps, p0=p0, n=n: e.activation(out=SEQB[:, p0:p0 + n], in_=ps[:, :n], func=AF.Copy),
                          reads=[rps], writes=[rSEQB])
                for (t0, n, s) in cfg.tiles:
                    ps, rps = self.conv1d_tile(DG, rDG, SEQB, rSEQB, t0, n)
                    lcb = self.vcol("lcb", j, 1)
                    sy.op("act", lambda e, ps=ps, t0=t0, n=n, lcb=lcb: e.activation(out=Ub[:, t0:t0 + n], in_=ps[:, :n], func=AF.Identity,
                                                                                    bias=lcb), reads=[rps, self.rVEC], writes=[rU])
                for d in range(2):
                    Bd, rBd = (B0, rB0) if d == 0 else (B1, rB1)
                    for (t0, n, s) in cfg.tiles:
                        psr, rpsr = self.nextps()
                        self.mm_group(psr[:, :n], [(GW[:, 2 * d, :], Ub[:, t0:t0 + n])], [rGW, rU], rpsr)
                        psi, rpsi = self.nextps()
                        self.mm_group(psi[:, :n], [(GW[:, 2 * d + 1, :], Ub[:, t0:t0 + n])], [rGW, rU], rpsi)
                        r_, i_, a2, sq = RT[:, 0, :n], RT[:, 1, :n], RT[:, 2, :n], RT[:, 3, :n]
                        lba = self.vcol("lba", d * cfg.LWC + j, 1)
                        lbx = self.vcol("lbx", d * cfg.LWC + j, 1)
                        sy.op("act", lambda e, psr=psr, n=n, r_=r_, lba=lba: e.activation(out=r_, in_=psr[:, :n], func=AF.Sigmoid,
                                                                                          bias=lba),
                              reads=[rpsr, self.rVEC], writes=[rRT])
                        sy.op("act", lambda e, psi=psi, n=n, i_=i_, lbx=lbx: e.activation(out=i_, in_=psi[:, :n], func=AF.Sigmoid,
                                                                                          bias=lbx),
                              reads=[rpsi, self.rVEC], writes=[rRT])
                        c8 = self.c8[:, d * cfg.LWC + j: d * cfg.LWC + j + 1]
                        sy.op("act", lambda e, t0=t0, n=n, r_=r_, c8=c8: e.activation(out=A[:, t0:t0 + n], in_=r_, func=AF.Exp, scale=c8),
                              reads=[rRT, self.rMOD], writes=[rA])
                        sy.op("pool", lambda e, t0=t0, n=n, a2=a2: e.tensor_tensor(out=a2, in0=A[:, t0:t0 + n], in1=A[:, t0:t0 + n], op=ALU.mult),
                              reads=[rA], writes=[rRT])
                        sy.op("dve", lambda e, a2=a2: e.tensor_scalar(out=a2, in0=a2, scalar1=-1.0, scalar2=1.0, op0=ALU.mult, op1=ALU.add),
                              reads=[rRT], writes=[rRT])
                        sy.op("act", lambda e, a2=a2, sq=sq: e.activation(out=sq, in_=a2, func=AF.Sqrt), reads=[rRT], writes=[rRT])
                        sy.op("pool", lambda e, t0=t0, n=n, i_=i_: e.tensor_tensor(out=i_, in0=i_, in1=Ub[:, t0:t0 + n], op=ALU.mult),
                              reads=[rRT, rU], writes=[rRT])
                        sy.op("dve", lambda e, t0=t0, n=n, i_=i_, sq=sq, Bd=Bd: e.tensor_tensor(out=Bd[:, t0:t0 + n], in0=sq, in1=i_, op=ALU.mult),
                              reads=[rRT], writes=[rBd])
                    if d == 0:
                        sy.op("dve", lambda e: e.tensor_tensor_scan(out=B0[:, :], data0=A[:, :], data1=B0[:, :], initial=0.0,
                                                                    op0=ALU.mult, op1=ALU.add), reads=[rA, rB0], writes=[rB0])
                    else:
                        sy.op("dve", lambda e: e.tensor_tensor_scan(out=B1[:, 0:CTX][:, ::-1], data0=A[:, 0:CTX][:, ::-1],
                                                                    data1=B1[:, 0:CTX][:, ::-1], initial=0.0, op0=ALU.mult, op1=ALU.add),
                              reads=[rA, rB1], writes=[rB1])
                        sy.op("dve", lambda e: e.tensor_tensor_scan(out=B1[:, CTX:T][:, ::-1], data0=A[:, CTX:T][:, ::-1],
                                                                    data1=B1[:, CTX:T][:, ::-1], initial=B1[:, 0:1], op0=ALU.mult, op1=ALU.add),
                              reads=[rA, rB1], writes=[rB1])
                sy.op("pool", lambda e: e.tensor_tensor(out=B0[:], in0=B0[:], in1=B1[:], op=ALU.add), reads=[rB0, rB1], writes=[rB0])
                for (t0, n, s) in cfg.tiles:
                    ps, rps = self.proj_tile(W[:, :, 128:256], rW, H, rH, t0, n)
                    self.gelu_from(TMP, rTMP, ps[:, :n], rps, YG[:, t0:t0 + n], rYG, n, mul=B0[:, t0:t0 + n], rmul=rB0)
                sy.dma(self.YAG[j * 128:(j + 1) * 128, :], YG[:], rYG, reads=[rYG], writes=[self.rYAG])


    def ssd_prep(self, l, H, rH):
        cfg, sy = self.cfg, self.sy
        KD, T = cfg.KD, cfg.T
        wsrc = self.i["w_in"][l]
        with Scope(self) as ss:
            Ws = [ss.sb("wp", [128, KD, 128], BF16) for i in range(2)]
            rWs = [R("wp%d" % i) for i in range(2)]
            OB = [ss.sb("ob", [128, T], BF16) for i in range(2)]
            rOB = [R("ob%d" % i) for i in range(2)]
            SEQB = ss.sb("seqb", [128, T + 6], BF16)
            rSEQB = R("seqb")
            DG = ss.sb("dg", [128, 4, 128], BF16)
            rDG = R("dg")
            sy.op("dve", lambda e: e.memset(SEQB[:], 0.0), writes=[rSEQB])
            k = 0
            jobs = [("z", cfg.o_z, j, self.Zs, self.rZs, AF.Silu) for j in range(cfg.SIC)]
            jobs += [("gt", cfg.o_gt, j, self.GT, self.rGT, AF.Sigmoid) for j in range(2 * KD)]
            jobs += [("xbc", cfg.o_xbc, j, self.XBC, self.rXBC, AF.Silu) for j in range(cfg.CDC)]
            for (kind, o, j, dst, rdst, func) in jobs:
                W, rW = Ws[k % 2], rWs[k % 2]
                ob, rob = OB[k % 2], rOB[k % 2]
                k += 1
                self.load_w(W, rW, wsrc[:, o + j * 128:o + (j + 1) * 128], KD, 128)
                if kind != "xbc":
                    for (t0, n, s) in cfg.tiles:
                        ps, rps = self.proj_tile(W, rW, H, rH, t0, n)
                        sy.op("act", lambda e, ps=ps, t0=t0, n=n, ob=ob, func=func: e.activation(out=ob[:, t0:t0 + n], in_=ps[:, :n], func=func),
                              reads=[rps], writes=[rob])
                else:
                    self.make_diag(DG, rDG, "scw", 4, cfg.CDC, j)
                    for (t0, n, s) in cfg.tiles:
                        ps, rps = self.proj_tile(W, rW, H, rH, t0, n)
                        p0 = self.seqpos(t0)
                        sy.op("act", lambda e, ps=ps, p0=p0, n=n: e.activation(out=SEQB[:, p0:p0 + n], in_=ps[:, :n], func=AF.Copy),
                              reads=[rps], writes=[rSEQB])
                    scb = self.vcol("scb", j, 1)
                    for (t0, n, s) in cfg.tiles:
                        ps, rps = self.conv1d_tile(DG, rDG, SEQB, rSEQB, t0, n)
                        sy.op("act", lambda e, ps=ps, t0=t0, n=n, ob=ob, scb=scb: e.activation(out=ob[:, t0:t0 + n], in_=ps[:, :n], func=AF.Silu, bias=scb),
                              reads=[rps, self.rVEC], writes=[rob])
                sy.dma(dst[j * 128:(j + 1) * 128, :], ob[:], rob, reads=[rob], writes=[rdst])
        with Scope(self) as ss:
            CS = ss.sb("cs", [64, T], F32)
            V = ss.sb("v", [64, T], F32)
            WEXP = ss.sb("wexp", [64, T], F32)
            TOT = ss.sb("tot", [64, cfg.NCH], F32)
            rDT = R("dtp")
            WDT = ss.sb("wdt", [128, KD, 64], BF16)
            rWDT = R("wdt")
            X1 = ss.sb("x1", [64, T], F32)
            RM = WEXP
            rX1, rRM = R("x1"), rDT
            NH = cfg.NH
            sy.op("pool", lambda e: e.memset(WDT[:], 0.0), writes=[rWDT])
            self.load_w(WDT[:, :, 0:NH], rWDT, wsrc[:, cfg.o_dt:cfg.o_dt + NH], KD, NH)
            self.load_w(WDT[:, :, 32:32 + NH], rWDT, wsrc[:, cfg.o_dt + NH:cfg.o_dt + 2 * NH], KD, NH)
            sy.dma(RM[0:32, :], self.i["c_rmask"][0, 0:32, :], rRM, writes=[rRM])
            sy.dma(RM[32:64, :], self.i["c_rmask"][1, 32:64, :], rRM, writes=[rRM])
            dtb = self.vcol("dtb", 0, 1, rows=64)
            for (t0, n, s) in cfg.tiles:
                ps, rps = self.proj_tile(WDT, rWDT, H, rH, t0, n, M=64)
                sy.op("act", lambda e, ps=ps, t0=t0, n=n: e.activation(out=X1[:, t0:t0 + n], in_=ps[:64, :n], func=AF.Exp, bias=dtb),
                      reads=[rps, self.rVEC], writes=[rX1])
            X2, X3 = V, CS
            sy.op("act", lambda e: e.activation(out=X1[:], in_=X1[:], func=AF.Ln, bias=1.0), reads=[rX1], writes=[rX1])
            sy.op("act", lambda e: e.activation(out=X2[:], in_=X1[:], func=AF.Ln), reads=[rX1], writes=[rDT])
            sy.op("dve", lambda e: e.tensor_scalar(out=X3[:], in0=X1[:], scalar1=self.aneg[:, 0:1], scalar2=None, op0=ALU.mult),
                  reads=[rX1, self.rMOD], writes=[rDT])
            sy.op("dve", lambda e: e.tensor_tensor_scan(out=X3[0:32, :], data0=RM[0:32, :], data1=X3[0:32, :], initial=0.0,
                                                        op0=ALU.mult, op1=ALU.add), reads=[rRM, rDT], writes=[rDT])
            sy.op("dve", lambda e: e.tensor_tensor_scan(out=X3[32:64, :][:, ::-1], data0=RM[32:64, :][:, ::-1], data1=X3[32:64, :][:, ::-1],
                                                        initial=0.0, op0=ALU.mult, op1=ALU.add), reads=[rRM, rDT], writes=[rDT])
            sy.op("dve", lambda e: e.tensor_tensor(out=V[:], in0=X2[:], in1=CS[:], op=ALU.subtract), reads=[rDT], writes=[rDT])
            CS3 = CS[:].rearrange("p (c q) -> p c q", q=128)
            V3 = V[:].rearrange("p (c q) -> p c q", q=128)
            W3 = WEXP[:].rearrange("p (c q) -> p c q", q=128)
            sy.op("dve", lambda e: e.tensor_copy(out=TOT[0:32, :], in_=CS3[0:32, :, 127]), reads=[rDT], writes=[rDT])
            sy.op("dve", lambda e: e.tensor_copy(out=TOT[32:64, :], in_=CS3[32:64, :, 0]), reads=[rDT], writes=[rDT])
            sy.op("dve", lambda e: e.tensor_tensor(out=W3, in0=V3, in1=TOT[:].unsqueeze(2).to_broadcast([64, cfg.NCH, 128]), op=ALU.add),
                  reads=[rDT], writes=[rDT])
            sy.op("act", lambda e: e.activation(out=WEXP[:], in_=WEXP[:], func=AF.Exp), reads=[rDT], writes=[rDT])
            for i_, t_ in enumerate((CS, V, WEXP)):
                sy.dma(self.DTS[i_], t_[:], rDT, reads=[rDT], writes=[self.rDTS])
            sy.dma(self.TOTS, TOT[:], rDT, reads=[rDT], writes=[self.rDTS])

    def ssd_sweep(self, d):
        cfg, sy = self.cfg, self.sy
        HPG, NPR, NG, SIC = cfg.HPG, cfg.HPG // 2, cfg.NG, cfg.SIC
        UB = min(4, HPG)
        nctx = cfg.CTX // 128
        order = list(range(cfg.NCH)) if d == 0 else (list(range(nctx - 1, -1, -1)) + list(range(cfg.NCH - 1, nctx - 1, -1)))
        with Scope(self) as ss:
            CS = ss.sb("cs", [64, cfg.T], F32)
            V = ss.sb("v", [64, cfg.T], F32)
            WEXP = ss.sb("wexp", [64, cfg.T], F32)
            TOT = ss.sb("tot", [64, cfg.NCH], F32)
            rDT = R("dts")
            for i_, t_ in enumerate((CS, V, WEXP)):
                sy.dma(t_[:], self.DTS[i_], rDT, reads=[self.rDTS], writes=[rDT])
            sy.dma(TOT[:], self.TOTS, rDT, reads=[self.rDTS], writes=[rDT])
            SEL = ss.sb("sel", [64, 64, 128], F32)
            rSEL = R("sel")
            sy.dma(SEL[:], self.i["c_sel"], rSEL, writes=[rSEL])
            BC = [ss.sb("bc", [128, 2, 128], BF16) for i in range(2)]
            XF = [ss.sb("xf", [128, NPR, 128], BF16) for i in range(2)]
            YL = [ss.sb("yl", [128, NPR, 128], F32) for i in range(2)]
            rIN = [R("ssdin%d" % i) for i in range(2)]
            rYL = [R("yl%d" % i) for i in range(2)]
            XT2 = ss.sb("xt2", [128, HPG, 128], BF16)
            BT = ss.sb("bt", [128, 128], BF16)
            WT = ss.sb("wt", [128, 32], F32)
            XW = ss.sb("xw", [128, HPG, 64], BF16)
            CBT = ss.sb("cbt", [128, 128], F32)
            rXT2, rBT, rWT, rXW, rCBT = R("xt2"), R("bt"), R("wt"), R("xw"), R("cbt")
            MT = ss.sb("mt", [128, UB, 128], F32)
            EC = ss.sb("ec", [128, UB, 128], F32)
            rMT, rEC = R("mt"), R("ec")
            MT2 = ss.sb("mt2", [128, HPG, 128], BF16)
            CSC = ss.sb("csc", [128, HPG, 128], BF16)
            rMT2, rCSC = R("mt2"), R("csc")
            YS = [ss.sb("ys", [128, NPR, 128], F32) for i in range(2)]
            rYS = [R("ys%d" % i) for i in range(2)]
            ET = ss.sb("et", [128, HPG], F32)
            rET = R("et")
            P32 = ss.sb("p32", [128, HPG, 64], F32)
            PB = ss.sb("pb", [128, HPG, 128], BF16)
            rP32, rPB = R("p32"), R("pb")
            PSe = [(self.PS[0], self.rPS[0]), (self.PS[1], self.rPS[1])]
            PSc = [(self.PS[2], self.rPS[2]), (self.PS[3], self.rPS[3])]
            PSy, rPSy = self.PS[4], self.rPS[4]
            PSs, rPSs = self.PS[5], self.rPS[5]
            PSm, rPSm = self.PS[6], self.rPS[6]
            PSB, rPSB = self.PSB, self.rPSB
            PSBv = PSB[:, 0:HPG * 64].rearrange("p (h q) -> p h q", q=64)
            mk = self.maskb[:, d, :]
            sy.op("dve", lambda e: e.memset(XT2[:], 0.0), writes=[rXT2])
            it = 0
            for g in range(NG):
                sy.op("dve", lambda e: e.memset(P32[:], 0.0), writes=[rP32])
                sy.op("dve", lambda e: e.memset(PB[:], 0.0), writes=[rPB])
                for c in order:
                    tok = c * 128
                    bi = it % 2
                    it += 1
                    bc, xf, yl, rin, ryl, ys, rys = BC[bi], XF[bi], YL[bi], rIN[bi], rYL[bi], YS[bi], rYS[bi]
                    for w_, ch in ((0, SIC + g), (1, SIC + NG + g)):
                        sy.dma(bc[:, w_, :], self.XBC[ch * 128:(ch + 1) * 128, tok:tok + 128], rin, reads=[self.rXBC], writes=[rin])
                    sy.dma(xf[:], self.XBC[g * NPR * 128:(g + 1) * NPR * 128, tok:tok + 128].rearrange("(q p) t -> p q t", p=128),
                           rin, reads=[self.rXBC], writes=[rin])
                    ydst = self.Y[g * NPR * 128:(g + 1) * NPR * 128, tok:tok + 128].rearrange("(q p) t -> p q t", p=128)
                    if d == 1:
                        sy.dma(yl[:], ydst, ryl, reads=[self.rY], writes=[ryl])
                    def tfn(e, xf=xf, bc=bc):
                        ins = None
                        for q in range(NPR):
                            ins = e.transpose(PSB[:, q * 128:(q + 1) * 128], xf[:, q, :], self.ident_b[:])
                        ins = e.transpose(PSB[:, NPR * 128:(NPR + 1) * 128], bc[:, 0, :], self.ident_b[:])
                        return ins
                    sy.op("pe", tfn, reads=[rin, self.rC], writes=[rPSB])
                    sy.op("dve", lambda e: e.tensor_copy(out=XT2[:, 0::2, 0:64], in_=PSBv[:, 0::2, :]), reads=[rPSB], writes=[rXT2])
                    sy.op("dve", lambda e: e.tensor_copy(out=XT2[:, 1::2, 64:128], in_=PSBv[:, 1::2, :]), reads=[rPSB], writes=[rXT2])
                    sy.op("pool", lambda e: e.tensor_copy(out=BT[:], in_=PSB[:, NPR * 128:(NPR + 1) * 128]), reads=[rPSB], writes=[rBT]) \
                        if False else sy.op("dve", lambda e: e.tensor_copy(out=BT[:], in_=PSB[:, NPR * 128:(NPR + 1) * 128]), reads=[rPSB], writes=[rBT])
                    sy.op("pe", lambda e, tok=tok: e.transpose(PSm[:, 128:160], WEXP[32 * d:32 * d + 32, tok:tok + 128],
                                                               self.ident_f[32 * d:32 * d + 32, 32 * d:32 * d + 32]),
                          reads=[rDT, self.rC], writes=[rPSm])
                    sy.op("dve", lambda e: e.tensor_copy(out=WT[:], in_=PSm[:, 128:160]), reads=[rPSm], writes=[rWT])
                    sy.op("dve", lambda e, g=g: e.tensor_tensor(out=XW[:], in0=PSBv, in1=WT[:, g * HPG:(g + 1) * HPG].unsqueeze(2).to_broadcast([128, HPG, 64]),
                                                                op=ALU.mult), reads=[rPSB, rWT], writes=[rXW])
                    self.mm_group(PSm[:, 0:128], [(bc[:, 0, :], bc[:, 1, :])], [rin], rPSm)
                    sy.op("act", lambda e: e.activation(out=CBT[:], in_=PSm[:, 0:128], func=AF.Copy), reads=[rPSm], writes=[rCBT])
                    for hb in range(0, HPG, UB):
                        pse, rpse = PSe[(hb // UB) % 2]
                        psc, rpsc = PSc[(hb // UB) % 2]

                        def efn(e, hb=hb, pse=pse, tok=tok, g=g):
                            ins = None
                            for u in range(UB):
                                hd = 32 * d + g * HPG + hb + u
                                o = pse[:, u * 128:(u + 1) * 128]
                                e.matmul(o, SEL[:, hd, :], CS[:, tok:tok + 128], start=True, stop=False)
                                e.matmul(o, V[:, tok:tok + 128], SEL[:, hd, :], start=False, stop=False)
                                ins = e.matmul(o, self.negi[:], mk, start=False, stop=True)
                            return ins
                        sy.op("pe", efn, reads=[rSEL, rDT, self.rC, self.rCm], writes=[rpse])

                        def cfn(e, hb=hb, psc=psc, tok=tok, g=g):
                            ins = None
                            for u in range(UB):
                                hd = 32 * d + g * HPG + hb + u
                                ins = e.matmul(psc[:, u * 128:(u + 1) * 128], SEL[:, hd, :], CS[:, tok:tok + 128], start=True, stop=True)
                            return ins
                        sy.op("pe", cfn, reads=[rSEL, rDT], writes=[rpsc])
                        sy.op("act", lambda e, pse=pse: e.activation(out=MT[:], in_=pse[:, 0:UB * 128].rearrange("p (u l) -> p u l", l=128), func=AF.Exp),
                              reads=[rpse], writes=[rMT])
                        sy.op("dve", lambda e, hb=hb: e.tensor_tensor(out=MT2[:, hb:hb + UB, :], in0=MT[:],
                                                                      in1=CBT[:].unsqueeze(1).to_broadcast([128, UB, 128]), op=ALU.mult),
                              reads=[rMT, rCBT], writes=[rMT2])
                        sy.op("act", lambda e, psc=psc: e.activation(out=EC[:], in_=psc[:, 0:UB * 128].rearrange("p (u l) -> p u l", l=128), func=AF.Exp),
                              reads=[rpsc], writes=[rEC])
                        sy.op("dve", lambda e, hb=hb, bc=bc: e.tensor_tensor(out=CSC[:, hb:hb + UB, :], in0=EC[:],
                                                                              in1=bc[:, 1, :].unsqueeze(1).to_broadcast([128, UB, 128]), op=ALU.mult),
                              reads=[rEC, rin], writes=[rCSC])
                    def yfn(e):
                        ins = None
                        for q in range(NPR):
                            o = PSy[:, q * 128:(q + 1) * 128]
                            e.matmul(o, XT2[:, 2 * q, :], MT2[:, 2 * q, :], start=True, stop=False)
                            e.matmul(o, XT2[:, 2 * q + 1, :], MT2[:, 2 * q + 1, :], start=False, stop=False)
                            e.matmul(o, PB[:, 2 * q, :], CSC[:, 2 * q, :], start=False, stop=False)
                            ins = e.matmul(o, PB[:, 2 * q + 1, :], CSC[:, 2 * q + 1, :], start=False, stop=True)
                        return ins
                    sy.op("pe", yfn, reads=[rXT2, rMT2, rPB, rCSC], writes=[rPSy])
                    psyv = PSy[:, 0:NPR * 128].rearrange("p (q l) -> p q l", l=128)
                    if d == 0:
                        sy.op("act", lambda e, ys=ys: e.activation(out=ys[:], in_=psyv, func=AF.Copy), reads=[rPSy], writes=[rys])
                    else:
                        sy.op("dve", lambda e, ys=ys, yl=yl: e.tensor_tensor(out=ys[:], in0=psyv, in1=yl[:], op=ALU.add), reads=[rPSy, ryl], writes=[rys])
                    sy.dma(ydst, ys[:], rys, reads=[rys], writes=[self.rY])
                    self.mm_group(PSs[:, 0:HPG * 64], [(BT[:], XW[:].rearrange("p h q -> p (h q)"))], [rBT, rXW], rPSs)

                    def tfn2(e, c=c, g=g):
                        ins = None
                        for hh in range(HPG):
                            hd = 32 * d + g * HPG + hh
                            ins = e.matmul(PSm[:, 160 + hh:161 + hh], SEL[:, hd, :], TOT[:, c:c + 1], start=True, stop=True)
                        return ins
                    sy.op("pe", tfn2, reads=[rSEL, rDT], writes=[rPSm])
                    sy.op("act", lambda e: e.activation(out=ET[:], in_=PSm[:, 160:160 + HPG], func=AF.Exp), reads=[rPSm], writes=[rET])
                    sy.op("dve", lambda e: e.tensor_tensor(out=P32[:], in0=P32[:], in1=ET[:].unsqueeze(2).to_broadcast([128, HPG, 64]), op=ALU.mult),
                          reads=[rP32, rET], writes=[rP32])
                    sy.op("dve", lambda e: e.tensor_tensor(out=P32[:], in0=P32[:], in1=PSs[:, 0:HPG * 64].rearrange("p (h q) -> p h q", q=64), op=ALU.add),
                          reads=[rP32, rPSs], writes=[rP32])
                    sy.op("dve", lambda e: e.tensor_copy(out=PB[:, 0::2, 0:64], in_=P32[:, 0::2, :]), reads=[rP32], writes=[rPB])
                    sy.op("dve", lambda e: e.tensor_copy(out=PB[:, 1::2, 64:128], in_=P32[:, 1::2, :]), reads=[rP32], writes=[rPB])


    def ssd_out(self, l):
        cfg, sy = self.cfg, self.sy
        SIC, NG, TS = cfg.SIC, cfg.NG, cfg.TS2
        CPG = SIC // NG
        GS = cfg.SI // NG
        with Scope(self) as ss:
            YT = [ss.sb("yt", [128, SIC, TS], F32) for i in range(2)]
            ZT = [ss.sb("zt", [128, SIC, TS], BF16) for i in range(2)]
            XS = [ss.sb("xs", [128, SIC, TS], BF16) for i in range(2)]
            rI = [R("soin%d" % i) for i in range(2)]
            SQ = ss.sb("sqb", [128, SIC, TS], BF16)
            rSQ = R("sqb")
            RS = ss.sb("rs", [128, NG, TS], F32)
            rRS = R("rs")
            GNT = [ss.sb("gnt", [128, SIC, TS], BF16) for i in range(2)]
            rGNT = [R("gnt%d" % i) for i in range(2)]
            for ti, (t0, n, s) in enumerate(cfg.tiles2):
                bi = ti % 2
                yt, zt, xs, ri, gnt, rg = YT[bi], ZT[bi], XS[bi], rI[bi], GNT[bi], rGNT[bi]
                sy.dma(yt[:, :, :n], self.Y[:, t0:t0 + n].rearrange("(j p) t -> p j t", p=128), ri, reads=[self.rY], writes=[ri])
                sy.dma(zt[:, :, :n], self.Zs[:, t0:t0 + n].rearrange("(j p) t -> p j t", p=128), ri, reads=[self.rZs], writes=[ri])
                sy.dma(xs[:, :, :n], self.XBC[0:cfg.SI, t0:t0 + n].rearrange("(j p) t -> p j t", p=128), ri, reads=[self.rXBC], writes=[ri])
                for j in range(SIC):
                    sdd = self.vcol("sdd", j, 1)
                    eng = "dve"
                    sy.op(eng, lambda e, j=j, sdd=sdd, yt=yt, xs=xs, n=n: e.scalar_tensor_tensor(
                        out=yt[:, j, :n], in0=xs[:, j, :n], scalar=sdd, in1=yt[:, j, :n], op0=ALU.mult, op1=ALU.add),
                        reads=[ri, self.rVEC], writes=[ri])
                sy.op("dve", lambda e, yt=yt, zt=zt, n=n: e.tensor_tensor(out=yt[:, :, :n], in0=yt[:, :, :n], in1=zt[:, :, :n], op=ALU.mult),
                      reads=[ri], writes=[ri])
                sy.op("act", lambda e, yt=yt, n=n: e.activation(out=SQ[:, :, :n], in_=yt[:, :, :n], func=AF.Square), reads=[ri], writes=[rSQ])
                for g in range(NG):
                    ps, rps = self.nextps()
                    self.mm_group(ps[:, :n], [(self.ones_b[:], SQ[:, g * CPG + q, :n]) for q in range(CPG)], [rSQ, self.rC], rps)
                    sy.op("act", lambda e, ps=ps, g=g, n=n: e.activation(out=RS[:, g, :n], in_=ps[:, :n], func=AF.Sqrt, scale=1.0 / GS, bias=1e-6),
                          reads=[rps], writes=[rRS])
                sy.op("dve", lambda e, n=n: e.reciprocal(out=RS[:, :, :n], in_=RS[:, :, :n]), reads=[rRS], writes=[rRS])
                for j in range(SIC):
                    snw = self.vcol("snw", j, 1)
                    eng = "dve"
                    sy.op(eng, lambda e, j=j, snw=snw, yt=yt, gnt=gnt, n=n: e.scalar_tensor_tensor(
                        out=gnt[:, j, :n], in0=yt[:, j, :n], scalar=snw, in1=RS[:, j // CPG, :n], op0=ALU.mult, op1=ALU.mult),
                        reads=[ri, rRS, self.rVEC], writes=[rg])
                sy.dma(self.GN[:, t0:t0 + n].rearrange("(j p) t -> p j t", p=128), gnt[:, :, :n], rg, reads=[rg], writes=[self.rGN])

    def resid_tiles(self, first, KD):
        pass

    def merge(self, l, first):
        cfg, sy = self.cfg, self.sy
        KD, LWC, SIC, TS, D = cfg.KD, cfg.LWC, cfg.SIC, cfg.TS2, cfg.D
        with Scope(self) as ss:
            WL = ss.sb("wlp", [128, LWC, D], BF16)
            WS = ss.sb("wsp", [128, SIC, D], BF16)
            WO = ss.sb("wop", [128, KD, D], BF16)
            rWL, rWS, rWO = R("wlp"), R("wsp"), R("wop")
            self.load_w(WL, rWL, self.i["lru_proj"][l], LWC, D)
            self.load_w(WS, rWS, self.i["ssd_proj"][l], SIC, D)
            self.load_w(WO, rWO, self.i["w_out"][l], KD, D)
            YA = [ss.sb("ya", [128, LWC, TS], BF16) for i in range(2)]
            GNT = [ss.sb("gn", [128, SIC, TS], BF16) for i in range(2)]
            GTT = [ss.sb("gt", [128, 2 * KD, TS], BF16) for i in range(2)]
            XT = [ss.sb("xr", [128, KD, TS], F32) for i in range(2)]
            rI = [R("mgin%d" % i) for i in range(2)]
            rX = [R("mgx%d" % i) for i in range(2)]
            M1 = ss.sb("m1", [128, TS], F32)
            rM1 = R("m1")
            MB = ss.sb("mb", [128, KD, TS], BF16)
            rMB = R("mb")
            for ti, (t0, n, s) in enumerate(cfg.tiles2):
                bi = ti % 2
                ya, gn, gt, xt, ri, rx = YA[bi], GNT[bi], GTT[bi], XT[bi], rI[bi], rX[bi]
                sy.dma(ya[:, :, :n], self.YAG[:, t0:t0 + n].rearrange("(j p) t -> p j t", p=128), ri, reads=[self.rYAG], writes=[ri])
                sy.dma(gn[:, :, :n], self.GN[:, t0:t0 + n].rearrange("(j p) t -> p j t", p=128), ri, reads=[self.rGN], writes=[ri])
                sy.dma(gt[:, :, :n], self.GT[:, t0:t0 + n].rearrange("(j p) t -> p j t", p=128), ri, reads=[self.rGT], writes=[ri])
                xsrc = (self.i["xr0"] if first else self.XR)[:, t0:t0 + n].rearrange("(kc p) t -> p kc t", p=128)
                sy.dma(xt[:, :, :n], xsrc, rx, reads=([] if first else [self.rXR]), writes=[rx])
                for jo in range(KD):
                    pa, rpa = self.nextps()
                    self.mm_group(pa[:, :n], [(WL[:, k, jo * 128:(jo + 1) * 128], ya[:, k, :n]) for k in range(LWC)], [rWL, ri], rpa)
                    pb, rpb = self.nextps()
                    self.mm_group(pb[:, :n], [(WS[:, k, jo * 128:(jo + 1) * 128], gn[:, k, :n]) for k in range(SIC)], [rWS, ri], rpb)
                    sy.op("dve", lambda e, pa=pa, gt=gt, jo=jo, n=n: e.tensor_tensor(out=M1[:, :n], in0=pa[:, :n], in1=gt[:, jo, :n], op=ALU.mult),
                          reads=[rpa, ri], writes=[rM1])
                    sy.op("dve", lambda e, pb=pb, gt=gt, jo=jo, n=n: e.tensor_tensor(out=MB[:, jo, :n], in0=pb[:, :n], in1=gt[:, KD + jo, :n], op=ALU.mult),
                          reads=[rpb, ri], writes=[rMB])
                    sy.op("pool", lambda e, jo=jo, n=n: e.tensor_tensor(out=MB[:, jo, :n], in0=MB[:, jo, :n], in1=M1[:, :n], op=ALU.add),
                          reads=[rM1, rMB], writes=[rMB])
                for jo in range(KD):
                    po, rpo = self.nextps()
                    self.mm_group(po[:, :n], [(WO[:, k, jo * 128:(jo + 1) * 128], MB[:, k, :n]) for k in range(KD)], [rWO, rMB], rpo)
                    gate = self.mod[:, 2 * KD + jo, s:s + 1]
                    sy.op("dve", lambda e, po=po, xt=xt, jo=jo, n=n, gate=gate: e.scalar_tensor_tensor(
                        out=xt[:, jo, :n], in0=po[:, :n], scalar=gate, in1=xt[:, jo, :n], op0=ALU.mult, op1=ALU.add),
                        reads=[rpo, rx, self.rMOD], writes=[rx])
                sy.dma(self.XR[:, t0:t0 + n].rearrange("(kc p) t -> p kc t", p=128), xt[:, :, :n], rx, reads=[rx], writes=[self.rXR])

    def ffn_up(self, l, H, rH):
        cfg, sy = self.cfg, self.sy
        KD, T, CTX, SEQ, GW, FFC, TS = cfg.KD, cfg.T, cfg.CTX, cfg.SEQ, cfg.GW, cfg.FFC, cfg.TS
        ROWS = SEQ // GW
        wsrc = self.i["ffn_w_up"][l]
        with Scope(self) as ss:
            Ws = [ss.sb("wu", [128, KD, 256], BF16) for i in range(2)]
            rWs = [R("wu%d" % i) for i in range(2)]
            PADX = ss.sb("padx", [128, ROWS + 2, GW + 2], BF16)
            PADC = ss.sb("padc", [128, 3, CTX + 2], BF16)
            rPAD = R("pad")
            VB = ss.sb("vb", [128, T], BF16)
            rVB = R("vb")
            AO = [ss.sb("ao", [128, T], BF16) for i in range(2)]
            rAO = [R("ao%d" % i) for i in range(2)]
            DG = ss.sb("dg9", [128, 9, 128], BF16)
            rDG = R("dg9")
            TMP = ss.sb("tmpf", [128, 3, TS], F32)
            rTMP = R("tmpf")
            sy.op("dve", lambda e: e.memset(PADX[:], 0.0), writes=[rPAD])
            sy.op("dve", lambda e: e.memset(PADC[:], 0.0), writes=[rPAD])
            for j in range(FFC):
                W, rW = Ws[j % 2], rWs[j % 2]
                ao, rao = AO[j % 2], rAO[j % 2]
                self.load_w(W[:, :, 0:128], rW, wsrc[:, j * 128:(j + 1) * 128], KD, 128)
                self.load_w(W[:, :, 128:256], rW, wsrc[:, cfg.FF + j * 128:cfg.FF + (j + 1) * 128], KD, 128)
                self.make_diag(DG, rDG, "fcw", 9, FFC, j)
                for (t0, n, s) in cfg.tiles:
                    ps, rps = self.proj_tile(W[:, :, 0:128], rW, H, rH, t0, n)
                    if s == 1:
                        dst = PADC[:, 1, 1 + t0:1 + t0 + n]
                        src = ps[:, :n]
                    else:
                        r0 = (t0 - CTX) // GW
                        nr = n // GW
                        dst = PADX[:, 1 + r0:1 + r0 + nr, 1:1 + GW]
                        src = ps[:, :n].rearrange("p (r c) -> p r c", c=GW)
                    sy.op("act", lambda e, dst=dst, src=src: e.activation(out=dst, in_=src, func=AF.Copy), reads=[rps], writes=[rPAD])
                    ps, rps = self.proj_tile(W[:, :, 128:256], rW, H, rH, t0, n)
                    sy.op("act", lambda e, ps=ps, t0=t0, n=n: e.activation(out=VB[:, t0:t0 + n], in_=ps[:, :n], func=AF.Copy), reads=[rps], writes=[rVB])
                fcb = self.vcol("fcb", j, 1)
                for (t0, n, s) in cfg.tiles:
                    ps, rps = self.nextps()
                    pairs = []
                    for kr in range(3):
                        for kc in range(3):
                            if s == 1:
                                rhs = PADC[:, kr, kc + t0:kc + t0 + n]
                            else:
                                r0 = (t0 - CTX) // GW
                                nr = n // GW
                                rhs = PADX[:, r0 + kr:r0 + kr + nr, kc:kc + GW]
                            pairs.append((DG[:, kr * 3 + kc, :], rhs))
                    outp = ps[:, :n] if s == 1 else ps[:, :n].rearrange("p (r c) -> p r c", c=GW)
                    self.mm_group(outp, pairs, [rDG, rPAD], rps)
                    self.gelu_from(TMP, rTMP, ps[:, :n], rps, ao[:, t0:t0 + n], rao, n, bias=fcb, mul=VB[:, t0:t0 + n], rmul=rVB)
                sy.dma(self.ACTV[j * 128:(j + 1) * 128, :], ao[:], rao, reads=[rao], writes=[self.rACTV])

    def ffn_down(self, l):
        cfg, sy = self.cfg, self.sy
        KD, FFC, TS, D = cfg.KD, cfg.FFC, cfg.TS2, cfg.D
        with Scope(self) as ss:
            WD = ss.sb("wd", [128, FFC, D], BF16)
            rWD = R("wd")
            self.load_w(WD, rWD, self.i["ffn_w_down"][l], FFC, D)
            AT = [ss.sb("at", [128, FFC, TS], BF16) for i in range(2)]
            XT = [ss.sb("xr2", [128, KD, TS], F32) for i in range(2)]
            rI = [R("fdin%d" % i) for i in range(2)]
            rX = [R("fdx%d" % i) for i in range(2)]
            for ti, (t0, n, s) in enumerate(cfg.tiles2):
                bi = ti % 2
                at, xt, ri, rx = AT[bi], XT[bi], rI[bi], rX[bi]
                sy.dma(at[:, :, :n], self.ACTV[:, t0:t0 + n].rearrange("(j p) t -> p j t", p=128), ri, reads=[self.rACTV], writes=[ri])
                sy.dma(xt[:, :, :n], self.XR[:, t0:t0 + n].rearrange("(kc p) t -> p kc t", p=128), rx, reads=[self.rXR], writes=[rx])
                for jo in range(KD):
                    po, rpo = self.nextps()
                    self.mm_group(po[:, :n], [(WD[:, k, jo * 128:(jo + 1) * 128], at[:, k, :n]) for k in range(FFC)], [rWD, ri], rpo)
                    gate = self.mod[:, 5 * KD + jo, s:s + 1]
                    sy.op("dve", lambda e, po=po, xt=xt, jo=jo, n=n, gate=gate: e.scalar_tensor_tensor(
                        out=xt[:, jo, :n], in0=po[:, :n], scalar=gate, in1=xt[:, jo, :n], op0=ALU.mult, op1=ALU.add),
                        reads=[rpo, rx, self.rMOD], writes=[rx])
                sy.dma(self.XR[:, t0:t0 + n].rearrange("(kc p) t -> p kc t", p=128), xt[:, :, :n], rx, reads=[rx], writes=[self.rXR])

    def layer(self, l, first):
        cfg = self.cfg
        on = lambda n: (self.stages is None) or (n in self.stages)
        self.adaln(l)
        with Scope(self) as hs:
            H = hs.sb("H", [128, cfg.KD, cfg.T], BF16)
            rH = R("H")
            if on("norm1"):
                self.norm_mod(first, self.g1, 0, H, rH)
            if on("lru"):
                self.lru(l, H, rH)
            if on("prep"):
                self.ssd_prep(l, H, rH)
        for d in range(2):
            if on("sweep%d" % d):
                self.ssd_sweep(d)
        if on("ssdout"):
            self.ssd_out(l)
        if on("merge"):
            self.merge(l, first)
        with Scope(self) as hs:
            H = hs.sb("H2", [128, cfg.KD, cfg.T], BF16)
            rH = R("H2")
            if on("norm2"):
                self.norm_mod(False, self.g2, 3, H, rH)
            if on("ffnup"):
                self.ffn_up(l, H, rH)
        if on("ffndown"):
            self.ffn_down(l)

    def final_norm(self):
        cfg, sy = self.cfg, self.sy
        KD, TS = cfg.KD, cfg.TS2
        with Scope(self) as ss:
            xts = [ss.sb("xtf", [128, KD, TS], F32) for i in range(2)]
            rxs = [R("xtf%d" % i) for i in range(2)]
            sq = ss.sb("sqf", [128, KD, TS], F32)
            rsq = R("sqf")
            rstd = ss.sb("rstdf", [128, TS], F32)
            rrs = R("rstdf")
            rOUT = R("out")
            ti = 0
            for (t0, n, s) in cfg.tiles2:
                if s == 1:
                    continue
                xt, rx = xts[ti % 2], rxs[ti % 2]
                ti += 1
                sy.dma(xt[:, :, :n], self.XR[:, t0:t0 + n].rearrange("(kc p) t -> p kc t", p=128), rx, reads=[self.rXR], writes=[rx])
                self.rstd_of(xt, rx, n, sq, rsq, rstd, rrs, cfg.D)
                sy.op("dve", lambda e, xt=xt, n=n: e.tensor_tensor(out=xt[:, :, :n], in0=xt[:, :, :n],
                                                                   in1=rstd[:, :n].unsqueeze(1).to_broadcast([128, KD, n]), op=ALU.mult),
                      reads=[rx, rrs], writes=[rx])
                sy.op("dve", lambda e, xt=xt, n=n: e.tensor_tensor(out=xt[:, :, :n], in0=xt[:, :, :n],
                                                                   in1=self.vcol("fnw", 0, KD).unsqueeze(2).to_broadcast([128, KD, n]), op=ALU.mult),
                      reads=[rx, self.rVEC], writes=[rx])
                sy.dma(self.outf[:, t0 - cfg.CTX:t0 - cfg.CTX + n].rearrange("(kc p) t -> p kc t", p=128), xt[:, :, :n], rx, reads=[rx], writes=[rOUT])

    def copy_xr_out(self):
        cfg, sy = self.cfg, self.sy
        KD, TS = cfg.KD, cfg.TS
        with Scope(self) as ss:
            xts = [ss.sb("xtc", [128, KD, TS], F32) for i in range(2)]
            rxs = [R("xtc%d" % i) for i in range(2)]
            rOUT = R("out")
            for ti, (t0, n, s) in enumerate(cfg.tiles):
                xt, rx = xts[ti % 2], rxs[ti % 2]
                sy.dma(xt[:, :, :n], self.XR[:, t0:t0 + n].rearrange("(kc p) t -> p kc t", p=128), rx, reads=[self.rXR], writes=[rx])
                sy.dma(self.out[:, t0:t0 + n].rearrange("(kc p) t -> p kc t", p=128), xt[:, :, :n], rx, reads=[rx], writes=[rOUT])


FUSED = False
CFG_KW = {}
PER_LAYER = ("ada_w", "ada_b", "norm_mix_w", "norm_ffn_w", "w_in", "lru_conv_w", "lru_conv_b", "lru_wa", "lru_ba", "lru_wx",
             "lru_bx", "lru_lambda", "lru_proj", "ssd_conv_w", "ssd_conv_b", "ssd_dt_bias", "ssd_a_log", "ssd_d", "ssd_norm_w",
             "ssd_proj", "w_out", "ffn_w_up", "ffn_conv_w", "ffn_conv_b", "ffn_w_down")


def _maps(cfg, inp, B):
    base = host_prep(cfg, inp, 0)
    maps = [base]
    for b in range(1, B):
        m = dict(base)
        pb = host_prep_core(cfg, inp, b)
        m.update(pb)
        maps.append(m)
    return maps


def host_prep_core(cfg, inp, b):
    m = {}
    m["xr0"] = np.ascontiguousarray(np.concatenate([inp["ctx"][b].T, inp["x"][b].T], axis=1).astype(np.float32))
    cv = np.zeros((128, cfg.KD, 2), np.float32)
    cv[:, :, 0] = colmajor(inp["c"][b])
    cv[:, :, 1] = colmajor(inp["c_ctx"])
    m["cvec"] = cv
    return m


def kernel(**inputs):
    from concourse.bass_utils import run_bass_kernel_spmd
    inp = {k: np.asarray(v) for k, v in inputs.items()}
    B = inp["x"].shape[0]
    L = inp["w_in"].shape[0]
    if FUSED:
        cfg = Cfg(DEPTH=L, **CFG_KW)
        nc = Prog(cfg, layers=list(range(L)), final=True, emit_xr=False).build()
        maps = _maps(cfg, inp, B)
        res = run_bass_kernel_spmd(nc, maps, core_ids=list(range(B)))
        outs = [res.results[b]["out"] for b in range(B)]
    else:
        cfg = Cfg(DEPTH=1, **CFG_KW)
        xr = None
        for l in range(L):
            inpl = {k: (v[l:l + 1] if k in PER_LAYER else v) for k, v in inp.items()}
            maps = _maps(cfg, inpl, B)
            if xr is not None:
                for b in range(B):
                    maps[b]["xr0"] = xr[b]
            nc = Prog(cfg, layers=[0], final=True, emit_xr=True).build()
            res = run_bass_kernel_spmd(nc, maps, core_ids=list(range(B)))
            xr = [np.ascontiguousarray(res.results[b]["outx"]) for b in range(B)]
            outs = [res.results[b]["outf"] for b in range(B)]
    out = np.stack([np.ascontiguousarray(o.T) for o in outs]).astype(np.float32)
    return out
```

```python
import numpy as np
from contextlib import ExitStack
import concourse.bass as bass
import concourse.mybir as mybir

F32 = mybir.dt.float32
BF16 = mybir.dt.bfloat16
AF = mybir.ActivationFunctionType
ALU = mybir.AluOpType

ENGS = ("pe", "act", "dve", "pool", "sp")


class R:
    __slots__ = ("name", "w", "rs", "dsem", "dcnt")

    def __init__(self, name):
        self.name = name
        self.w = None
        self.rs = {}
        self.dsem = None
        self.dcnt = 0


class Sync:
    def __init__(self, nc, stack):
        self.nc = nc
        self.stack = stack
        self.ops = {e: [] for e in ENGS}
        self.sems = {}
        for e in ENGS:
            self.sems[("E", e)] = stack.enter_context(nc.semaphore("S_" + e))
        self.cnt = {e: 0 for e in ENGS}
        self.seen = {e: {} for e in ENGS}
        self.ndsem = 0
        self.dcount = {}
        self.dsem_by_name = {}

    def _waits(self, eng, reads, writes, extra=()):
        w = {}

        def add(ev):
            if ev is None:
                return
            k, v = ev
            if w.get(k, 0) < v:
                w[k] = v
        for r in reads:
            add(r.w)
        for r in writes:
            add(r.w)
            for k, v in r.rs.items():
                add((k, v))
        for ev in extra:
            add(ev)
        out = []
        seen = self.seen[eng]
        for k, v in w.items():
            if eng == "pe" and k == ("E", "pe"):
                continue
            if seen.get(k, 0) >= v:
                continue
            seen[k] = v
            out.append((k, v))
        return out

    def _mark(self, ev, reads, writes):
        k, v = ev
        for r in reads:
            if r.rs.get(k, 0) < v:
                r.rs[k] = v
        for r in writes:
            r.w = ev
            r.rs = {}

    def op(self, eng, fn, reads=(), writes=()):
        waits = self._waits(eng, reads, writes)
        self.cnt[eng] += 1
        ev = (("E", eng), self.cnt[eng])
        self.ops[eng].append((waits, fn, ("E", eng), 1))
        self._mark(ev, reads, writes)
        return ev

    def dma(self, out_ap, in_ap, sres, reads=(), writes=(), q="sp"):
        key = self.dsem_by_name.get(sres.name)
        if key is None:
            key = ("D", self.ndsem)
            self.ndsem += 1
            self.sems[key] = self.stack.enter_context(self.nc.semaphore("D%d" % key[1]))
            self.dsem_by_name[sres.name] = key
            self.dcount[key] = 0
        extra = []
        if self.dcount[key]:
            extra.append((key, self.dcount[key]))
        waits = self._waits(q, reads, writes, extra)
        self.dcount[key] += 16
        ev = (key, self.dcount[key])

        def fn(e, out_ap=out_ap, in_ap=in_ap):
            return e.dma_start(out=out_ap, in_=in_ap)
        self.ops[q].append((waits, fn, key, 16))
        self._mark(ev, reads, writes)
        return ev

    def barrier(self):
        evs = [(("E", e), self.cnt[e]) for e in ENGS if self.cnt[e]]
        evs += [(k, None) for k in self.sems if k[0] == "D"]
        for e in ENGS:
            waits = []
            for k, v in evs:
                if v is None:
                    v = self.dcount.get(k, 0)
                if v and self.seen[e].get(k, 0) < v and not (k == ("E", e)):
                    self.seen[e][k] = v
                    waits.append((k, v))
            if waits:
                self.ops[e].append((waits, None, None, 0))

    def fence(self, eng, resources):
        waits = self._waits(eng, [], resources)
        if waits:
            self.ops[eng].append((waits, None, None, 0))

    def emit(self):
        nc = self.nc
        engmap = {"pe": "tensor", "act": "scalar", "dve": "vector", "pool": "gpsimd", "sp": "sync"}
        with nc.Block() as block:
            for e in ENGS:
                ops = self.ops[e]

                def body(eng, ops=ops):
                    for waits, fn, semkey, inc in ops:
                        for k, v in waits:
                            eng.wait_ge(self.sems[k], v)
                        if fn is not None:
                            ins = fn(eng)
                            ins.then_inc(self.sems[semkey], inc)
                getattr(block, engmap[e])(body)
        self.ops = {e: [] for e in ENGS}


class Cfg:
    def __init__(self, D=1024, SEQ=4096, CTX=256, GW=64, LW=1024, SI=2048, NG=4, FF=2816, DEPTH=4, TS=512):
        self.D, self.SEQ, self.CTX, self.GW, self.LW, self.SI, self.NG, self.FF, self.DEPTH, self.TS = \
            D, SEQ, CTX, GW, LW, SI, NG, FF, DEPTH, TS
        self.T = CTX + SEQ
        self.KD = D // 128
        self.LWC = LW // 128
        self.SIC = SI // 128
        self.NH = SI // 64
        self.HPG = self.NH // NG
        self.CD = SI + 2 * NG * 128
        self.CDC = self.CD // 128
        self.FFC = FF // 128
        self.IN_DIM = 2 * LW + SI + self.CD + 2 * self.NH + 2 * D
        self.o_lx, self.o_lg = 0, LW
        self.o_z = 2 * LW
        self.o_xbc = 2 * LW + SI
        self.o_dt = self.o_xbc + self.CD
        self.o_gt = self.o_dt + 2 * self.NH
        self.tiles = []
        for t0 in range(0, CTX, TS):
            self.tiles.append((t0, min(TS, CTX - t0), 1))
        for t0 in range(0, SEQ, TS):
            self.tiles.append((CTX + t0, min(TS, SEQ - t0), 0))
        self.NCH = self.T // 128
        self.TS2 = min(256, TS)
        self.tiles2 = []
        for t0 in range(0, CTX, self.TS2):
            self.tiles2.append((t0, min(self.TS2, CTX - t0), 1))
        for t0 in range(0, SEQ, self.TS2):
            self.tiles2.append((CTX + t0, min(self.TS2, SEQ - t0), 0))
        off = {}
        o = 0
        for name, n in (("nmw", self.KD), ("nfw", self.KD), ("adab", 6 * self.KD), ("lcw", 4 * self.LWC),
                        ("lcb", self.LWC), ("lba", 2 * self.LWC), ("lbx", 2 * self.LWC), ("llam", 2 * self.LWC),
                        ("scw", 4 * self.CDC), ("scb", self.CDC), ("sdd", self.SIC), ("snw", self.SIC),
                        ("fcw", 9 * self.FFC), ("fcb", self.FFC), ("fnw", self.KD), ("dtb", 1), ("alog", 1)):
            off[name] = o
            o += n
        self.voff = off
        self.NV = o


def colmajor(v):
    v = np.asarray(v, np.float32)
    return np.ascontiguousarray(v.reshape(-1, 128).T)


def host_prep(cfg, inp, b):
    L = cfg.DEPTH
    m = {}
    m["xr0"] = np.ascontiguousarray(np.concatenate([inp["ctx"][b].T, inp["x"][b].T], axis=1).astype(np.float32))
    cv = np.zeros((128, cfg.KD, 2), np.float32)
    cv[:, :, 0] = colmajor(inp["c"][b])
    cv[:, :, 1] = colmajor(inp["c_ctx"])
    m["cvec"] = cv
    vecs = np.zeros((L, 128, cfg.NV), np.float32)
    vo = cfg.voff
    for l in range(L):
        def put(name, arr2d):
            vecs[l, :arr2d.shape[0], vo[name]:vo[name] + arr2d.shape[1]] = arr2d
        put("nmw", colmajor(inp["norm_mix_w"][l]))
        put("nfw", colmajor(inp["norm_ffn_w"][l]))
        put("adab", colmajor(inp["ada_b"][l]))
        put("lcw", np.concatenate([colmajor(inp["lru_conv_w"][l][k]) for k in range(4)], axis=1))
        put("lcb", colmajor(inp["lru_conv_b"][l]))
        put("lba", np.concatenate([colmajor(inp["lru_ba"][l][d]) for d in range(2)], axis=1))
        put("lbx", np.concatenate([colmajor(inp["lru_bx"][l][d]) for d in range(2)], axis=1))
        put("llam", np.concatenate([colmajor(inp["lru_lambda"][l][d]) for d in range(2)], axis=1))
        put("scw", np.concatenate([colmajor(inp["ssd_conv_w"][l][k]) for k in range(4)], axis=1))
        put("scb", colmajor(inp["ssd_conv_b"][l]))
        put("sdd", colmajor(np.repeat(np.asarray(inp["ssd_d"][l]), 64)))
        put("snw", colmajor(inp["ssd_norm_w"][l]))
        fw_ = np.asarray(inp["ffn_conv_w"][l]).reshape(9, cfg.FF)
        put("fcw", np.concatenate([colmajor(fw_[k]) for k in range(9)], axis=1))
        put("fcb", colmajor(inp["ffn_conv_b"][l]))
        put("fnw", colmajor(inp["final_norm_w"]))
        dtb = np.zeros((64, 1), np.float32)
        alog = np.zeros((64, 1), np.float32)
        for d in range(2):
            dtb[d * 32:d * 32 + cfg.NH, 0] = inp["ssd_dt_bias"][l][d]
            alog[d * 32:d * 32 + cfg.NH, 0] = inp["ssd_a_log"][l][d]
        put("dtb", dtb)
        put("alog", alog)
    m["vecs"] = vecs
    for k in ("ada_w", "w_in", "lru_wa", "lru_wx", "lru_proj", "ssd_proj", "w_out", "ffn_w_up", "ffn_w_down"):
        m[k] = np.ascontiguousarray(np.asarray(inp[k], np.float32))
    ident = np.eye(128, dtype=np.float32)
    m["c_ident"] = ident
    idx = np.arange(128)
    mk = np.zeros((2, 128, 128), np.float32)
    mk[0] = (idx[None, :] < idx[:, None])
    mk[1] = (idx[None, :] > idx[:, None])
    m["c_mask"] = mk
    sel = np.zeros((64, 64, 128), np.float32)
    for k in range(64):
        sel[k, k, :] = 1.0
    m["c_sel"] = sel
    rm = np.ones((2, 64, cfg.T), np.float32)
    rm[0, :, 0::128] = 0.0
    rm[1, :, 127::128] = 0.0
    m["c_rmask"] = rm
    return m


GELU_C = 1.5957691216057308
_uid = [0]


class Scope:
    def __init__(self, prog):
        self.p = prog
        self.st = ExitStack()

    def __enter__(self):
        return self

    def sb(self, name, shape, dt):
        _uid[0] += 1
        return self.st.enter_context(self.p.nc.sbuf_tensor("%s_%d" % (name, _uid[0]), list(shape), dt))

    def __exit__(self, *a):
        if a[0] is None:
            self.p.sy.barrier()
            self.p.sy.emit()
        self.st.close()
        return False


class Prog:
    def __init__(self, cfg, layers, debug=(), final=True, stop_after=None, stages=None, emit_xr=None):
        self.cfg = cfg
        self.layers = list(layers)
        self.debug = set(debug)
        self.final = final
        self.stop_after = stop_after
        self.stages = stages
        self.emit_xr = (not final) if emit_xr is None else emit_xr

    def build(self):
        nc = bass.Bass("TRN2", target_bir_lowering=False)
        self.nc = nc
        with ExitStack() as st:
            self.st = st
            self.sy = Sync(nc, st)
            self.declare_io()
            with Scope(self) as top:
                self.setup(top)
                for li, l in enumerate(self.layers):
                    self.layer(l, first=(li == 0))
                    self.sy.barrier()
                    self.sy.emit()
                if self.final:
                    self.final_norm()
                if self.emit_xr:
                    self.copy_xr_out()
        return nc

    def din(self, name, shape, dt=F32):
        return self.nc.dram_tensor(name, list(shape), dt, kind="ExternalInput").ap()

    def dscr(self, name, shape, dt, out=False):
        dbg = name in self.debug
        kind = "ExternalOutput" if (out or dbg) else "Internal"
        return self.nc.dram_tensor(name if out else ("d_" + name), list(shape), dt, kind=kind).ap()

    def declare_io(self):
        cfg = self.cfg
        L = cfg.DEPTH
        self.i = {}
        self.i["xr0"] = self.din("xr0", [cfg.D, cfg.T])
        self.i["cvec"] = self.din("cvec", [128, cfg.KD, 2])
        self.i["vecs"] = self.din("vecs", [L, 128, cfg.NV])
        self.i["ada_w"] = self.din("ada_w", [L, cfg.D, 6 * cfg.D])
        self.i["w_in"] = self.din("w_in", [L, cfg.D, cfg.IN_DIM])
        self.i["lru_wa"] = self.din("lru_wa", [L, 2, cfg.LW // 64, 64, 64])
        self.i["lru_wx"] = self.din("lru_wx", [L, 2, cfg.LW // 64, 64, 64])
        self.i["lru_proj"] = self.din("lru_proj", [L, cfg.LW, cfg.D])
        self.i["ssd_proj"] = self.din("ssd_proj", [L, cfg.SI, cfg.D])
        self.i["w_out"] = self.din("w_out", [L, cfg.D, cfg.D])
        self.i["ffn_w_up"] = self.din("ffn_w_up", [L, cfg.D, 2 * cfg.FF])
        self.i["ffn_w_down"] = self.din("ffn_w_down", [L, cfg.FF, cfg.D])
        self.i["c_ident"] = self.din("c_ident", [128, 128])
        self.i["c_mask"] = self.din("c_mask", [2, 128, 128])
        self.i["c_sel"] = self.din("c_sel", [64, 64, 128])
        self.i["c_rmask"] = self.din("c_rmask", [2, 64, cfg.T])
        if self.final:
            self.outf = self.dscr("out" if not self.emit_xr else "outf", [cfg.D, cfg.SEQ], F32, out=True)
        if self.emit_xr:
            self.out = self.dscr("out" if not self.final else "outx", [cfg.D, cfg.T], F32, out=True)
        self.XR = self.dscr("XR", [cfg.D, cfg.T], F32)
        self.YAG = self.dscr("YAG", [cfg.LW, cfg.T], BF16)
        self.Zs = self.dscr("Zs", [cfg.SI, cfg.T], BF16)
        self.GT = self.dscr("GT", [2 * cfg.D, cfg.T], BF16)
        self.XBC = self.dscr("XBC", [cfg.CD, cfg.T], BF16)
        self.Y = self.dscr("Y", [cfg.SI, cfg.T], F32)
        self.GN = self.dscr("GN", [cfg.SI, cfg.T], BF16)
        self.ACTV = self.dscr("ACTV", [cfg.FF, cfg.T], BF16)
        self.DTS = self.dscr("DTS", [3, 64, cfg.T], F32)
        self.TOTS = self.dscr("TOTS", [64, cfg.NCH], F32)
        self.rDTS = R("DTS")
        self.rXR, self.rYAG, self.rZs, self.rGT, self.rXBC, self.rY, self.rGN, self.rACTV = [R(n) for n in (
            "XR", "YAG", "Zs", "GT", "XBC", "Y", "GN", "ACTV")]

    def setup(self, top):
        cfg, sy = self.cfg, self.sy
        self.ident_f = top.sb("ident_f", [128, 128], F32)
        self.ident_b = top.sb("ident_b", [128, 128], BF16)
        self.ones_f = top.sb("ones_f", [128, 128], F32)
        self.ones_b = top.sb("ones_b", [128, 128], BF16)
        self.negi = top.sb("negi", [128, 128], BF16)
        self.maskf = top.sb("maskf", [128, 2, 128], F32)
        self.maskb = top.sb("maskb", [128, 2, 128], BF16)
        self.rC = R("consts")
        self.rCm = R("constsm")
        rC = self.rC
        sy.dma(self.ident_f[:], self.i["c_ident"], rC, writes=[rC])
        sy.dma(self.maskf[:], self.i["c_mask"].rearrange("d s l -> s d l"), self.rCm, writes=[self.rCm])
        sy.op("dve", lambda e: e.tensor_copy(out=self.ident_b[:], in_=self.ident_f[:]), reads=[rC], writes=[rC])
        sy.op("dve", lambda e: e.tensor_copy(out=self.maskb[:], in_=self.maskf[:]), reads=[self.rCm], writes=[self.rCm])
        sy.op("dve", lambda e: e.tensor_scalar(out=self.negi[:], in0=self.ident_f[:], scalar1=-30000.0, scalar2=None, op0=ALU.mult),
              reads=[rC], writes=[rC])
        sy.op("dve", lambda e: e.memset(self.ones_f[:], 1.0), writes=[rC])
        sy.op("dve", lambda e: e.memset(self.ones_b[:], 1.0), writes=[rC])
        self.sc = top.sb("sc", [128, cfg.KD, 2], F32)
        self.rSC = R("sc")
        sy.dma(self.sc[:], self.i["cvec"], self.rSC, writes=[self.rSC])
        sy.op("act", lambda e: e.activation(out=self.sc[:], in_=self.sc[:], func=AF.Silu), reads=[self.rSC], writes=[self.rSC])
        self.vec = top.sb("vec", [128, cfg.NV], F32)
        self.rVEC = R("vec")
        self.mod = top.sb("mod", [128, 6 * cfg.KD, 2], F32)
        self.rMOD = R("mod")
        self.g1 = top.sb("g1", [128, cfg.KD, 2], F32)
        self.g2 = top.sb("g2", [128, cfg.KD, 2], F32)
        self.c8 = top.sb("c8", [128, 2 * cfg.LWC], F32)
        self.aneg = top.sb("aneg", [64, 1], F32)
        self.PS = [top.st.enter_context(self.nc.psum_tensor("ps%d" % i, [128, 512], F32)) for i in range(7)]
        self.rPS = [R("ps%d" % i) for i in range(7)]
        self.PSB = top.st.enter_context(self.nc.psum_tensor("psb", [128, 1024], BF16))
        self.rPSB = R("psb")
        self.ps_i = 0
        self.wst = [top.sb("wst%d" % i, [128, 8, 256], F32) for i in range(2)]
        self.rWST = [R("wst%d" % i) for i in range(2)]
        self.wst_i = 0
        sy.barrier()
        sy.emit()

    def vcol(self, name, j=0, n=1, rows=128):
        o = self.cfg.voff[name] + j
        return self.vec[:rows, o:o + n]

    def nextps(self, k=4):
        i = self.ps_i % k
        self.ps_i += 1
        return self.PS[i], self.rPS[i]

    def load_w(self, dst, rdst, src2d, K, C, rows_last=128):
        sy = self.sy
        for k0 in range(0, K, 8):
            kn = min(8, K - k0)
            for c0 in range(0, C, 256):
                cn = min(256, C - c0)
                bi = self.wst_i
                self.wst_i ^= 1
                ws, rw = self.wst[bi], self.rWST[bi]
                src = src2d[k0 * 128:(k0 + kn) * 128, c0:c0 + cn].rearrange("(kc p) c -> p kc c", p=128)
                sy.dma(ws[:, :kn, :cn], src, rw, writes=[rw])
                sy.op("pool", lambda e, ws=ws, kn=kn, cn=cn, k0=k0, c0=c0: e.tensor_copy(
                    out=dst[:, k0:k0 + kn, c0:c0 + cn], in_=ws[:, :kn, :cn]), reads=[rw], writes=[rdst])

    def mm_group(self, ps_ap, pairs, reads, rps):
        def fn(e):
            ins = None
            n = len(pairs)
            for i, (a, b) in enumerate(pairs):
                ins = e.matmul(ps_ap, a, b, start=(i == 0), stop=(i == n - 1))
            return ins
        self.sy.op("pe", fn, reads=reads, writes=[rps])

    def adaln(self, l):
        cfg, sy = self.cfg, self.sy
        KD = cfg.KD
        with Scope(self) as ss:
            adaw = [ss.sb("adaw", [128, KD, 512], F32) for i in range(2)]
            rA = [R("adaw%d" % i) for i in range(2)]
            sy.dma(self.vec[:], self.i["vecs"][l], self.rVEC, writes=[self.rVEC])
            psb, rps = self.PS[6], self.rPS[6]
            ncol = 6 * cfg.D
            bi = 0
            for g0 in range(0, ncol, 512):
                gw = min(512, ncol - g0)
                wt, rw = adaw[bi], rA[bi]
                bi ^= 1
                src = self.i["ada_w"][l][:, g0:g0 + gw].rearrange("(kc p) c -> p kc c", p=128)
                sy.dma(wt[:, :, :gw], src, rw, writes=[rw])
                for oc in range(gw // 128):
                    o = g0 // 128 + oc
                    self.mm_group(psb[:, 2 * o:2 * o + 2],
                                  [(wt[:, kc, oc * 128:(oc + 1) * 128], self.sc[:, kc, :]) for kc in range(KD)],
                                  [rw, self.rSC], rps)
            ab = self.vcol("adab", 0, 6 * KD)
            sy.op("dve", lambda e: e.tensor_tensor(out=self.mod[:], in0=psb[:, 0:12 * KD].rearrange("p (o s) -> p o s", s=2),
                                                   in1=ab.unsqueeze(2).to_broadcast([128, 6 * KD, 2]), op=ALU.add),
                  reads=[rps, self.rVEC], writes=[self.rMOD])
            for g, wname, m in ((self.g1, "nmw", 1), (self.g2, "nfw", 4)):
                def fn(e, g=g, wname=wname, m=m):
                    return e.scalar_tensor_tensor(out=g[:], in0=self.mod[:, m * KD:(m + 1) * KD, :], scalar=1.0,
                                                  in1=self.vcol(wname, 0, KD).unsqueeze(2).to_broadcast([128, KD, 2]),
                                                  op0=ALU.add, op1=ALU.mult)
                sy.op("dve", fn, reads=[self.rMOD, self.rVEC], writes=[self.rMOD])
            n8 = 2 * cfg.LWC
            sy.op("act", lambda e: e.activation(out=self.c8[:], in_=self.vcol("llam", 0, n8), func=AF.Exp, scale=-1.0),
                  reads=[self.rVEC], writes=[self.rMOD])
            sy.op("act", lambda e: e.activation(out=self.c8[:], in_=self.c8[:], func=AF.Ln, bias=1.0), reads=[self.rMOD], writes=[self.rMOD])
            sy.op("dve", lambda e: e.tensor_scalar(out=self.c8[:], in0=self.c8[:], scalar1=-8.0, scalar2=None, op0=ALU.mult),
                  reads=[self.rMOD], writes=[self.rMOD])
            sy.op("act", lambda e: e.activation(out=self.aneg[:], in_=self.vcol("alog", 0, 1, rows=64), func=AF.Exp),
                  reads=[self.rVEC], writes=[self.rMOD])
            sy.op("dve", lambda e: e.tensor_scalar(out=self.aneg[:], in0=self.aneg[:], scalar1=-1.0, scalar2=None, op0=ALU.mult),
                  reads=[self.rMOD], writes=[self.rMOD])

    def norm_mod(self, first, g, shift_m, H, rH):
        cfg, sy = self.cfg, self.sy
        KD = cfg.KD
        with Scope(self) as ss:
            xts = [ss.sb("xt", [128, KD, cfg.TS2], F32) for i in range(2)]
            rxs = [R("xt%d" % i) for i in range(2)]
            sq = ss.sb("sq", [128, KD, cfg.TS2], F32)
            rsq = R("sq")
            rstd = ss.sb("rstd", [128, cfg.TS2], F32)
            rrs = R("rstd")
            for ti, (t0, n, s) in enumerate(cfg.tiles2):
                xt, rx = xts[ti % 2], rxs[ti % 2]
                src = (self.i["xr0"] if first else self.XR)[:, t0:t0 + n].rearrange("(kc p) t -> p kc t", p=128)
                sy.dma(xt[:, :, :n], src, rx, reads=([] if first else [self.rXR]), writes=[rx])
                self.rstd_of(xt, rx, n, sq, rsq, rstd, rrs, cfg.D)
                sy.op("dve", lambda e, xt=xt, n=n: e.tensor_tensor(out=xt[:, :, :n], in0=xt[:, :, :n],
                                                                   in1=rstd[:, :n].unsqueeze(1).to_broadcast([128, KD, n]), op=ALU.mult),
                      reads=[rx, rrs], writes=[rx])
                for kc in range(KD):
                    sy.op("dve", lambda e, xt=xt, n=n, kc=kc, t0=t0, s=s: e.tensor_scalar(
                        out=H[:, kc, t0:t0 + n], in0=xt[:, kc, :n], scalar1=g[:, kc, s:s + 1],
                        scalar2=self.mod[:, shift_m * KD + kc, s:s + 1], op0=ALU.mult, op1=ALU.add),
                        reads=[rx, self.rMOD], writes=[rH])

    def rstd_of(self, xt, rx, n, sq, rsq, rstd, rrs, nchan):
        cfg, sy = self.cfg, self.sy
        K = nchan // 128
        sy.op("act", lambda e: e.activation(out=sq[:, :K, :n], in_=xt[:, :K, :n], func=AF.Square), reads=[rx], writes=[rsq])
        psb, rps = self.PS[5], self.rPS[5]
        self.mm_group(psb[:, :n], [(self.ones_f[:], sq[:, kc, :n]) for kc in range(K)], [rsq, self.rC], rps)
        sy.op("act", lambda e: e.activation(out=rstd[:, :n], in_=psb[:, :n], func=AF.Sqrt, scale=1.0 / nchan, bias=1e-6),
              reads=[rps], writes=[rrs])
        sy.op("dve", lambda e: e.reciprocal(out=rstd[:, :n], in_=rstd[:, :n]), reads=[rrs], writes=[rrs])

    def proj_tile(self, W, rW, H, rH, t0, n, M=128):
        KD = self.cfg.KD
        ps, rps = self.nextps()
        self.mm_group(ps[:M, :n], [(W[:, kc, :M], H[:, kc, t0:t0 + n]) for kc in range(KD)], [rW, rH], rps)
        return ps, rps

    def seqpos(self, t):
        return t + 2 if t < self.cfg.CTX else t + 5

    def make_diag(self, DG, rDG, name, ntap, nch, j):
        for k in range(ntap):
            self.sy.op("dve", lambda e, k=k: e.tensor_scalar(out=DG[:, k, :], in0=self.ident_f[:], scalar1=self.vcol(name, k * nch + j, 1),
                                                             scalar2=None, op0=ALU.mult), reads=[self.rC, self.rVEC], writes=[rDG])

    def conv1d_tile(self, DG, rDG, SEQB, rSEQB, t0, n):
        ps, rps = self.nextps()
        p0 = self.seqpos(t0)
        self.mm_group(ps[:, :n], [(DG[:, k, :], SEQB[:, p0 + k - 2:p0 + k - 2 + n]) for k in range(4)], [rDG, rSEQB], rps)
        return ps, rps

    def gelu_from(self, ss_tmp, rtmp, src_ap, rsrc, out_ap, rout, n, bias=None, mul=None, rmul=None):
        sy = self.sy
        x, x2, u = ss_tmp[:, 0, :n], ss_tmp[:, 1, :n], ss_tmp[:, 2, :n]
        if bias is not None:
            sy.op("act", lambda e: e.activation(out=x, in_=src_ap, func=AF.Identity, bias=bias), reads=[rsrc, self.rVEC], writes=[rtmp])
        else:
            sy.op("act", lambda e: e.activation(out=x, in_=src_ap, func=AF.Identity), reads=[rsrc], writes=[rtmp])
        sy.op("pool", lambda e: e.tensor_tensor(out=x2, in0=x, in1=x, op=ALU.mult), reads=[rtmp], writes=[rtmp])
        sy.op("dve", lambda e: e.tensor_scalar(out=x2, in0=x2, scalar1=0.044715, scalar2=1.0, op0=ALU.mult, op1=ALU.add), reads=[rtmp], writes=[rtmp])
        sy.op("pool", lambda e: e.tensor_tensor(out=u, in0=x2, in1=x, op=ALU.mult), reads=[rtmp], writes=[rtmp])
        sy.op("act", lambda e: e.activation(out=u, in_=u, func=AF.Sigmoid, scale=GELU_C), reads=[rtmp], writes=[rtmp])
        if mul is None:
            sy.op("dve", lambda e: e.tensor_tensor(out=out_ap, in0=u, in1=x, op=ALU.mult), reads=[rtmp], writes=[rout])
        else:
            sy.op("dve", lambda e: e.tensor_tensor(out=u, in0=u, in1=x, op=ALU.mult), reads=[rtmp], writes=[rtmp])
            sy.op("dve", lambda e: e.tensor_tensor(out=out_ap, in0=u, in1=mul, op=ALU.mult), reads=[rtmp, rmul], writes=[rout])

    def lru(self, l, H, rH):
        cfg, sy = self.cfg, self.sy
        KD, T, CTX = cfg.KD, cfg.T, cfg.CTX
        with Scope(self) as ss:
            W = ss.sb("wl", [128, KD, 256], BF16)
            rW = R("wl")
            SEQB = ss.sb("seqb", [128, T + 6], BF16)
            rSEQB = R("seqb")
            Ub = ss.sb("ub", [128, T], BF16)
            rU = R("u")
            A = ss.sb("a", [128, T], F32)
            B0 = ss.sb("b0", [128, T], F32)
            B1 = ss.sb("b1", [128, T], F32)
            rA, rB0, rB1 = R("a"), R("b0"), R("b1")
            YG = ss.sb("yg", [128, T], BF16)
            rYG = R("yg")
            DG = ss.sb("dg", [128, 4, 128], BF16)
            rDG = R("dg")
            GST = ss.sb("gst", [128, 4, 128], F32)
            rGST = R("gst")
            GW = ss.sb("gw", [128, 4, 128], BF16)
            rGW = R("gw")
            TMP = ss.sb("tmp", [128, 3, cfg.TS], F32)
            rTMP = R("tmp")
            RT = ss.sb("rt", [128, 4, cfg.TS], F32)
            rRT = R("rt")
            sy.op("dve", lambda e: e.memset(SEQB[:], 0.0), writes=[rSEQB])
            for j in range(cfg.LWC):
                wsrc = self.i["w_in"][l]
                self.load_w(W[:, :, 0:128], rW, wsrc[:, cfg.o_lx + j * 128: cfg.o_lx + (j + 1) * 128], KD, 128)
                self.load_w(W[:, :, 128:256], rW, wsrc[:, cfg.o_lg + j * 128: cfg.o_lg + (j + 1) * 128], KD, 128)
                self.make_diag(DG, rDG, "lcw", 4, cfg.LWC, j)
                sy.op("pool", lambda e: e.memset(GST[:], 0.0), writes=[rGST])
                for d in range(2):
                    for wi, wname in enumerate(("lru_wa", "lru_wx")):
                        for hh in range(2):
                            sy.dma(GST[hh * 64:(hh + 1) * 64, d * 2 + wi, hh * 64:(hh + 1) * 64],
                                   self.i[wname][l, d, 2 * j + hh], rGST, writes=[rGST])
                sy.op("pool", lambda e: e.tensor_copy(out=GW[:], in_=GST[:]), reads=[rGST], writes=[rGW])
                for (t0, n, s) in cfg.tiles:
                    ps, rps = self.proj_tile(W[:, :, 0:128], rW, H, rH, t0, n)
                    p0 = self.seqpos(t0)
                    sy.op("act", lambda e, ps=ps, p0=p0, n=n: e.activation(out=SEQB[:, p0:p0 + n], in_=ps[:, :n], func=AF.Copy),
                          reads=[rps], writes=[rSEQB])
                for (t0, n, s) in cfg.tiles:
                    ps, rps = self.conv1d_tile(DG, rDG, SEQB, rSEQB, t0, n)
                    lcb = self.vcol("lcb", j, 1)
                    sy.op("act", lambda e, ps=ps, t0=t0, n=n, lcb=lcb: e.activation(out=Ub[:, t0:t0 + n], in_=ps[:, :n], func=AF.Identity,
                                                                                    bias=lcb), reads=[rps, self.rVEC], writes=[rU])
                for d in range(2):
                    Bd, rBd = (B0, rB0) if d == 0 else (B1, rB1)
                    for (t0, n, s) in cfg.tiles:
                        psr, rpsr = self.nextps()
                        self.mm_group(psr[:, :n], [(GW[:, 2 * d, :], Ub[:, t0:t0 + n])], [rGW, rU], rpsr)
                        psi, rpsi = self.nextps()
                        self.mm_group(psi[:, :n], [(GW[:, 2 * d + 1, :], Ub[:, t0:t0 + n])], [rGW, rU], rpsi)
                        r_, i_, a2, sq = RT[:, 0, :n], RT[:, 1, :n], RT[:, 2, :n], RT[:, 3, :n]
                        lba = self.vcol("lba", d * cfg.LWC + j, 1)
                        lbx = self.vcol("lbx", d * cfg.LWC + j, 1)
                        sy.op("act", lambda e, psr=psr, n=n, r_=r_, lba=lba: e.activation(out=r_, in_=psr[:, :n], func=AF.Sigmoid,
                                                                                          bias=lba),
                              reads=[rpsr, self.rVEC], writes=[rRT])
                        sy.op("act", lambda e, psi=psi, n=n, i_=i_, lbx=lbx: e.activation(out=i_, in_=psi[:, :n], func=AF.Sigmoid,
                                                                                          bias=lbx),
                              reads=[rpsi, self.rVEC], writes=[rRT])
                        c8 = self.c8[:, d * cfg.LWC + j: d * cfg.LWC + j + 1]
                        sy.op("act", lambda e, t0=t0, n=n, r_=r_, c8=c8: e.activation(out=A[:, t0:t0 + n], in_=r_, func=AF.Exp, scale=c8),
                              reads=[rRT, self.rMOD], writes=[rA])
                        sy.op("pool", lambda e, t0=t0, n=n, a2=a2: e.tensor_tensor(out=a2, in0=A[:, t0:t0 + n], in1=A[:, t0:t0 + n], op=ALU.mult),
                              reads=[rA], writes=[rRT])
                        sy.op("dve", lambda e, a2=a2: e.tensor_scalar(out=a2, in0=a2, scalar1=-1.0, scalar2=1.0, op0=ALU.mult, op1=ALU.add),
                              reads=[rRT], writes=[rRT])
                        sy.op("act", lambda e, a2=a2, sq=sq: e.activation(out=sq, in_=a2, func=AF.Sqrt), reads=[rRT], writes=[rRT])
                        sy.op("pool", lambda e, t0=t0, n=n, i_=i_: e.tensor_tensor(out=i_, in0=i_, in1=Ub[:, t0:t0 + n], op=ALU.mult),
                              reads=[rRT, rU], writes=[rRT])
                        sy.op("dve", lambda e, t0=t0, n=n, i_=i_, sq=sq, Bd=Bd: e.tensor_tensor(out=Bd[:, t0:t0 + n], in0=sq, in1=i_, op=ALU.mult),
                              reads=[rRT], writes=[rBd])
                    if d == 0:
                        sy.op("dve", lambda e: e.tensor_tensor_scan(out=B0[:, :], data0=A[:, :], data1=B0[:, :], initial=0.0,
                                                                    op0=ALU.mult, op1=ALU.add), reads=[rA, rB0], writes=[rB0])
                    else:
                        sy.op("dve", lambda e: e.tensor_tensor_scan(out=B1[:, 0:CTX][:, ::-1], data0=A[:, 0:CTX][:, ::-1],
                                                                    data1=B1[:, 0:CTX][:, ::-1], initial=0.0, op0=ALU.mult, op1=ALU.add),
                              reads=[rA, rB1], writes=[rB1])
                        sy.op("dve", lambda e: e.tensor_tensor_scan(out=B1[:, CTX:T][:, ::-1], data0=A[:, CTX:T][:, ::-1],
                                                                    data1=B1[:, CTX:T][:, ::-1], initial=B1[:, 0:1], op0=ALU.mult, op1=ALU.add),
                              reads=[rA, rB1], writes=[rB1])
                sy.op("pool", lambda e: e.tensor_tensor(out=B0[:], in0=B0[:], in1=B1[:], op=ALU.add), reads=[rB0, rB1], writes=[rB0])
                for (t0, n, s) in cfg.tiles:
                    ps, rps = self.proj_tile(W[:, :, 128:256], rW, H, rH, t0, n)
                    self.gelu_from(TMP, rTMP, ps[:, :n], rps, YG[:, t0:t0 + n], rYG, n, mul=B0[:, t0:t0 + n], rmul=rB0)
                sy.dma(self.YAG[j * 128:(j + 1) * 128, :], YG[:], rYG, reads=[rYG], writes=[self.rYAG])


    def ssd_prep(self, l, H, rH):
        cfg, sy = self.cfg, self.sy
        KD, T = cfg.KD, cfg.T
        wsrc = self.i["w_in"][l]
        with Scope(self) as ss:
            Ws = [ss.sb("wp", [128, KD, 128], BF16) for i in range(2)]
            rWs = [R("wp%d" % i) for i in range(2)]
            OB = [ss.sb("ob", [128, T], BF16) for i in range(2)]
            rOB = [R("ob%d" % i) for i in range(2)]
            SEQB = ss.sb("seqb", [128, T + 6], BF16)
            rSEQB = R("seqb")
            DG = ss.sb("dg", [128, 4, 128], BF16)
            rDG = R("dg")
            sy.op("dve", lambda e: e.memset(SEQB[:], 0.0), writes=[rSEQB])
            k = 0
            jobs = [("z", cfg.o_z, j, self.Zs, self.rZs, AF.Silu) for j in range(cfg.SIC)]
            jobs += [("gt", cfg.o_gt, j, self.GT, self.rGT, AF.Sigmoid) for j in range(2 * KD)]
            jobs += [("xbc", cfg.o_xbc, j, self.XBC, self.rXBC, AF.Silu) for j in range(cfg.CDC)]
            for (kind, o, j, dst, rdst, func) in jobs:
                W, rW = Ws[k % 2], rWs[k % 2]
                ob, rob = OB[k % 2], rOB[k % 2]
                k += 1
                self.load_w(W, rW, wsrc[:, o + j * 128:o + (j + 1) * 128], KD, 128)
                if kind != "xbc":
                    for (t0, n, s) in cfg.tiles:
                        ps, rps = self.proj_tile(W, rW, H, rH, t0, n)
                        sy.op("act", lambda e, ps=ps, t0=t0, n=n, ob=ob, func=func: e.activation(out=ob[:, t0:t0 + n], in_=ps[:, :n], func=func),
                              reads=[rps], writes=[rob])
                else:
                    self.make_diag(DG, rDG, "scw", 4, cfg.CDC, j)
                    for (t0, n, s) in cfg.tiles:
                        ps, rps = self.proj_tile(W, rW, H, rH, t0, n)
                        p0 = self.seqpos(t0)
                        sy.op("act", lambda e, ps=ps, p0=p0, n=n: e.activation(out=SEQB[:, p0:p0 + n], in_=ps[:, :n], func=AF.Copy),
                              reads=[rps], writes=[rSEQB])
                    scb = self.vcol("scb", j, 1)
                    for (t0, n, s) in cfg.tiles:
                        ps, rps = self.conv1d_tile(DG, rDG, SEQB, rSEQB, t0, n)
                        sy.op("act", lambda e, ps=ps, t0=t0, n=n, ob=ob, scb=scb: e.activation(out=ob[:, t0:t0 + n], in_=ps[:, :n], func=AF.Silu, bias=scb),
                              reads=[rps, self.rVEC], writes=[rob])
                sy.dma(dst[j * 128:(j + 1) * 128, :], ob[:], rob, reads=[rob], writes=[rdst])
        with Scope(self) as ss:
            CS = ss.sb("cs", [64, T], F32)
            V = ss.sb("v", [64, T], F32)
            WEXP = ss.sb("wexp", [64, T], F32)
            TOT = ss.sb("tot", [64, cfg.NCH], F32)
            rDT = R("dtp")
            WDT = ss.sb("wdt", [128, KD, 64], BF16)
            rWDT = R("wdt")
            X1 = ss.sb("x1", [64, T], F32)
            RM = WEXP
            rX1, rRM = R("x1"), rDT
            NH = cfg.NH
            sy.op("pool", lambda e: e.memset(WDT[:], 0.0), writes=[rWDT])
            self.load_w(WDT[:, :, 0:NH], rWDT, wsrc[:, cfg.o_dt:cfg.o_dt + NH], KD, NH)
            self.load_w(WDT[:, :, 32:32 + NH], rWDT, wsrc[:, cfg.o_dt + NH:cfg.o_dt + 2 * NH], KD, NH)
            sy.dma(RM[0:32, :], self.i["c_rmask"][0, 0:32, :], rRM, writes=[rRM])
            sy.dma(RM[32:64, :], self.i["c_rmask"][1, 32:64, :], rRM, writes=[rRM])
            dtb = self.vcol("dtb", 0, 1, rows=64)
            for (t0, n, s) in cfg.tiles:
                ps, rps = self.proj_tile(WDT, rWDT, H, rH, t0, n, M=64)
                sy.op("act", lambda e, ps=ps, t0=t0, n=n: e.activation(out=X1[:, t0:t0 + n], in_=ps[:64, :n], func=AF.Exp, bias=dtb),
                      reads=[rps, self.rVEC], writes=[rX1])
            X2, X3 = V, CS
            sy.op("act", lambda e: e.activation(out=X1[:], in_=X1[:], func=AF.Ln, bias=1.0), reads=[rX1], writes=[rX1])
            sy.op("act", lambda e: e.activation(out=X2[:], in_=X1[:], func=AF.Ln), reads=[rX1], writes=[rDT])
            sy.op("dve", lambda e: e.tensor_scalar(out=X3[:], in0=X1[:], scalar1=self.aneg[:, 0:1], scalar2=None, op0=ALU.mult),
                  reads=[rX1, self.rMOD], writes=[rDT])
            sy.op("dve", lambda e: e.tensor_tensor_scan(out=X3[0:32, :], data0=RM[0:32, :], data1=X3[0:32, :], initial=0.0,
                                                        op0=ALU.mult, op1=ALU.add), reads=[rRM, rDT], writes=[rDT])
            sy.op("dve", lambda e: e.tensor_tensor_scan(out=X3[32:64, :][:, ::-1], data0=RM[32:64, :][:, ::-1], data1=X3[32:64, :][:, ::-1],
                                                        initial=0.0, op0=ALU.mult, op1=ALU.add), reads=[rRM, rDT], writes=[rDT])
            sy.op("dve", lambda e: e.tensor_tensor(out=V[:], in0=X2[:], in1=CS[:], op=ALU.subtract), reads=[rDT], writes=[rDT])
            CS3 = CS[:].rearrange("p (c q) -> p c q", q=128)
            V3 = V[:].rearrange("p (c q) -> p c q", q=128)
            W3 = WEXP[:].rearrange("p (c q) -> p c q", q=128)
            sy.op("dve", lambda e: e.tensor_copy(out=TOT[0:32, :], in_=CS3[0:32, :, 127]), reads=[rDT], writes=[rDT])
            sy.op("dve", lambda e: e.tensor_copy(out=TOT[32:64, :], in_=CS3[32:64, :, 0]), reads=[rDT], writes=[rDT])
            sy.op("dve", lambda e: e.tensor_tensor(out=W3, in0=V3, in1=TOT[:].unsqueeze(2).to_broadcast([64, cfg.NCH, 128]), op=ALU.add),
                  reads=[rDT], writes=[rDT])
            sy.op("act", lambda e: e.activation(out=WEXP[:], in_=WEXP[:], func=AF.Exp), reads=[rDT], writes=[rDT])
            for i_, t_ in enumerate((CS, V, WEXP)):
                sy.dma(self.DTS[i_], t_[:], rDT, reads=[rDT], writes=[self.rDTS])
            sy.dma(self.TOTS, TOT[:], rDT, reads=[rDT], writes=[self.rDTS])

    def ssd_sweep(self, d):
        cfg, sy = self.cfg, self.sy
        HPG, NPR, NG, SIC = cfg.HPG, cfg.HPG // 2, cfg.NG, cfg.SIC
        UB = min(4, HPG)
        nctx = cfg.CTX // 128
        order = list(range(cfg.NCH)) if d == 0 else (list(range(nctx - 1, -1, -1)) + list(range(cfg.NCH - 1, nctx - 1, -1)))
        with Scope(self) as ss:
            CS = ss.sb("cs", [64, cfg.T], F32)
            V = ss.sb("v", [64, cfg.T], F32)
            WEXP = ss.sb("wexp", [64, cfg.T], F32)
            TOT = ss.sb("tot", [64, cfg.NCH], F32)
            rDT = R("dts")
            for i_, t_ in enumerate((CS, V, WEXP)):
                sy.dma(t_[:], self.DTS[i_], rDT, reads=[self.rDTS], writes=[rDT])
            sy.dma(TOT[:], self.TOTS, rDT, reads=[self.rDTS], writes=[rDT])
            SEL = ss.sb("sel", [64, 64, 128], F32)
            rSEL = R("sel")
            sy.dma(SEL[:], self.i["c_sel"], rSEL, writes=[rSEL])
            BC = [ss.sb("bc", [128, 2, 128], BF16) for i in range(2)]
            XF = [ss.sb("xf", [128, NPR, 128], BF16) for i in range(2)]
            YL = [ss.sb("yl", [128, NPR, 128], F32) for i in range(2)]
            rIN = [R("ssdin%d" % i) for i in range(2)]
            rYL = [R("yl%d" % i) for i in range(2)]
            XT2 = ss.sb("xt2", [128, HPG, 128], BF16)
            BT = ss.sb("bt", [128, 128], BF16)
            WT = ss.sb("wt", [128, 32], F32)
            XW = ss.sb("xw", [128, HPG, 64], BF16)
            CBT = ss.sb("cbt", [128, 128], F32)
            rXT2, rBT, rWT, rXW, rCBT = R("xt2"), R("bt"), R("wt"), R("xw"), R("cbt")
            MT = ss.sb("mt", [128, UB, 128], F32)
            EC = ss.sb("ec", [128, UB, 128], F32)
            rMT, rEC = R("mt"), R("ec")
            MT2 = ss.sb("mt2", [128, HPG, 128], BF16)
            CSC = ss.sb("csc", [128, HPG, 128], BF16)
            rMT2, rCSC = R("mt2"), R("csc")
            YS = [ss.sb("ys", [128, NPR, 128], F32) for i in range(2)]
            rYS = [R("ys%d" % i) for i in range(2)]
            ET = ss.sb("et", [128, HPG], F32)
            rET = R("et")
            P32 = ss.sb("p32", [128, HPG, 64], F32)
            PB = ss.sb("pb", [128, HPG, 128], BF16)
            rP32, rPB = R("p32"), R("pb")
            PSe = [(self.PS[0], self.rPS[0]), (self.PS[1], self.rPS[1])]
            PSc = [(self.PS[2], self.rPS[2]), (self.PS[3], self.rPS[3])]
            PSy, rPSy = self.PS[4], self.rPS[4]
            PSs, rPSs = self.PS[5], self.rPS[5]
            PSm, rPSm = self.PS[6], self.rPS[6]
            PSB, rPSB = self.PSB, self.rPSB
            PSBv = PSB[:, 0:HPG * 64].rearrange("p (h q) -> p h q", q=64)
            mk = self.maskb[:, d, :]
            sy.op("dve", lambda e: e.memset(XT2[:], 0.0), writes=[rXT2])
            it = 0
            for g in range(NG):
                sy.op("dve", lambda e: e.memset(P32[:], 0.0), writes=[rP32])
                sy.op("dve", lambda e: e.memset(PB[:], 0.0), writes=[rPB])
                for c in order:
                    tok = c * 128
                    bi = it % 2
                    it += 1
                    bc, xf, yl, rin, ryl, ys, rys = BC[bi], XF[bi], YL[bi], rIN[bi], rYL[bi], YS[bi], rYS[bi]
                    for w_, ch in ((0, SIC + g), (1, SIC + NG + g)):
                        sy.dma(bc[:, w_, :], self.XBC[ch * 128:(ch + 1) * 128, tok:tok + 128], rin, reads=[self.rXBC], writes=[rin])
                    sy.dma(xf[:], self.XBC[g * NPR * 128:(g + 1) * NPR * 128, tok:tok + 128].rearrange("(q p) t -> p q t", p=128),
                           rin, reads=[self.rXBC], writes=[rin])
                    ydst = self.Y[g * NPR * 128:(g + 1) * NPR * 128, tok:tok + 128].rearrange("(q p) t -> p q t", p=128)
                    if d == 1:
                        sy.dma(yl[:], ydst, ryl, reads=[self.rY], writes=[ryl])
                    def tfn(e, xf=xf, bc=bc):
                        ins = None
                        for q in range(NPR):
                            ins = e.transpose(PSB[:, q * 128:(q + 1) * 128], xf[:, q, :], self.ident_b[:])
                        ins = e.transpose(PSB[:, NPR * 128:(NPR + 1) * 128], bc[:, 0, :], self.ident_b[:])
                        return ins
                    sy.op("pe", tfn, reads=[rin, self.rC], writes=[rPSB])
                    sy.op("dve", lambda e: e.tensor_copy(out=XT2[:, 0::2, 0:64], in_=PSBv[:, 0::2, :]), reads=[rPSB], writes=[rXT2])
                    sy.op("dve", lambda e: e.tensor_copy(out=XT2[:, 1::2, 64:128], in_=PSBv[:, 1::2, :]), reads=[rPSB], writes=[rXT2])
                    sy.op("pool", lambda e: e.tensor_copy(out=BT[:], in_=PSB[:, NPR * 128:(NPR + 1) * 128]), reads=[rPSB], writes=[rBT]) \
                        if False else sy.op("dve", lambda e: e.tensor_copy(out=BT[:], in_=PSB[:, NPR * 128:(NPR + 1) * 128]), reads=[rPSB], writes=[rBT])
                    sy.op("pe", lambda e, tok=tok: e.transpose(PSm[:, 128:160], WEXP[32 * d:32 * d + 32, tok:tok + 128],
                                                               self.ident_f[32 * d:32 * d + 32, 32 * d:32 * d + 32]),
                          reads=[rDT, self.rC], writes=[rPSm])
                    sy.op("dve", lambda e: e.tensor_copy(out=WT[:], in_=PSm[:, 128:160]), reads=[rPSm], writes=[rWT])
                    sy.op("dve", lambda e, g=g: e.tensor_tensor(out=XW[:], in0=PSBv, in1=WT[:, g * HPG:(g + 1) * HPG].unsqueeze(2).to_broadcast([128, HPG, 64]),
                                                                op=ALU.mult), reads=[rPSB, rWT], writes=[rXW])
                    self.mm_group(PSm[:, 0:128], [(bc[:, 0, :], bc[:, 1, :])], [rin], rPSm)
                    sy.op("act", lambda e: e.activation(out=CBT[:], in_=PSm[:, 0:128], func=AF.Copy), reads=[rPSm], writes=[rCBT])
                    for hb in range(0, HPG, UB):
                        pse, rpse = PSe[(hb // UB) % 2]
                        psc, rpsc = PSc[(hb // UB) % 2]

                        def efn(e, hb=hb, pse=pse, tok=tok, g=g):
                            ins = None
                            for u in range(UB):
                                hd = 32 * d + g * HPG + hb + u
                                o = pse[:, u * 128:(u + 1) * 128]
                                e.matmul(o, SEL[:, hd, :], CS[:, tok:tok + 128], start=True, stop=False)
                                e.matmul(o, V[:, tok:tok + 128], SEL[:, hd, :], start=False, stop=False)
                                ins = e.matmul(o, self.negi[:], mk, start=False, stop=True)
                            return ins
                        sy.op("pe", efn, reads=[rSEL, rDT, self.rC, self.rCm], writes=[rpse])

                        def cfn(e, hb=hb, psc=psc, tok=tok, g=g):
                            ins = None
                            for u in range(UB):
                                hd = 32 * d + g * HPG + hb + u
                                ins = e.matmul(psc[:, u * 128:(u + 1) * 128], SEL[:, hd, :], CS[:, tok:tok + 128], start=True, stop=True)
                            return ins
                        sy.op("pe", cfn, reads=[rSEL, rDT], writes=[rpsc])
                        sy.op("act", lambda e, pse=pse: e.activation(out=MT[:], in_=pse[:, 0:UB * 128].rearrange("p (u l) -> p u l", l=128), func=AF.Exp),
                              reads=[rpse], writes=[rMT])
                        sy.op("dve", lambda e, hb=hb: e.tensor_tensor(out=MT2[:, hb:hb + UB, :], in0=MT[:],
                                                                      in1=CBT[:].unsqueeze(1).to_broadcast([128, UB, 128]), op=ALU.mult),
                              reads=[rMT, rCBT], writes=[rMT2])
                        sy.op("act", lambda e, psc=psc: e.activation(out=EC[:], in_=psc[:, 0:UB * 128].rearrange("p (u l) -> p u l", l=128), func=AF.Exp),
                              reads=[rpsc], writes=[rEC])
                        sy.op("dve", lambda e, hb=hb, bc=bc: e.tensor_tensor(out=CSC[:, hb:hb + UB, :], in0=EC[:],
                                                                              in1=bc[:, 1, :].unsqueeze(1).to_broadcast([128, UB, 128]), op=ALU.mult),
                              reads=[rEC, rin], writes=[rCSC])
                    def yfn(e):
                        ins = None
                        for q in range(NPR):
                            o = PSy[:, q * 128:(q + 1) * 128]
                            e.matmul(o, XT2[:, 2 * q, :], MT2[:, 2 * q, :], start=True, stop=False)
                            e.matmul(o, XT2[:, 2 * q + 1, :], MT2[:, 2 * q + 1, :], start=False, stop=False)
                            e.matmul(o, PB[:, 2 * q, :], CSC[:, 2 * q, :], start=False, stop=False)
                            ins = e.matmul(o, PB[:, 2 * q + 1, :], CSC[:, 2 * q + 1, :], start=False, stop=True)
                        return ins
                    sy.op("pe", yfn, reads=[rXT2, rMT2, rPB, rCSC], writes=[rPSy])
                    psyv = PSy[:, 0:NPR * 128].rearrange("p (q l) -> p q l", l=128)
                    if d == 0:
                        sy.op("act", lambda e, ys=ys: e.activation(out=ys[:], in_=psyv, func=AF.Copy), reads=[rPSy], writes=[rys])
                    else:
                        sy.op("dve", lambda e, ys=ys, yl=yl: e.tensor_tensor(out=ys[:], in0=psyv, in1=yl[:], op=ALU.add), reads=[rPSy, ryl], writes=[rys])
                    sy.dma(ydst, ys[:], rys, reads=[rys], writes=[self.rY])
                    self.mm_group(PSs[:, 0:HPG * 64], [(BT[:], XW[:].rearrange("p h q -> p (h q)"))], [rBT, rXW], rPSs)

                    def tfn2(e, c=c, g=g):
                        ins = None
                        for hh in range(HPG):
                            hd = 32 * d + g * HPG + hh
                            ins = e.matmul(PSm[:, 160 + hh:161 + hh], SEL[:, hd, :], TOT[:, c:c + 1], start=True, stop=True)
                        return ins
                    sy.op("pe", tfn2, reads=[rSEL, rDT], writes=[rPSm])
                    sy.op("act", lambda e: e.activation(out=ET[:], in_=PSm[:, 160:160 + HPG], func=AF.Exp), reads=[rPSm], writes=[rET])
                    sy.op("dve", lambda e: e.tensor_tensor(out=P32[:], in0=P32[:], in1=ET[:].unsqueeze(2).to_broadcast([128, HPG, 64]), op=ALU.mult),
                          reads=[rP32, rET], writes=[rP32])
                    sy.op("dve", lambda e: e.tensor_tensor(out=P32[:], in0=P32[:], in1=PSs[:, 0:HPG * 64].rearrange("p (h q) -> p h q", q=64), op=ALU.add),
                          reads=[rP32, rPSs], writes=[rP32])
                    sy.op("dve", lambda e: e.tensor_copy(out=PB[:, 0::2, 0:64], in_=P32[:, 0::2, :]), reads=[rP32], writes=[rPB])
                    sy.op("dve", lambda e: e.tensor_copy(out=PB[:, 1::2, 64:128], in_=P32[:, 1::2, :]), reads=[rP32], writes=[rPB])


    def ssd_out(self, l):
        cfg, sy = self.cfg, self.sy
        SIC, NG, TS = cfg.SIC, cfg.NG, cfg.TS2
        CPG = SIC // NG
        GS = cfg.SI // NG
        with Scope(self) as ss:
            YT = [ss.sb("yt", [128, SIC, TS], F32) for i in range(2)]
            ZT = [ss.sb("zt", [128, SIC, TS], BF16) for i in range(2)]
            XS = [ss.sb("xs", [128, SIC, TS], BF16) for i in range(2)]
            rI = [R("soin%d" % i) for i in range(2)]
            SQ = ss.sb("sqb", [128, SIC, TS], BF16)
            rSQ = R("sqb")
            RS = ss.sb("rs", [128, NG, TS], F32)
            rRS = R("rs")
            GNT = [ss.sb("gnt", [128, SIC, TS], BF16) for i in range(2)]
            rGNT = [R("gnt%d" % i) for i in range(2)]
            for ti, (t0, n, s) in enumerate(cfg.tiles2):
                bi = ti % 2
                yt, zt, xs, ri, gnt, rg = YT[bi], ZT[bi], XS[bi], rI[bi], GNT[bi], rGNT[bi]
                sy.dma(yt[:, :, :n], self.Y[:, t0:t0 + n].rearrange("(j p) t -> p j t", p=128), ri, reads=[self.rY], writes=[ri])
                sy.dma(zt[:, :, :n], self.Zs[:, t0:t0 + n].rearrange("(j p) t -> p j t", p=128), ri, reads=[self.rZs], writes=[ri])
                sy.dma(xs[:, :, :n], self.XBC[0:cfg.SI, t0:t0 + n].rearrange("(j p) t -> p j t", p=128), ri, reads=[self.rXBC], writes=[ri])
                for j in range(SIC):
                    sdd = self.vcol("sdd", j, 1)
                    eng = "dve"
                    sy.op(eng, lambda e, j=j, sdd=sdd, yt=yt, xs=xs, n=n: e.scalar_tensor_tensor(
                        out=yt[:, j, :n], in0=xs[:, j, :n], scalar=sdd, in1=yt[:, j, :n], op0=ALU.mult, op1=ALU.add),
                        reads=[ri, self.rVEC], writes=[ri])
                sy.op("dve", lambda e, yt=yt, zt=zt, n=n: e.tensor_tensor(out=yt[:, :, :n], in0=yt[:, :, :n], in1=zt[:, :, :n], op=ALU.mult),
                      reads=[ri], writes=[ri])
                sy.op("act", lambda e, yt=yt, n=n: e.activation(out=SQ[:, :, :n], in_=yt[:, :, :n], func=AF.Square), reads=[ri], writes=[rSQ])
                for g in range(NG):
                    ps, rps = self.nextps()
                    self.mm_group(ps[:, :n], [(self.ones_b[:], SQ[:, g * CPG + q, :n]) for q in range(CPG)], [rSQ, self.rC], rps)
                    sy.op("act", lambda e, ps=ps, g=g, n=n: e.activation(out=RS[:, g, :n], in_=ps[:, :n], func=AF.Sqrt, scale=1.0 / GS, bias=1e-6),
                          reads=[rps], writes=[rRS])
                sy.op("dve", lambda e, n=n: e.reciprocal(out=RS[:, :, :n], in_=RS[:, :, :n]), reads=[rRS], writes=[rRS])
                for j in range(SIC):
                    snw = self.vcol("snw", j, 1)
                    eng = "dve"
                    sy.op(eng, lambda e, j=j, snw=snw, yt=yt, gnt=gnt, n=n: e.scalar_tensor_tensor(
                        out=gnt[:, j, :n], in0=yt[:, j, :n], scalar=snw, in1=RS[:, j // CPG, :n], op0=ALU.mult, op1=ALU.mult),
                        reads=[ri, rRS, self.rVEC], writes=[rg])
                sy.dma(self.GN[:, t0:t0 + n].rearrange("(j p) t -> p j t", p=128), gnt[:, :, :n], rg, reads=[rg], writes=[self.rGN])

    def resid_tiles(self, first, KD):
        pass

    def merge(self, l, first):
        cfg, sy = self.cfg, self.sy
        KD, LWC, SIC, TS, D = cfg.KD, cfg.LWC, cfg.SIC, cfg.TS2, cfg.D
        with Scope(self) as ss:
            WL = ss.sb("wlp", [128, LWC, D], BF16)
            WS = ss.sb("wsp", [128, SIC, D], BF16)
            WO = ss.sb("wop", [128, KD, D], BF16)
            rWL, rWS, rWO = R("wlp"), R("wsp"), R("wop")
            self.load_w(WL, rWL, self.i["lru_proj"][l], LWC, D)
            self.load_w(WS, rWS, self.i["ssd_proj"][l], SIC, D)
            self.load_w(WO, rWO, self.i["w_out"][l], KD, D)
            YA = [ss.sb("ya", [128, LWC, TS], BF16) for i in range(2)]
            GNT = [ss.sb("gn", [128, SIC, TS], BF16) for i in range(2)]
            GTT = [ss.sb("gt", [128, 2 * KD, TS], BF16) for i in range(2)]
            XT = [ss.sb("xr", [128, KD, TS], F32) for i in range(2)]
            rI = [R("mgin%d" % i) for i in range(2)]
            rX = [R("mgx%d" % i) for i in range(2)]
            M1 = ss.sb("m1", [128, TS], F32)
            rM1 = R("m1")
            MB = ss.sb("mb", [128, KD, TS], BF16)
            rMB = R("mb")
            for ti, (t0, n, s) in enumerate(cfg.tiles2):
                bi = ti % 2
                ya, gn, gt, xt, ri, rx = YA[bi], GNT[bi], GTT[bi], XT[bi], rI[bi], rX[bi]
                sy.dma(ya[:, :, :n], self.YAG[:, t0:t0 + n].rearrange("(j p) t -> p j t", p=128), ri, reads=[self.rYAG], writes=[ri])
                sy.dma(gn[:, :, :n], self.GN[:, t0:t0 + n].rearrange("(j p) t -> p j t", p=128), ri, reads=[self.rGN], writes=[ri])
                sy.dma(gt[:, :, :n], self.GT[:, t0:t0 + n].rearrange("(j p) t -> p j t", p=128), ri, reads=[self.rGT], writes=[ri])
                xsrc = (self.i["xr0"] if first else self.XR)[:, t0:t0 + n].rearrange("(kc p) t -> p kc t", p=128)
                sy.dma(xt[:, :, :n], xsrc, rx, reads=([] if first else [self.rXR]), writes=[rx])
                for jo in range(KD):
                    pa, rpa = self.nextps()
                    self.mm_group(pa[:, :n], [(WL[:, k, jo * 128:(jo + 1) * 128], ya[:, k, :n]) for k in range(LWC)], [rWL, ri], rpa)
                    pb, rpb = self.nextps()
                    self.mm_group(pb[:, :n], [(WS[:, k, jo * 128:(jo + 1) * 128], gn[:, k, :n]) for k in range(SIC)], [rWS, ri], rpb)
                    sy.op("dve", lambda e, pa=pa, gt=gt, jo=jo, n=n: e.tensor_tensor(out=M1[:, :n], in0=pa[:, :n], in1=gt[:, jo, :n], op=ALU.mult),
                          reads=[rpa, ri], writes=[rM1])
                    sy.op("dve", lambda e, pb=pb, gt=gt, jo=jo, n=n: e.tensor_tensor(out=MB[:, jo, :n], in0=pb[:, :n], in1=gt[:, KD + jo, :n], op=ALU.mult),
                          reads=[rpb, ri], writes=[rMB])
                    sy.op("pool", lambda e, jo=jo, n=n: e.tensor_tensor(out=MB[:, jo, :n], in0=MB[:, jo, :n], in1=M1[:, :n], op=ALU.add),
                          reads=[rM1, rMB], writes=[rMB])
                for jo in range(KD):
                    po, rpo = self.nextps()
                    self.mm_group(po[:, :n], [(WO[:, k, jo * 128:(jo + 1) * 128], MB[:, k, :n]) for k in range(KD)], [rWO, rMB], rpo)
                    gate = self.mod[:, 2 * KD + jo, s:s + 1]
                    sy.op("dve", lambda e, po=po, xt=xt, jo=jo, n=n, gate=gate: e.scalar_tensor_tensor(
                        out=xt[:, jo, :n], in0=po[:, :n], scalar=gate, in1=xt[:, jo, :n], op0=ALU.mult, op1=ALU.add),
                        reads=[rpo, rx, self.rMOD], writes=[rx])
                sy.dma(self.XR[:, t0:t0 + n].rearrange("(kc p) t -> p kc t", p=128), xt[:, :, :n], rx, reads=[rx], writes=[self.rXR])

    def ffn_up(self, l, H, rH):
        cfg, sy = self.cfg, self.sy
        KD, T, CTX, SEQ, GW, FFC, TS = cfg.KD, cfg.T, cfg.CTX, cfg.SEQ, cfg.GW, cfg.FFC, cfg.TS
        ROWS = SEQ // GW
        wsrc = self.i["ffn_w_up"][l]
        with Scope(self) as ss:
            Ws = [ss.sb("wu", [128, KD, 256], BF16) for i in range(2)]
            rWs = [R("wu%d" % i) for i in range(2)]
            PADX = ss.sb("padx", [128, ROWS + 2, GW + 2], BF16)
            PADC = ss.sb("padc", [128, 3, CTX + 2], BF16)
            rPAD = R("pad")
            VB = ss.sb("vb", [128, T], BF16)
            rVB = R("vb")
            AO = [ss.sb("ao", [128, T], BF16) for i in range(2)]
            rAO = [R("ao%d" % i) for i in range(2)]
            DG = ss.sb("dg9", [128, 9, 128], BF16)
            rDG = R("dg9")
            TMP = ss.sb("tmpf", [128, 3, TS], F32)
            rTMP = R("tmpf")
            sy.op("dve", lambda e: e.memset(PADX[:], 0.0), writes=[rPAD])
            sy.op("dve", lambda e: e.memset(PADC[:], 0.0), writes=[rPAD])
            for j in range(FFC):
                W, rW = Ws[j % 2], rWs[j % 2]
                ao, rao = AO[j % 2], rAO[j % 2]
                self.load_w(W[:, :, 0:128], rW, wsrc[:, j * 128:(j + 1) * 128], KD, 128)
                self.load_w(W[:, :, 128:256], rW, wsrc[:, cfg.FF + j * 128:cfg.FF + (j + 1) * 128], KD, 128)
                self.make_diag(DG, rDG, "fcw", 9, FFC, j)
                for (t0, n, s) in cfg.tiles:
                    ps, rps = self.proj_tile(W[:, :, 0:128], rW, H, rH, t0, n)
                    if s == 1:
                        dst = PADC[:, 1, 1 + t0:1 + t0 + n]
                        src = ps[:, :n]
                    else:
                        r0 = (t0 - CTX) // GW
                        nr = n // GW
                        dst = PADX[:, 1 + r0:1 + r0 + nr, 1:1 + GW]
                        src = ps[:, :n].rearrange("p (r c) -> p r c", c=GW)
                    sy.op("act", lambda e, dst=dst, src=src: e.activation(out=dst, in_=src, func=AF.Copy), reads=[rps], writes=[rPAD])
                    ps, rps = self.proj_tile(W[:, :, 128:256], rW, H, rH, t0, n)
                    sy.op("act", lambda e, ps=ps, t0=t0, n=n: e.activation(out=VB[:, t0:t0 + n], in_=ps[:, :n], func=AF.Copy), reads=[rps], writes=[rVB])
                fcb = self.vcol("fcb", j, 1)
                for (t0, n, s) in cfg.tiles:
                    ps, rps = self.nextps()
                    pairs = []
                    for kr in range(3):
                        for kc in range(3):
                            if s == 1:
                                rhs = PADC[:, kr, kc + t0:kc + t0 + n]
                            else:
                                r0 = (t0 - CTX) // GW
                                nr = n // GW
                                rhs = PADX[:, r0 + kr:r0 + kr + nr, kc:kc + GW]
                            pairs.append((DG[:, kr * 3 + kc, :], rhs))
                    outp = ps[:, :n] if s == 1 else ps[:, :n].rearrange("p (r c) -> p r c", c=GW)
                    self.mm_group(outp, pairs, [rDG, rPAD], rps)
                    self.gelu_from(TMP, rTMP, ps[:, :n], rps, ao[:, t0:t0 + n], rao, n, bias=fcb, mul=VB[:, t0:t0 + n], rmul=rVB)
                sy.dma(self.ACTV[j * 128:(j + 1) * 128, :], ao[:], rao, reads=[rao], writes=[self.rACTV])

    def ffn_down(self, l):
        cfg, sy = self.cfg, self.sy
        KD, FFC, TS, D = cfg.KD, cfg.FFC, cfg.TS2, cfg.D
        with Scope(self) as ss:
            WD = ss.sb("wd", [128, FFC, D], BF16)
            rWD = R("wd")
            self.load_w(WD, rWD, self.i["ffn_w_down"][l], FFC, D)
            AT = [ss.sb("at", [128, FFC, TS], BF16) for i in range(2)]
            XT = [ss.sb("xr2", [128, KD, TS], F32) for i in range(2)]
            rI = [R("fdin%d" % i) for i in range(2)]
            rX = [R("fdx%d" % i) for i in range(2)]
            for ti, (t0, n, s) in enumerate(cfg.tiles2):
                bi = ti % 2
                at, xt, ri, rx = AT[bi], XT[bi], rI[bi], rX[bi]
                sy.dma(at[:, :, :n], self.ACTV[:, t0:t0 + n].rearrange("(j p) t -> p j t", p=128), ri, reads=[self.rACTV], writes=[ri])
                sy.dma(xt[:, :, :n], self.XR[:, t0:t0 + n].rearrange("(kc p) t -> p kc t", p=128), rx, reads=[self.rXR], writes=[rx])
                for jo in range(KD):
                    po, rpo = self.nextps()
                    self.mm_group(po[:, :n], [(WD[:, k, jo * 128:(jo + 1) * 128], at[:, k, :n]) for k in range(FFC)], [rWD, ri], rpo)
                    gate = self.mod[:, 5 * KD + jo, s:s + 1]
                    sy.op("dve", lambda e, po=po, xt=xt, jo=jo, n=n, gate=gate: e.scalar_tensor_tensor(
                        out=xt[:, jo, :n], in0=po[:, :n], scalar=gate, in1=xt[:, jo, :n], op0=ALU.mult, op1=ALU.add),
                        reads=[rpo, rx, self.rMOD], writes=[rx])
                sy.dma(self.XR[:, t0:t0 + n].rearrange("(kc p) t -> p kc t", p=128), xt[:, :, :n], rx, reads=[rx], writes=[self.rXR])

    def layer(self, l, first):
        cfg = self.cfg
        on = lambda n: (self.stages is None) or (n in self.stages)
        self.adaln(l)
        with Scope(self) as hs:
            H = hs.sb("H", [128, cfg.KD, cfg.T], BF16)
            rH = R("H")
            if on("norm1"):
                self.norm_mod(first, self.g1, 0, H, rH)
            if on("lru"):
                self.lru(l, H, rH)
            if on("prep"):
                self.ssd_prep(l, H, rH)
        for d in range(2):
            if on("sweep%d" % d):
                self.ssd_sweep(d)
        if on("ssdout"):
            self.ssd_out(l)
        if on("merge"):
            self.merge(l, first)
        with Scope(self) as hs:
            H = hs.sb("H2", [128, cfg.KD, cfg.T], BF16)
            rH = R("H2")
            if on("norm2"):
                self.norm_mod(False, self.g2, 3, H, rH)
            if on("ffnup"):
                self.ffn_up(l, H, rH)
        if on("ffndown"):
            self.ffn_down(l)

    def final_norm(self):
        cfg, sy = self.cfg, self.sy
        KD, TS = cfg.KD, cfg.TS2
        with Scope(self) as ss:
            xts = [ss.sb("xtf", [128, KD, TS], F32) for i in range(2)]
            rxs = [R("xtf%d" % i) for i in range(2)]
            sq = ss.sb("sqf", [128, KD, TS], F32)
            rsq = R("sqf")
            rstd = ss.sb("rstdf", [128, TS], F32)
            rrs = R("rstdf")
            rOUT = R("out")
            ti = 0
            for (t0, n, s) in cfg.tiles2:
                if s == 1:
                    continue
                xt, rx = xts[ti % 2], rxs[ti % 2]
                ti += 1
                sy.dma(xt[:, :, :n], self.XR[:, t0:t0 + n].rearrange("(kc p) t -> p kc t", p=128), rx, reads=[self.rXR], writes=[rx])
                self.rstd_of(xt, rx, n, sq, rsq, rstd, rrs, cfg.D)
                sy.op("dve", lambda e, xt=xt, n=n: e.tensor_tensor(out=xt[:, :, :n], in0=xt[:, :, :n],
                                                                   in1=rstd[:, :n].unsqueeze(1).to_broadcast([128, KD, n]), op=ALU.mult),
                      reads=[rx, rrs], writes=[rx])
                sy.op("dve", lambda e, xt=xt, n=n: e.tensor_tensor(out=xt[:, :, :n], in0=xt[:, :, :n],
                                                                   in1=self.vcol("fnw", 0, KD).unsqueeze(2).to_broadcast([128, KD, n]), op=ALU.mult),
                      reads=[rx, self.rVEC], writes=[rx])
                sy.dma(self.outf[:, t0 - cfg.CTX:t0 - cfg.CTX + n].rearrange("(kc p) t -> p kc t", p=128), xt[:, :, :n], rx, reads=[rx], writes=[rOUT])

    def copy_xr_out(self):
        cfg, sy = self.cfg, self.sy
        KD, TS = cfg.KD, cfg.TS
        with Scope(self) as ss:
            xts = [ss.sb("xtc", [128, KD, TS], F32) for i in range(2)]
            rxs = [R("xtc%d" % i) for i in range(2)]
            rOUT = R("out")
            for ti, (t0, n, s) in enumerate(cfg.tiles):
                xt, rx = xts[ti % 2], rxs[ti % 2]
                sy.dma(xt[:, :, :n], self.XR[:, t0:t0 + n].rearrange("(kc p) t -> p kc t", p=128), rx, reads=[self.rXR], writes=[rx])
                sy.dma(self.out[:, t0:t0 + n].rearrange("(kc p) t -> p kc t", p=128), xt[:, :, :n], rx, reads=[rx], writes=[rOUT])


FUSED = True
CFG_KW = {}
PER_LAYER = ("ada_w", "ada_b", "norm_mix_w", "norm_ffn_w", "w_in", "lru_conv_w", "lru_conv_b", "lru_wa", "lru_ba", "lru_wx",
             "lru_bx", "lru_lambda", "lru_proj", "ssd_conv_w", "ssd_conv_b", "ssd_dt_bias", "ssd_a_log", "ssd_d", "ssd_norm_w",
             "ssd_proj", "w_out", "ffn_w_up", "ffn_conv_w", "ffn_conv_b", "ffn_w_down")


def _maps(cfg, inp, B):
    base = host_prep(cfg, inp, 0)
    maps = [base]
    for b in range(1, B):
        m = dict(base)
        pb = host_prep_core(cfg, inp, b)
        m.update(pb)
        maps.append(m)
    return maps


def host_prep_core(cfg, inp, b):
    m = {}
    m["xr0"] = np.ascontiguousarray(np.concatenate([inp["ctx"][b].T, inp["x"][b].T], axis=1).astype(np.float32))
    cv = np.zeros((128, cfg.KD, 2), np.float32)
    cv[:, :, 0] = colmajor(inp["c"][b])
    cv[:, :, 1] = colmajor(inp["c_ctx"])
    m["cvec"] = cv
    return m


def kernel(**inputs):
    from concourse.bass_utils import run_bass_kernel_spmd
    inp = {k: np.asarray(v) for k, v in inputs.items()}
    B = inp["x"].shape[0]
    L = inp["w_in"].shape[0]
    if FUSED:
        cfg = Cfg(DEPTH=L, **CFG_KW)
        nc = Prog(cfg, layers=list(range(L)), final=True, emit_xr=False).build()
        maps = _maps(cfg, inp, B)
        res = run_bass_kernel_spmd(nc, maps, core_ids=list(range(B)))
        outs = [res.results[b]["out"] for b in range(B)]
    else:
        cfg = Cfg(DEPTH=1, **CFG_KW)
        xr = None
        for l in range(L):
            inpl = {k: (v[l:l + 1] if k in PER_LAYER else v) for k, v in inp.items()}
            maps = _maps(cfg, inpl, B)
            if xr is not None:
                for b in range(B):
                    maps[b]["xr0"] = xr[b]
            nc = Prog(cfg, layers=[0], final=True, emit_xr=True).build()
            res = run_bass_kernel_spmd(nc, maps, core_ids=list(range(B)))
            xr = [np.ascontiguousarray(res.results[b]["outx"]) for b in range(B)]
            outs = [res.results[b]["outf"] for b in range(B)]
    out = np.stack([np.ascontiguousarray(o.T) for o in outs]).astype(np.float32)
    return out
```

```python
import numpy as np
from contextlib import ExitStack
import concourse.bass as bass
import concourse.mybir as mybir

F32 = mybir.dt.float32
BF16 = mybir.dt.bfloat16
AF = mybir.ActivationFunctionType
ALU = mybir.AluOpType

ENGS = ("pe", "act", "dve", "pool", "sp")


class R:
    __slots__ = ("name", "w", "rs", "dsem", "dcnt")

    def __init__(self, name):
        self.name = name
        self.w = None
        self.rs = {}
        self.dsem = None
        self.dcnt = 0


class Sync:
    def __init__(self, nc, stack):
        self.nc = nc
        self.stack = stack
        self.ops = {e: [] for e in ENGS}
        self.sems = {}
        for e in ENGS:
            self.sems[("E", e)] = stack.enter_context(nc.semaphore("S_" + e))
        self.cnt = {e: 0 for e in ENGS}
        self.seen = {e: {} for e in ENGS}
        self.ndsem = 0
        self.dcount = {}
        self.dsem_by_name = {}

    def _waits(self, eng, reads, writes, extra=()):
        w = {}

        def add(ev):
            if ev is None:
                return
            k, v = ev
            if w.get(k, 0) < v:
                w[k] = v
        for r in reads:
            add(r.w)
        for r in writes:
            add(r.w)
            for k, v in r.rs.items():
                add((k, v))
        for ev in extra:
            add(ev)
        out = []
        seen = self.seen[eng]
        for k, v in w.items():
            if eng == "pe" and k == ("E", "pe"):
                continue
            if seen.get(k, 0) >= v:
                continue
            seen[k] = v
            out.append((k, v))
        return out

    def _mark(self, ev, reads, writes):
        k, v = ev
        for r in reads:
            if r.rs.get(k, 0) < v:
                r.rs[k] = v
        for r in writes:
            r.w = ev
            r.rs = {}

    def op(self, eng, fn, reads=(), writes=()):
        waits = self._waits(eng, reads, writes)
        self.cnt[eng] += 1
        ev = (("E", eng), self.cnt[eng])
        self.ops[eng].append((waits, fn, ("E", eng), 1))
        self._mark(ev, reads, writes)
        return ev

    def dma(self, out_ap, in_ap, sres, reads=(), writes=(), q="sp"):
        key = self.dsem_by_name.get(sres.name)
        if key is None:
            key = ("D", self.ndsem)
            self.ndsem += 1
            self.sems[key] = self.stack.enter_context(self.nc.semaphore("D%d" % key[1]))
            self.dsem_by_name[sres.name] = key
            self.dcount[key] = 0
        extra = []
        if self.dcount[key]:
            extra.append((key, self.dcount[key]))
        waits = self._waits(q, reads, writes, extra)
        self.dcount[key] += 16
        ev = (key, self.dcount[key])

        def fn(e, out_ap=out_ap, in_ap=in_ap):
            return e.dma_start(out=out_ap, in_=in_ap)
        self.ops[q].append((waits, fn, key, 16))
        self._mark(ev, reads, writes)
        return ev

    def barrier(self):
        evs = [(("E", e), self.cnt[e]) for e in ENGS if self.cnt[e]]
        evs += [(k, None) for k in self.sems if k[0] == "D"]
        for e in ENGS:
            waits = []
            for k, v in evs:
                if v is None:
                    v = self.dcount.get(k, 0)
                if v and self.seen[e].get(k, 0) < v and not (k == ("E", e)):
                    self.seen[e][k] = v
                    waits.append((k, v))
            if waits:
                self.ops[e].append((waits, None, None, 0))

    def fence(self, eng, resources):
        waits = self._waits(eng, [], resources)
        if waits:
            self.ops[eng].append((waits, None, None, 0))

    def emit(self):
        nc = self.nc
        engmap = {"pe": "tensor", "act": "scalar", "dve": "vector", "pool": "gpsimd", "sp": "sync"}
        with nc.Block() as block:
            for e in ENGS:
                ops = self.ops[e]

                def body(eng, ops=ops):
                    for waits, fn, semkey, inc in ops:
                        for k, v in waits:
                            eng.wait_ge(self.sems[k], v)
                        if fn is not None:
                            ins = fn(eng)
                            ins.then_inc(self.sems[semkey], inc)
                getattr(block, engmap[e])(body)
        self.ops = {e: [] for e in ENGS}


class Cfg:
    def __init__(self, D=1024, SEQ=4096, CTX=256, GW=64, LW=1024, SI=2048, NG=4, FF=2816, DEPTH=4, TS=512):
        self.D, self.SEQ, self.CTX, self.GW, self.LW, self.SI, self.NG, self.FF, self.DEPTH, self.TS = \
            D, SEQ, CTX, GW, LW, SI, NG, FF, DEPTH, TS
        self.T = CTX + SEQ
        self.KD = D // 128
        self.LWC = LW // 128
        self.SIC = SI // 128
        self.NH = SI // 64
        self.HPG = self.NH // NG
        self.CD = SI + 2 * NG * 128
        self.CDC = self.CD // 128
        self.FFC = FF // 128
        self.IN_DIM = 2 * LW + SI + self.CD + 2 * self.NH + 2 * D
        self.o_lx, self.o_lg = 0, LW
        self.o_z = 2 * LW
        self.o_xbc = 2 * LW + SI
        self.o_dt = self.o_xbc + self.CD
        self.o_gt = self.o_dt + 2 * self.NH
        self.tiles = []
        for t0 in range(0, CTX, TS):
            self.tiles.append((t0, min(TS, CTX - t0), 1))
        for t0 in range(0, SEQ, TS):
            self.tiles.append((CTX + t0, min(TS, SEQ - t0), 0))
        self.NCH = self.T // 128
        self.TS2 = min(256, TS)
        self.tiles2 = []
        for t0 in range(0, CTX, self.TS2):
            self.tiles2.append((t0, min(self.TS2, CTX - t0), 1))
        for t0 in range(0, SEQ, self.TS2):
            self.tiles2.append((CTX + t0, min(self.TS2, SEQ - t0), 0))
        off = {}
        o = 0
        for name, n in (("nmw", self.KD), ("nfw", self.KD), ("adab", 6 * self.KD), ("lcw", 4 * self.LWC),
                        ("lcb", self.LWC), ("lba", 2 * self.LWC), ("lbx", 2 * self.LWC), ("llam", 2 * self.LWC),
                        ("scw", 4 * self.CDC), ("scb", self.CDC), ("sdd", self.SIC), ("snw", self.SIC),
                        ("fcw", 9 * self.FFC), ("fcb", self.FFC), ("fnw", self.KD), ("dtb", 1), ("alog", 1)):
            off[name] = o
            o += n
        self.voff = off
        self.NV = o


def colmajor(v):
    v = np.asarray(v, np.float32)
    return np.ascontiguousarray(v.reshape(-1, 128).T)


def host_prep(cfg, inp, b):
    L = cfg.DEPTH
    m = {}
    m["xr0"] = np.ascontiguousarray(np.concatenate([inp["ctx"][b].T, inp["x"][b].T], axis=1).astype(np.float32))
    cv = np.zeros((128, cfg.KD, 2), np.float32)
    cv[:, :, 0] = colmajor(inp["c"][b])
    cv[:, :, 1] = colmajor(inp["c_ctx"])
    m["cvec"] = cv
    vecs = np.zeros((L, 128, cfg.NV), np.float32)
    vo = cfg.voff
    for l in range(L):
        def put(name, arr2d):
            vecs[l, :arr2d.shape[0], vo[name]:vo[name] + arr2d.shape[1]] = arr2d
        put("nmw", colmajor(inp["norm_mix_w"][l]))
        put("nfw", colmajor(inp["norm_ffn_w"][l]))
        put("adab", colmajor(inp["ada_b"][l]))
        put("lcw", np.concatenate([colmajor(inp["lru_conv_w"][l][k]) for k in range(4)], axis=1))
        put("lcb", colmajor(inp["lru_conv_b"][l]))
        put("lba", np.concatenate([colmajor(inp["lru_ba"][l][d]) for d in range(2)], axis=1))
        put("lbx", np.concatenate([colmajor(inp["lru_bx"][l][d]) for d in range(2)], axis=1))
        put("llam", np.concatenate([colmajor(inp["lru_lambda"][l][d]) for d in range(2)], axis=1))
        put("scw", np.concatenate([colmajor(inp["ssd_conv_w"][l][k]) for k in range(4)], axis=1))
        put("scb", colmajor(inp["ssd_conv_b"][l]))
        put("sdd", colmajor(np.repeat(np.asarray(inp["ssd_d"][l]), 64)))
        put("snw", colmajor(inp["ssd_norm_w"][l]))
        fw_ = np.asarray(inp["ffn_conv_w"][l]).reshape(9, cfg.FF)
        put("fcw", np.concatenate([colmajor(fw_[k]) for k in range(9)], axis=1))
        put("fcb", colmajor(inp["ffn_conv_b"][l]))
        put("fnw", colmajor(inp["final_norm_w"]))
        dtb = np.zeros((64, 1), np.float32)
        alog = np.zeros((64, 1), np.float32)
        for d in range(2):
            dtb[d * 32:d * 32 + cfg.NH, 0] = inp["ssd_dt_bias"][l][d]
            alog[d * 32:d * 32 + cfg.NH, 0] = inp["ssd_a_log"][l][d]
        put("dtb", dtb)
        put("alog", alog)
    m["vecs"] = vecs
    for k in ("ada_w", "w_in", "lru_wa", "lru_wx", "lru_proj", "ssd_proj", "w_out", "ffn_w_up", "ffn_w_down"):
        m[k] = np.ascontiguousarray(np.asarray(inp[k], np.float32))
    ident = np.eye(128, dtype=np.float32)
    m["c_ident"] = ident
    idx = np.arange(128)
    mk = np.zeros((2, 128, 128), np.float32)
    mk[0] = (idx[None, :] < idx[:, None])
    mk[1] = (idx[None, :] > idx[:, None])
    m["c_mask"] = mk
    sel = np.zeros((64, 64, 128), np.float32)
    for k in range(64):
        sel[k, k, :] = 1.0
    m["c_sel"] = sel
    rm = np.ones((2, 64, cfg.T), np.float32)
    rm[0, :, 0::128] = 0.0
    rm[1, :, 127::128] = 0.0
    m["c_rmask"] = rm
    return m


GELU_C = 1.5957691216057308
_uid = [0]


class Scope:
    def __init__(self, prog):
        self.p = prog
        self.st = ExitStack()

    def __enter__(self):
        return self

    def sb(self, name, shape, dt):
        _uid[0] += 1
        return self.st.enter_context(self.p.nc.sbuf_tensor("%s_%d" % (name, _uid[0]), list(shape), dt))

    def __exit__(self, *a):
        if a[0] is None:
            self.p.sy.barrier()
            self.p.sy.emit()
        self.st.close()
        return False


class Prog:
    def __init__(self, cfg, layers, debug=(), final=True, stop_after=None, stages=None, emit_xr=None):
        self.cfg = cfg
        self.layers = list(layers)
        self.debug = set(debug)
        self.final = final
        self.stop_after = stop_after
        self.stages = stages
        self.emit_xr = (not final) if emit_xr is None else emit_xr

    def build(self):
        nc = bass.Bass("TRN2", target_bir_lowering=False)
        self.nc = nc
        with ExitStack() as st:
            self.st = st
            self.sy = Sync(nc, st)
            self.declare_io()
            with Scope(self) as top:
                self.setup(top)
                for li, l in enumerate(self.layers):
                    self.layer(l, first=(li == 0))
                    self.sy.barrier()
                    self.sy.emit()
                if self.final:
                    self.final_norm()
                if self.emit_xr:
                    self.copy_xr_out()
        return nc

    def din(self, name, shape, dt=F32):
        return self.nc.dram_tensor(name, list(shape), dt, kind="ExternalInput").ap()

    def dscr(self, name, shape, dt, out=False):
        dbg = name in self.debug
        kind = "ExternalOutput" if (out or dbg) else "Internal"
        return self.nc.dram_tensor(name if out else ("d_" + name), list(shape), dt, kind=kind).ap()

    def declare_io(self):
        cfg = self.cfg
        L = cfg.DEPTH
        self.i = {}
        self.i["xr0"] = self.din("xr0", [cfg.D, cfg.T])
        self.i["cvec"] = self.din("cvec", [128, cfg.KD, 2])
        self.i["vecs"] = self.din("vecs", [L, 128, cfg.NV])
        self.i["ada_w"] = self.din("ada_w", [L, cfg.D, 6 * cfg.D])
        self.i["w_in"] = self.din("w_in", [L, cfg.D, cfg.IN_DIM])
        self.i["lru_wa"] = self.din("lru_wa", [L, 2, cfg.LW // 64, 64, 64])
        self.i["lru_wx"] = self.din("lru_wx", [L, 2, cfg.LW // 64, 64, 64])
        self.i["lru_proj"] = self.din("lru_proj", [L, cfg.LW, cfg.D])
        self.i["ssd_proj"] = self.din("ssd_proj", [L, cfg.SI, cfg.D])
        self.i["w_out"] = self.din("w_out", [L, cfg.D, cfg.D])
        self.i["ffn_w_up"] = self.din("ffn_w_up", [L, cfg.D, 2 * cfg.FF])
        self.i["ffn_w_down"] = self.din("ffn_w_down", [L, cfg.FF, cfg.D])
        self.i["c_ident"] = self.din("c_ident", [128, 128])
        self.i["c_mask"] = self.din("c_mask", [2, 128, 128])
        self.i["c_sel"] = self.din("c_sel", [64, 64, 128])
        self.i["c_rmask"] = self.din("c_rmask", [2, 64, cfg.T])
        if self.final:
            self.outf = self.dscr("out" if not self.emit_xr else "outf", [cfg.D, cfg.SEQ], F32, out=True)
        if self.emit_xr:
            self.out = self.dscr("out" if not self.final else "outx", [cfg.D, cfg.T], F32, out=True)
        self.XR = self.dscr("XR", [cfg.D, cfg.T], F32)
        self.YAG = self.dscr("YAG", [cfg.LW, cfg.T], BF16)
        self.Zs = self.dscr("Zs", [cfg.SI, cfg.T], BF16)
        self.GT = self.dscr("GT", [2 * cfg.D, cfg.T], BF16)
        self.XBC = self.dscr("XBC", [cfg.CD, cfg.T], BF16)
        self.Y = self.dscr("Y", [cfg.SI, cfg.T], F32)
        self.GN = self.dscr("GN", [cfg.SI, cfg.T], BF16)
        self.ACTV = self.dscr("ACTV", [cfg.FF, cfg.T], BF16)
        self.DTS = self.dscr("DTS", [3, 64, cfg.T], F32)
        self.TOTS = self.dscr("TOTS", [64, cfg.NCH], F32)
        self.rDTS = R("DTS")
        self.rXR, self.rYAG, self.rZs, self.rGT, self.rXBC, self.rY, self.rGN, self.rACTV = [R(n) for n in (
            "XR", "YAG", "Zs", "GT", "XBC", "Y", "GN", "ACTV")]

    def setup(self, top):
        cfg, sy = self.cfg, self.sy
        self.ident_f = top.sb("ident_f", [128, 128], F32)
        self.ident_b = top.sb("ident_b", [128, 128], BF16)
        self.ones_f = top.sb("ones_f", [128, 128], F32)
        self.ones_b = top.sb("ones_b", [128, 128], BF16)
        self.negi = top.sb("negi", [128, 128], BF16)
        self.maskf = top.sb("maskf", [128, 2, 128], F32)
        self.maskb = top.sb("maskb", [128, 2, 128], BF16)
        self.maskn = top.sb("maskn", [128, 2, 128], F32)
        self.rC = R("consts")
        self.rCm = R("constsm")
        rC = self.rC
        sy.dma(self.ident_f[:], self.i["c_ident"], rC, writes=[rC])
        sy.dma(self.maskf[:], self.i["c_mask"].rearrange("d s l -> s d l"), self.rCm, writes=[self.rCm])
        sy.op("dve", lambda e: e.tensor_copy(out=self.ident_b[:], in_=self.ident_f[:]), reads=[rC], writes=[rC])
        sy.op("dve", lambda e: e.tensor_copy(out=self.maskb[:], in_=self.maskf[:]), reads=[self.rCm], writes=[self.rCm])
        sy.op("dve", lambda e: e.tensor_scalar(out=self.maskn[:], in0=self.maskf[:], scalar1=-30000.0, scalar2=None, op0=ALU.mult),
              reads=[self.rCm], writes=[self.rCm])
        sy.op("dve", lambda e: e.tensor_scalar(out=self.maskf[:], in0=self.maskf[:], scalar1=-1.0, scalar2=1.0, op0=ALU.mult, op1=ALU.add),
              reads=[self.rCm], writes=[self.rCm])
        sy.op("dve", lambda e: e.tensor_scalar(out=self.negi[:], in0=self.ident_f[:], scalar1=-30000.0, scalar2=None, op0=ALU.mult),
              reads=[rC], writes=[rC])
        sy.op("dve", lambda e: e.memset(self.ones_f[:], 1.0), writes=[rC])
        sy.op("dve", lambda e: e.memset(self.ones_b[:], 1.0), writes=[rC])
        self.sc = top.sb("sc", [128, cfg.KD, 2], F32)
        self.rSC = R("sc")
        sy.dma(self.sc[:], self.i["cvec"], self.rSC, writes=[self.rSC])
        sy.op("act", lambda e: e.activation(out=self.sc[:], in_=self.sc[:], func=AF.Silu), reads=[self.rSC], writes=[self.rSC])
        self.vec = top.sb("vec", [128, cfg.NV], F32)
        self.rVEC = R("vec")
        self.mod = top.sb("mod", [128, 6 * cfg.KD, 2], F32)
        self.rMOD = R("mod")
        self.g1 = top.sb("g1", [128, cfg.KD, 2], F32)
        self.g2 = top.sb("g2", [128, cfg.KD, 2], F32)
        self.c8 = top.sb("c8", [128, 2 * cfg.LWC], F32)
        self.aneg = top.sb("aneg", [64, 1], F32)
        self.PS = [top.st.enter_context(self.nc.psum_tensor("ps%d" % i, [128, 512], F32)) for i in range(7)]
        self.rPS = [R("ps%d" % i) for i in range(7)]
        self.PSB = top.st.enter_context(self.nc.psum_tensor("psb", [128, 1024], BF16))
        self.rPSB = R("psb")
        self.ps_i = 0
        self.wst = [top.sb("wst%d" % i, [128, 8, 256], F32) for i in range(2)]
        self.rWST = [R("wst%d" % i) for i in range(2)]
        self.wst_i = 0
        sy.barrier()
        sy.emit()

    def vcol(self, name, j=0, n=1, rows=128):
        o = self.cfg.voff[name] + j
        return self.vec[:rows, o:o + n]

    def nextps(self, k=4):
        i = self.ps_i % k
        self.ps_i += 1
        return self.PS[i], self.rPS[i]

    def load_w(self, dst, rdst, src2d, K, C, rows_last=128):
        sy = self.sy
        for k0 in range(0, K, 8):
            kn = min(8, K - k0)
            for c0 in range(0, C, 256):
                cn = min(256, C - c0)
                bi = self.wst_i
                self.wst_i ^= 1
                ws, rw = self.wst[bi], self.rWST[bi]
                src = src2d[k0 * 128:(k0 + kn) * 128, c0:c0 + cn].rearrange("(kc p) c -> p kc c", p=128)
                sy.dma(ws[:, :kn, :cn], src, rw, writes=[rw])
                sy.op("pool", lambda e, ws=ws, kn=kn, cn=cn, k0=k0, c0=c0: e.tensor_copy(
                    out=dst[:, k0:k0 + kn, c0:c0 + cn], in_=ws[:, :kn, :cn]), reads=[rw], writes=[rdst])

    def mm_group(self, ps_ap, pairs, reads, rps):
        def fn(e):
            ins = None
            n = len(pairs)
            for i, (a, b) in enumerate(pairs):
                ins = e.matmul(ps_ap, a, b, start=(i == 0), stop=(i == n - 1))
            return ins
        self.sy.op("pe", fn, reads=reads, writes=[rps])

    def adaln(self, l):
        cfg, sy = self.cfg, self.sy
        KD = cfg.KD
        with Scope(self) as ss:
            adaw = [ss.sb("adaw", [128, KD, 512], F32) for i in range(2)]
            rA = [R("adaw%d" % i) for i in range(2)]
            sy.dma(self.vec[:], self.i["vecs"][l], self.rVEC, writes=[self.rVEC])
            psb, rps = self.PS[6], self.rPS[6]
            ncol = 6 * cfg.D
            bi = 0
            for g0 in range(0, ncol, 512):
                gw = min(512, ncol - g0)
                wt, rw = adaw[bi], rA[bi]
                bi ^= 1
                src = self.i["ada_w"][l][:, g0:g0 + gw].rearrange("(kc p) c -> p kc c", p=128)
                sy.dma(wt[:, :, :gw], src, rw, writes=[rw])
                for oc in range(gw // 128):
                    o = g0 // 128 + oc
                    self.mm_group(psb[:, 2 * o:2 * o + 2],
                                  [(wt[:, kc, oc * 128:(oc + 1) * 128], self.sc[:, kc, :]) for kc in range(KD)],
                                  [rw, self.rSC], rps)
            ab = self.vcol("adab", 0, 6 * KD)
            sy.op("dve", lambda e: e.tensor_tensor(out=self.mod[:], in0=psb[:, 0:12 * KD].rearrange("p (o s) -> p o s", s=2),
                                                   in1=ab.unsqueeze(2).to_broadcast([128, 6 * KD, 2]), op=ALU.add),
                  reads=[rps, self.rVEC], writes=[self.rMOD])
            for g, wname, m in ((self.g1, "nmw", 1), (self.g2, "nfw", 4)):
                def fn(e, g=g, wname=wname, m=m):
                    return e.scalar_tensor_tensor(out=g[:], in0=self.mod[:, m * KD:(m + 1) * KD, :], scalar=1.0,
                                                  in1=self.vcol(wname, 0, KD).unsqueeze(2).to_broadcast([128, KD, 2]),
                                                  op0=ALU.add, op1=ALU.mult)
                sy.op("dve", fn, reads=[self.rMOD, self.rVEC], writes=[self.rMOD])
            n8 = 2 * cfg.LWC
            sy.op("act", lambda e: e.activation(out=self.c8[:], in_=self.vcol("llam", 0, n8), func=AF.Exp, scale=-1.0),
                  reads=[self.rVEC], writes=[self.rMOD])
            sy.op("act", lambda e: e.activation(out=self.c8[:], in_=self.c8[:], func=AF.Ln, bias=1.0), reads=[self.rMOD], writes=[self.rMOD])
            sy.op("dve", lambda e: e.tensor_scalar(out=self.c8[:], in0=self.c8[:], scalar1=-8.0, scalar2=None, op0=ALU.mult),
                  reads=[self.rMOD], writes=[self.rMOD])
            sy.op("act", lambda e: e.activation(out=self.aneg[:], in_=self.vcol("alog", 0, 1, rows=64), func=AF.Exp),
                  reads=[self.rVEC], writes=[self.rMOD])
            sy.op("dve", lambda e: e.tensor_scalar(out=self.aneg[:], in0=self.aneg[:], scalar1=-1.0, scalar2=None, op0=ALU.mult),
                  reads=[self.rMOD], writes=[self.rMOD])

    def norm_mod(self, first, g, shift_m, H, rH):
        cfg, sy = self.cfg, self.sy
        KD = cfg.KD
        with Scope(self) as ss:
            xts = [ss.sb("xt", [128, KD, cfg.TS2], F32) for i in range(2)]
            rxs = [R("xt%d" % i) for i in range(2)]
            sq = ss.sb("sq", [128, KD, cfg.TS2], F32)
            rsq = R("sq")
            rstd = ss.sb("rstd", [128, cfg.TS2], F32)
            rrs = R("rstd")
            for ti, (t0, n, s) in enumerate(cfg.tiles2):
                xt, rx = xts[ti % 2], rxs[ti % 2]
                src = (self.i["xr0"] if first else self.XR)[:, t0:t0 + n].rearrange("(kc p) t -> p kc t", p=128)
                sy.dma(xt[:, :, :n], src, rx, reads=([] if first else [self.rXR]), writes=[rx])
                self.rstd_of(xt, rx, n, sq, rsq, rstd, rrs, cfg.D)
                sy.op("dve", lambda e, xt=xt, n=n: e.tensor_tensor(out=xt[:, :, :n], in0=xt[:, :, :n],
                                                                   in1=rstd[:, :n].unsqueeze(1).to_broadcast([128, KD, n]), op=ALU.mult),
                      reads=[rx, rrs], writes=[rx])
                for kc in range(KD):
                    sy.op("dve", lambda e, xt=xt, n=n, kc=kc, t0=t0, s=s: e.tensor_scalar(
                        out=H[:, kc, t0:t0 + n], in0=xt[:, kc, :n], scalar1=g[:, kc, s:s + 1],
                        scalar2=self.mod[:, shift_m * KD + kc, s:s + 1], op0=ALU.mult, op1=ALU.add),
                        reads=[rx, self.rMOD], writes=[rH])

    def rstd_of(self, xt, rx, n, sq, rsq, rstd, rrs, nchan):
        cfg, sy = self.cfg, self.sy
        K = nchan // 128
        sy.op("act", lambda e: e.activation(out=sq[:, :K, :n], in_=xt[:, :K, :n], func=AF.Square), reads=[rx], writes=[rsq])
        psb, rps = self.PS[5], self.rPS[5]
        self.mm_group(psb[:, :n], [(self.ones_f[:], sq[:, kc, :n]) for kc in range(K)], [rsq, self.rC], rps)
        sy.op("act", lambda e: e.activation(out=rstd[:, :n], in_=psb[:, :n], func=AF.Sqrt, scale=1.0 / nchan, bias=1e-6),
              reads=[rps], writes=[rrs])
        sy.op("dve", lambda e: e.reciprocal(out=rstd[:, :n], in_=rstd[:, :n]), reads=[rrs], writes=[rrs])

    def proj_tile(self, W, rW, H, rH, t0, n, M=128):
        KD = self.cfg.KD
        ps, rps = self.nextps()
        self.mm_group(ps[:M, :n], [(W[:, kc, :M], H[:, kc, t0:t0 + n]) for kc in range(KD)], [rW, rH], rps)
        return ps, rps

    def seqpos(self, t):
        return t + 2 if t < self.cfg.CTX else t + 5

    def make_diag(self, DG, rDG, name, ntap, nch, j):
        for k in range(ntap):
            self.sy.op("dve", lambda e, k=k: e.tensor_scalar(out=DG[:, k, :], in0=self.ident_f[:], scalar1=self.vcol(name, k * nch + j, 1),
                                                             scalar2=None, op0=ALU.mult), reads=[self.rC, self.rVEC], writes=[rDG])

    def conv1d_tile(self, DG, rDG, SEQB, rSEQB, t0, n):
        ps, rps = self.nextps()
        p0 = self.seqpos(t0)
        self.mm_group(ps[:, :n], [(DG[:, k, :], SEQB[:, p0 + k - 2:p0 + k - 2 + n]) for k in range(4)], [rDG, rSEQB], rps)
        return ps, rps

    def gelu_from(self, ss_tmp, rtmp, src_ap, rsrc, out_ap, rout, n, bias=None, mul=None, rmul=None):
        sy = self.sy
        x, x2, u = ss_tmp[:, 0, :n], ss_tmp[:, 1, :n], ss_tmp[:, 2, :n]
        if bias is not None:
            sy.op("act", lambda e: e.activation(out=x, in_=src_ap, func=AF.Identity, bias=bias), reads=[rsrc, self.rVEC], writes=[rtmp])
        else:
            sy.op("act", lambda e: e.activation(out=x, in_=src_ap, func=AF.Identity), reads=[rsrc], writes=[rtmp])
        sy.op("pool", lambda e: e.tensor_tensor(out=x2, in0=x, in1=x, op=ALU.mult), reads=[rtmp], writes=[rtmp])
        sy.op("dve", lambda e: e.tensor_scalar(out=x2, in0=x2, scalar1=0.044715, scalar2=1.0, op0=ALU.mult, op1=ALU.add), reads=[rtmp], writes=[rtmp])
        sy.op("pool", lambda e: e.tensor_tensor(out=u, in0=x2, in1=x, op=ALU.mult), reads=[rtmp], writes=[rtmp])
        sy.op("act", lambda e: e.activation(out=u, in_=u, func=AF.Sigmoid, scale=GELU_C), reads=[rtmp], writes=[rtmp])
        if mul is None:
            sy.op("dve", lambda e: e.tensor_tensor(out=out_ap, in0=u, in1=x, op=ALU.mult), reads=[rtmp], writes=[rout])
        else:
            sy.op("dve", lambda e: e.tensor_tensor(out=u, in0=u, in1=x, op=ALU.mult), reads=[rtmp], writes=[rtmp])
            sy.op("dve", lambda e: e.tensor_tensor(out=out_ap, in0=u, in1=mul, op=ALU.mult), reads=[rtmp, rmul], writes=[rout])

    def lru(self, l, H, rH):
        cfg, sy = self.cfg, self.sy
        KD, T, CTX = cfg.KD, cfg.T, cfg.CTX
        with Scope(self) as ss:
            W = ss.sb("wl", [128, KD, 256], BF16)
            rW = R("wl")
            SEQB = ss.sb("seqb", [128, T + 6], BF16)
            rSEQB = R("seqb")
            Ub = ss.sb("ub", [128, T], BF16)
            rU = R("u")
            A = ss.sb("a", [128, T], F32)
            B0 = ss.sb("b0", [128, T], F32)
            B1 = ss.sb("b1", [128, T], F32)
            rA, rB0, rB1 = R("a"), R("b0"), R("b1")
            YG = ss.sb("yg", [128, T], BF16)
            rYG = R("yg")
            DG = ss.sb("dg", [128, 4, 128], BF16)
            rDG = R("dg")
            GST = ss.sb("gst", [128, 4, 128], F32)
            rGST = R("gst")
            GW = ss.sb("gw", [128, 4, 128], BF16)
            rGW = R("gw")
            TMP = ss.sb("tmp", [128, 3, cfg.TS], F32)
            rTMP = R("tmp")
            RT = ss.sb("rt", [128, 4, cfg.TS], F32)
            rRT = R("rt")
            sy.op("dve", lambda e: e.memset(SEQB[:], 0.0), writes=[rSEQB])
            for j in range(cfg.LWC):
                wsrc = self.i["w_in"][l]
                self.load_w(W[:, :, 0:128], rW, wsrc[:, cfg.o_lx + j * 128: cfg.o_lx + (j + 1) * 128], KD, 128)
                self.load_w(W[:, :, 128:256], rW, wsrc[:, cfg.o_lg + j * 128: cfg.o_lg + (j + 1) * 128], KD, 128)
                self.make_diag(DG, rDG, "lcw", 4, cfg.LWC, j)
                sy.op("pool", lambda e: e.memset(GST[:], 0.0), writes=[rGST])
                for d in range(2):
                    for wi, wname in enumerate(("lru_wa", "lru_wx")):
                        for hh in range(2):
                            sy.dma(GST[hh * 64:(hh + 1) * 64, d * 2 + wi, hh * 64:(hh + 1) * 64],
                                   self.i[wname][l, d, 2 * j + hh], rGST, writes=[rGST])
                sy.op("pool", lambda e: e.tensor_copy(out=GW[:], in_=GST[:]), reads=[rGST], writes=[rGW])
                for (t0, n, s) in cfg.tiles:
                    ps, rps = self.proj_tile(W[:, :, 0:128], rW, H, rH, t0, n)
                    p0 = self.seqpos(t0)
                    sy.op("act", lambda e, ps=ps, p0=p0, n=n: e.activation(out=SEQB[:, p0:p0 + n], in_=ps[:, :n], func=AF.Copy),
                          reads=[rps], writes=[rSEQB])
                for (t0, n, s) in cfg.tiles:
                    ps, rps = self.conv1d_tile(DG, rDG, SEQB, rSEQB, t0, n)
                    lcb = self.vcol("lcb", j, 1)
                    sy.op("act", lambda e, ps=ps, t0=t0, n=n, lcb=lcb: e.activation(out=Ub[:, t0:t0 + n], in_=ps[:, :n], func=AF.Identity,
                                                                                    bias=lcb), reads=[rps, self.rVEC], writes=[rU])
                for d in range(2):
                    Bd, rBd = (B0, rB0) if d == 0 else (B1, rB1)
                    for (t0, n, s) in cfg.tiles:
                        psr, rpsr = self.nextps()
                        self.mm_group(psr[:, :n], [(GW[:, 2 * d, :], Ub[:, t0:t0 + n])], [rGW, rU], rpsr)
                        psi, rpsi = self.nextps()
                        self.mm_group(psi[:, :n], [(GW[:, 2 * d + 1, :], Ub[:, t0:t0 + n])], [rGW, rU], rpsi)
                        r_, i_, a2, sq = RT[:, 0, :n], RT[:, 1, :n], RT[:, 2, :n], RT[:, 3, :n]
                        lba = self.vcol("lba", d * cfg.LWC + j, 1)
                        lbx = self.vcol("lbx", d * cfg.LWC + j, 1)
                        sy.op("act", lambda e, psr=psr, n=n, r_=r_, lba=lba: e.activation(out=r_, in_=psr[:, :n], func=AF.Sigmoid,
                                                                                          bias=lba),
                              reads=[rpsr, self.rVEC], writes=[rRT])
                        sy.op("act", lambda e, psi=psi, n=n, i_=i_, lbx=lbx: e.activation(out=i_, in_=psi[:, :n], func=AF.Sigmoid,
                                                                                          bias=lbx),
                              reads=[rpsi, self.rVEC], writes=[rRT])
                        c8 = self.c8[:, d * cfg.LWC + j: d * cfg.LWC + j + 1]
                        sy.op("act", lambda e, t0=t0, n=n, r_=r_, c8=c8: e.activation(out=A[:, t0:t0 + n], in_=r_, func=AF.Exp, scale=c8),
                              reads=[rRT, self.rMOD], writes=[rA])
                        sy.op("pool", lambda e, t0=t0, n=n, a2=a2: e.tensor_tensor(out=a2, in0=A[:, t0:t0 + n], in1=A[:, t0:t0 + n], op=ALU.mult),
                              reads=[rA], writes=[rRT])
                        sy.op("dve", lambda e, a2=a2: e.tensor_scalar(out=a2, in0=a2, scalar1=-1.0, scalar2=1.0, op0=ALU.mult, op1=ALU.add),
                              reads=[rRT], writes=[rRT])
                        sy.op("act", lambda e, a2=a2, sq=sq: e.activation(out=sq, in_=a2, func=AF.Sqrt), reads=[rRT], writes=[rRT])
                        sy.op("pool", lambda e, t0=t0, n=n, i_=i_: e.tensor_tensor(out=i_, in0=i_, in1=Ub[:, t0:t0 + n], op=ALU.mult),
                              reads=[rRT, rU], writes=[rRT])
                        sy.op("dve", lambda e, t0=t0, n=n, i_=i_, sq=sq, Bd=Bd: e.tensor_tensor(out=Bd[:, t0:t0 + n], in0=sq, in1=i_, op=ALU.mult),
                              reads=[rRT], writes=[rBd])
                    if d == 0:
                        sy.op("dve", lambda e: e.tensor_tensor_scan(out=B0[:, :], data0=A[:, :], data1=B0[:, :], initial=0.0,
                                                                    op0=ALU.mult, op1=ALU.add), reads=[rA, rB0], writes=[rB0])
                    else:
                        sy.op("dve", lambda e: e.tensor_tensor_scan(out=B1[:, 0:CTX][:, ::-1], data0=A[:, 0:CTX][:, ::-1],
                                                                    data1=B1[:, 0:CTX][:, ::-1], initial=0.0, op0=ALU.mult, op1=ALU.add),
                              reads=[rA, rB1], writes=[rB1])
                        sy.op("dve", lambda e: e.tensor_tensor_scan(out=B1[:, CTX:T][:, ::-1], data0=A[:, CTX:T][:, ::-1],
                                                                    data1=B1[:, CTX:T][:, ::-1], initial=B1[:, 0:1], op0=ALU.mult, op1=ALU.add),
                              reads=[rA, rB1], writes=[rB1])
                sy.op("pool", lambda e: e.tensor_tensor(out=B0[:], in0=B0[:], in1=B1[:], op=ALU.add), reads=[rB0, rB1], writes=[rB0])
                for (t0, n, s) in cfg.tiles:
                    ps, rps = self.proj_tile(W[:, :, 128:256], rW, H, rH, t0, n)
                    self.gelu_from(TMP, rTMP, ps[:, :n], rps, YG[:, t0:t0 + n], rYG, n, mul=B0[:, t0:t0 + n], rmul=rB0)
                sy.dma(self.YAG[j * 128:(j + 1) * 128, :], YG[:], rYG, reads=[rYG], writes=[self.rYAG])


    def ssd_prep(self, l, H, rH):
        cfg, sy = self.cfg, self.sy
        KD, T = cfg.KD, cfg.T
        wsrc = self.i["w_in"][l]
        with Scope(self) as ss:
            Ws = [ss.sb("wp", [128, KD, 128], BF16) for i in range(2)]
            rWs = [R("wp%d" % i) for i in range(2)]
            OB = [ss.sb("ob", [128, T], BF16) for i in range(2)]
            rOB = [R("ob%d" % i) for i in range(2)]
            SEQB = ss.sb("seqb", [128, T + 6], BF16)
            rSEQB = R("seqb")
            DG = ss.sb("dg", [128, 4, 128], BF16)
            rDG = R("dg")
            sy.op("dve", lambda e: e.memset(SEQB[:], 0.0), writes=[rSEQB])
            k = 0
            jobs = [("z", cfg.o_z, j, self.Zs, self.rZs, AF.Silu) for j in range(cfg.SIC)]
            jobs += [("gt", cfg.o_gt, j, self.GT, self.rGT, AF.Sigmoid) for j in range(2 * KD)]
            jobs += [("xbc", cfg.o_xbc, j, self.XBC, self.rXBC, AF.Silu) for j in range(cfg.CDC)]
            for (kind, o, j, dst, rdst, func) in jobs:
                W, rW = Ws[k % 2], rWs[k % 2]
                ob, rob = OB[k % 2], rOB[k % 2]
                k += 1
                self.load_w(W, rW, wsrc[:, o + j * 128:o + (j + 1) * 128], KD, 128)
                if kind != "xbc":
                    for (t0, n, s) in cfg.tiles:
                        ps, rps = self.proj_tile(W, rW, H, rH, t0, n)
                        sy.op("act", lambda e, ps=ps, t0=t0, n=n, ob=ob, func=func: e.activation(out=ob[:, t0:t0 + n], in_=ps[:, :n], func=func),
                              reads=[rps], writes=[rob])
                else:
                    self.make_diag(DG, rDG, "scw", 4, cfg.CDC, j)
                    for (t0, n, s) in cfg.tiles:
                        ps, rps = self.proj_tile(W, rW, H, rH, t0, n)
                        p0 = self.seqpos(t0)
                        sy.op("act", lambda e, ps=ps, p0=p0, n=n: e.activation(out=SEQB[:, p0:p0 + n], in_=ps[:, :n], func=AF.Copy),
                              reads=[rps], writes=[rSEQB])
                    scb = self.vcol("scb", j, 1)
                    for (t0, n, s) in cfg.tiles:
                        ps, rps = self.conv1d_tile(DG, rDG, SEQB, rSEQB, t0, n)
                        sy.op("act", lambda e, ps=ps, t0=t0, n=n, ob=ob, scb=scb: e.activation(out=ob[:, t0:t0 + n], in_=ps[:, :n], func=AF.Silu, bias=scb),
                              reads=[rps, self.rVEC], writes=[rob])
                sy.dma(dst[j * 128:(j + 1) * 128, :], ob[:], rob, reads=[rob], writes=[rdst])
        with Scope(self) as ss:
            CS = ss.sb("cs", [64, T], F32)
            V = ss.sb("v", [64, T], F32)
            WEXP = ss.sb("wexp", [64, T], F32)
            TOT = ss.sb("tot", [64, cfg.NCH], F32)
            rDT = R("dtp")
            WDT = ss.sb("wdt", [128, KD, 64], BF16)
            rWDT = R("wdt")
            X1 = ss.sb("x1", [64, T], F32)
            RM = WEXP
            rX1, rRM = R("x1"), rDT
            NH = cfg.NH
            sy.op("pool", lambda e: e.memset(WDT[:], 0.0), writes=[rWDT])
            self.load_w(WDT[:, :, 0:NH], rWDT, wsrc[:, cfg.o_dt:cfg.o_dt + NH], KD, NH)
            self.load_w(WDT[:, :, 32:32 + NH], rWDT, wsrc[:, cfg.o_dt + NH:cfg.o_dt + 2 * NH], KD, NH)
            sy.dma(RM[0:32, :], self.i["c_rmask"][0, 0:32, :], rRM, writes=[rRM])
            sy.dma(RM[32:64, :], self.i["c_rmask"][1, 32:64, :], rRM, writes=[rRM])
            dtb = self.vcol("dtb", 0, 1, rows=64)
            for (t0, n, s) in cfg.tiles:
                ps, rps = self.proj_tile(WDT, rWDT, H, rH, t0, n, M=64)
                sy.op("act", lambda e, ps=ps, t0=t0, n=n: e.activation(out=X1[:, t0:t0 + n], in_=ps[:64, :n], func=AF.Exp, bias=dtb),
                      reads=[rps, self.rVEC], writes=[rX1])
            X2, X3 = V, CS
            sy.op("act", lambda e: e.activation(out=X1[:], in_=X1[:], func=AF.Ln, bias=1.0), reads=[rX1], writes=[rX1])
            sy.op("act", lambda e: e.activation(out=X2[:], in_=X1[:], func=AF.Ln), reads=[rX1], writes=[rDT])
            sy.op("dve", lambda e: e.tensor_scalar(out=X3[:], in0=X1[:], scalar1=self.aneg[:, 0:1], scalar2=None, op0=ALU.mult),
                  reads=[rX1, self.rMOD], writes=[rDT])
            sy.op("dve", lambda e: e.tensor_tensor_scan(out=X3[0:32, :], data0=RM[0:32, :], data1=X3[0:32, :], initial=0.0,
                                                        op0=ALU.mult, op1=ALU.add), reads=[rRM, rDT], writes=[rDT])
            sy.op("dve", lambda e: e.tensor_tensor_scan(out=X3[32:64, :][:, ::-1], data0=RM[32:64, :][:, ::-1], data1=X3[32:64, :][:, ::-1],
                                                        initial=0.0, op0=ALU.mult, op1=ALU.add), reads=[rRM, rDT], writes=[rDT])
            sy.op("dve", lambda e: e.tensor_tensor(out=V[:], in0=X2[:], in1=CS[:], op=ALU.subtract), reads=[rDT], writes=[rDT])
            CS3 = CS[:].rearrange("p (c q) -> p c q", q=128)
            V3 = V[:].rearrange("p (c q) -> p c q", q=128)
            W3 = WEXP[:].rearrange("p (c q) -> p c q", q=128)
            sy.op("dve", lambda e: e.tensor_copy(out=TOT[0:32, :], in_=CS3[0:32, :, 127]), reads=[rDT], writes=[rDT])
            sy.op("dve", lambda e: e.tensor_copy(out=TOT[32:64, :], in_=CS3[32:64, :, 0]), reads=[rDT], writes=[rDT])
            sy.op("dve", lambda e: e.tensor_tensor(out=W3, in0=V3, in1=TOT[:].unsqueeze(2).to_broadcast([64, cfg.NCH, 128]), op=ALU.add),
                  reads=[rDT], writes=[rDT])
            sy.op("act", lambda e: e.activation(out=WEXP[:], in_=WEXP[:], func=AF.Exp), reads=[rDT], writes=[rDT])
            for i_, t_ in enumerate((CS, V, WEXP)):
                sy.dma(self.DTS[i_], t_[:], rDT, reads=[rDT], writes=[self.rDTS])
            sy.op("act", lambda e: e.activation(out=TOT[:], in_=TOT[:], func=AF.Exp), reads=[rDT], writes=[rDT])
            sy.dma(self.TOTS, TOT[:], rDT, reads=[rDT], writes=[self.rDTS])

    def ssd_sweep(self, d):
        cfg, sy = self.cfg, self.sy
        HPG, NPR, NG, SIC, NCH = cfg.HPG, cfg.HPG // 2, cfg.NG, cfg.SIC, cfg.NCH
        nctx = cfg.CTX // 128
        order = list(range(NCH)) if d == 0 else (list(range(nctx - 1, -1, -1)) + list(range(NCH - 1, nctx - 1, -1)))
        with Scope(self) as ss:
            V = ss.sb("v", [64, cfg.T], F32)
            WEXP = ss.sb("wexp", [64, cfg.T], F32)
            ETB = ss.sb("etb", [128, 64, NCH], F32)
            rDT = R("dts")
            sy.dma(V[:], self.DTS[1], rDT, reads=[self.rDTS], writes=[rDT])
            sy.dma(WEXP[:], self.DTS[2], rDT, reads=[self.rDTS], writes=[rDT])
            sy.dma(ETB[:], self.TOTS.partition_broadcast(128), rDT, reads=[self.rDTS], writes=[rDT])
            BC = [ss.sb("bc", [128, 2, 128], BF16) for i in range(2)]
            XF = [ss.sb("xf", [128, NPR, 128], BF16) for i in range(2)]
            YL = [ss.sb("yl", [128, NPR, 128], F32) for i in range(2)]
            CSB = [ss.sb("csb", [128, HPG, 128], F32) for i in range(2)]
            rIN = [R("ssdin%d" % i) for i in range(2)]
            rYL = [R("yl%d" % i) for i in range(2)]
            rCSB = [R("csb%d" % i) for i in range(2)]
            XT2 = ss.sb("xt2", [128, HPG, 128], BF16)
            BT = ss.sb("bt", [128, 128], BF16)
            WVT = ss.sb("wvt", [128, 64], F32)
            XW = ss.sb("xw", [128, HPG, 64], BF16)
            CBM = ss.sb("cbm", [128, 128], BF16)
            rXT2, rBT, rWT, rXW, rCBT = R("xt2"), R("bt"), R("wt"), R("xw"), R("cbt")
            EX = ss.sb("ex", [128, HPG, 128], F32)
            MT = ss.sb("mt", [128, HPG, 128], BF16)
            EC = ss.sb("ec", [128, HPG, 128], BF16)
            rEX, rMT, rEC = R("ex"), R("mt"), R("ec")
            MT2 = ss.sb("mt2", [128, HPG, 128], BF16)
            CSC = ss.sb("csc", [128, HPG, 128], BF16)
            rMT2, rCSC = R("mt2"), R("csc")
            YS = [ss.sb("ys", [128, NPR, 128], F32) for i in range(2)]
            rYS = [R("ys%d" % i) for i in range(2)]
            P32 = ss.sb("p32", [128, HPG, 64], F32)
            PB = ss.sb("pb", [128, HPG, 128], BF16)
            rP32, rPB = R("p32"), R("pb")
            PSy, rPSy = self.PS[4], self.rPS[4]
            PSs, rPSs = self.PS[5], self.rPS[5]
            PSm, rPSm = self.PS[6], self.rPS[6]
            PSw, rPSw = self.PS[3], self.rPS[3]
            PSB, rPSB = self.PSB, self.rPSB
            PSBv = PSB[:, 0:HPG * 64].rearrange("p (h q) -> p h q", q=64)
            mkn = self.maskn[:, d, :]
            mkv = self.maskf[:, d, :]
            sy.op("dve", lambda e: e.memset(XT2[:], 0.0), writes=[rXT2])
            it = 0
            for g in range(NG):
                sy.op("dve", lambda e: e.memset(P32[:], 0.0), writes=[rP32])
                sy.op("dve", lambda e: e.memset(PB[:], 0.0), writes=[rPB])
                hd0 = 32 * d + g * HPG
                for c in order:
                    tok = c * 128
                    bi = it % 2
                    it += 1
                    bc, xf, yl, rin, ryl, ys, rys, csb, rcsb = BC[bi], XF[bi], YL[bi], rIN[bi], rYL[bi], YS[bi], rYS[bi], CSB[bi], rCSB[bi]
                    for w_, ch in ((0, SIC + g), (1, SIC + NG + g)):
                        sy.dma(bc[:, w_, :], self.XBC[ch * 128:(ch + 1) * 128, tok:tok + 128], rin, reads=[self.rXBC], writes=[rin])
                    sy.dma(xf[:], self.XBC[g * NPR * 128:(g + 1) * NPR * 128, tok:tok + 128].rearrange("(q p) t -> p q t", p=128),
                           rin, reads=[self.rXBC], writes=[rin])
                    sy.dma(csb[:], self.DTS[0][hd0:hd0 + HPG, tok:tok + 128].partition_broadcast(128), rcsb, reads=[self.rDTS], writes=[rcsb])
                    ydst = self.Y[g * NPR * 128:(g + 1) * NPR * 128, tok:tok + 128].rearrange("(q p) t -> p q t", p=128)
                    if d == 1:
                        sy.dma(yl[:], ydst, ryl, reads=[self.rY], writes=[ryl])
                    def tfn(e, xf=xf, bc=bc):
                        ins = None
                        for q in range(NPR):
                            ins = e.transpose(PSB[:, q * 128:(q + 1) * 128], xf[:, q, :], self.ident_b[:])
                        ins = e.transpose(PSB[:, NPR * 128:(NPR + 1) * 128], bc[:, 0, :], self.ident_b[:])
                        return ins
                    sy.op("pe", tfn, reads=[rin, self.rC], writes=[rPSB])
                    sy.op("dve", lambda e: e.tensor_copy(out=XT2[:, 0::2, 0:64], in_=PSBv[:, 0::2, :]), reads=[rPSB], writes=[rXT2])
                    sy.op("dve", lambda e: e.tensor_copy(out=XT2[:, 1::2, 64:128], in_=PSBv[:, 1::2, :]), reads=[rPSB], writes=[rXT2])
                    sy.op("dve", lambda e: e.tensor_copy(out=BT[:], in_=PSB[:, NPR * 128:(NPR + 1) * 128]), reads=[rPSB], writes=[rBT])
                    def wfn(e, tok=tok):
                        idn = self.ident_f[32 * d:32 * d + 32, 32 * d:32 * d + 32]
                        e.transpose(PSw[:, 0:32], WEXP[32 * d:32 * d + 32, tok:tok + 128], idn)
                        return e.transpose(PSw[:, 32:64], V[32 * d:32 * d + 32, tok:tok + 128], idn)
                    sy.op("pe", wfn, reads=[rDT, self.rC], writes=[rPSw])
                    sy.op("act", lambda e: e.activation(out=WVT[:], in_=PSw[:, 0:64], func=AF.Copy), reads=[rPSw], writes=[rWT])
                    sy.op("dve", lambda e, g=g: e.tensor_tensor(out=XW[:], in0=PSBv, in1=WVT[:, g * HPG:(g + 1) * HPG].unsqueeze(2).to_broadcast([128, HPG, 64]),
                                                                op=ALU.mult), reads=[rPSB, rWT], writes=[rXW])
                    self.mm_group(PSm[:, 0:128], [(bc[:, 0, :], bc[:, 1, :])], [rin], rPSm)
                    sy.op("dve", lambda e: e.tensor_tensor(out=CBM[:], in0=PSm[:, 0:128], in1=mkv, op=ALU.mult), reads=[rPSm, self.rCm], writes=[rCBT])
                    for hh in range(HPG):
                        vcolap = WVT[:, 32 + g * HPG + hh:32 + g * HPG + hh + 1]
                        sy.op("dve", lambda e, hh=hh, csb=csb, vcolap=vcolap: e.scalar_tensor_tensor(
                            out=EX[:, hh, :], in0=csb[:, hh, :], scalar=vcolap, in1=mkn, op0=ALU.add, op1=ALU.add),
                            reads=[rcsb, rWT, self.rCm], writes=[rEX])
                    sy.op("act", lambda e: e.activation(out=MT[:], in_=EX[:], func=AF.Exp), reads=[rEX], writes=[rMT])
                    sy.op("act", lambda e, csb=csb: e.activation(out=EC[:], in_=csb[:], func=AF.Exp), reads=[rcsb], writes=[rEC])
                    sy.op("dve", lambda e: e.tensor_tensor(out=MT2[:], in0=MT[:], in1=CBM[:].unsqueeze(1).to_broadcast([128, HPG, 128]), op=ALU.mult),
                          reads=[rMT, rCBT], writes=[rMT2])
                    sy.op("dve", lambda e, bc=bc: e.tensor_tensor(out=CSC[:], in0=EC[:], in1=bc[:, 1, :].unsqueeze(1).to_broadcast([128, HPG, 128]), op=ALU.mult),
                          reads=[rEC, rin], writes=[rCSC])
                    def yfn(e):
                        ins = None
                        for q in range(NPR):
                            o = PSy[:, q * 128:(q + 1) * 128]
                            e.matmul(o, XT2[:, 2 * q, :], MT2[:, 2 * q, :], start=True, stop=False)
                            e.matmul(o, XT2[:, 2 * q + 1, :], MT2[:, 2 * q + 1, :], start=False, stop=False)
                            e.matmul(o, PB[:, 2 * q, :], CSC[:, 2 * q, :], start=False, stop=False)
                            ins = e.matmul(o, PB[:, 2 * q + 1, :], CSC[:, 2 * q + 1, :], start=False, stop=True)
                        return ins
                    sy.op("pe", yfn, reads=[rXT2, rMT2, rPB, rCSC], writes=[rPSy])
                    psyv = PSy[:, 0:NPR * 128].rearrange("p (q l) -> p q l", l=128)
                    if d == 0:
                        sy.op("act", lambda e, ys=ys: e.activation(out=ys[:], in_=psyv, func=AF.Copy), reads=[rPSy], writes=[rys])
                    else:
                        sy.op("dve", lambda e, ys=ys, yl=yl: e.tensor_tensor(out=ys[:], in0=psyv, in1=yl[:], op=ALU.add), reads=[rPSy, ryl], writes=[rys])
                    sy.dma(ydst, ys[:], rys, reads=[rys], writes=[self.rY])
                    self.mm_group(PSs[:, 0:HPG * 64], [(BT[:], XW[:].rearrange("p h q -> p (h q)"))], [rBT, rXW], rPSs)
                    et = ETB[:, hd0:hd0 + HPG, c]
                    sy.op("dve", lambda e, et=et: e.tensor_tensor(out=P32[:], in0=P32[:], in1=et.unsqueeze(2).to_broadcast([128, HPG, 64]), op=ALU.mult),
                          reads=[rP32, rDT], writes=[rP32])
                    sy.op("dve", lambda e: e.tensor_tensor(out=P32[:], in0=P32[:], in1=PSs[:, 0:HPG * 64].rearrange("p (h q) -> p h q", q=64), op=ALU.add),
                          reads=[rP32, rPSs], writes=[rP32])
                    sy.op("dve", lambda e: e.tensor_copy(out=PB[:, 0::2, 0:64], in_=P32[:, 0::2, :]), reads=[rP32], writes=[rPB])
                    sy.op("dve", lambda e: e.tensor_copy(out=PB[:, 1::2, 64:128], in_=P32[:, 1::2, :]), reads=[rP32], writes=[rPB])

    def ssd_out(self, l):
        cfg, sy = self.cfg, self.sy
        SIC, NG, TS = cfg.SIC, cfg.NG, cfg.TS2
        CPG = SIC // NG
        GS = cfg.SI // NG
        with Scope(self) as ss:
            YT = [ss.sb("yt", [128, SIC, TS], F32) for i in range(2)]
            ZT = [ss.sb("zt", [128, SIC, TS], BF16) for i in range(2)]
            XS = [ss.sb("xs", [128, SIC, TS], BF16) for i in range(2)]
            rI = [R("soin%d" % i) for i in range(2)]
            SQ = ss.sb("sqb", [128, SIC, TS], BF16)
            rSQ = R("sqb")
            RS = ss.sb("rs", [128, NG, TS], F32)
            rRS = R("rs")
            GNT = [ss.sb("gnt", [128, SIC, TS], BF16) for i in range(2)]
            rGNT = [R("gnt%d" % i) for i in range(2)]
            for ti, (t0, n, s) in enumerate(cfg.tiles2):
                bi = ti % 2
                yt, zt, xs, ri, gnt, rg = YT[bi], ZT[bi], XS[bi], rI[bi], GNT[bi], rGNT[bi]
                sy.dma(yt[:, :, :n], self.Y[:, t0:t0 + n].rearrange("(j p) t -> p j t", p=128), ri, reads=[self.rY], writes=[ri])
                sy.dma(zt[:, :, :n], self.Zs[:, t0:t0 + n].rearrange("(j p) t -> p j t", p=128), ri, reads=[self.rZs], writes=[ri])
                sy.dma(xs[:, :, :n], self.XBC[0:cfg.SI, t0:t0 + n].rearrange("(j p) t -> p j t", p=128), ri, reads=[self.rXBC], writes=[ri])
                for j in range(SIC):
                    sdd = self.vcol("sdd", j, 1)
                    eng = "dve"
                    sy.op(eng, lambda e, j=j, sdd=sdd, yt=yt, xs=xs, n=n: e.scalar_tensor_tensor(
                        out=yt[:, j, :n], in0=xs[:, j, :n], scalar=sdd, in1=yt[:, j, :n], op0=ALU.mult, op1=ALU.add),
                        reads=[ri, self.rVEC], writes=[ri])
                sy.op("dve", lambda e, yt=yt, zt=zt, n=n: e.tensor_tensor(out=yt[:, :, :n], in0=yt[:, :, :n], in1=zt[:, :, :n], op=ALU.mult),
                      reads=[ri], writes=[ri])
                sy.op("act", lambda e, yt=yt, n=n: e.activation(out=SQ[:, :, :n], in_=yt[:, :, :n], func=AF.Square), reads=[ri], writes=[rSQ])
                for g in range(NG):
                    ps, rps = self.nextps()
                    self.mm_group(ps[:, :n], [(self.ones_b[:], SQ[:, g * CPG + q, :n]) for q in range(CPG)], [rSQ, self.rC], rps)
                    sy.op("act", lambda e, ps=ps, g=g, n=n: e.activation(out=RS[:, g, :n], in_=ps[:, :n], func=AF.Sqrt, scale=1.0 / GS, bias=1e-6),
                          reads=[rps], writes=[rRS])
                sy.op("dve", lambda e, n=n: e.reciprocal(out=RS[:, :, :n], in_=RS[:, :, :n]), reads=[rRS], writes=[rRS])
                for j in range(SIC):
                    snw = self.vcol("snw", j, 1)
                    eng = "dve"
                    sy.op(eng, lambda e, j=j, snw=snw, yt=yt, gnt=gnt, n=n: e.scalar_tensor_tensor(
                        out=gnt[:, j, :n], in0=yt[:, j, :n], scalar=snw, in1=RS[:, j // CPG, :n], op0=ALU.mult, op1=ALU.mult),
                        reads=[ri, rRS, self.rVEC], writes=[rg])
                sy.dma(self.GN[:, t0:t0 + n].rearrange("(j p) t -> p j t", p=128), gnt[:, :, :n], rg, reads=[rg], writes=[self.rGN])

    def resid_tiles(self, first, KD):
        pass

    def merge(self, l, first):
        cfg, sy = self.cfg, self.sy
        KD, LWC, SIC, TS, D = cfg.KD, cfg.LWC, cfg.SIC, cfg.TS2, cfg.D
        with Scope(self) as ss:
            WL = ss.sb("wlp", [128, LWC, D], BF16)
            WS = ss.sb("wsp", [128, SIC, D], BF16)
            WO = ss.sb("wop", [128, KD, D], BF16)
            rWL, rWS, rWO = R("wlp"), R("wsp"), R("wop")
            self.load_w(WL, rWL, self.i["lru_proj"][l], LWC, D)
            self.load_w(WS, rWS, self.i["ssd_proj"][l], SIC, D)
            self.load_w(WO, rWO, self.i["w_out"][l], KD, D)
            YA = [ss.sb("ya", [128, LWC, TS], BF16) for i in range(2)]
            GNT = [ss.sb("gn", [128, SIC, TS], BF16) for i in range(2)]
            GTT = [ss.sb("gt", [128, 2 * KD, TS], BF16) for i in range(2)]
            XT = [ss.sb("xr", [128, KD, TS], F32) for i in range(2)]
            rI = [R("mgin%d" % i) for i in range(2)]
            rX = [R("mgx%d" % i) for i in range(2)]
            M1 = ss.sb("m1", [128, TS], F32)
            rM1 = R("m1")
            MB = ss.sb("mb", [128, KD, TS], BF16)
            rMB = R("mb")
            for ti, (t0, n, s) in enumerate(cfg.tiles2):
                bi = ti % 2
                ya, gn, gt, xt, ri, rx = YA[bi], GNT[bi], GTT[bi], XT[bi], rI[bi], rX[bi]
                sy.dma(ya[:, :, :n], self.YAG[:, t0:t0 + n].rearrange("(j p) t -> p j t", p=128), ri, reads=[self.rYAG], writes=[ri])
                sy.dma(gn[:, :, :n], self.GN[:, t0:t0 + n].rearrange("(j p) t -> p j t", p=128), ri, reads=[self.rGN], writes=[ri])
                sy.dma(gt[:, :, :n], self.GT[:, t0:t0 + n].rearrange("(j p) t -> p j t", p=128), ri, reads=[self.rGT], writes=[ri])
                xsrc = (self.i["xr0"] if first else self.XR)[:, t0:t0 + n].rearrange("(kc p) t -> p kc t", p=128)
                sy.dma(xt[:, :, :n], xsrc, rx, reads=([] if first else [self.rXR]), writes=[rx])
                for jo in range(KD):
                    pa, rpa = self.nextps()
                    self.mm_group(pa[:, :n], [(WL[:, k, jo * 128:(jo + 1) * 128], ya[:, k, :n]) for k in range(LWC)], [rWL, ri], rpa)
                    pb, rpb = self.nextps()
                    self.mm_group(pb[:, :n], [(WS[:, k, jo * 128:(jo + 1) * 128], gn[:, k, :n]) for k in range(SIC)], [rWS, ri], rpb)
                    sy.op("dve", lambda e, pa=pa, gt=gt, jo=jo, n=n: e.tensor_tensor(out=M1[:, :n], in0=pa[:, :n], in1=gt[:, jo, :n], op=ALU.mult),
                          reads=[rpa, ri], writes=[rM1])
                    sy.op("dve", lambda e, pb=pb, gt=gt, jo=jo, n=n: e.tensor_tensor(out=MB[:, jo, :n], in0=pb[:, :n], in1=gt[:, KD + jo, :n], op=ALU.mult),
                          reads=[rpb, ri], writes=[rMB])
                    sy.op("pool", lambda e, jo=jo, n=n: e.tensor_tensor(out=MB[:, jo, :n], in0=MB[:, jo, :n], in1=M1[:, :n], op=ALU.add),
                          reads=[rM1, rMB], writes=[rMB])
                for jo in range(KD):
                    po, rpo = self.nextps()
                    self.mm_group(po[:, :n], [(WO[:, k, jo * 128:(jo + 1) * 128], MB[:, k, :n]) for k in range(KD)], [rWO, rMB], rpo)
                    gate = self.mod[:, 2 * KD + jo, s:s + 1]
                    sy.op("dve", lambda e, po=po, xt=xt, jo=jo, n=n, gate=gate: e.scalar_tensor_tensor(
                        out=xt[:, jo, :n], in0=po[:, :n], scalar=gate, in1=xt[:, jo, :n], op0=ALU.mult, op1=ALU.add),
                        reads=[rpo, rx, self.rMOD], writes=[rx])
                sy.dma(self.XR[:, t0:t0 + n].rearrange("(kc p) t -> p kc t", p=128), xt[:, :, :n], rx, reads=[rx], writes=[self.rXR])

    def ffn_up(self, l, H, rH):
        cfg, sy = self.cfg, self.sy
        KD, T, CTX, SEQ, GW, FFC, TS = cfg.KD, cfg.T, cfg.CTX, cfg.SEQ, cfg.GW, cfg.FFC, cfg.TS
        ROWS = SEQ // GW
        wsrc = self.i["ffn_w_up"][l]
        with Scope(self) as ss:
            Ws = [ss.sb("wu", [128, KD, 256], BF16) for i in range(2)]
            rWs = [R("wu%d" % i) for i in range(2)]
            PADX = ss.sb("padx", [128, ROWS + 2, GW + 2], BF16)
            PADC = ss.sb("padc", [128, 3, CTX + 2], BF16)
            rPAD = R("pad")
            VB = ss.sb("vb", [128, T], BF16)
            rVB = R("vb")
            AO = [ss.sb("ao", [128, T], BF16) for i in range(2)]
            rAO = [R("ao%d" % i) for i in range(2)]
            DG = ss.sb("dg9", [128, 9, 128], BF16)
            rDG = R("dg9")
            TMP = ss.sb("tmpf", [128, 3, TS], F32)
            rTMP = R("tmpf")
            sy.op("dve", lambda e: e.memset(PADX[:], 0.0), writes=[rPAD])
            sy.op("dve", lambda e: e.memset(PADC[:], 0.0), writes=[rPAD])
            for j in range(FFC):
                W, rW = Ws[j % 2], rWs[j % 2]
                ao, rao = AO[j % 2], rAO[j % 2]
                self.load_w(W[:, :, 0:128], rW, wsrc[:, j * 128:(j + 1) * 128], KD, 128)
                self.load_w(W[:, :, 128:256], rW, wsrc[:, cfg.FF + j * 128:cfg.FF + (j + 1) * 128], KD, 128)
                self.make_diag(DG, rDG, "fcw", 9, FFC, j)
                for (t0, n, s) in cfg.tiles:
                    ps, rps = self.proj_tile(W[:, :, 0:128], rW, H, rH, t0, n)
                    if s == 1:
                        dst = PADC[:, 1, 1 + t0:1 + t0 + n]
                        src = ps[:, :n]
                    else:
                        r0 = (t0 - CTX) // GW
                        nr = n // GW
                        dst = PADX[:, 1 + r0:1 + r0 + nr, 1:1 + GW]
                        src = ps[:, :n].rearrange("p (r c) -> p r c", c=GW)
                    sy.op("act", lambda e, dst=dst, src=src: e.activation(out=dst, in_=src, func=AF.Copy), reads=[rps], writes=[rPAD])
                    ps, rps = self.proj_tile(W[:, :, 128:256], rW, H, rH, t0, n)
                    sy.op("act", lambda e, ps=ps, t0=t0, n=n: e.activation(out=VB[:, t0:t0 + n], in_=ps[:, :n], func=AF.Copy), reads=[rps], writes=[rVB])
                fcb = self.vcol("fcb", j, 1)
                for (t0, n, s) in cfg.tiles:
                    ps, rps = self.nextps()
                    pairs = []
                    for kr in range(3):
                        for kc in range(3):
                            if s == 1:
                                rhs = PADC[:, kr, kc + t0:kc + t0 + n]
                            else:
                                r0 = (t0 - CTX) // GW
                                nr = n // GW
                                rhs = PADX[:, r0 + kr:r0 + kr + nr, kc:kc + GW]
                            pairs.append((DG[:, kr * 3 + kc, :], rhs))
                    outp = ps[:, :n] if s == 1 else ps[:, :n].rearrange("p (r c) -> p r c", c=GW)
                    self.mm_group(outp, pairs, [rDG, rPAD], rps)
                    self.gelu_from(TMP, rTMP, ps[:, :n], rps, ao[:, t0:t0 + n], rao, n, bias=fcb, mul=VB[:, t0:t0 + n], rmul=rVB)
                sy.dma(self.ACTV[j * 128:(j + 1) * 128, :], ao[:], rao, reads=[rao], writes=[self.rACTV])

    def ffn_down(self, l):
        cfg, sy = self.cfg, self.sy
        KD, FFC, TS, D = cfg.KD, cfg.FFC, cfg.TS2, cfg.D
        with Scope(self) as ss:
            WD = ss.sb("wd", [128, FFC, D], BF16)
            rWD = R("wd")
            self.load_w(WD, rWD, self.i["ffn_w_down"][l], FFC, D)
            AT = [ss.sb("at", [128, FFC, TS], BF16) for i in range(2)]
            XT = [ss.sb("xr2", [128, KD, TS], F32) for i in range(2)]
            rI = [R("fdin%d" % i) for i in range(2)]
            rX = [R("fdx%d" % i) for i in range(2)]
            for ti, (t0, n, s) in enumerate(cfg.tiles2):
                bi = ti % 2
                at, xt, ri, rx = AT[bi], XT[bi], rI[bi], rX[bi]
                sy.dma(at[:, :, :n], self.ACTV[:, t0:t0 + n].rearrange("(j p) t -> p j t", p=128), ri, reads=[self.rACTV], writes=[ri])
                sy.dma(xt[:, :, :n], self.XR[:, t0:t0 + n].rearrange("(kc p) t -> p kc t", p=128), rx, reads=[self.rXR], writes=[rx])
                for jo in range(KD):
                    po, rpo = self.nextps()
                    self.mm_group(po[:, :n], [(WD[:, k, jo * 128:(jo + 1) * 128], at[:, k, :n]) for k in range(FFC)], [rWD, ri], rpo)
                    gate = self.mod[:, 5 * KD + jo, s:s + 1]
                    sy.op("dve", lambda e, po=po, xt=xt, jo=jo, n=n, gate=gate: e.scalar_tensor_tensor(
                        out=xt[:, jo, :n], in0=po[:, :n], scalar=gate, in1=xt[:, jo, :n], op0=ALU.mult, op1=ALU.add),
                        reads=[rpo, rx, self.rMOD], writes=[rx])
                sy.dma(self.XR[:, t0:t0 + n].rearrange("(kc p) t -> p kc t", p=128), xt[:, :, :n], rx, reads=[rx], writes=[self.rXR])

    def layer(self, l, first):
        cfg = self.cfg
        on = lambda n: (self.stages is None) or (n in self.stages)
        self.adaln(l)
        with Scope(self) as hs:
            H = hs.sb("H", [128, cfg.KD, cfg.T], BF16)
            rH = R("H")
            if on("norm1"):
                self.norm_mod(first, self.g1, 0, H, rH)
            if on("lru"):
                self.lru(l, H, rH)
            if on("prep"):
                self.ssd_prep(l, H, rH)
        for d in range(2):
            if on("sweep%d" % d):
                self.ssd_sweep(d)
        if on("ssdout"):
            self.ssd_out(l)
        if on("merge"):
            self.merge(l, first)
        with Scope(self) as hs:
            H = hs.sb("H2", [128, cfg.KD, cfg.T], BF16)
            rH = R("H2")
            if on("norm2"):
                self.norm_mod(False, self.g2, 3, H, rH)
            if on("ffnup"):
                self.ffn_up(l, H, rH)
        if on("ffndown"):
            self.ffn_down(l)

    def final_norm(self):
        cfg, sy = self.cfg, self.sy
        KD, TS = cfg.KD, cfg.TS2
        with Scope(self) as ss:
            xts = [ss.sb("xtf", [128, KD, TS], F32) for i in range(2)]
            rxs = [R("xtf%d" % i) for i in range(2)]
            sq = ss.sb("sqf", [128, KD, TS], F32)
            rsq = R("sqf")
            rstd = ss.sb("rstdf", [128, TS], F32)
            rrs = R("rstdf")
            rOUT = R("out")
            ti = 0
            for (t0, n, s) in cfg.tiles2:
                if s == 1:
                    continue
                xt, rx = xts[ti % 2], rxs[ti % 2]
                ti += 1
                sy.dma(xt[:, :, :n], self.XR[:, t0:t0 + n].rearrange("(kc p) t -> p kc t", p=128), rx, reads=[self.rXR], writes=[rx])
                self.rstd_of(xt, rx, n, sq, rsq, rstd, rrs, cfg.D)
                sy.op("dve", lambda e, xt=xt, n=n: e.tensor_tensor(out=xt[:, :, :n], in0=xt[:, :, :n],
                                                                   in1=rstd[:, :n].unsqueeze(1).to_broadcast([128, KD, n]), op=ALU.mult),
                      reads=[rx, rrs], writes=[rx])
                sy.op("dve", lambda e, xt=xt, n=n: e.tensor_tensor(out=xt[:, :, :n], in0=xt[:, :, :n],
                                                                   in1=self.vcol("fnw", 0, KD).unsqueeze(2).to_broadcast([128, KD, n]), op=ALU.mult),
                      reads=[rx, self.rVEC], writes=[rx])
                sy.dma(self.outf[:, t0 - cfg.CTX:t0 - cfg.CTX + n].rearrange("(kc p) t -> p kc t", p=128), xt[:, :, :n], rx, reads=[rx], writes=[rOUT])

    def copy_xr_out(self):
        cfg, sy = self.cfg, self.sy
        KD, TS = cfg.KD, cfg.TS
        with Scope(self) as ss:
            xts = [ss.sb("xtc", [128, KD, TS], F32) for i in range(2)]
            rxs = [R("xtc%d" % i) for i in range(2)]
            rOUT = R("out")
            for ti, (t0, n, s) in enumerate(cfg.tiles):
                xt, rx = xts[ti % 2], rxs[ti % 2]
                sy.dma(xt[:, :, :n], self.XR[:, t0:t0 + n].rearrange("(kc p) t -> p kc t", p=128), rx, reads=[self.rXR], writes=[rx])
                sy.dma(self.out[:, t0:t0 + n].rearrange("(kc p) t -> p kc t", p=128), xt[:, :, :n], rx, reads=[rx], writes=[rOUT])


FUSED = True
CFG_KW = {}
PER_LAYER = ("ada_w", "ada_b", "norm_mix_w", "norm_ffn_w", "w_in", "lru_conv_w", "lru_conv_b", "lru_wa", "lru_ba", "lru_wx",
             "lru_bx", "lru_lambda", "lru_proj", "ssd_conv_w", "ssd_conv_b", "ssd_dt_bias", "ssd_a_log", "ssd_d", "ssd_norm_w",
             "ssd_proj", "w_out", "ffn_w_up", "ffn_conv_w", "ffn_conv_b", "ffn_w_down")


def _maps(cfg, inp, B):
    base = host_prep(cfg, inp, 0)
    maps = [base]
    for b in range(1, B):
        m = dict(base)
        pb = host_prep_core(cfg, inp, b)
        m.update(pb)
        maps.append(m)
    return maps


def host_prep_core(cfg, inp, b):
    m = {}
    m["xr0"] = np.ascontiguousarray(np.concatenate([inp["ctx"][b].T, inp["x"][b].T], axis=1).astype(np.float32))
    cv = np.zeros((128, cfg.KD, 2), np.float32)
    cv[:, :, 0] = colmajor(inp["c"][b])
    cv[:, :, 1] = colmajor(inp["c_ctx"])
    m["cvec"] = cv
    return m


def kernel(**inputs):
    from concourse.bass_utils import run_bass_kernel_spmd
    inp = {k: np.asarray(v) for k, v in inputs.items()}
    B = inp["x"].shape[0]
    L = inp["w_in"].shape[0]
    if FUSED:
        cfg = Cfg(DEPTH=L, **CFG_KW)
        nc = Prog(cfg, layers=list(range(L)), final=True, emit_xr=False).build()
        maps = _maps(cfg, inp, B)
        res = run_bass_kernel_spmd(nc, maps, core_ids=list(range(B)))
        outs = [res.results[b]["out"] for b in range(B)]
    else:
        cfg = Cfg(DEPTH=1, **CFG_KW)
        xr = None
        for l in range(L):
            inpl = {k: (v[l:l + 1] if k in PER_LAYER else v) for k, v in inp.items()}
            maps = _maps(cfg, inpl, B)
            if xr is not None:
                for b in range(B):
                    maps[b]["xr0"] = xr[b]
            nc = Prog(cfg, layers=[0], final=True, emit_xr=True).build()
            res = run_bass_kernel_spmd(nc, maps, core_ids=list(range(B)))
            xr = [np.ascontiguousarray(res.results[b]["outx"]) for b in range(B)]
            outs = [res.results[b]["outf"] for b in range(B)]
    out = np.stack([np.ascontiguousarray(o.T) for o in outs]).astype(np.float32)
    return out
```

```python
import numpy as np
from contextlib import ExitStack
import concourse.bass as bass
import concourse.mybir as mybir

F32 = mybir.dt.float32
BF16 = mybir.dt.bfloat16
AF = mybir.ActivationFunctionType
ALU = mybir.AluOpType

ENGS = ("pe", "act", "dve", "pool", "sp")


class R:
    __slots__ = ("name", "w", "rs", "dsem", "dcnt")

    def __init__(self, name):
        self.name = name
        self.w = None
        self.rs = {}
        self.dsem = None
        self.dcnt = 0


class Sync:
    def __init__(self, nc, stack):
        self.nc = nc
        self.stack = stack
        self.ops = {e: [] for e in ENGS}
        self.sems = {}
        for e in ENGS:
            self.sems[("E", e)] = stack.enter_context(nc.semaphore("S_" + e))
        self.cnt = {e: 0 for e in ENGS}
        self.seen = {e: {} for e in ENGS}
        self.ndsem = 0
        self.dcount = {}
        self.dsem_by_name = {}

    def _waits(self, eng, reads, writes, extra=()):
        w = {}

        def add(ev):
            if ev is None:
                return
            k, v = ev
            if w.get(k, 0) < v:
                w[k] = v
        for r in reads:
            add(r.w)
        for r in writes:
            add(r.w)
            for k, v in r.rs.items():
                add((k, v))
        for ev in extra:
            add(ev)
        out = []
        seen = self.seen[eng]
        for k, v in w.items():
            if eng == "pe" and k == ("E", "pe"):
                continue
            if seen.get(k, 0) >= v:
                continue
            seen[k] = v
            out.append((k, v))
        return out

    def _mark(self, ev, reads, writes):
        k, v = ev
        for r in reads:
            if r.rs.get(k, 0) < v:
                r.rs[k] = v
        for r in writes:
            r.w = ev
            r.rs = {}

    def op(self, eng, fn, reads=(), writes=()):
        waits = self._waits(eng, reads, writes)
        self.cnt[eng] += 1
        ev = (("E", eng), self.cnt[eng])
        self.ops[eng].append((waits, fn, ("E", eng), 1))
        self._mark(ev, reads, writes)
        return ev

    def dma(self, out_ap, in_ap, sres, reads=(), writes=(), q="sp"):
        key = self.dsem_by_name.get(sres.name)
        if key is None:
            key = ("D", self.ndsem)
            self.ndsem += 1
            self.sems[key] = self.stack.enter_context(self.nc.semaphore("D%d" % key[1]))
            self.dsem_by_name[sres.name] = key
            self.dcount[key] = 0
        extra = []
        if self.dcount[key]:
            extra.append((key, self.dcount[key]))
        waits = self._waits(q, reads, writes, extra)
        self.dcount[key] += 16
        ev = (key, self.dcount[key])

        def fn(e, out_ap=out_ap, in_ap=in_ap):
            return e.dma_start(out=out_ap, in_=in_ap)
        self.ops[q].append((waits, fn, key, 16))
        self._mark(ev, reads, writes)
        return ev

    def barrier(self):
        evs = [(("E", e), self.cnt[e]) for e in ENGS if self.cnt[e]]
        evs += [(k, None) for k in self.sems if k[0] == "D"]
        for e in ENGS:
            waits = []
            for k, v in evs:
                if v is None:
                    v = self.dcount.get(k, 0)
                if v and self.seen[e].get(k, 0) < v and not (k == ("E", e)):
                    self.seen[e][k] = v
                    waits.append((k, v))
            if waits:
                self.ops[e].append((waits, None, None, 0))

    def fence(self, eng, resources):
        waits = self._waits(eng, [], resources)
        if waits:
            self.ops[eng].append((waits, None, None, 0))

    def emit(self):
        nc = self.nc
        engmap = {"pe": "tensor", "act": "scalar", "dve": "vector", "pool": "gpsimd", "sp": "sync"}
        with nc.Block() as block:
            for e in ENGS:
                ops = self.ops[e]

                def body(eng, ops=ops):
                    for waits, fn, semkey, inc in ops:
                        for k, v in waits:
                            eng.wait_ge(self.sems[k], v)
                        if fn is not None:
                            ins = fn(eng)
                            ins.then_inc(self.sems[semkey], inc)
                getattr(block, engmap[e])(body)
        self.ops = {e: [] for e in ENGS}


class Cfg:
    def __init__(self, D=1024, SEQ=4096, CTX=256, GW=64, LW=1024, SI=2048, NG=4, FF=2816, DEPTH=4, TS=512):
        self.D, self.SEQ, self.CTX, self.GW, self.LW, self.SI, self.NG, self.FF, self.DEPTH, self.TS = \
            D, SEQ, CTX, GW, LW, SI, NG, FF, DEPTH, TS
        self.T = CTX + SEQ
        self.KD = D // 128
        self.LWC = LW // 128
        self.SIC = SI // 128
        self.NH = SI // 64
        self.HPG = self.NH // NG
        self.CD = SI + 2 * NG * 128
        self.CDC = self.CD // 128
        self.FFC = FF // 128
        self.IN_DIM = 2 * LW + SI + self.CD + 2 * self.NH + 2 * D
        self.o_lx, self.o_lg = 0, LW
        self.o_z = 2 * LW
        self.o_xbc = 2 * LW + SI
        self.o_dt = self.o_xbc + self.CD
        self.o_gt = self.o_dt + 2 * self.NH
        self.tiles = []
        for t0 in range(0, CTX, TS):
            self.tiles.append((t0, min(TS, CTX - t0), 1))
        for t0 in range(0, SEQ, TS):
            self.tiles.append((CTX + t0, min(TS, SEQ - t0), 0))
        self.NCH = self.T // 128
        self.TS2 = min(256, TS)
        self.tiles2 = []
        for t0 in range(0, CTX, self.TS2):
            self.tiles2.append((t0, min(self.TS2, CTX - t0), 1))
        for t0 in range(0, SEQ, self.TS2):
            self.tiles2.append((CTX + t0, min(self.TS2, SEQ - t0), 0))
        off = {}
        o = 0
        for name, n in (("nmw", self.KD), ("nfw", self.KD), ("adab", 6 * self.KD), ("lcw", 4 * self.LWC),
                        ("lcb", self.LWC), ("lba", 2 * self.LWC), ("lbx", 2 * self.LWC), ("llam", 2 * self.LWC),
                        ("scw", 4 * self.CDC), ("scb", self.CDC), ("sdd", self.SIC), ("snw", self.SIC),
                        ("fcw", 9 * self.FFC), ("fcb", self.FFC), ("fnw", self.KD), ("dtb", 1), ("alog", 1)):
            off[name] = o
            o += n
        self.voff = off
        self.NV = o


def colmajor(v):
    v = np.asarray(v, np.float32)
    return np.ascontiguousarray(v.reshape(-1, 128).T)


def host_prep(cfg, inp, b):
    L = cfg.DEPTH
    m = {}
    m["xr0"] = np.ascontiguousarray(np.concatenate([inp["ctx"][b].T, inp["x"][b].T], axis=1).astype(np.float32))
    cv = np.zeros((128, cfg.KD, 2), np.float32)
    cv[:, :, 0] = colmajor(inp["c"][b])
    cv[:, :, 1] = colmajor(inp["c_ctx"])
    m["cvec"] = cv
    vecs = np.zeros((L, 128, cfg.NV), np.float32)
    vo = cfg.voff
    for l in range(L):
        def put(name, arr2d):
            vecs[l, :arr2d.shape[0], vo[name]:vo[name] + arr2d.shape[1]] = arr2d
        put("nmw", colmajor(inp["norm_mix_w"][l]))
        put("nfw", colmajor(inp["norm_ffn_w"][l]))
        put("adab", colmajor(inp["ada_b"][l]))
        put("lcw", np.concatenate([colmajor(inp["lru_conv_w"][l][k]) for k in range(4)], axis=1))
        put("lcb", colmajor(inp["lru_conv_b"][l]))
        put("lba", np.concatenate([colmajor(inp["lru_ba"][l][d]) for d in range(2)], axis=1))
        put("lbx", np.concatenate([colmajor(inp["lru_bx"][l][d]) for d in range(2)], axis=1))
        put("llam", np.concatenate([colmajor(inp["lru_lambda"][l][d]) for d in range(2)], axis=1))
        put("scw", np.concatenate([colmajor(inp["ssd_conv_w"][l][k]) for k in range(4)], axis=1))
        put("scb", colmajor(inp["ssd_conv_b"][l]))
        put("sdd", colmajor(np.repeat(np.asarray(inp["ssd_d"][l]), 64)))
        put("snw", colmajor(inp["ssd_norm_w"][l]))
        fw_ = np.asarray(inp["ffn_conv_w"][l]).reshape(9, cfg.FF)
        put("fcw", np.concatenate([colmajor(fw_[k]) for k in range(9)], axis=1))
        put("fcb", colmajor(inp["ffn_conv_b"][l]))
        put("fnw", colmajor(inp["final_norm_w"]))
        dtb = np.zeros((64, 1), np.float32)
        alog = np.zeros((64, 1), np.float32)
        for d in range(2):
            dtb[d * 32:d * 32 + cfg.NH, 0] = inp["ssd_dt_bias"][l][d]
            alog[d * 32:d * 32 + cfg.NH, 0] = inp["ssd_a_log"][l][d]
        put("dtb", dtb)
        put("alog", alog)
    m["vecs"] = vecs
    for k in ("ada_w", "w_in", "lru_wa", "lru_wx", "lru_proj", "ssd_proj", "w_out", "ffn_w_up", "ffn_w_down"):
        m[k] = np.ascontiguousarray(np.asarray(inp[k], np.float32))
    ident = np.eye(128, dtype=np.float32)
    m["c_ident"] = ident
    idx = np.arange(128)
    mk = np.zeros((2, 128, 128), np.float32)
    mk[0] = (idx[None, :] < idx[:, None])
    mk[1] = (idx[None, :] > idx[:, None])
    m["c_mask"] = mk
    sel = np.zeros((64, 64, 128), np.float32)
    for k in range(64):
        sel[k, k, :] = 1.0
    m["c_sel"] = sel
    rm = np.ones((2, 64, cfg.T), np.float32)
    rm[0, :, 0::128] = 0.0
    rm[1, :, 127::128] = 0.0
    m["c_rmask"] = rm
    return m


GELU_C = 1.5957691216057308
_uid = [0]


class Scope:
    def __init__(self, prog):
        self.p = prog
        self.st = ExitStack()

    def __enter__(self):
        return self

    def sb(self, name, shape, dt):
        _uid[0] += 1
        return self.st.enter_context(self.p.nc.sbuf_tensor("%s_%d" % (name, _uid[0]), list(shape), dt))

    def __exit__(self, *a):
        if a[0] is None:
            self.p.sy.barrier()
            self.p.sy.emit()
        self.st.close()
        return False


class Prog:
    def __init__(self, cfg, layers, debug=(), final=True, stop_after=None, stages=None, emit_xr=None):
        self.cfg = cfg
        self.layers = list(layers)
        self.debug = set(debug)
        self.final = final
        self.stop_after = stop_after
        self.stages = stages
        self.emit_xr = (not final) if emit_xr is None else emit_xr

    def build(self):
        nc = bass.Bass("TRN2", target_bir_lowering=False)
        self.nc = nc
        with ExitStack() as st:
            self.st = st
            self.sy = Sync(nc, st)
            self.declare_io()
            with Scope(self) as top:
                self.setup(top)
                for li, l in enumerate(self.layers):
                    self.layer(l, first=(li == 0))
                    self.sy.barrier()
                    self.sy.emit()
                if self.final:
                    self.final_norm()
                if self.emit_xr:
                    self.copy_xr_out()
        return nc

    def din(self, name, shape, dt=F32):
        return self.nc.dram_tensor(name, list(shape), dt, kind="ExternalInput").ap()

    def dscr(self, name, shape, dt, out=False):
        dbg = name in self.debug
        kind = "ExternalOutput" if (out or dbg) else "Internal"
        return self.nc.dram_tensor(name if out else ("d_" + name), list(shape), dt, kind=kind).ap()

    def declare_io(self):
        cfg = self.cfg
        L = cfg.DEPTH
        self.i = {}
        self.i["xr0"] = self.din("xr0", [cfg.D, cfg.T])
        self.i["cvec"] = self.din("cvec", [128, cfg.KD, 2])
        self.i["vecs"] = self.din("vecs", [L, 128, cfg.NV])
        self.i["ada_w"] = self.din("ada_w", [L, cfg.D, 6 * cfg.D])
        self.i["w_in"] = self.din("w_in", [L, cfg.D, cfg.IN_DIM])
        self.i["lru_wa"] = self.din("lru_wa", [L, 2, cfg.LW // 64, 64, 64])
        self.i["lru_wx"] = self.din("lru_wx", [L, 2, cfg.LW // 64, 64, 64])
        self.i["lru_proj"] = self.din("lru_proj", [L, cfg.LW, cfg.D])
        self.i["ssd_proj"] = self.din("ssd_proj", [L, cfg.SI, cfg.D])
        self.i["w_out"] = self.din("w_out", [L, cfg.D, cfg.D])
        self.i["ffn_w_up"] = self.din("ffn_w_up", [L, cfg.D, 2 * cfg.FF])
        self.i["ffn_w_down"] = self.din("ffn_w_down", [L, cfg.FF, cfg.D])
        self.i["c_ident"] = self.din("c_ident", [128, 128])
        self.i["c_mask"] = self.din("c_mask", [2, 128, 128])
        self.i["c_sel"] = self.din("c_sel", [64, 64, 128])
        self.i["c_rmask"] = self.din("c_rmask", [2, 64, cfg.T])
        if self.final:
            self.outf = self.dscr("out" if not self.emit_xr else "outf", [cfg.D, cfg.SEQ], F32, out=True)
        if self.emit_xr:
            self.out = self.dscr("out" if not self.final else "outx", [cfg.D, cfg.T], F32, out=True)
        self.XR = self.dscr("XR", [cfg.D, cfg.T], F32)
        self.YAG = self.dscr("YAG", [cfg.LW, cfg.T], BF16)
        self.Zs = self.dscr("Zs", [cfg.SI, cfg.T], BF16)
        self.GT = self.dscr("GT", [2 * cfg.D, cfg.T], BF16)
        self.XBC = self.dscr("XBC", [cfg.CD, cfg.T], BF16)
        self.Y = self.dscr("Y", [cfg.SI, cfg.T], F32)
        self.GN = self.dscr("GN", [cfg.SI, cfg.T], BF16)
        self.ACTV = self.dscr("ACTV", [cfg.FF, cfg.T], BF16)
        self.DTS = self.dscr("DTS", [3, 64, cfg.T], F32)
        self.TOTS = self.dscr("TOTS", [64, cfg.NCH], F32)
        self.rDTS = R("DTS")
        self.rXR, self.rYAG, self.rZs, self.rGT, self.rXBC, self.rY, self.rGN, self.rACTV = [R(n) for n in (
            "XR", "YAG", "Zs", "GT", "XBC", "Y", "GN", "ACTV")]

    def setup(self, top):
        cfg, sy = self.cfg, self.sy
        self.ident_f = top.sb("ident_f", [128, 128], F32)
        self.ident_b = top.sb("ident_b", [128, 128], BF16)
        self.ones_f = top.sb("ones_f", [128, 128], F32)
        self.ones_b = top.sb("ones_b", [128, 128], BF16)
        self.negi = top.sb("negi", [128, 128], BF16)
        self.maskf = top.sb("maskf", [128, 2, 128], F32)
        self.maskb = top.sb("maskb", [128, 2, 128], BF16)
        self.maskn = top.sb("maskn", [128, 2, 128], F32)
        self.rC = R("consts")
        self.rCm = R("constsm")
        rC = self.rC
        sy.dma(self.ident_f[:], self.i["c_ident"], rC, writes=[rC])
        sy.dma(self.maskf[:], self.i["c_mask"].rearrange("d s l -> s d l"), self.rCm, writes=[self.rCm])
        sy.op("dve", lambda e: e.tensor_copy(out=self.ident_b[:], in_=self.ident_f[:]), reads=[rC], writes=[rC])
        sy.op("dve", lambda e: e.tensor_copy(out=self.maskb[:], in_=self.maskf[:]), reads=[self.rCm], writes=[self.rCm])
        sy.op("dve", lambda e: e.tensor_scalar(out=self.maskn[:], in0=self.maskf[:], scalar1=-30000.0, scalar2=None, op0=ALU.mult),
              reads=[self.rCm], writes=[self.rCm])
        sy.op("dve", lambda e: e.tensor_scalar(out=self.maskf[:], in0=self.maskf[:], scalar1=-1.0, scalar2=1.0, op0=ALU.mult, op1=ALU.add),
              reads=[self.rCm], writes=[self.rCm])
        sy.op("dve", lambda e: e.tensor_scalar(out=self.negi[:], in0=self.ident_f[:], scalar1=-30000.0, scalar2=None, op0=ALU.mult),
              reads=[rC], writes=[rC])
        sy.op("dve", lambda e: e.memset(self.ones_f[:], 1.0), writes=[rC])
        sy.op("dve", lambda e: e.memset(self.ones_b[:], 1.0), writes=[rC])
        self.sc = top.sb("sc", [128, cfg.KD, 2], F32)
        self.rSC = R("sc")
        sy.dma(self.sc[:], self.i["cvec"], self.rSC, writes=[self.rSC])
        sy.op("act", lambda e: e.activation(out=self.sc[:], in_=self.sc[:], func=AF.Silu), reads=[self.rSC], writes=[self.rSC])
        self.vec = top.sb("vec", [128, cfg.NV], F32)
        self.rVEC = R("vec")
        self.mod = top.sb("mod", [128, 6 * cfg.KD, 2], F32)
        self.rMOD = R("mod")
        self.g1 = top.sb("g1", [128, cfg.KD, 2], F32)
        self.g2 = top.sb("g2", [128, cfg.KD, 2], F32)
        self.c8 = top.sb("c8", [128, 2 * cfg.LWC], F32)
        self.aneg = top.sb("aneg", [64, 1], F32)
        self.PS = [top.st.enter_context(self.nc.psum_tensor("ps%d" % i, [128, 512], F32)) for i in range(7)]
        self.rPS = [R("ps%d" % i) for i in range(7)]
        self.PSB = top.st.enter_context(self.nc.psum_tensor("psb", [128, 1024], BF16))
        self.rPSB = R("psb")
        self.ps_i = 0
        self.wst = [top.sb("wst%d" % i, [128, 8, 256], F32) for i in range(2)]
        self.rWST = [R("wst%d" % i) for i in range(2)]
        self.wst_i = 0
        sy.barrier()
        sy.emit()

    def vcol(self, name, j=0, n=1, rows=128):
        o = self.cfg.voff[name] + j
        return self.vec[:rows, o:o + n]

    def nextps(self, k=4):
        i = self.ps_i % k
        self.ps_i += 1
        return self.PS[i], self.rPS[i]

    def load_w(self, dst, rdst, src2d, K, C, rows_last=128):
        sy = self.sy
        for k0 in range(0, K, 8):
            kn = min(8, K - k0)
            for c0 in range(0, C, 256):
                cn = min(256, C - c0)
                bi = self.wst_i
                self.wst_i ^= 1
                ws, rw = self.wst[bi], self.rWST[bi]
                src = src2d[k0 * 128:(k0 + kn) * 128, c0:c0 + cn].rearrange("(kc p) c -> p kc c", p=128)
                sy.dma(ws[:, :kn, :cn], src, rw, writes=[rw])
                sy.op("pool", lambda e, ws=ws, kn=kn, cn=cn, k0=k0, c0=c0: e.tensor_copy(
                    out=dst[:, k0:k0 + kn, c0:c0 + cn], in_=ws[:, :kn, :cn]), reads=[rw], writes=[rdst])

    def mm_group(self, ps_ap, pairs, reads, rps):
        def fn(e):
            ins = None
            n = len(pairs)
            for i, (a, b) in enumerate(pairs):
                ins = e.matmul(ps_ap, a, b, start=(i == 0), stop=(i == n - 1))
            return ins
        self.sy.op("pe", fn, reads=reads, writes=[rps])

    def adaln(self, l):
        cfg, sy = self.cfg, self.sy
        KD = cfg.KD
        with Scope(self) as ss:
            adaw = [ss.sb("adaw", [128, KD, 512], F32) for i in range(2)]
            rA = [R("adaw%d" % i) for i in range(2)]
            sy.dma(self.vec[:], self.i["vecs"][l], self.rVEC, writes=[self.rVEC])
            psb, rps = self.PS[6], self.rPS[6]
            ncol = 6 * cfg.D
            bi = 0
            for g0 in range(0, ncol, 512):
                gw = min(512, ncol - g0)
                wt, rw = adaw[bi], rA[bi]
                bi ^= 1
                src = self.i["ada_w"][l][:, g0:g0 + gw].rearrange("(kc p) c -> p kc c", p=128)
                sy.dma(wt[:, :, :gw], src, rw, writes=[rw])
                for oc in range(gw // 128):
                    o = g0 // 128 + oc
                    self.mm_group(psb[:, 2 * o:2 * o + 2],
                                  [(wt[:, kc, oc * 128:(oc + 1) * 128], self.sc[:, kc, :]) for kc in range(KD)],
                                  [rw, self.rSC], rps)
            ab = self.vcol("adab", 0, 6 * KD)
            sy.op("dve", lambda e: e.tensor_tensor(out=self.mod[:], in0=psb[:, 0:12 * KD].rearrange("p (o s) -> p o s", s=2),
                                                   in1=ab.unsqueeze(2).to_broadcast([128, 6 * KD, 2]), op=ALU.add),
                  reads=[rps, self.rVEC], writes=[self.rMOD])
            for g, wname, m in ((self.g1, "nmw", 1), (self.g2, "nfw", 4)):
                def fn(e, g=g, wname=wname, m=m):
                    return e.scalar_tensor_tensor(out=g[:], in0=self.mod[:, m * KD:(m + 1) * KD, :], scalar=1.0,
                                                  in1=self.vcol(wname, 0, KD).unsqueeze(2).to_broadcast([128, KD, 2]),
                                                  op0=ALU.add, op1=ALU.mult)
                sy.op("dve", fn, reads=[self.rMOD, self.rVEC], writes=[self.rMOD])
            n8 = 2 * cfg.LWC
            sy.op("act", lambda e: e.activation(out=self.c8[:], in_=self.vcol("llam", 0, n8), func=AF.Exp, scale=-1.0),
                  reads=[self.rVEC], writes=[self.rMOD])
            sy.op("act", lambda e: e.activation(out=self.c8[:], in_=self.c8[:], func=AF.Ln, bias=1.0), reads=[self.rMOD], writes=[self.rMOD])
            sy.op("dve", lambda e: e.tensor_scalar(out=self.c8[:], in0=self.c8[:], scalar1=-8.0, scalar2=None, op0=ALU.mult),
                  reads=[self.rMOD], writes=[self.rMOD])
            sy.op("act", lambda e: e.activation(out=self.aneg[:], in_=self.vcol("alog", 0, 1, rows=64), func=AF.Exp),
                  reads=[self.rVEC], writes=[self.rMOD])
            sy.op("dve", lambda e: e.tensor_scalar(out=self.aneg[:], in0=self.aneg[:], scalar1=-1.0, scalar2=None, op0=ALU.mult),
                  reads=[self.rMOD], writes=[self.rMOD])

    def norm_mod(self, first, g, shift_m, H, rH):
        cfg, sy = self.cfg, self.sy
        KD = cfg.KD
        with Scope(self) as ss:
            xts = [ss.sb("xt", [128, KD, cfg.TS2], F32) for i in range(2)]
            rxs = [R("xt%d" % i) for i in range(2)]
            sq = ss.sb("sq", [128, KD, cfg.TS2], F32)
            rsq = R("sq")
            rstd = ss.sb("rstd", [128, cfg.TS2], F32)
            rrs = R("rstd")
            for ti, (t0, n, s) in enumerate(cfg.tiles2):
                xt, rx = xts[ti % 2], rxs[ti % 2]
                src = (self.i["xr0"] if first else self.XR)[:, t0:t0 + n].rearrange("(kc p) t -> p kc t", p=128)
                sy.dma(xt[:, :, :n], src, rx, reads=([] if first else [self.rXR]), writes=[rx])
                self.rstd_of(xt, rx, n, sq, rsq, rstd, rrs, cfg.D)
                sy.op("dve", lambda e, xt=xt, n=n: e.tensor_tensor(out=xt[:, :, :n], in0=xt[:, :, :n],
                                                                   in1=rstd[:, :n].unsqueeze(1).to_broadcast([128, KD, n]), op=ALU.mult),
                      reads=[rx, rrs], writes=[rx])
                for kc in range(KD):
                    sy.op("dve", lambda e, xt=xt, n=n, kc=kc, t0=t0, s=s: e.tensor_scalar(
                        out=H[:, kc, t0:t0 + n], in0=xt[:, kc, :n], scalar1=g[:, kc, s:s + 1],
                        scalar2=self.mod[:, shift_m * KD + kc, s:s + 1], op0=ALU.mult, op1=ALU.add),
                        reads=[rx, self.rMOD], writes=[rH])

    def rstd_of(self, xt, rx, n, sq, rsq, rstd, rrs, nchan):
        cfg, sy = self.cfg, self.sy
        K = nchan // 128
        sy.op("act", lambda e: e.activation(out=sq[:, :K, :n], in_=xt[:, :K, :n], func=AF.Square), reads=[rx], writes=[rsq])
        psb, rps = self.PS[5], self.rPS[5]
        self.mm_group(psb[:, :n], [(self.ones_f[:], sq[:, kc, :n]) for kc in range(K)], [rsq, self.rC], rps)
        sy.op("act", lambda e: e.activation(out=rstd[:, :n], in_=psb[:, :n], func=AF.Sqrt, scale=1.0 / nchan, bias=1e-6),
              reads=[rps], writes=[rrs])
        sy.op("dve", lambda e: e.reciprocal(out=rstd[:, :n], in_=rstd[:, :n]), reads=[rrs], writes=[rrs])

    def proj_tile(self, W, rW, H, rH, t0, n, M=128):
        KD = self.cfg.KD
        ps, rps = self.nextps()
        self.mm_group(ps[:M, :n], [(W[:, kc, :M], H[:, kc, t0:t0 + n]) for kc in range(KD)], [rW, rH], rps)
        return ps, rps

    def seqpos(self, t):
        return t + 2 if t < self.cfg.CTX else t + 5

    def make_diag(self, DG, rDG, name, ntap, nch, j):
        for k in range(ntap):
            self.sy.op("dve", lambda e, k=k: e.tensor_scalar(out=DG[:, k, :], in0=self.ident_f[:], scalar1=self.vcol(name, k * nch + j, 1),
                                                             scalar2=None, op0=ALU.mult), reads=[self.rC, self.rVEC], writes=[rDG])

    def conv1d_tile(self, DG, rDG, SEQB, rSEQB, t0, n):
        ps, rps = self.nextps()
        p0 = self.seqpos(t0)
        self.mm_group(ps[:, :n], [(DG[:, k, :], SEQB[:, p0 + k - 2:p0 + k - 2 + n]) for k in range(4)], [rDG, rSEQB], rps)
        return ps, rps

    def gelu_from(self, ss_tmp, rtmp, src_ap, rsrc, out_ap, rout, n, bias=None, mul=None, rmul=None):
        sy = self.sy
        x, x2, u = ss_tmp[:, 0, :n], ss_tmp[:, 1, :n], ss_tmp[:, 2, :n]
        if bias is not None:
            sy.op("act", lambda e: e.activation(out=x, in_=src_ap, func=AF.Identity, bias=bias), reads=[rsrc, self.rVEC], writes=[rtmp])
        else:
            sy.op("act", lambda e: e.activation(out=x, in_=src_ap, func=AF.Identity), reads=[rsrc], writes=[rtmp])
        sy.op("pool", lambda e: e.tensor_tensor(out=x2, in0=x, in1=x, op=ALU.mult), reads=[rtmp], writes=[rtmp])
        sy.op("dve", lambda e: e.tensor_scalar(out=x2, in0=x2, scalar1=0.044715, scalar2=1.0, op0=ALU.mult, op1=ALU.add), reads=[rtmp], writes=[rtmp])
        sy.op("pool", lambda e: e.tensor_tensor(out=u, in0=x2, in1=x, op=ALU.mult), reads=[rtmp], writes=[rtmp])
        sy.op("act", lambda e: e.activation(out=u, in_=u, func=AF.Sigmoid, scale=GELU_C), reads=[rtmp], writes=[rtmp])
        if mul is None:
            sy.op("dve", lambda e: e.tensor_tensor(out=out_ap, in0=u, in1=x, op=ALU.mult), reads=[rtmp], writes=[rout])
        else:
            sy.op("dve", lambda e: e.tensor_tensor(out=u, in0=u, in1=x, op=ALU.mult), reads=[rtmp], writes=[rtmp])
            sy.op("dve", lambda e: e.tensor_tensor(out=out_ap, in0=u, in1=mul, op=ALU.mult), reads=[rtmp, rmul], writes=[rout])

    def lru(self, l, H, rH):
        cfg, sy = self.cfg, self.sy
        KD, T, CTX = cfg.KD, cfg.T, cfg.CTX
        with Scope(self) as ss:
            W = ss.sb("wl", [128, KD, 256], BF16)
            rW = R("wl")
            SEQB = ss.sb("seqb", [128, T + 6], BF16)
            rSEQB = R("seqb")
            Ub = ss.sb("ub", [128, T], BF16)
            rU = R("u")
            A = ss.sb("a", [128, T], F32)
            B0 = ss.sb("b0", [128, T], F32)
            B1 = ss.sb("b1", [128, T], F32)
            rA, rB0, rB1 = R("a"), R("b0"), R("b1")
            YG = ss.sb("yg", [128, T], BF16)
            rYG = R("yg")
            DG = ss.sb("dg", [128, 4, 128], BF16)
            rDG = R("dg")
            GST = ss.sb("gst", [128, 4, 128], F32)
            rGST = R("gst")
            GW = ss.sb("gw", [128, 4, 128], BF16)
            rGW = R("gw")
            TMPs = [ss.sb("tmp", [128, 3, cfg.TS], F32) for i in range(2)]
            rTMPs = [R("tmp%d" % i) for i in range(2)]
            RTs = [ss.sb("rt", [128, 4, cfg.TS], F32) for i in range(2)]
            rRTs = [R("rt%d" % i) for i in range(2)]
            tcount = [0]
            sy.op("dve", lambda e: e.memset(SEQB[:], 0.0), writes=[rSEQB])
            for j in range(cfg.LWC):
                wsrc = self.i["w_in"][l]
                self.load_w(W[:, :, 0:128], rW, wsrc[:, cfg.o_lx + j * 128: cfg.o_lx + (j + 1) * 128], KD, 128)
                self.load_w(W[:, :, 128:256], rW, wsrc[:, cfg.o_lg + j * 128: cfg.o_lg + (j + 1) * 128], KD, 128)
                self.make_diag(DG, rDG, "lcw", 4, cfg.LWC, j)
                sy.op("pool", lambda e: e.memset(GST[:], 0.0), writes=[rGST])
                for d in range(2):
                    for wi, wname in enumerate(("lru_wa", "lru_wx")):
                        for hh in range(2):
                            sy.dma(GST[hh * 64:(hh + 1) * 64, d * 2 + wi, hh * 64:(hh + 1) * 64],
                                   self.i[wname][l, d, 2 * j + hh], rGST, writes=[rGST])
                sy.op("pool", lambda e: e.tensor_copy(out=GW[:], in_=GST[:]), reads=[rGST], writes=[rGW])
                for (t0, n, s) in cfg.tiles:
                    ps, rps = self.proj_tile(W[:, :, 0:128], rW, H, rH, t0, n)
                    p0 = self.seqpos(t0)
                    sy.op("act", lambda e, ps=ps, p0=p0, n=n: e.activation(out=SEQB[:, p0:p0 + n], in_=ps[:, :n], func=AF.Copy),
                          reads=[rps], writes=[rSEQB])
                for (t0, n, s) in cfg.tiles:
                    ps, rps = self.conv1d_tile(DG, rDG, SEQB, rSEQB, t0, n)
                    lcb = self.vcol("lcb", j, 1)
                    sy.op("act", lambda e, ps=ps, t0=t0, n=n, lcb=lcb: e.activation(out=Ub[:, t0:t0 + n], in_=ps[:, :n], func=AF.Identity,
                                                                                    bias=lcb), reads=[rps, self.rVEC], writes=[rU])
                for d in range(2):
                    Bd, rBd = (B0, rB0) if d == 0 else (B1, rB1)
                    for (t0, n, s) in cfg.tiles:
                        tcount[0] += 1
                        RT, rRT = RTs[tcount[0] % 2], rRTs[tcount[0] % 2]
                        psr, rpsr = self.nextps()
                        self.mm_group(psr[:, :n], [(GW[:, 2 * d, :], Ub[:, t0:t0 + n])], [rGW, rU], rpsr)
                        psi, rpsi = self.nextps()
                        self.mm_group(psi[:, :n], [(GW[:, 2 * d + 1, :], Ub[:, t0:t0 + n])], [rGW, rU], rpsi)
                        r_, i_, a2, sq = RT[:, 0, :n], RT[:, 1, :n], RT[:, 2, :n], RT[:, 3, :n]
                        lba = self.vcol("lba", d * cfg.LWC + j, 1)
                        lbx = self.vcol("lbx", d * cfg.LWC + j, 1)
                        sy.op("act", lambda e, psr=psr, n=n, r_=r_, lba=lba: e.activation(out=r_, in_=psr[:, :n], func=AF.Sigmoid,
                                                                                          bias=lba),
                              reads=[rpsr, self.rVEC], writes=[rRT])
                        sy.op("act", lambda e, psi=psi, n=n, i_=i_, lbx=lbx: e.activation(out=i_, in_=psi[:, :n], func=AF.Sigmoid,
                                                                                          bias=lbx),
                              reads=[rpsi, self.rVEC], writes=[rRT])
                        c8 = self.c8[:, d * cfg.LWC + j: d * cfg.LWC + j + 1]
                        sy.op("act", lambda e, t0=t0, n=n, r_=r_, c8=c8: e.activation(out=A[:, t0:t0 + n], in_=r_, func=AF.Exp, scale=c8),
                              reads=[rRT, self.rMOD], writes=[rA])
                        sy.op("pool", lambda e, t0=t0, n=n, a2=a2: e.tensor_tensor(out=a2, in0=A[:, t0:t0 + n], in1=A[:, t0:t0 + n], op=ALU.mult),
                              reads=[rA], writes=[rRT])
                        sy.op("dve", lambda e, a2=a2: e.tensor_scalar(out=a2, in0=a2, scalar1=-1.0, scalar2=1.0, op0=ALU.mult, op1=ALU.add),
                              reads=[rRT], writes=[rRT])
                        sy.op("act", lambda e, a2=a2, sq=sq: e.activation(out=sq, in_=a2, func=AF.Sqrt), reads=[rRT], writes=[rRT])
                        sy.op("pool", lambda e, t0=t0, n=n, i_=i_: e.tensor_tensor(out=i_, in0=i_, in1=Ub[:, t0:t0 + n], op=ALU.mult),
                              reads=[rRT, rU], writes=[rRT])
                        sy.op("dve", lambda e, t0=t0, n=n, i_=i_, sq=sq, Bd=Bd: e.tensor_tensor(out=Bd[:, t0:t0 + n], in0=sq, in1=i_, op=ALU.mult),
                              reads=[rRT], writes=[rBd])
                    if d == 0:
                        sy.op("dve", lambda e: e.tensor_tensor_scan(out=B0[:, :], data0=A[:, :], data1=B0[:, :], initial=0.0,
                                                                    op0=ALU.mult, op1=ALU.add), reads=[rA, rB0], writes=[rB0])
                    else:
                        sy.op("dve", lambda e: e.tensor_tensor_scan(out=B1[:, 0:CTX][:, ::-1], data0=A[:, 0:CTX][:, ::-1],
                                                                    data1=B1[:, 0:CTX][:, ::-1], initial=0.0, op0=ALU.mult, op1=ALU.add),
                              reads=[rA, rB1], writes=[rB1])
                        sy.op("dve", lambda e: e.tensor_tensor_scan(out=B1[:, CTX:T][:, ::-1], data0=A[:, CTX:T][:, ::-1],
                                                                    data1=B1[:, CTX:T][:, ::-1], initial=B1[:, 0:1], op0=ALU.mult, op1=ALU.add),
                              reads=[rA, rB1], writes=[rB1])
                sy.op("pool", lambda e: e.tensor_tensor(out=B0[:], in0=B0[:], in1=B1[:], op=ALU.add), reads=[rB0, rB1], writes=[rB0])
                for (t0, n, s) in cfg.tiles:
                    tcount[0] += 1
                    TMP, rTMP = TMPs[tcount[0] % 2], rTMPs[tcount[0] % 2]
                    ps, rps = self.proj_tile(W[:, :, 128:256], rW, H, rH, t0, n)
                    self.gelu_from(TMP, rTMP, ps[:, :n], rps, YG[:, t0:t0 + n], rYG, n, mul=B0[:, t0:t0 + n], rmul=rB0)
                sy.dma(self.YAG[j * 128:(j + 1) * 128, :], YG[:], rYG, reads=[rYG], writes=[self.rYAG])


    def ssd_prep(self, l, H, rH):
        cfg, sy = self.cfg, self.sy
        KD, T = cfg.KD, cfg.T
        wsrc = self.i["w_in"][l]
        with Scope(self) as ss:
            Ws = [ss.sb("wp", [128, KD, 128], BF16) for i in range(2)]
            rWs = [R("wp%d" % i) for i in range(2)]
            OB = [ss.sb("ob", [128, T], BF16) for i in range(2)]
            rOB = [R("ob%d" % i) for i in range(2)]
            SEQB = ss.sb("seqb", [128, T + 6], BF16)
            rSEQB = R("seqb")
            DG = ss.sb("dg", [128, 4, 128], BF16)
            rDG = R("dg")
            sy.op("dve", lambda e: e.memset(SEQB[:], 0.0), writes=[rSEQB])
            k = 0
            jobs = [("z", cfg.o_z, j, self.Zs, self.rZs, AF.Silu) for j in range(cfg.SIC)]
            jobs += [("gt", cfg.o_gt, j, self.GT, self.rGT, AF.Sigmoid) for j in range(2 * KD)]
            jobs += [("xbc", cfg.o_xbc, j, self.XBC, self.rXBC, AF.Silu) for j in range(cfg.CDC)]
            for (kind, o, j, dst, rdst, func) in jobs:
                W, rW = Ws[k % 2], rWs[k % 2]
                ob, rob = OB[k % 2], rOB[k % 2]
                k += 1
                self.load_w(W, rW, wsrc[:, o + j * 128:o + (j + 1) * 128], KD, 128)
                if kind != "xbc":
                    for (t0, n, s) in cfg.tiles:
                        ps, rps = self.proj_tile(W, rW, H, rH, t0, n)
                        sy.op("act", lambda e, ps=ps, t0=t0, n=n, ob=ob, func=func: e.activation(out=ob[:, t0:t0 + n], in_=ps[:, :n], func=func),
                              reads=[rps], writes=[rob])
                else:
                    self.make_diag(DG, rDG, "scw", 4, cfg.CDC, j)
                    for (t0, n, s) in cfg.tiles:
                        ps, rps = self.proj_tile(W, rW, H, rH, t0, n)
                        p0 = self.seqpos(t0)
                        sy.op("act", lambda e, ps=ps, p0=p0, n=n: e.activation(out=SEQB[:, p0:p0 + n], in_=ps[:, :n], func=AF.Copy),
                              reads=[rps], writes=[rSEQB])
                    scb = self.vcol("scb", j, 1)
                    for (t0, n, s) in cfg.tiles:
                        ps, rps = self.conv1d_tile(DG, rDG, SEQB, rSEQB, t0, n)
                        sy.op("act", lambda e, ps=ps, t0=t0, n=n, ob=ob, scb=scb: e.activation(out=ob[:, t0:t0 + n], in_=ps[:, :n], func=AF.Silu, bias=scb),
                              reads=[rps, self.rVEC], writes=[rob])
                sy.dma(dst[j * 128:(j + 1) * 128, :], ob[:], rob, reads=[rob], writes=[rdst])
        with Scope(self) as ss:
            CS = ss.sb("cs", [64, T], F32)
            V = ss.sb("v", [64, T], F32)
            WEXP = ss.sb("wexp", [64, T], F32)
            TOT = ss.sb("tot", [64, cfg.NCH], F32)
            rDT = R("dtp")
            WDT = ss.sb("wdt", [128, KD, 64], BF16)
            rWDT = R("wdt")
            X1 = ss.sb("x1", [64, T], F32)
            RM = WEXP
            rX1, rRM = R("x1"), rDT
            NH = cfg.NH
            sy.op("pool", lambda e: e.memset(WDT[:], 0.0), writes=[rWDT])
            self.load_w(WDT[:, :, 0:NH], rWDT, wsrc[:, cfg.o_dt:cfg.o_dt + NH], KD, NH)
            self.load_w(WDT[:, :, 32:32 + NH], rWDT, wsrc[:, cfg.o_dt + NH:cfg.o_dt + 2 * NH], KD, NH)
            sy.dma(RM[0:32, :], self.i["c_rmask"][0, 0:32, :], rRM, writes=[rRM])
            sy.dma(RM[32:64, :], self.i["c_rmask"][1, 32:64, :], rRM, writes=[rRM])
            dtb = self.vcol("dtb", 0, 1, rows=64)
            for (t0, n, s) in cfg.tiles:
                ps, rps = self.proj_tile(WDT, rWDT, H, rH, t0, n, M=64)
                sy.op("act", lambda e, ps=ps, t0=t0, n=n: e.activation(out=X1[:, t0:t0 + n], in_=ps[:64, :n], func=AF.Exp, bias=dtb),
                      reads=[rps, self.rVEC], writes=[rX1])
            X2, X3 = V, CS
            sy.op("act", lambda e: e.activation(out=X1[:], in_=X1[:], func=AF.Ln, bias=1.0), reads=[rX1], writes=[rX1])
            sy.op("act", lambda e: e.activation(out=X2[:], in_=X1[:], func=AF.Ln), reads=[rX1], writes=[rDT])
            sy.op("dve", lambda e: e.tensor_scalar(out=X3[:], in0=X1[:], scalar1=self.aneg[:, 0:1], scalar2=None, op0=ALU.mult),
                  reads=[rX1, self.rMOD], writes=[rDT])
            sy.op("dve", lambda e: e.tensor_tensor_scan(out=X3[0:32, :], data0=RM[0:32, :], data1=X3[0:32, :], initial=0.0,
                                                        op0=ALU.mult, op1=ALU.add), reads=[rRM, rDT], writes=[rDT])
            sy.op("dve", lambda e: e.tensor_tensor_scan(out=X3[32:64, :][:, ::-1], data0=RM[32:64, :][:, ::-1], data1=X3[32:64, :][:, ::-1],
                                                        initial=0.0, op0=ALU.mult, op1=ALU.add), reads=[rRM, rDT], writes=[rDT])
            sy.op("dve", lambda e: e.tensor_tensor(out=V[:], in0=X2[:], in1=CS[:], op=ALU.subtract), reads=[rDT], writes=[rDT])
            CS3 = CS[:].rearrange("p (c q) -> p c q", q=128)
            V3 = V[:].rearrange("p (c q) -> p c q", q=128)
            W3 = WEXP[:].rearrange("p (c q) -> p c q", q=128)
            sy.op("dve", lambda e: e.tensor_copy(out=TOT[0:32, :], in_=CS3[0:32, :, 127]), reads=[rDT], writes=[rDT])
            sy.op("dve", lambda e: e.tensor_copy(out=TOT[32:64, :], in_=CS3[32:64, :, 0]), reads=[rDT], writes=[rDT])
            sy.op("dve", lambda e: e.tensor_tensor(out=W3, in0=V3, in1=TOT[:].unsqueeze(2).to_broadcast([64, cfg.NCH, 128]), op=ALU.add),
                  reads=[rDT], writes=[rDT])
            sy.op("act", lambda e: e.activation(out=WEXP[:], in_=WEXP[:], func=AF.Exp), reads=[rDT], writes=[rDT])
            for i_, t_ in enumerate((CS, V, WEXP)):
                sy.dma(self.DTS[i_], t_[:], rDT, reads=[rDT], writes=[self.rDTS])
            sy.op("act", lambda e: e.activation(out=TOT[:], in_=TOT[:], func=AF.Exp), reads=[rDT], writes=[rDT])
            sy.dma(self.TOTS, TOT[:], rDT, reads=[rDT], writes=[self.rDTS])

    def ssd_sweep(self, d):
        cfg, sy = self.cfg, self.sy
        HPG, NPR, NG, SIC, NCH = cfg.HPG, cfg.HPG // 2, cfg.NG, cfg.SIC, cfg.NCH
        nctx = cfg.CTX // 128
        order = list(range(NCH)) if d == 0 else (list(range(nctx - 1, -1, -1)) + list(range(NCH - 1, nctx - 1, -1)))
        with Scope(self) as ss:
            V = ss.sb("v", [64, cfg.T], F32)
            WEXP = ss.sb("wexp", [64, cfg.T], F32)
            ETB = ss.sb("etb", [128, 64, NCH], F32)
            rDT = R("dts")
            sy.dma(V[:], self.DTS[1], rDT, reads=[self.rDTS], writes=[rDT])
            sy.dma(WEXP[:], self.DTS[2], rDT, reads=[self.rDTS], writes=[rDT])
            sy.dma(ETB[:], self.TOTS.partition_broadcast(128), rDT, reads=[self.rDTS], writes=[rDT])
            BC = [ss.sb("bc", [128, 2, 128], BF16) for i in range(2)]
            XF = [ss.sb("xf", [128, NPR, 128], BF16) for i in range(2)]
            YL = [ss.sb("yl", [128, NPR, 128], F32) for i in range(2)]
            CSB = [ss.sb("csb", [128, HPG, 128], F32) for i in range(2)]
            rIN = [R("ssdin%d" % i) for i in range(2)]
            rYL = [R("yl%d" % i) for i in range(2)]
            rCSB = [R("csb%d" % i) for i in range(2)]
            XT2s = [ss.sb("xt2", [128, HPG, 128], BF16) for i in range(2)]
            BTs = [ss.sb("bt", [128, 128], BF16) for i in range(2)]
            WVTs = [ss.sb("wvt", [128, 64], F32) for i in range(2)]
            XWs = [ss.sb("xw", [128, HPG, 64], BF16) for i in range(2)]
            CBMs = [ss.sb("cbm", [128, 128], BF16) for i in range(2)]
            EXs = [ss.sb("ex", [128, HPG, 128], F32) for i in range(2)]
            MTs = [ss.sb("mt", [128, HPG, 128], BF16) for i in range(2)]
            ECs = [ss.sb("ec", [128, HPG, 128], BF16) for i in range(2)]
            MT2s = [ss.sb("mt2", [128, HPG, 128], BF16) for i in range(2)]
            CSCs = [ss.sb("csc", [128, HPG, 128], BF16) for i in range(2)]
            rXT2s, rBTs, rWTs, rXWs, rCBTs, rEXs, rMTs, rECs, rMT2s, rCSCs = [[R("%s%d" % (n, i)) for i in range(2)] for n in (
                "xt2", "bt", "wt", "xw", "cbt", "ex", "mt", "ec", "mt2", "csc")]
            YS = [ss.sb("ys", [128, NPR, 128], F32) for i in range(2)]
            rYS = [R("ys%d" % i) for i in range(2)]
            P32 = ss.sb("p32", [128, HPG, 64], F32)
            PB = ss.sb("pb", [128, HPG, 128], BF16)
            rP32, rPB = R("p32"), R("pb")
            PSy, rPSy = self.PS[4], self.rPS[4]
            PSs, rPSs = self.PS[5], self.rPS[5]
            PSm, rPSm = self.PS[6], self.rPS[6]
            PSw, rPSw = self.PS[3], self.rPS[3]
            PSB, rPSB = self.PSB, self.rPSB
            PSBv = PSB[:, 0:HPG * 64].rearrange("p (h q) -> p h q", q=64)
            mkn = self.maskn[:, d, :]
            mkv = self.maskf[:, d, :]
            for i_ in range(2):
                sy.op("dve", lambda e, i_=i_: e.memset(XT2s[i_][:], 0.0), writes=[rXT2s[i_]])
            it = 0
            for g in range(NG):
                sy.op("dve", lambda e: e.memset(P32[:], 0.0), writes=[rP32])
                sy.op("dve", lambda e: e.memset(PB[:], 0.0), writes=[rPB])
                hd0 = 32 * d + g * HPG
                for c in order:
                    tok = c * 128
                    bi = it % 2
                    it += 1
                    bc, xf, yl, rin, ryl, ys, rys, csb, rcsb = BC[bi], XF[bi], YL[bi], rIN[bi], rYL[bi], YS[bi], rYS[bi], CSB[bi], rCSB[bi]
                    XT2, BT, WVT, XW, CBM, EX, MT, EC, MT2, CSC = XT2s[bi], BTs[bi], WVTs[bi], XWs[bi], CBMs[bi], EXs[bi], MTs[bi], ECs[bi], MT2s[bi], CSCs[bi]
                    rXT2, rBT, rWT, rXW, rCBT, rEX, rMT, rEC, rMT2, rCSC = (rXT2s[bi], rBTs[bi], rWTs[bi], rXWs[bi], rCBTs[bi], rEXs[bi], rMTs[bi],
                                                                           rECs[bi], rMT2s[bi], rCSCs[bi])
                    for w_, ch in ((0, SIC + g), (1, SIC + NG + g)):
                        sy.dma(bc[:, w_, :], self.XBC[ch * 128:(ch + 1) * 128, tok:tok + 128], rin, reads=[self.rXBC], writes=[rin])
                    sy.dma(xf[:], self.XBC[g * NPR * 128:(g + 1) * NPR * 128, tok:tok + 128].rearrange("(q p) t -> p q t", p=128),
                           rin, reads=[self.rXBC], writes=[rin])
                    sy.dma(csb[:], self.DTS[0][hd0:hd0 + HPG, tok:tok + 128].partition_broadcast(128), rcsb, reads=[self.rDTS], writes=[rcsb])
                    ydst = self.Y[g * NPR * 128:(g + 1) * NPR * 128, tok:tok + 128].rearrange("(q p) t -> p q t", p=128)
                    if d == 1:
                        sy.dma(yl[:], ydst, ryl, reads=[self.rY], writes=[ryl])
                    def tfn(e, xf=xf, bc=bc, XT2=XT2, BT=BT, WVT=WVT, XW=XW, CBM=CBM, EX=EX, MT=MT, EC=EC, MT2=MT2, CSC=CSC):
                        ins = None
                        for q in range(NPR):
                            ins = e.transpose(PSB[:, q * 128:(q + 1) * 128], xf[:, q, :], self.ident_b[:])
                        ins = e.transpose(PSB[:, NPR * 128:(NPR + 1) * 128], bc[:, 0, :], self.ident_b[:])
                        return ins
                    sy.op("pe", tfn, reads=[rin, self.rC], writes=[rPSB])
                    sy.op("dve", lambda e, XT2=XT2, BT=BT, WVT=WVT, XW=XW, CBM=CBM, EX=EX, MT=MT, EC=EC, MT2=MT2, CSC=CSC: e.tensor_copy(out=XT2[:, 0::2, 0:64], in_=PSBv[:, 0::2, :]), reads=[rPSB], writes=[rXT2])
                    sy.op("dve", lambda e, XT2=XT2, BT=BT, WVT=WVT, XW=XW, CBM=CBM, EX=EX, MT=MT, EC=EC, MT2=MT2, CSC=CSC: e.tensor_copy(out=XT2[:, 1::2, 64:128], in_=PSBv[:, 1::2, :]), reads=[rPSB], writes=[rXT2])
                    sy.op("dve", lambda e, XT2=XT2, BT=BT, WVT=WVT, XW=XW, CBM=CBM, EX=EX, MT=MT, EC=EC, MT2=MT2, CSC=CSC: e.tensor_copy(out=BT[:], in_=PSB[:, NPR * 128:(NPR + 1) * 128]), reads=[rPSB], writes=[rBT])
                    def wfn(e, tok=tok, XT2=XT2, BT=BT, WVT=WVT, XW=XW, CBM=CBM, EX=EX, MT=MT, EC=EC, MT2=MT2, CSC=CSC):
                        idn = self.ident_f[32 * d:32 * d + 32, 32 * d:32 * d + 32]
                        e.transpose(PSw[:, 0:32], WEXP[32 * d:32 * d + 32, tok:tok + 128], idn)
                        return e.transpose(PSw[:, 32:64], V[32 * d:32 * d + 32, tok:tok + 128], idn)
                    sy.op("pe", wfn, reads=[rDT, self.rC], writes=[rPSw])
                    sy.op("act", lambda e, XT2=XT2, BT=BT, WVT=WVT, XW=XW, CBM=CBM, EX=EX, MT=MT, EC=EC, MT2=MT2, CSC=CSC: e.activation(out=WVT[:], in_=PSw[:, 0:64], func=AF.Copy), reads=[rPSw], writes=[rWT])
                    sy.op("dve", lambda e, g=g, XT2=XT2, BT=BT, WVT=WVT, XW=XW, CBM=CBM, EX=EX, MT=MT, EC=EC, MT2=MT2, CSC=CSC: e.tensor_tensor(out=XW[:], in0=PSBv, in1=WVT[:, g * HPG:(g + 1) * HPG].unsqueeze(2).to_broadcast([128, HPG, 64]),
                                                                op=ALU.mult), reads=[rPSB, rWT], writes=[rXW])
                    self.mm_group(PSm[:, 0:128], [(bc[:, 0, :], bc[:, 1, :])], [rin], rPSm)
                    sy.op("dve", lambda e, XT2=XT2, BT=BT, WVT=WVT, XW=XW, CBM=CBM, EX=EX, MT=MT, EC=EC, MT2=MT2, CSC=CSC: e.tensor_tensor(out=CBM[:], in0=PSm[:, 0:128], in1=mkv, op=ALU.mult), reads=[rPSm, self.rCm], writes=[rCBT])
                    for hh in range(HPG):
                        vcolap = WVT[:, 32 + g * HPG + hh:32 + g * HPG + hh + 1]
                        sy.op("dve", lambda e, hh=hh, csb=csb, vcolap=vcolap, XT2=XT2, BT=BT, WVT=WVT, XW=XW, CBM=CBM, EX=EX, MT=MT, EC=EC, MT2=MT2, CSC=CSC: e.scalar_tensor_tensor(
                            out=EX[:, hh, :], in0=csb[:, hh, :], scalar=vcolap, in1=mkn, op0=ALU.add, op1=ALU.add),
                            reads=[rcsb, rWT, self.rCm], writes=[rEX])
                    sy.op("act", lambda e, XT2=XT2, BT=BT, WVT=WVT, XW=XW, CBM=CBM, EX=EX, MT=MT, EC=EC, MT2=MT2, CSC=CSC: e.activation(out=MT[:], in_=EX[:], func=AF.Exp), reads=[rEX], writes=[rMT])
                    sy.op("act", lambda e, csb=csb, XT2=XT2, BT=BT, WVT=WVT, XW=XW, CBM=CBM, EX=EX, MT=MT, EC=EC, MT2=MT2, CSC=CSC: e.activation(out=EC[:], in_=csb[:], func=AF.Exp), reads=[rcsb], writes=[rEC])
                    sy.op("dve", lambda e, XT2=XT2, BT=BT, WVT=WVT, XW=XW, CBM=CBM, EX=EX, MT=MT, EC=EC, MT2=MT2, CSC=CSC: e.tensor_tensor(out=MT2[:], in0=MT[:], in1=CBM[:].unsqueeze(1).to_broadcast([128, HPG, 128]), op=ALU.mult),
                          reads=[rMT, rCBT], writes=[rMT2])
                    sy.op("dve", lambda e, bc=bc, XT2=XT2, BT=BT, WVT=WVT, XW=XW, CBM=CBM, EX=EX, MT=MT, EC=EC, MT2=MT2, CSC=CSC: e.tensor_tensor(out=CSC[:], in0=EC[:], in1=bc[:, 1, :].unsqueeze(1).to_broadcast([128, HPG, 128]), op=ALU.mult),
                          reads=[rEC, rin], writes=[rCSC])
                    def yfn(e, XT2=XT2, BT=BT, WVT=WVT, XW=XW, CBM=CBM, EX=EX, MT=MT, EC=EC, MT2=MT2, CSC=CSC):
                        ins = None
                        for q in range(NPR):
                            o = PSy[:, q * 128:(q + 1) * 128]
                            e.matmul(o, XT2[:, 2 * q, :], MT2[:, 2 * q, :], start=True, stop=False)
                            e.matmul(o, XT2[:, 2 * q + 1, :], MT2[:, 2 * q + 1, :], start=False, stop=False)
                            e.matmul(o, PB[:, 2 * q, :], CSC[:, 2 * q, :], start=False, stop=False)
                            ins = e.matmul(o, PB[:, 2 * q + 1, :], CSC[:, 2 * q + 1, :], start=False, stop=True)
                        return ins
                    sy.op("pe", yfn, reads=[rXT2, rMT2, rPB, rCSC], writes=[rPSy])
                    psyv = PSy[:, 0:NPR * 128].rearrange("p (q l) -> p q l", l=128)
                    if d == 0:
                        sy.op("act", lambda e, ys=ys, XT2=XT2, BT=BT, WVT=WVT, XW=XW, CBM=CBM, EX=EX, MT=MT, EC=EC, MT2=MT2, CSC=CSC: e.activation(out=ys[:], in_=psyv, func=AF.Copy), reads=[rPSy], writes=[rys])
                    else:
                        sy.op("dve", lambda e, ys=ys, yl=yl, XT2=XT2, BT=BT, WVT=WVT, XW=XW, CBM=CBM, EX=EX, MT=MT, EC=EC, MT2=MT2, CSC=CSC: e.tensor_tensor(out=ys[:], in0=psyv, in1=yl[:], op=ALU.add), reads=[rPSy, ryl], writes=[rys])
                    sy.dma(ydst, ys[:], rys, reads=[rys], writes=[self.rY])
                    self.mm_group(PSs[:, 0:HPG * 64], [(BT[:], XW[:].rearrange("p h q -> p (h q)"))], [rBT, rXW], rPSs)
                    et = ETB[:, hd0:hd0 + HPG, c]
                    sy.op("dve", lambda e, et=et, XT2=XT2, BT=BT, WVT=WVT, XW=XW, CBM=CBM, EX=EX, MT=MT, EC=EC, MT2=MT2, CSC=CSC: e.tensor_tensor(out=P32[:], in0=P32[:], in1=et.unsqueeze(2).to_broadcast([128, HPG, 64]), op=ALU.mult),
                          reads=[rP32, rDT], writes=[rP32])
                    sy.op("dve", lambda e, XT2=XT2, BT=BT, WVT=WVT, XW=XW, CBM=CBM, EX=EX, MT=MT, EC=EC, MT2=MT2, CSC=CSC: e.tensor_tensor(out=P32[:], in0=P32[:], in1=PSs[:, 0:HPG * 64].rearrange("p (h q) -> p h q", q=64), op=ALU.add),
                          reads=[rP32, rPSs], writes=[rP32])
                    sy.op("dve", lambda e, XT2=XT2, BT=BT, WVT=WVT, XW=XW, CBM=CBM, EX=EX, MT=MT, EC=EC, MT2=MT2, CSC=CSC: e.tensor_copy(out=PB[:, 0::2, 0:64], in_=P32[:, 0::2, :]), reads=[rP32], writes=[rPB])
                    sy.op("dve", lambda e, XT2=XT2, BT=BT, WVT=WVT, XW=XW, CBM=CBM, EX=EX, MT=MT, EC=EC, MT2=MT2, CSC=CSC: e.tensor_copy(out=PB[:, 1::2, 64:128], in_=P32[:, 1::2, :]), reads=[rP32], writes=[rPB])

    def ssd_out(self, l):
        cfg, sy = self.cfg, self.sy
        SIC, NG, TS = cfg.SIC, cfg.NG, cfg.TS2
        CPG = SIC // NG
        GS = cfg.SI // NG
        with Scope(self) as ss:
            YT = [ss.sb("yt", [128, SIC, TS], F32) for i in range(2)]
            ZT = [ss.sb("zt", [128, SIC, TS], BF16) for i in range(2)]
            XS = [ss.sb("xs", [128, SIC, TS], BF16) for i in range(2)]
            rI = [R("soin%d" % i) for i in range(2)]
            SQ = ss.sb("sqb", [128, SIC, TS], BF16)
            rSQ = R("sqb")
            RS = ss.sb("rs", [128, NG, TS], F32)
            rRS = R("rs")
            GNT = [ss.sb("gnt", [128, SIC, TS], BF16) for i in range(2)]
            rGNT = [R("gnt%d" % i) for i in range(2)]
            for ti, (t0, n, s) in enumerate(cfg.tiles2):
                bi = ti % 2
                yt, zt, xs, ri, gnt, rg = YT[bi], ZT[bi], XS[bi], rI[bi], GNT[bi], rGNT[bi]
                sy.dma(yt[:, :, :n], self.Y[:, t0:t0 + n].rearrange("(j p) t -> p j t", p=128), ri, reads=[self.rY], writes=[ri])
                sy.dma(zt[:, :, :n], self.Zs[:, t0:t0 + n].rearrange("(j p) t -> p j t", p=128), ri, reads=[self.rZs], writes=[ri])
                sy.dma(xs[:, :, :n], self.XBC[0:cfg.SI, t0:t0 + n].rearrange("(j p) t -> p j t", p=128), ri, reads=[self.rXBC], writes=[ri])
                for j in range(SIC):
                    sdd = self.vcol("sdd", j, 1)
                    eng = "dve"
                    sy.op(eng, lambda e, j=j, sdd=sdd, yt=yt, xs=xs, n=n: e.scalar_tensor_tensor(
                        out=yt[:, j, :n], in0=xs[:, j, :n], scalar=sdd, in1=yt[:, j, :n], op0=ALU.mult, op1=ALU.add),
                        reads=[ri, self.rVEC], writes=[ri])
                sy.op("dve", lambda e, yt=yt, zt=zt, n=n: e.tensor_tensor(out=yt[:, :, :n], in0=yt[:, :, :n], in1=zt[:, :, :n], op=ALU.mult),
                      reads=[ri], writes=[ri])
                sy.op("act", lambda e, yt=yt, n=n: e.activation(out=SQ[:, :, :n], in_=yt[:, :, :n], func=AF.Square), reads=[ri], writes=[rSQ])
                for g in range(NG):
                    ps, rps = self.nextps()
                    self.mm_group(ps[:, :n], [(self.ones_b[:], SQ[:, g * CPG + q, :n]) for q in range(CPG)], [rSQ, self.rC], rps)
                    sy.op("act", lambda e, ps=ps, g=g, n=n: e.activation(out=RS[:, g, :n], in_=ps[:, :n], func=AF.Sqrt, scale=1.0 / GS, bias=1e-6),
                          reads=[rps], writes=[rRS])
                sy.op("dve", lambda e, n=n: e.reciprocal(out=RS[:, :, :n], in_=RS[:, :, :n]), reads=[rRS], writes=[rRS])
                for j in range(SIC):
                    snw = self.vcol("snw", j, 1)
                    eng = "dve"
                    sy.op(eng, lambda e, j=j, snw=snw, yt=yt, gnt=gnt, n=n: e.scalar_tensor_tensor(
                        out=gnt[:, j, :n], in0=yt[:, j, :n], scalar=snw, in1=RS[:, j // CPG, :n], op0=ALU.mult, op1=ALU.mult),
                        reads=[ri, rRS, self.rVEC], writes=[rg])
                sy.dma(self.GN[:, t0:t0 + n].rearrange("(j p) t -> p j t", p=128), gnt[:, :, :n], rg, reads=[rg], writes=[self.rGN])

    def resid_tiles(self, first, KD):
        pass

    def merge(self, l, first):
        cfg, sy = self.cfg, self.sy
        KD, LWC, SIC, TS, D = cfg.KD, cfg.LWC, cfg.SIC, cfg.TS2, cfg.D
        with Scope(self) as ss:
            WL = ss.sb("wlp", [128, LWC, D], BF16)
            WS = ss.sb("wsp", [128, SIC, D], BF16)
            WO = ss.sb("wop", [128, KD, D], BF16)
            rWL, rWS, rWO = R("wlp"), R("wsp"), R("wop")
            self.load_w(WL, rWL, self.i["lru_proj"][l], LWC, D)
            self.load_w(WS, rWS, self.i["ssd_proj"][l], SIC, D)
            self.load_w(WO, rWO, self.i["w_out"][l], KD, D)
            YA = [ss.sb("ya", [128, LWC, TS], BF16) for i in range(2)]
            GNT = [ss.sb("gn", [128, SIC, TS], BF16) for i in range(2)]
            GTT = [ss.sb("gt", [128, 2 * KD, TS], BF16) for i in range(2)]
            XT = [ss.sb("xr", [128, KD, TS], F32) for i in range(2)]
            rI = [R("mgin%d" % i) for i in range(2)]
            rX = [R("mgx%d" % i) for i in range(2)]
            M1 = ss.sb("m1", [128, TS], F32)
            rM1 = R("m1")
            MB = ss.sb("mb", [128, KD, TS], BF16)
            rMB = R("mb")
            for ti, (t0, n, s) in enumerate(cfg.tiles2):
                bi = ti % 2
                ya, gn, gt, xt, ri, rx = YA[bi], GNT[bi], GTT[bi], XT[bi], rI[bi], rX[bi]
                sy.dma(ya[:, :, :n], self.YAG[:, t0:t0 + n].rearrange("(j p) t -> p j t", p=128), ri, reads=[self.rYAG], writes=[ri])
                sy.dma(gn[:, :, :n], self.GN[:, t0:t0 + n].rearrange("(j p) t -> p j t", p=128), ri, reads=[self.rGN], writes=[ri])
                sy.dma(gt[:, :, :n], self.GT[:, t0:t0 + n].rearrange("(j p) t -> p j t", p=128), ri, reads=[self.rGT], writes=[ri])
                xsrc = (self.i["xr0"] if first else self.XR)[:, t0:t0 + n].rearrange("(kc p) t -> p kc t", p=128)
                sy.dma(xt[:, :, :n], xsrc, rx, reads=([] if first else [self.rXR]), writes=[rx])
                for jo in range(KD):
                    pa, rpa = self.nextps()
                    self.mm_group(pa[:, :n], [(WL[:, k, jo * 128:(jo + 1) * 128], ya[:, k, :n]) for k in range(LWC)], [rWL, ri], rpa)
                    pb, rpb = self.nextps()
                    self.mm_group(pb[:, :n], [(WS[:, k, jo * 128:(jo + 1) * 128], gn[:, k, :n]) for k in range(SIC)], [rWS, ri], rpb)
                    sy.op("dve", lambda e, pa=pa, gt=gt, jo=jo, n=n: e.tensor_tensor(out=M1[:, :n], in0=pa[:, :n], in1=gt[:, jo, :n], op=ALU.mult),
                          reads=[rpa, ri], writes=[rM1])
                    sy.op("dve", lambda e, pb=pb, gt=gt, jo=jo, n=n: e.tensor_tensor(out=MB[:, jo, :n], in0=pb[:, :n], in1=gt[:, KD + jo, :n], op=ALU.mult),
                          reads=[rpb, ri], writes=[rMB])
                    sy.op("pool", lambda e, jo=jo, n=n: e.tensor_tensor(out=MB[:, jo, :n], in0=MB[:, jo, :n], in1=M1[:, :n], op=ALU.add),
                          reads=[rM1, rMB], writes=[rMB])
                for jo in range(KD):
                    po, rpo = self.nextps()
                    self.mm_group(po[:, :n], [(WO[:, k, jo * 128:(jo + 1) * 128], MB[:, k, :n]) for k in range(KD)], [rWO, rMB], rpo)
                    gate = self.mod[:, 2 * KD + jo, s:s + 1]
                    sy.op("dve", lambda e, po=po, xt=xt, jo=jo, n=n, gate=gate: e.scalar_tensor_tensor(
                        out=xt[:, jo, :n], in0=po[:, :n], scalar=gate, in1=xt[:, jo, :n], op0=ALU.mult, op1=ALU.add),
                        reads=[rpo, rx, self.rMOD], writes=[rx])
                sy.dma(self.XR[:, t0:t0 + n].rearrange("(kc p) t -> p kc t", p=128), xt[:, :, :n], rx, reads=[rx], writes=[self.rXR])

    def ffn_up(self, l, H, rH):
        cfg, sy = self.cfg, self.sy
        KD, T, CTX, SEQ, GW, FFC, TS = cfg.KD, cfg.T, cfg.CTX, cfg.SEQ, cfg.GW, cfg.FFC, cfg.TS
        ROWS = SEQ // GW
        wsrc = self.i["ffn_w_up"][l]
        with Scope(self) as ss:
            Ws = [ss.sb("wu", [128, KD, 256], BF16) for i in range(2)]
            rWs = [R("wu%d" % i) for i in range(2)]
            PADX = ss.sb("padx", [128, ROWS + 2, GW + 2], BF16)
            PADC = ss.sb("padc", [128, 3, CTX + 2], BF16)
            rPAD = R("pad")
            VB = ss.sb("vb", [128, T], BF16)
            rVB = R("vb")
            AO = [ss.sb("ao", [128, T], BF16) for i in range(2)]
            rAO = [R("ao%d" % i) for i in range(2)]
            DG = ss.sb("dg9", [128, 9, 128], BF16)
            rDG = R("dg9")
            TMPs = [ss.sb("tmpf", [128, 3, TS], F32) for i in range(2)]
            rTMPs = [R("tmpf%d" % i) for i in range(2)]
            tcount = [0]
            sy.op("dve", lambda e: e.memset(PADX[:], 0.0), writes=[rPAD])
            sy.op("dve", lambda e: e.memset(PADC[:], 0.0), writes=[rPAD])
            for j in range(FFC):
                W, rW = Ws[j % 2], rWs[j % 2]
                ao, rao = AO[j % 2], rAO[j % 2]
                self.load_w(W[:, :, 0:128], rW, wsrc[:, j * 128:(j + 1) * 128], KD, 128)
                self.load_w(W[:, :, 128:256], rW, wsrc[:, cfg.FF + j * 128:cfg.FF + (j + 1) * 128], KD, 128)
                self.make_diag(DG, rDG, "fcw", 9, FFC, j)
                for (t0, n, s) in cfg.tiles:
                    ps, rps = self.proj_tile(W[:, :, 0:128], rW, H, rH, t0, n)
                    if s == 1:
                        dst = PADC[:, 1, 1 + t0:1 + t0 + n]
                        src = ps[:, :n]
                    else:
                        r0 = (t0 - CTX) // GW
                        nr = n // GW
                        dst = PADX[:, 1 + r0:1 + r0 + nr, 1:1 + GW]
                        src = ps[:, :n].rearrange("p (r c) -> p r c", c=GW)
                    sy.op("act", lambda e, dst=dst, src=src: e.activation(out=dst, in_=src, func=AF.Copy), reads=[rps], writes=[rPAD])
                    ps, rps = self.proj_tile(W[:, :, 128:256], rW, H, rH, t0, n)
                    sy.op("act", lambda e, ps=ps, t0=t0, n=n: e.activation(out=VB[:, t0:t0 + n], in_=ps[:, :n], func=AF.Copy), reads=[rps], writes=[rVB])
                fcb = self.vcol("fcb", j, 1)
                for (t0, n, s) in cfg.tiles:
                    ps, rps = self.nextps()
                    pairs = []
                    for kr in range(3):
                        for kc in range(3):
                            if s == 1:
                                rhs = PADC[:, kr, kc + t0:kc + t0 + n]
                            else:
                                r0 = (t0 - CTX) // GW
                                nr = n // GW
                                rhs = PADX[:, r0 + kr:r0 + kr + nr, kc:kc + GW]
                            pairs.append((DG[:, kr * 3 + kc, :], rhs))
                    outp = ps[:, :n] if s == 1 else ps[:, :n].rearrange("p (r c) -> p r c", c=GW)
                    self.mm_group(outp, pairs, [rDG, rPAD], rps)
                    tcount[0] += 1
                    TMP, rTMP = TMPs[tcount[0] % 2], rTMPs[tcount[0] % 2]
                    self.gelu_from(TMP, rTMP, ps[:, :n], rps, ao[:, t0:t0 + n], rao, n, bias=fcb, mul=VB[:, t0:t0 + n], rmul=rVB)
                sy.dma(self.ACTV[j * 128:(j + 1) * 128, :], ao[:], rao, reads=[rao], writes=[self.rACTV])

    def ffn_down(self, l):
        cfg, sy = self.cfg, self.sy
        KD, FFC, TS, D = cfg.KD, cfg.FFC, cfg.TS2, cfg.D
        with Scope(self) as ss:
            WD = ss.sb("wd", [128, FFC, D], BF16)
            rWD = R("wd")
            self.load_w(WD, rWD, self.i["ffn_w_down"][l], FFC, D)
            AT = [ss.sb("at", [128, FFC, TS], BF16) for i in range(2)]
            XT = [ss.sb("xr2", [128, KD, TS], F32) for i in range(2)]
            rI = [R("fdin%d" % i) for i in range(2)]
            rX = [R("fdx%d" % i) for i in range(2)]
            for ti, (t0, n, s) in enumerate(cfg.tiles2):
                bi = ti % 2
                at, xt, ri, rx = AT[bi], XT[bi], rI[bi], rX[bi]
                sy.dma(at[:, :, :n], self.ACTV[:, t0:t0 + n].rearrange("(j p) t -> p j t", p=128), ri, reads=[self.rACTV], writes=[ri])
                sy.dma(xt[:, :, :n], self.XR[:, t0:t0 + n].rearrange("(kc p) t -> p kc t", p=128), rx, reads=[self.rXR], writes=[rx])
                for jo in range(KD):
                    po, rpo = self.nextps()
                    self.mm_group(po[:, :n], [(WD[:, k, jo * 128:(jo + 1) * 128], at[:, k, :n]) for k in range(FFC)], [rWD, ri], rpo)
                    gate = self.mod[:, 5 * KD + jo, s:s + 1]
                    sy.op("dve", lambda e, po=po, xt=xt, jo=jo, n=n, gate=gate: e.scalar_tensor_tensor(
                        out=xt[:, jo, :n], in0=po[:, :n], scalar=gate, in1=xt[:, jo, :n], op0=ALU.mult, op1=ALU.add),
                        reads=[rpo, rx, self.rMOD], writes=[rx])
                sy.dma(self.XR[:, t0:t0 + n].rearrange("(kc p) t -> p kc t", p=128), xt[:, :, :n], rx, reads=[rx], writes=[self.rXR])

    def layer(self, l, first):
        cfg = self.cfg
        on = lambda n: (self.stages is None) or (n in self.stages)
        self.adaln(l)
        with Scope(self) as hs:
            H = hs.sb("H", [128, cfg.KD, cfg.T], BF16)
            rH = R("H")
            if on("norm1"):
                self.norm_mod(first, self.g1, 0, H, rH)
            if on("lru"):
                self.lru(l, H, rH)
            if on("prep"):
                self.ssd_prep(l, H, rH)
        for d in range(2):
            if on("sweep%d" % d):
                self.ssd_sweep(d)
        if on("ssdout"):
            self.ssd_out(l)
        if on("merge"):
            self.merge(l, first)
        with Scope(self) as hs:
            H = hs.sb("H2", [128, cfg.KD, cfg.T], BF16)
            rH = R("H2")
            if on("norm2"):
                self.norm_mod(False, self.g2, 3, H, rH)
            if on("ffnup"):
                self.ffn_up(l, H, rH)
        if on("ffndown"):
            self.ffn_down(l)

    def final_norm(self):
        cfg, sy = self.cfg, self.sy
        KD, TS = cfg.KD, cfg.TS2
        with Scope(self) as ss:
            xts = [ss.sb("xtf", [128, KD, TS], F32) for i in range(2)]
            rxs = [R("xtf%d" % i) for i in range(2)]
            sq = ss.sb("sqf", [128, KD, TS], F32)
            rsq = R("sqf")
            rstd = ss.sb("rstdf", [128, TS], F32)
            rrs = R("rstdf")
            rOUT = R("out")
            ti = 0
            for (t0, n, s) in cfg.tiles2:
                if s == 1:
                    continue
                xt, rx = xts[ti % 2], rxs[ti % 2]
                ti += 1
                sy.dma(xt[:, :, :n], self.XR[:, t0:t0 + n].rearrange("(kc p) t -> p kc t", p=128), rx, reads=[self.rXR], writes=[rx])
                self.rstd_of(xt, rx, n, sq, rsq, rstd, rrs, cfg.D)
                sy.op("dve", lambda e, xt=xt, n=n: e.tensor_tensor(out=xt[:, :, :n], in0=xt[:, :, :n],
                                                                   in1=rstd[:, :n].unsqueeze(1).to_broadcast([128, KD, n]), op=ALU.mult),
                      reads=[rx, rrs], writes=[rx])
                sy.op("dve", lambda e, xt=xt, n=n: e.tensor_tensor(out=xt[:, :, :n], in0=xt[:, :, :n],
                                                                   in1=self.vcol("fnw", 0, KD).unsqueeze(2).to_broadcast([128, KD, n]), op=ALU.mult),
                      reads=[rx, self.rVEC], writes=[rx])
                sy.dma(self.outf[:, t0 - cfg.CTX:t0 - cfg.CTX + n].rearrange("(kc p) t -> p kc t", p=128), xt[:, :, :n], rx, reads=[rx], writes=[rOUT])

    def copy_xr_out(self):
        cfg, sy = self.cfg, self.sy
        KD, TS = cfg.KD, cfg.TS
        with Scope(self) as ss:
            xts = [ss.sb("xtc", [128, KD, TS], F32) for i in range(2)]
            rxs = [R("xtc%d" % i) for i in range(2)]
            rOUT = R("out")
            for ti, (t0, n, s) in enumerate(cfg.tiles):
                xt, rx = xts[ti % 2], rxs[ti % 2]
                sy.dma(xt[:, :, :n], self.XR[:, t0:t0 + n].rearrange("(kc p) t -> p kc t", p=128), rx, reads=[self.rXR], writes=[rx])
                sy.dma(self.out[:, t0:t0 + n].rearrange("(kc p) t -> p kc t", p=128), xt[:, :, :n], rx, reads=[rx], writes=[rOUT])


FUSED = True
CFG_KW = {}
PER_LAYER = ("ada_w", "ada_b", "norm_mix_w", "norm_ffn_w", "w_in", "lru_conv_w", "lru_conv_b", "lru_wa", "lru_ba", "lru_wx",
             "lru_bx", "lru_lambda", "lru_proj", "ssd_conv_w", "ssd_conv_b", "ssd_dt_bias", "ssd_a_log", "ssd_d", "ssd_norm_w",
             "ssd_proj", "w_out", "ffn_w_up", "ffn_conv_w", "ffn_conv_b", "ffn_w_down")


def _maps(cfg, inp, B):
    base = host_prep(cfg, inp, 0)
    maps = [base]
    for b in range(1, B):
        m = dict(base)
        pb = host_prep_core(cfg, inp, b)
        m.update(pb)
        maps.append(m)
    return maps


def host_prep_core(cfg, inp, b):
    m = {}
    m["xr0"] = np.ascontiguousarray(np.concatenate([inp["ctx"][b].T, inp["x"][b].T], axis=1).astype(np.float32))
    cv = np.zeros((128, cfg.KD, 2), np.float32)
    cv[:, :, 0] = colmajor(inp["c"][b])
    cv[:, :, 1] = colmajor(inp["c_ctx"])
    m["cvec"] = cv
    return m


def kernel(**inputs):
    from concourse.bass_utils import run_bass_kernel_spmd
    inp = {k: np.asarray(v) for k, v in inputs.items()}
    B = inp["x"].shape[0]
    L = inp["w_in"].shape[0]
    if FUSED:
        cfg = Cfg(DEPTH=L, **CFG_KW)
        nc = Prog(cfg, layers=list(range(L)), final=True, emit_xr=False).build()
        maps = _maps(cfg, inp, B)
        res = run_bass_kernel_spmd(nc, maps, core_ids=list(range(B)))
        outs = [res.results[b]["out"] for b in range(B)]
    else:
        cfg = Cfg(DEPTH=1, **CFG_KW)
        xr = None
        for l in range(L):
            inpl = {k: (v[l:l + 1] if k in PER_LAYER else v) for k, v in inp.items()}
            maps = _maps(cfg, inpl, B)
            if xr is not None:
                for b in range(B):
                    maps[b]["xr0"] = xr[b]
            nc = Prog(cfg, layers=[0], final=True, emit_xr=True).build()
            res = run_bass_kernel_spmd(nc, maps, core_ids=list(range(B)))
            xr = [np.ascontiguousarray(res.results[b]["outx"]) for b in range(B)]
            outs = [res.results[b]["outf"] for b in range(B)]
    out = np.stack([np.ascontiguousarray(o.T) for o in outs]).astype(np.float32)
    return out
```

```python
import numpy as np
from contextlib import ExitStack
import concourse.bass as bass
import concourse.mybir as mybir

F32 = mybir.dt.float32
BF16 = mybir.dt.bfloat16
AF = mybir.ActivationFunctionType
ALU = mybir.AluOpType

ENGS = ("pe", "act", "dve", "pool", "sp")


class R:
    __slots__ = ("name", "w", "rs", "dsem", "dcnt")

    def __init__(self, name):
        self.name = name
        self.w = None
        self.rs = {}
        self.dsem = None
        self.dcnt = 0


class Sync:
    def __init__(self, nc, stack):
        self.nc = nc
        self.stack = stack
        self.ops = {e: [] for e in ENGS}
        self.sems = {}
        for e in ENGS:
            self.sems[("E", e)] = stack.enter_context(nc.semaphore("S_" + e))
        self.cnt = {e: 0 for e in ENGS}
        self.seen = {e: {} for e in ENGS}
        self.ndsem = 0
        self.dcount = {}
        self.dsem_by_name = {}

    def _waits(self, eng, reads, writes, extra=()):
        w = {}

        def add(ev):
            if ev is None:
                return
            k, v = ev
            if w.get(k, 0) < v:
                w[k] = v
        for r in reads:
            add(r.w)
        for r in writes:
            add(r.w)
            for k, v in r.rs.items():
                add((k, v))
        for ev in extra:
            add(ev)
        out = []
        seen = self.seen[eng]
        for k, v in w.items():
            if eng == "pe" and k == ("E", "pe"):
                continue
            if seen.get(k, 0) >= v:
                continue
            seen[k] = v
            out.append((k, v))
        return out

    def _mark(self, ev, reads, writes):
        k, v = ev
        for r in reads:
            if r.rs.get(k, 0) < v:
                r.rs[k] = v
        for r in writes:
            r.w = ev
            r.rs = {}

    def op(self, eng, fn, reads=(), writes=()):
        waits = self._waits(eng, reads, writes)
        self.cnt[eng] += 1
        ev = (("E", eng), self.cnt[eng])
        self.ops[eng].append((waits, fn, ("E", eng), 1))
        self._mark(ev, reads, writes)
        return ev

    def dma(self, out_ap, in_ap, sres, reads=(), writes=(), q="sp"):
        key = self.dsem_by_name.get(sres.name)
        if key is None:
            key = ("D", self.ndsem)
            self.ndsem += 1
            self.sems[key] = self.stack.enter_context(self.nc.semaphore("D%d" % key[1]))
            self.dsem_by_name[sres.name] = key
            self.dcount[key] = 0
        extra = []
        if self.dcount[key]:
            extra.append((key, self.dcount[key]))
        waits = self._waits(q, reads, writes, extra)
        self.dcount[key] += 16
        ev = (key, self.dcount[key])

        def fn(e, out_ap=out_ap, in_ap=in_ap):
            return e.dma_start(out=out_ap, in_=in_ap)
        self.ops[q].append((waits, fn, key, 16))
        self._mark(ev, reads, writes)
        return ev

    def barrier(self):
        evs = [(("E", e), self.cnt[e]) for e in ENGS if self.cnt[e]]
        evs += [(k, None) for k in self.sems if k[0] == "D"]
        for e in ENGS:
            waits = []
            for k, v in evs:
                if v is None:
                    v = self.dcount.get(k, 0)
                if v and self.seen[e].get(k, 0) < v and not (k == ("E", e)):
                    self.seen[e][k] = v
                    waits.append((k, v))
            if waits:
                self.ops[e].append((waits, None, None, 0))

    def fence(self, eng, resources):
        waits = self._waits(eng, [], resources)
        if waits:
            self.ops[eng].append((waits, None, None, 0))

    def emit(self):
        nc = self.nc
        engmap = {"pe": "tensor", "act": "scalar", "dve": "vector", "pool": "gpsimd", "sp": "sync"}
        with nc.Block() as block:
            for e in ENGS:
                ops = self.ops[e]

                def body(eng, ops=ops):
                    for waits, fn, semkey, inc in ops:
                        for k, v in waits:
                            eng.wait_ge(self.sems[k], v)
                        if fn is not None:
                            ins = fn(eng)
                            ins.then_inc(self.sems[semkey], inc)
                getattr(block, engmap[e])(body)
        self.ops = {e: [] for e in ENGS}


class Cfg:
    def __init__(self, D=1024, SEQ=4096, CTX=256, GW=64, LW=1024, SI=2048, NG=4, FF=2816, DEPTH=4, TS=512):
        self.D, self.SEQ, self.CTX, self.GW, self.LW, self.SI, self.NG, self.FF, self.DEPTH, self.TS = \
            D, SEQ, CTX, GW, LW, SI, NG, FF, DEPTH, TS
        self.T = CTX + SEQ
        self.KD = D // 128
        self.LWC = LW // 128
        self.SIC = SI // 128
        self.NH = SI // 64
        self.HPG = self.NH // NG
        self.CD = SI + 2 * NG * 128
        self.CDC = self.CD // 128
        self.FFC = FF // 128
        self.IN_DIM = 2 * LW + SI + self.CD + 2 * self.NH + 2 * D
        self.o_lx, self.o_lg = 0, LW
        self.o_z = 2 * LW
        self.o_xbc = 2 * LW + SI
        self.o_dt = self.o_xbc + self.CD
        self.o_gt = self.o_dt + 2 * self.NH
        self.tiles = []
        for t0 in range(0, CTX, TS):
            self.tiles.append((t0, min(TS, CTX - t0), 1))
        for t0 in range(0, SEQ, TS):
            self.tiles.append((CTX + t0, min(TS, SEQ - t0), 0))
        self.NCH = self.T // 128
        self.TS2 = min(256, TS)
        self.tiles2 = []
        for t0 in range(0, CTX, self.TS2):
            self.tiles2.append((t0, min(self.TS2, CTX - t0), 1))
        for t0 in range(0, SEQ, self.TS2):
            self.tiles2.append((CTX + t0, min(self.TS2, SEQ - t0), 0))
        off = {}
        o = 0
        for name, n in (("nmw", self.KD), ("nfw", self.KD), ("adab", 6 * self.KD), ("lcw", 4 * self.LWC),
                        ("lcb", self.LWC), ("lba", 2 * self.LWC), ("lbx", 2 * self.LWC), ("llam", 2 * self.LWC),
                        ("scw", 4 * self.CDC), ("scb", self.CDC), ("sdd", self.SIC), ("snw", self.SIC),
                        ("fcw", 9 * self.FFC), ("fcb", self.FFC), ("fnw", self.KD), ("dtb", 1), ("alog", 1)):
            off[name] = o
            o += n
        self.voff = off
        self.NV = o


def colmajor(v):
    v = np.asarray(v, np.float32)
    return np.ascontiguousarray(v.reshape(-1, 128).T)


def host_prep(cfg, inp, b):
    L = cfg.DEPTH
    m = {}
    m["xr0"] = np.ascontiguousarray(np.concatenate([inp["ctx"][b].T, inp["x"][b].T], axis=1).astype(np.float32))
    cv = np.zeros((128, cfg.KD, 2), np.float32)
    cv[:, :, 0] = colmajor(inp["c"][b])
    cv[:, :, 1] = colmajor(inp["c_ctx"])
    m["cvec"] = cv
    vecs = np.zeros((L, 128, cfg.NV), np.float32)
    vo = cfg.voff
    for l in range(L):
        def put(name, arr2d):
            vecs[l, :arr2d.shape[0], vo[name]:vo[name] + arr2d.shape[1]] = arr2d
        put("nmw", colmajor(inp["norm_mix_w"][l]))
        put("nfw", colmajor(inp["norm_ffn_w"][l]))
        put("adab", colmajor(inp["ada_b"][l]))
        put("lcw", np.concatenate([colmajor(inp["lru_conv_w"][l][k]) for k in range(4)], axis=1))
        put("lcb", colmajor(inp["lru_conv_b"][l]))
        put("lba", np.concatenate([colmajor(inp["lru_ba"][l][d]) for d in range(2)], axis=1))
        put("lbx", np.concatenate([colmajor(inp["lru_bx"][l][d]) for d in range(2)], axis=1))
        put("llam", np.concatenate([colmajor(inp["lru_lambda"][l][d]) for d in range(2)], axis=1))
        put("scw", np.concatenate([colmajor(inp["ssd_conv_w"][l][k]) for k in range(4)], axis=1))
        put("scb", colmajor(inp["ssd_conv_b"][l]))
        put("sdd", colmajor(np.repeat(np.asarray(inp["ssd_d"][l]), 64)))
        put("snw", colmajor(inp["ssd_norm_w"][l]))
        fw_ = np.asarray(inp["ffn_conv_w"][l]).reshape(9, cfg.FF)
        put("fcw", np.concatenate([colmajor(fw_[k]) for k in range(9)], axis=1))
        put("fcb", colmajor(inp["ffn_conv_b"][l]))
        put("fnw", colmajor(inp["final_norm_w"]))
        dtb = np.zeros((64, 1), np.float32)
        alog = np.zeros((64, 1), np.float32)
        for d in range(2):
            dtb[d * 32:d * 32 + cfg.NH, 0] = inp["ssd_dt_bias"][l][d]
            alog[d * 32:d * 32 + cfg.NH, 0] = inp["ssd_a_log"][l][d]
        put("dtb", dtb)
        put("alog", alog)
    m["vecs"] = vecs
    for k in ("ada_w", "w_in", "lru_wa", "lru_wx", "lru_proj", "ssd_proj", "w_out", "ffn_w_up", "ffn_w_down"):
        m[k] = np.ascontiguousarray(np.asarray(inp[k], np.float32))
    ident = np.eye(128, dtype=np.float32)
    m["c_ident"] = ident
    idx = np.arange(128)
    mk = np.zeros((2, 128, 128), np.float32)
    mk[0] = (idx[None, :] < idx[:, None])
    mk[1] = (idx[None, :] > idx[:, None])
    m["c_mask"] = mk
    sel = np.zeros((64, 64, 128), np.float32)
    for k in range(64):
        sel[k, k, :] = 1.0
    m["c_sel"] = sel
    rm = np.ones((2, 64, cfg.T), np.float32)
    rm[0, :, 0::128] = 0.0
    rm[1, :, 127::128] = 0.0
    m["c_rmask"] = rm
    return m


GELU_C = 1.5957691216057308
_uid = [0]


class Scope:
    def __init__(self, prog):
        self.p = prog
        self.st = ExitStack()

    def __enter__(self):
        return self

    def sb(self, name, shape, dt):
        _uid[0] += 1
        return self.st.enter_context(self.p.nc.sbuf_tensor("%s_%d" % (name, _uid[0]), list(shape), dt))

    def __exit__(self, *a):
        if a[0] is None:
            self.p.sy.barrier()
            self.p.sy.emit()
        self.st.close()
        return False


class Prog:
    def __init__(self, cfg, layers, debug=(), final=True, stop_after=None, stages=None, emit_xr=None):
        self.cfg = cfg
        self.layers = list(layers)
        self.debug = set(debug)
        self.final = final
        self.stop_after = stop_after
        self.stages = stages
        self.emit_xr = (not final) if emit_xr is None else emit_xr

    def build(self):
        nc = bass.Bass("TRN2", target_bir_lowering=False)
        self.nc = nc
        with ExitStack() as st:
            self.st = st
            self.sy = Sync(nc, st)
            self.declare_io()
            with Scope(self) as top:
                self.setup(top)
                for li, l in enumerate(self.layers):
                    self.layer(l, first=(li == 0))
                    self.sy.barrier()
                    self.sy.emit()
                if self.final:
                    self.final_norm()
                if self.emit_xr:
                    self.copy_xr_out()
        return nc

    def din(self, name, shape, dt=F32):
        return self.nc.dram_tensor(name, list(shape), dt, kind="ExternalInput").ap()

    def dscr(self, name, shape, dt, out=False):
        dbg = name in self.debug
        kind = "ExternalOutput" if (out or dbg) else "Internal"
        return self.nc.dram_tensor(name if out else ("d_" + name), list(shape), dt, kind=kind).ap()

    def declare_io(self):
        cfg = self.cfg
        L = cfg.DEPTH
        self.i = {}
        self.i["xr0"] = self.din("xr0", [cfg.D, cfg.T])
        self.i["cvec"] = self.din("cvec", [128, cfg.KD, 2])
        self.i["vecs"] = self.din("vecs", [L, 128, cfg.NV])
        self.i["ada_w"] = self.din("ada_w", [L, cfg.D, 6 * cfg.D])
        self.i["w_in"] = self.din("w_in", [L, cfg.D, cfg.IN_DIM])
        self.i["lru_wa"] = self.din("lru_wa", [L, 2, cfg.LW // 64, 64, 64])
        self.i["lru_wx"] = self.din("lru_wx", [L, 2, cfg.LW // 64, 64, 64])
        self.i["lru_proj"] = self.din("lru_proj", [L, cfg.LW, cfg.D])
        self.i["ssd_proj"] = self.din("ssd_proj", [L, cfg.SI, cfg.D])
        self.i["w_out"] = self.din("w_out", [L, cfg.D, cfg.D])
        self.i["ffn_w_up"] = self.din("ffn_w_up", [L, cfg.D, 2 * cfg.FF])
        self.i["ffn_w_down"] = self.din("ffn_w_down", [L, cfg.FF, cfg.D])
        self.i["c_ident"] = self.din("c_ident", [128, 128])
        self.i["c_mask"] = self.din("c_mask", [2, 128, 128])
        self.i["c_sel"] = self.din("c_sel", [64, 64, 128])
        self.i["c_rmask"] = self.din("c_rmask", [2, 64, cfg.T])
        if self.final:
            self.outf = self.dscr("out" if not self.emit_xr else "outf", [cfg.D, cfg.SEQ], F32, out=True)
        if self.emit_xr:
            self.out = self.dscr("out" if not self.final else "outx", [cfg.D, cfg.T], F32, out=True)
        self.XR = self.dscr("XR", [cfg.D, cfg.T], F32)
        self.YAG = self.dscr("YAG", [cfg.LW, cfg.T], BF16)
        self.Zs = self.dscr("Zs", [cfg.SI, cfg.T], BF16)
        self.GT = self.dscr("GT", [2 * cfg.D, cfg.T], BF16)
        self.XBC = self.dscr("XBC", [cfg.CD, cfg.T], BF16)
        self.Y = self.dscr("Y", [cfg.SI, cfg.T], F32)
        self.GN = self.dscr("GN", [cfg.SI, cfg.T], BF16)
        self.ACTV = self.dscr("ACTV", [cfg.FF, cfg.T], BF16)
        self.DTS = self.dscr("DTS", [3, 64, cfg.T], F32)
        self.TOTS = self.dscr("TOTS", [64, cfg.NCH], F32)
        self.rDTS = R("DTS")
        self.rXR, self.rYAG, self.rZs, self.rGT, self.rXBC, self.rY, self.rGN, self.rACTV = [R(n) for n in (
            "XR", "YAG", "Zs", "GT", "XBC", "Y", "GN", "ACTV")]

    def setup(self, top):
        cfg, sy = self.cfg, self.sy
        self.ident_f = top.sb("ident_f", [128, 128], F32)
        self.ident_b = top.sb("ident_b", [128, 128], BF16)
        self.ones_f = top.sb("ones_f", [128, 128], F32)
        self.ones_b = top.sb("ones_b", [128, 128], BF16)
        self.negi = top.sb("negi", [128, 128], BF16)
        self.maskf = top.sb("maskf", [128, 2, 128], F32)
        self.maskb = top.sb("maskb", [128, 2, 128], BF16)
        self.maskn = top.sb("maskn", [128, 2, 128], F32)
        self.rC = R("consts")
        self.rCm = R("constsm")
        rC = self.rC
        sy.dma(self.ident_f[:], self.i["c_ident"], rC, writes=[rC])
        sy.dma(self.maskf[:], self.i["c_mask"].rearrange("d s l -> s d l"), self.rCm, writes=[self.rCm])
        sy.op("dve", lambda e: e.tensor_copy(out=self.ident_b[:], in_=self.ident_f[:]), reads=[rC], writes=[rC])
        sy.op("dve", lambda e: e.tensor_copy(out=self.maskb[:], in_=self.maskf[:]), reads=[self.rCm], writes=[self.rCm])
        sy.op("dve", lambda e: e.tensor_scalar(out=self.maskn[:], in0=self.maskf[:], scalar1=-30000.0, scalar2=None, op0=ALU.mult),
              reads=[self.rCm], writes=[self.rCm])
        sy.op("dve", lambda e: e.tensor_scalar(out=self.maskf[:], in0=self.maskf[:], scalar1=-1.0, scalar2=1.0, op0=ALU.mult, op1=ALU.add),
              reads=[self.rCm], writes=[self.rCm])
        sy.op("dve", lambda e: e.tensor_scalar(out=self.negi[:], in0=self.ident_f[:], scalar1=-30000.0, scalar2=None, op0=ALU.mult),
              reads=[rC], writes=[rC])
        sy.op("dve", lambda e: e.memset(self.ones_f[:], 1.0), writes=[rC])
        sy.op("dve", lambda e: e.memset(self.ones_b[:], 1.0), writes=[rC])
        self.sc = top.sb("sc", [128, cfg.KD, 2], F32)
        self.rSC = R("sc")
        sy.dma(self.sc[:], self.i["cvec"], self.rSC, writes=[self.rSC])
        sy.op("act", lambda e: e.activation(out=self.sc[:], in_=self.sc[:], func=AF.Silu), reads=[self.rSC], writes=[self.rSC])
        self.vec = top.sb("vec", [128, cfg.NV], F32)
        self.rVEC = R("vec")
        self.mod = top.sb("mod", [128, 6 * cfg.KD, 2], F32)
        self.rMOD = R("mod")
        self.g1 = top.sb("g1", [128, cfg.KD, 2], F32)
        self.g2 = top.sb("g2", [128, cfg.KD, 2], F32)
        self.c8 = top.sb("c8", [128, 2 * cfg.LWC], F32)
        self.aneg = top.sb("aneg", [64, 1], F32)
        self.PS = [top.st.enter_context(self.nc.psum_tensor("ps%d" % i, [128, 512], F32)) for i in range(7)]
        self.rPS = [R("ps%d" % i) for i in range(7)]
        self.PSB = top.st.enter_context(self.nc.psum_tensor("psb", [128, 1024], BF16))
        self.rPSB = R("psb")
        self.ps_i = 0
        self.wst = [top.sb("wst%d" % i, [128, 8, 256], F32) for i in range(2)]
        self.rWST = [R("wst%d" % i) for i in range(2)]
        self.wst_i = 0
        sy.barrier()
        sy.emit()

    def vcol(self, name, j=0, n=1, rows=128):
        o = self.cfg.voff[name] + j
        return self.vec[:rows, o:o + n]

    def nextps(self, k=4):
        i = self.ps_i % k
        self.ps_i += 1
        return self.PS[i], self.rPS[i]

    def load_w(self, dst, rdst, src2d, K, C, rows_last=128):
        sy = self.sy
        for k0 in range(0, K, 8):
            kn = min(8, K - k0)
            for c0 in range(0, C, 256):
                cn = min(256, C - c0)
                bi = self.wst_i
                self.wst_i ^= 1
                ws, rw = self.wst[bi], self.rWST[bi]
                src = src2d[k0 * 128:(k0 + kn) * 128, c0:c0 + cn].rearrange("(kc p) c -> p kc c", p=128)
                sy.dma(ws[:, :kn, :cn], src, rw, writes=[rw])
                sy.op("pool", lambda e, ws=ws, kn=kn, cn=cn, k0=k0, c0=c0: e.tensor_copy(
                    out=dst[:, k0:k0 + kn, c0:c0 + cn], in_=ws[:, :kn, :cn]), reads=[rw], writes=[rdst])

    def mm_group(self, ps_ap, pairs, reads, rps):
        def fn(e):
            ins = None
            n = len(pairs)
            for i, (a, b) in enumerate(pairs):
                ins = e.matmul(ps_ap, a, b, start=(i == 0), stop=(i == n - 1))
            return ins
        self.sy.op("pe", fn, reads=reads, writes=[rps])

    def adaln(self, l):
        cfg, sy = self.cfg, self.sy
        KD = cfg.KD
        with Scope(self) as ss:
            adaw = [ss.sb("adaw", [128, KD, 512], F32) for i in range(2)]
            rA = [R("adaw%d" % i) for i in range(2)]
            sy.dma(self.vec[:], self.i["vecs"][l], self.rVEC, writes=[self.rVEC])
            psb, rps = self.PS[6], self.rPS[6]
            ncol = 6 * cfg.D
            bi = 0
            for g0 in range(0, ncol, 512):
                gw = min(512, ncol - g0)
                wt, rw = adaw[bi], rA[bi]
                bi ^= 1
                src = self.i["ada_w"][l][:, g0:g0 + gw].rearrange("(kc p) c -> p kc c", p=128)
                sy.dma(wt[:, :, :gw], src, rw, writes=[rw])
                for oc in range(gw // 128):
                    o = g0 // 128 + oc
                    self.mm_group(psb[:, 2 * o:2 * o + 2],
                                  [(wt[:, kc, oc * 128:(oc + 1) * 128], self.sc[:, kc, :]) for kc in range(KD)],
                                  [rw, self.rSC], rps)
            ab = self.vcol("adab", 0, 6 * KD)
            sy.op("dve", lambda e: e.tensor_tensor(out=self.mod[:], in0=psb[:, 0:12 * KD].rearrange("p (o s) -> p o s", s=2),
                                                   in1=ab.unsqueeze(2).to_broadcast([128, 6 * KD, 2]), op=ALU.add),
                  reads=[rps, self.rVEC], writes=[self.rMOD])
            for g, wname, m in ((self.g1, "nmw", 1), (self.g2, "nfw", 4)):
                def fn(e, g=g, wname=wname, m=m):
                    return e.scalar_tensor_tensor(out=g[:], in0=self.mod[:, m * KD:(m + 1) * KD, :], scalar=1.0,
                                                  in1=self.vcol(wname, 0, KD).unsqueeze(2).to_broadcast([128, KD, 2]),
                                                  op0=ALU.add, op1=ALU.mult)
                sy.op("dve", fn, reads=[self.rMOD, self.rVEC], writes=[self.rMOD])
            n8 = 2 * cfg.LWC
            sy.op("act", lambda e: e.activation(out=self.c8[:], in_=self.vcol("llam", 0, n8), func=AF.Exp, scale=-1.0),
                  reads=[self.rVEC], writes=[self.rMOD])
            sy.op("act", lambda e: e.activation(out=self.c8[:], in_=self.c8[:], func=AF.Ln, bias=1.0), reads=[self.rMOD], writes=[self.rMOD])
            sy.op("dve", lambda e: e.tensor_scalar(out=self.c8[:], in0=self.c8[:], scalar1=-8.0, scalar2=None, op0=ALU.mult),
                  reads=[self.rMOD], writes=[self.rMOD])
            sy.op("act", lambda e: e.activation(out=self.aneg[:], in_=self.vcol("alog", 0, 1, rows=64), func=AF.Exp),
                  reads=[self.rVEC], writes=[self.rMOD])
            sy.op("dve", lambda e: e.tensor_scalar(out=self.aneg[:], in0=self.aneg[:], scalar1=-1.0, scalar2=None, op0=ALU.mult),
                  reads=[self.rMOD], writes=[self.rMOD])

    def norm_mod(self, first, g, shift_m, H, rH):
        cfg, sy = self.cfg, self.sy
        KD = cfg.KD
        with Scope(self) as ss:
            xts = [ss.sb("xt", [128, KD, cfg.TS2], F32) for i in range(2)]
            rxs = [R("xt%d" % i) for i in range(2)]
            sq = ss.sb("sq", [128, KD, cfg.TS2], F32)
            rsq = R("sq")
            rstd = ss.sb("rstd", [128, cfg.TS2], F32)
            rrs = R("rstd")
            for ti, (t0, n, s) in enumerate(cfg.tiles2):
                xt, rx = xts[ti % 2], rxs[ti % 2]
                src = (self.i["xr0"] if first else self.XR)[:, t0:t0 + n].rearrange("(kc p) t -> p kc t", p=128)
                sy.dma(xt[:, :, :n], src, rx, reads=([] if first else [self.rXR]), writes=[rx])
                self.rstd_of(xt, rx, n, sq, rsq, rstd, rrs, cfg.D)
                sy.op("dve", lambda e, xt=xt, n=n: e.tensor_tensor(out=xt[:, :, :n], in0=xt[:, :, :n],
                                                                   in1=rstd[:, :n].unsqueeze(1).to_broadcast([128, KD, n]), op=ALU.mult),
                      reads=[rx, rrs], writes=[rx])
                for kc in range(KD):
                    sy.op("dve", lambda e, xt=xt, n=n, kc=kc, t0=t0, s=s: e.tensor_scalar(
                        out=H[:, kc, t0:t0 + n], in0=xt[:, kc, :n], scalar1=g[:, kc, s:s + 1],
                        scalar2=self.mod[:, shift_m * KD + kc, s:s + 1], op0=ALU.mult, op1=ALU.add),
                        reads=[rx, self.rMOD], writes=[rH])

    def rstd_of(self, xt, rx, n, sq, rsq, rstd, rrs, nchan):
        cfg, sy = self.cfg, self.sy
        K = nchan // 128
        sy.op("act", lambda e: e.activation(out=sq[:, :K, :n], in_=xt[:, :K, :n], func=AF.Square), reads=[rx], writes=[rsq])
        psb, rps = self.PS[5], self.rPS[5]
        self.mm_group(psb[:, :n], [(self.ones_f[:], sq[:, kc, :n]) for kc in range(K)], [rsq, self.rC], rps)
        sy.op("act", lambda e: e.activation(out=rstd[:, :n], in_=psb[:, :n], func=AF.Sqrt, scale=1.0 / nchan, bias=1e-6),
              reads=[rps], writes=[rrs])
        sy.op("dve", lambda e: e.reciprocal(out=rstd[:, :n], in_=rstd[:, :n]), reads=[rrs], writes=[rrs])

    def proj_tile(self, W, rW, H, rH, t0, n, M=128):
        KD = self.cfg.KD
        ps, rps = self.nextps()
        self.mm_group(ps[:M, :n], [(W[:, kc, :M], H[:, kc, t0:t0 + n]) for kc in range(KD)], [rW, rH], rps)
        return ps, rps

    def seqpos(self, t):
        return t + 2 if t < self.cfg.CTX else t + 5

    def make_diag(self, DG, rDG, name, ntap, nch, j):
        for k in range(ntap):
            self.sy.op("dve", lambda e, k=k: e.tensor_scalar(out=DG[:, k, :], in0=self.ident_f[:], scalar1=self.vcol(name, k * nch + j, 1),
                                                             scalar2=None, op0=ALU.mult), reads=[self.rC, self.rVEC], writes=[rDG])

    def conv1d_tile(self, DG, rDG, SEQB, rSEQB, t0, n):
        ps, rps = self.nextps()
        p0 = self.seqpos(t0)
        self.mm_group(ps[:, :n], [(DG[:, k, :], SEQB[:, p0 + k - 2:p0 + k - 2 + n]) for k in range(4)], [rDG, rSEQB], rps)
        return ps, rps

    def gelu_from(self, ss_tmp, rtmp, src_ap, rsrc, out_ap, rout, n, bias=None, mul=None, rmul=None):
        sy = self.sy
        x, x2, u = ss_tmp[:, 0, :n], ss_tmp[:, 1, :n], ss_tmp[:, 2, :n]
        if bias is not None:
            sy.op("act", lambda e: e.activation(out=x, in_=src_ap, func=AF.Identity, bias=bias), reads=[rsrc, self.rVEC], writes=[rtmp])
        else:
            sy.op("act", lambda e: e.activation(out=x, in_=src_ap, func=AF.Identity), reads=[rsrc], writes=[rtmp])
        sy.op("pool", lambda e: e.tensor_tensor(out=x2, in0=x, in1=x, op=ALU.mult), reads=[rtmp], writes=[rtmp])
        sy.op("dve", lambda e: e.tensor_scalar(out=x2, in0=x2, scalar1=0.044715, scalar2=1.0, op0=ALU.mult, op1=ALU.add), reads=[rtmp], writes=[rtmp])
        sy.op("pool", lambda e: e.tensor_tensor(out=u, in0=x2, in1=x, op=ALU.mult), reads=[rtmp], writes=[rtmp])
        sy.op("act", lambda e: e.activation(out=u, in_=u, func=AF.Sigmoid, scale=GELU_C), reads=[rtmp], writes=[rtmp])
        if mul is None:
            sy.op("dve", lambda e: e.tensor_tensor(out=out_ap, in0=u, in1=x, op=ALU.mult), reads=[rtmp], writes=[rout])
        else:
            sy.op("dve", lambda e: e.tensor_tensor(out=u, in0=u, in1=x, op=ALU.mult), reads=[rtmp], writes=[rtmp])
            sy.op("dve", lambda e: e.tensor_tensor(out=out_ap, in0=u, in1=mul, op=ALU.mult), reads=[rtmp, rmul], writes=[rout])

    def lru(self, l, H, rH):
        cfg, sy = self.cfg, self.sy
        KD, T, CTX = cfg.KD, cfg.T, cfg.CTX
        with Scope(self) as ss:
            W = ss.sb("wl", [128, KD, 256], BF16)
            rW = R("wl")
            SEQB = ss.sb("seqb", [128, T + 6], BF16)
            rSEQB = R("seqb")
            Ub = ss.sb("ub", [128, T], BF16)
            rU = R("u")
            A = ss.sb("a", [128, T], F32)
            B0 = ss.sb("b0", [128, T], F32)
            B1 = ss.sb("b1", [128, T], F32)
            rA, rB0, rB1 = R("a"), R("b0"), R("b1")
            YG = ss.sb("yg", [128, T], BF16)
            rYG = R("yg")
            DG = ss.sb("dg", [128, 4, 128], BF16)
            rDG = R("dg")
            GST = ss.sb("gst", [128, 4, 128], F32)
            rGST = R("gst")
            GW = ss.sb("gw", [128, 4, 128], BF16)
            rGW = R("gw")
            TMPs = [ss.sb("tmp", [128, 3, cfg.TS], F32) for i in range(2)]
            rTMPs = [R("tmp%d" % i) for i in range(2)]
            RTs = [ss.sb("rt", [128, 4, cfg.TS], F32) for i in range(2)]
            rRTs = [R("rt%d" % i) for i in range(2)]
            tcount = [0]
            sy.op("dve", lambda e: e.memset(SEQB[:], 0.0), writes=[rSEQB])
            for j in range(cfg.LWC):
                wsrc = self.i["w_in"][l]
                self.load_w(W[:, :, 0:128], rW, wsrc[:, cfg.o_lx + j * 128: cfg.o_lx + (j + 1) * 128], KD, 128)
                self.load_w(W[:, :, 128:256], rW, wsrc[:, cfg.o_lg + j * 128: cfg.o_lg + (j + 1) * 128], KD, 128)
                self.make_diag(DG, rDG, "lcw", 4, cfg.LWC, j)
                sy.op("pool", lambda e: e.memset(GST[:], 0.0), writes=[rGST])
                for d in range(2):
                    for wi, wname in enumerate(("lru_wa", "lru_wx")):
                        for hh in range(2):
                            sy.dma(GST[hh * 64:(hh + 1) * 64, d * 2 + wi, hh * 64:(hh + 1) * 64],
                                   self.i[wname][l, d, 2 * j + hh], rGST, writes=[rGST])
                sy.op("pool", lambda e: e.tensor_copy(out=GW[:], in_=GST[:]), reads=[rGST], writes=[rGW])
                for (t0, n, s) in cfg.tiles:
                    ps, rps = self.proj_tile(W[:, :, 0:128], rW, H, rH, t0, n)
                    p0 = self.seqpos(t0)
                    sy.op("act", lambda e, ps=ps, p0=p0, n=n: e.activation(out=SEQB[:, p0:p0 + n], in_=ps[:, :n], func=AF.Copy),
                          reads=[rps], writes=[rSEQB])
                for (t0, n, s) in cfg.tiles:
                    ps, rps = self.conv1d_tile(DG, rDG, SEQB, rSEQB, t0, n)
                    lcb = self.vcol("lcb", j, 1)
                    sy.op("act", lambda e, ps=ps, t0=t0, n=n, lcb=lcb: e.activation(out=Ub[:, t0:t0 + n], in_=ps[:, :n], func=AF.Identity,
                                                                                    bias=lcb), reads=[rps, self.rVEC], writes=[rU])
                for d in range(2):
                    Bd, rBd = (B0, rB0) if d == 0 else (B1, rB1)
                    for (t0, n, s) in cfg.tiles:
                        tcount[0] += 1
                        RT, rRT = RTs[tcount[0] % 2], rRTs[tcount[0] % 2]
                        psr, rpsr = self.nextps()
                        self.mm_group(psr[:, :n], [(GW[:, 2 * d, :], Ub[:, t0:t0 + n])], [rGW, rU], rpsr)
                        psi, rpsi = self.nextps()
                        self.mm_group(psi[:, :n], [(GW[:, 2 * d + 1, :], Ub[:, t0:t0 + n])], [rGW, rU], rpsi)
                        r_, i_, a2, sq = RT[:, 0, :n], RT[:, 1, :n], RT[:, 2, :n], RT[:, 3, :n]
                        lba = self.vcol("lba", d * cfg.LWC + j, 1)
                        lbx = self.vcol("lbx", d * cfg.LWC + j, 1)
                        sy.op("act", lambda e, psr=psr, n=n, r_=r_, lba=lba: e.activation(out=r_, in_=psr[:, :n], func=AF.Sigmoid,
                                                                                          bias=lba),
                              reads=[rpsr, self.rVEC], writes=[rRT])
                        sy.op("act", lambda e, psi=psi, n=n, i_=i_, lbx=lbx: e.activation(out=i_, in_=psi[:, :n], func=AF.Sigmoid,
                                                                                          bias=lbx),
                              reads=[rpsi, self.rVEC], writes=[rRT])
                        c8 = self.c8[:, d * cfg.LWC + j: d * cfg.LWC + j + 1]
                        sy.op("act", lambda e, t0=t0, n=n, r_=r_, c8=c8: e.activation(out=A[:, t0:t0 + n], in_=r_, func=AF.Exp, scale=c8),
                              reads=[rRT, self.rMOD], writes=[rA])
                        sy.op("pool", lambda e, t0=t0, n=n, a2=a2: e.tensor_tensor(out=a2, in0=A[:, t0:t0 + n], in1=A[:, t0:t0 + n], op=ALU.mult),
                              reads=[rA], writes=[rRT])
                        sy.op("dve", lambda e, a2=a2: e.tensor_scalar(out=a2, in0=a2, scalar1=-1.0, scalar2=1.0, op0=ALU.mult, op1=ALU.add),
                              reads=[rRT], writes=[rRT])
                        sy.op("act", lambda e, a2=a2, sq=sq: e.activation(out=sq, in_=a2, func=AF.Sqrt), reads=[rRT], writes=[rRT])
                        sy.op("pool", lambda e, t0=t0, n=n, i_=i_: e.tensor_tensor(out=i_, in0=i_, in1=Ub[:, t0:t0 + n], op=ALU.mult),
                              reads=[rRT, rU], writes=[rRT])
                        sy.op("dve", lambda e, t0=t0, n=n, i_=i_, sq=sq, Bd=Bd: e.tensor_tensor(out=Bd[:, t0:t0 + n], in0=sq, in1=i_, op=ALU.mult),
                              reads=[rRT], writes=[rBd])
                    if d == 0:
                        sy.op("dve", lambda e: e.tensor_tensor_scan(out=B0[:, :], data0=A[:, :], data1=B0[:, :], initial=0.0,
                                                                    op0=ALU.mult, op1=ALU.add), reads=[rA, rB0], writes=[rB0])
                    else:
                        sy.op("dve", lambda e: e.tensor_tensor_scan(out=B1[:, 0:CTX][:, ::-1], data0=A[:, 0:CTX][:, ::-1],
                                                                    data1=B1[:, 0:CTX][:, ::-1], initial=0.0, op0=ALU.mult, op1=ALU.add),
                              reads=[rA, rB1], writes=[rB1])
                        sy.op("dve", lambda e: e.tensor_tensor_scan(out=B1[:, CTX:T][:, ::-1], data0=A[:, CTX:T][:, ::-1],
                                                                    data1=B1[:, CTX:T][:, ::-1], initial=B1[:, 0:1], op0=ALU.mult, op1=ALU.add),
                              reads=[rA, rB1], writes=[rB1])
                sy.op("pool", lambda e: e.tensor_tensor(out=B0[:], in0=B0[:], in1=B1[:], op=ALU.add), reads=[rB0, rB1], writes=[rB0])
                for (t0, n, s) in cfg.tiles:
                    tcount[0] += 1
                    TMP, rTMP = TMPs[tcount[0] % 2], rTMPs[tcount[0] % 2]
                    ps, rps = self.proj_tile(W[:, :, 128:256], rW, H, rH, t0, n)
                    self.gelu_from(TMP, rTMP, ps[:, :n], rps, YG[:, t0:t0 + n], rYG, n, mul=B0[:, t0:t0 + n], rmul=rB0)
                sy.dma(self.YAG[j * 128:(j + 1) * 128, :], YG[:], rYG, reads=[rYG], writes=[self.rYAG])


    def ssd_prep(self, l, H, rH):
        cfg, sy = self.cfg, self.sy
        KD, T = cfg.KD, cfg.T
        wsrc = self.i["w_in"][l]
        with Scope(self) as ss:
            Ws = [ss.sb("wp", [128, KD, 128], BF16) for i in range(2)]
            rWs = [R("wp%d" % i) for i in range(2)]
            OB = [ss.sb("ob", [128, T], BF16) for i in range(2)]
            rOB = [R("ob%d" % i) for i in range(2)]
            SEQB = ss.sb("seqb", [128, T + 6], BF16)
            rSEQB = R("seqb")
            DG = ss.sb("dg", [128, 4, 128], BF16)
            rDG = R("dg")
            sy.op("dve", lambda e: e.memset(SEQB[:], 0.0), writes=[rSEQB])
            k = 0
            jobs = [("z", cfg.o_z, j, self.Zs, self.rZs, AF.Silu) for j in range(cfg.SIC)]
            jobs += [("gt", cfg.o_gt, j, self.GT, self.rGT, AF.Sigmoid) for j in range(2 * KD)]
            jobs += [("xbc", cfg.o_xbc, j, self.XBC, self.rXBC, AF.Silu) for j in range(cfg.CDC)]
            for (kind, o, j, dst, rdst, func) in jobs:
                W, rW = Ws[k % 2], rWs[k % 2]
                ob, rob = OB[k % 2], rOB[k % 2]
                k += 1
                self.load_w(W, rW, wsrc[:, o + j * 128:o + (j + 1) * 128], KD, 128)
                if kind != "xbc":
                    for (t0, n, s) in cfg.tiles:
                        ps, rps = self.proj_tile(W, rW, H, rH, t0, n)
                        sy.op("act", lambda e, ps=ps, t0=t0, n=n, ob=ob, func=func: e.activation(out=ob[:, t0:t0 + n], in_=ps[:, :n], func=func),
                              reads=[rps], writes=[rob])
                else:
                    self.make_diag(DG, rDG, "scw", 4, cfg.CDC, j)
                    for (t0, n, s) in cfg.tiles:
                        ps, rps = self.proj_tile(W, rW, H, rH, t0, n)
                        p0 = self.seqpos(t0)
                        sy.op("act", lambda e, ps=ps, p0=p0, n=n: e.activation(out=SEQB[:, p0:p0 + n], in_=ps[:, :n], func=AF.Copy),
                              reads=[rps], writes=[rSEQB])
                    scb = self.vcol("scb", j, 1)
                    for (t0, n, s) in cfg.tiles:
                        ps, rps = self.conv1d_tile(DG, rDG, SEQB, rSEQB, t0, n)
                        sy.op("act", lambda e, ps=ps, t0=t0, n=n, ob=ob, scb=scb: e.activation(out=ob[:, t0:t0 + n], in_=ps[:, :n], func=AF.Silu, bias=scb),
                              reads=[rps, self.rVEC], writes=[rob])
                sy.dma(dst[j * 128:(j + 1) * 128, :], ob[:], rob, reads=[rob], writes=[rdst])
        with Scope(self) as ss:
            CS = ss.sb("cs", [64, T], F32)
            V = ss.sb("v", [64, T], F32)
            WEXP = ss.sb("wexp", [64, T], F32)
            TOT = ss.sb("tot", [64, cfg.NCH], F32)
            rDT = R("dtp")
            WDT = ss.sb("wdt", [128, KD, 64], BF16)
            rWDT = R("wdt")
            X1 = ss.sb("x1", [64, T], F32)
            RM = WEXP
            rX1, rRM = R("x1"), rDT
            NH = cfg.NH
            sy.op("pool", lambda e: e.memset(WDT[:], 0.0), writes=[rWDT])
            self.load_w(WDT[:, :, 0:NH], rWDT, wsrc[:, cfg.o_dt:cfg.o_dt + NH], KD, NH)
            self.load_w(WDT[:, :, 32:32 + NH], rWDT, wsrc[:, cfg.o_dt + NH:cfg.o_dt + 2 * NH], KD, NH)
            sy.dma(RM[0:32, :], self.i["c_rmask"][0, 0:32, :], rRM, writes=[rRM])
            sy.dma(RM[32:64, :], self.i["c_rmask"][1, 32:64, :], rRM, writes=[rRM])
            dtb = self.vcol("dtb", 0, 1, rows=64)
            for (t0, n, s) in cfg.tiles:
                ps, rps = self.proj_tile(WDT, rWDT, H, rH, t0, n, M=64)
                sy.op("act", lambda e, ps=ps, t0=t0, n=n: e.activation(out=X1[:, t0:t0 + n], in_=ps[:64, :n], func=AF.Exp, bias=dtb),
                      reads=[rps, self.rVEC], writes=[rX1])
            X2, X3 = V, CS
            sy.op("act", lambda e: e.activation(out=X1[:], in_=X1[:], func=AF.Ln, bias=1.0), reads=[rX1], writes=[rX1])
            sy.op("act", lambda e: e.activation(out=X2[:], in_=X1[:], func=AF.Ln), reads=[rX1], writes=[rDT])
            sy.op("dve", lambda e: e.tensor_scalar(out=X3[:], in0=X1[:], scalar1=self.aneg[:, 0:1], scalar2=None, op0=ALU.mult),
                  reads=[rX1, self.rMOD], writes=[rDT])
            sy.op("dve", lambda e: e.tensor_tensor_scan(out=X3[0:32, :], data0=RM[0:32, :], data1=X3[0:32, :], initial=0.0,
                                                        op0=ALU.mult, op1=ALU.add), reads=[rRM, rDT], writes=[rDT])
            sy.op("dve", lambda e: e.tensor_tensor_scan(out=X3[32:64, :][:, ::-1], data0=RM[32:64, :][:, ::-1], data1=X3[32:64, :][:, ::-1],
                                                        initial=0.0, op0=ALU.mult, op1=ALU.add), reads=[rRM, rDT], writes=[rDT])
            sy.op("dve", lambda e: e.tensor_tensor(out=V[:], in0=X2[:], in1=CS[:], op=ALU.subtract), reads=[rDT], writes=[rDT])
            CS3 = CS[:].rearrange("p (c q) -> p c q", q=128)
            V3 = V[:].rearrange("p (c q) -> p c q", q=128)
            W3 = WEXP[:].rearrange("p (c q) -> p c q", q=128)
            sy.op("dve", lambda e: e.tensor_copy(out=TOT[0:32, :], in_=CS3[0:32, :, 127]), reads=[rDT], writes=[rDT])
            sy.op("dve", lambda e: e.tensor_copy(out=TOT[32:64, :], in_=CS3[32:64, :, 0]), reads=[rDT], writes=[rDT])
            sy.op("dve", lambda e: e.tensor_tensor(out=W3, in0=V3, in1=TOT[:].unsqueeze(2).to_broadcast([64, cfg.NCH, 128]), op=ALU.add),
                  reads=[rDT], writes=[rDT])
            sy.op("act", lambda e: e.activation(out=WEXP[:], in_=WEXP[:], func=AF.Exp), reads=[rDT], writes=[rDT])
            for i_, t_ in enumerate((CS, V, WEXP)):
                sy.dma(self.DTS[i_], t_[:], rDT, reads=[rDT], writes=[self.rDTS])
            sy.op("act", lambda e: e.activation(out=TOT[:], in_=TOT[:], func=AF.Exp), reads=[rDT], writes=[rDT])
            sy.dma(self.TOTS, TOT[:], rDT, reads=[rDT], writes=[self.rDTS])

    def ssd_sweep(self, d):
        cfg, sy = self.cfg, self.sy
        HPG, NPR, NG, SIC, NCH = cfg.HPG, cfg.HPG // 2, cfg.NG, cfg.SIC, cfg.NCH
        nctx = cfg.CTX // 128
        order = list(range(NCH)) if d == 0 else (list(range(nctx - 1, -1, -1)) + list(range(NCH - 1, nctx - 1, -1)))
        with Scope(self) as ss:
            V = ss.sb("v", [64, cfg.T], F32)
            WEXP = ss.sb("wexp", [64, cfg.T], F32)
            ETB = ss.sb("etb", [128, 64, NCH], F32)
            rDT = R("dts")
            sy.dma(V[:], self.DTS[1], rDT, reads=[self.rDTS], writes=[rDT])
            sy.dma(WEXP[:], self.DTS[2], rDT, reads=[self.rDTS], writes=[rDT])
            sy.dma(ETB[:], self.TOTS.partition_broadcast(128), rDT, reads=[self.rDTS], writes=[rDT])
            BC = [ss.sb("bc", [128, 2, 128], BF16) for i in range(2)]
            XF = [ss.sb("xf", [128, NPR, 128], BF16) for i in range(2)]
            YL = [ss.sb("yl", [128, NPR, 128], F32) for i in range(2)]
            CSB = [ss.sb("csb", [128, HPG, 128], F32) for i in range(2)]
            rIN = [R("ssdin%d" % i) for i in range(2)]
            rYL = [R("yl%d" % i) for i in range(2)]
            rCSB = [R("csb%d" % i) for i in range(2)]
            XT2s = [ss.sb("xt2", [128, HPG, 128], BF16) for i in range(2)]
            BTs = [ss.sb("bt", [128, 128], BF16) for i in range(2)]
            WVTs = [ss.sb("wvt", [128, 64], F32) for i in range(2)]
            XWs = [ss.sb("xw", [128, HPG, 64], BF16) for i in range(2)]
            CBMs = [ss.sb("cbm", [128, 128], BF16) for i in range(2)]
            EXs = [ss.sb("ex", [128, HPG, 128], F32) for i in range(2)]
            MTs = [ss.sb("mt", [128, HPG, 128], BF16) for i in range(2)]
            ECs = [ss.sb("ec", [128, HPG, 128], BF16) for i in range(2)]
            MT2s = [ss.sb("mt2", [128, HPG, 128], BF16) for i in range(2)]
            CSCs = [ss.sb("csc", [128, HPG, 128], BF16) for i in range(2)]
            rXT2s, rBTs, rWTs, rXWs, rCBTs, rEXs, rMTs, rECs, rMT2s, rCSCs = [[R("%s%d" % (n, i)) for i in range(2)] for n in (
                "xt2", "bt", "wt", "xw", "cbt", "ex", "mt", "ec", "mt2", "csc")]
            YS = [ss.sb("ys", [128, NPR, 128], F32) for i in range(2)]
            rYS = [R("ys%d" % i) for i in range(2)]
            P32 = ss.sb("p32", [128, HPG, 64], F32)
            PB = ss.sb("pb", [128, HPG, 128], BF16)
            rP32, rPB = R("p32"), R("pb")
            PSy, rPSy = self.PS[4], self.rPS[4]
            PSs, rPSs = self.PS[5], self.rPS[5]
            PSm, rPSm = self.PS[6], self.rPS[6]
            PSw, rPSw = self.PS[3], self.rPS[3]
            PSB, rPSB = self.PSB, self.rPSB
            PSBv = PSB[:, 0:HPG * 64].rearrange("p (h q) -> p h q", q=64)
            mkn = self.maskn[:, d, :]
            mkv = self.maskf[:, d, :]
            for i_ in range(2):
                sy.op("dve", lambda e, i_=i_: e.memset(XT2s[i_][:], 0.0), writes=[rXT2s[i_]])
            def stageA(g, c, bi):
                hd0 = 32 * d + g * HPG
                tok = c * 128
                bc, xf, yl, rin, ryl, ys, rys, csb, rcsb = BC[bi], XF[bi], YL[bi], rIN[bi], rYL[bi], YS[bi], rYS[bi], CSB[bi], rCSB[bi]
                XT2, BT, WVT, XW, CBM, EX, MT, EC, MT2, CSC = XT2s[bi], BTs[bi], WVTs[bi], XWs[bi], CBMs[bi], EXs[bi], MTs[bi], ECs[bi], MT2s[bi], CSCs[bi]
                rXT2, rBT, rWT, rXW, rCBT, rEX, rMT, rEC, rMT2, rCSC = (rXT2s[bi], rBTs[bi], rWTs[bi], rXWs[bi], rCBTs[bi], rEXs[bi], rMTs[bi],
                                                                       rECs[bi], rMT2s[bi], rCSCs[bi])
                for w_, ch in ((0, SIC + g), (1, SIC + NG + g)):
                    sy.dma(bc[:, w_, :], self.XBC[ch * 128:(ch + 1) * 128, tok:tok + 128], rin, reads=[self.rXBC], writes=[rin])
                sy.dma(xf[:], self.XBC[g * NPR * 128:(g + 1) * NPR * 128, tok:tok + 128].rearrange("(q p) t -> p q t", p=128),
                       rin, reads=[self.rXBC], writes=[rin])
                sy.dma(csb[:], self.DTS[0][hd0:hd0 + HPG, tok:tok + 128].partition_broadcast(128), rcsb, reads=[self.rDTS], writes=[rcsb])
                ydst = self.Y[g * NPR * 128:(g + 1) * NPR * 128, tok:tok + 128].rearrange("(q p) t -> p q t", p=128)
                if d == 1:
                    sy.dma(yl[:], ydst, ryl, reads=[self.rY], writes=[ryl])
                def tfn(e, xf=xf, bc=bc, XT2=XT2, BT=BT, WVT=WVT, XW=XW, CBM=CBM, EX=EX, MT=MT, EC=EC, MT2=MT2, CSC=CSC):
                    ins = None
                    for q in range(NPR):
                        ins = e.transpose(PSB[:, q * 128:(q + 1) * 128], xf[:, q, :], self.ident_b[:])
                    ins = e.transpose(PSB[:, NPR * 128:(NPR + 1) * 128], bc[:, 0, :], self.ident_b[:])
                    return ins
                sy.op("pe", tfn, reads=[rin, self.rC], writes=[rPSB])
                sy.op("dve", lambda e, XT2=XT2, BT=BT, WVT=WVT, XW=XW, CBM=CBM, EX=EX, MT=MT, EC=EC, MT2=MT2, CSC=CSC: e.tensor_copy(out=XT2[:, 0::2, 0:64], in_=PSBv[:, 0::2, :]), reads=[rPSB], writes=[rXT2])
                sy.op("dve", lambda e, XT2=XT2, BT=BT, WVT=WVT, XW=XW, CBM=CBM, EX=EX, MT=MT, EC=EC, MT2=MT2, CSC=CSC: e.tensor_copy(out=XT2[:, 1::2, 64:128], in_=PSBv[:, 1::2, :]), reads=[rPSB], writes=[rXT2])
                sy.op("dve", lambda e, XT2=XT2, BT=BT, WVT=WVT, XW=XW, CBM=CBM, EX=EX, MT=MT, EC=EC, MT2=MT2, CSC=CSC: e.tensor_copy(out=BT[:], in_=PSB[:, NPR * 128:(NPR + 1) * 128]), reads=[rPSB], writes=[rBT])
                def wfn(e, tok=tok, XT2=XT2, BT=BT, WVT=WVT, XW=XW, CBM=CBM, EX=EX, MT=MT, EC=EC, MT2=MT2, CSC=CSC):
                    idn = self.ident_f[32 * d:32 * d + 32, 32 * d:32 * d + 32]
                    e.transpose(PSw[:, 0:32], WEXP[32 * d:32 * d + 32, tok:tok + 128], idn)
                    return e.transpose(PSw[:, 32:64], V[32 * d:32 * d + 32, tok:tok + 128], idn)
                sy.op("pe", wfn, reads=[rDT, self.rC], writes=[rPSw])
                sy.op("act", lambda e, XT2=XT2, BT=BT, WVT=WVT, XW=XW, CBM=CBM, EX=EX, MT=MT, EC=EC, MT2=MT2, CSC=CSC: e.activation(out=WVT[:], in_=PSw[:, 0:64], func=AF.Copy), reads=[rPSw], writes=[rWT])
                sy.op("dve", lambda e, g=g, XT2=XT2, BT=BT, WVT=WVT, XW=XW, CBM=CBM, EX=EX, MT=MT, EC=EC, MT2=MT2, CSC=CSC: e.tensor_tensor(out=XW[:], in0=PSBv, in1=WVT[:, g * HPG:(g + 1) * HPG].unsqueeze(2).to_broadcast([128, HPG, 64]),
                                                            op=ALU.mult), reads=[rPSB, rWT], writes=[rXW])
                self.mm_group(PSm[:, 0:128], [(bc[:, 0, :], bc[:, 1, :])], [rin], rPSm)
                sy.op("dve", lambda e, XT2=XT2, BT=BT, WVT=WVT, XW=XW, CBM=CBM, EX=EX, MT=MT, EC=EC, MT2=MT2, CSC=CSC: e.tensor_tensor(out=CBM[:], in0=PSm[:, 0:128], in1=mkv, op=ALU.mult), reads=[rPSm, self.rCm], writes=[rCBT])
                for hh in range(HPG):
                    vcolap = WVT[:, 32 + g * HPG + hh:32 + g * HPG + hh + 1]
                    sy.op("dve", lambda e, hh=hh, csb=csb, vcolap=vcolap, XT2=XT2, BT=BT, WVT=WVT, XW=XW, CBM=CBM, EX=EX, MT=MT, EC=EC, MT2=MT2, CSC=CSC: e.scalar_tensor_tensor(
                        out=EX[:, hh, :], in0=csb[:, hh, :], scalar=vcolap, in1=mkn, op0=ALU.add, op1=ALU.add),
                        reads=[rcsb, rWT, self.rCm], writes=[rEX])
                sy.op("act", lambda e, XT2=XT2, BT=BT, WVT=WVT, XW=XW, CBM=CBM, EX=EX, MT=MT, EC=EC, MT2=MT2, CSC=CSC: e.activation(out=MT[:], in_=EX[:], func=AF.Exp), reads=[rEX], writes=[rMT])
                sy.op("act", lambda e, csb=csb, XT2=XT2, BT=BT, WVT=WVT, XW=XW, CBM=CBM, EX=EX, MT=MT, EC=EC, MT2=MT2, CSC=CSC: e.activation(out=EC[:], in_=csb[:], func=AF.Exp), reads=[rcsb], writes=[rEC])
                sy.op("dve", lambda e, XT2=XT2, BT=BT, WVT=WVT, XW=XW, CBM=CBM, EX=EX, MT=MT, EC=EC, MT2=MT2, CSC=CSC: e.tensor_tensor(out=MT2[:], in0=MT[:], in1=CBM[:].unsqueeze(1).to_broadcast([128, HPG, 128]), op=ALU.mult),
                      reads=[rMT, rCBT], writes=[rMT2])
                sy.op("dve", lambda e, bc=bc, XT2=XT2, BT=BT, WVT=WVT, XW=XW, CBM=CBM, EX=EX, MT=MT, EC=EC, MT2=MT2, CSC=CSC: e.tensor_tensor(out=CSC[:], in0=EC[:], in1=bc[:, 1, :].unsqueeze(1).to_broadcast([128, HPG, 128]), op=ALU.mult),
                      reads=[rEC, rin], writes=[rCSC])

            def stageB(g, c, bi, first):
                hd0 = 32 * d + g * HPG
                tok = c * 128
                bc, xf, yl, rin, ryl, ys, rys, csb, rcsb = BC[bi], XF[bi], YL[bi], rIN[bi], rYL[bi], YS[bi], rYS[bi], CSB[bi], rCSB[bi]
                XT2, BT, WVT, XW, CBM, EX, MT, EC, MT2, CSC = XT2s[bi], BTs[bi], WVTs[bi], XWs[bi], CBMs[bi], EXs[bi], MTs[bi], ECs[bi], MT2s[bi], CSCs[bi]
                rXT2, rBT, rWT, rXW, rCBT, rEX, rMT, rEC, rMT2, rCSC = (rXT2s[bi], rBTs[bi], rWTs[bi], rXWs[bi], rCBTs[bi], rEXs[bi], rMTs[bi],
                                                                       rECs[bi], rMT2s[bi], rCSCs[bi])
                ydst = self.Y[g * NPR * 128:(g + 1) * NPR * 128, tok:tok + 128].rearrange("(q p) t -> p q t", p=128)
                if first:
                    sy.op("dve", lambda e: e.memset(P32[:], 0.0), writes=[rP32])
                    sy.op("dve", lambda e: e.memset(PB[:], 0.0), writes=[rPB])
                def yfn(e, XT2=XT2, BT=BT, WVT=WVT, XW=XW, CBM=CBM, EX=EX, MT=MT, EC=EC, MT2=MT2, CSC=CSC):
                    ins = None
                    for q in range(NPR):
                        o = PSy[:, q * 128:(q + 1) * 128]
                        e.matmul(o, XT2[:, 2 * q, :], MT2[:, 2 * q, :], start=True, stop=False)
                        e.matmul(o, XT2[:, 2 * q + 1, :], MT2[:, 2 * q + 1, :], start=False, stop=False)
                        e.matmul(o, PB[:, 2 * q, :], CSC[:, 2 * q, :], start=False, stop=False)
                        ins = e.matmul(o, PB[:, 2 * q + 1, :], CSC[:, 2 * q + 1, :], start=False, stop=True)
                    return ins
                sy.op("pe", yfn, reads=[rXT2, rMT2, rPB, rCSC], writes=[rPSy])
                psyv = PSy[:, 0:NPR * 128].rearrange("p (q l) -> p q l", l=128)
                if d == 0:
                    sy.op("act", lambda e, ys=ys, XT2=XT2, BT=BT, WVT=WVT, XW=XW, CBM=CBM, EX=EX, MT=MT, EC=EC, MT2=MT2, CSC=CSC: e.activation(out=ys[:], in_=psyv, func=AF.Copy), reads=[rPSy], writes=[rys])
                else:
                    sy.op("dve", lambda e, ys=ys, yl=yl, XT2=XT2, BT=BT, WVT=WVT, XW=XW, CBM=CBM, EX=EX, MT=MT, EC=EC, MT2=MT2, CSC=CSC: e.tensor_tensor(out=ys[:], in0=psyv, in1=yl[:], op=ALU.add), reads=[rPSy, ryl], writes=[rys])
                sy.dma(ydst, ys[:], rys, reads=[rys], writes=[self.rY])
                self.mm_group(PSs[:, 0:HPG * 64], [(BT[:], XW[:].rearrange("p h q -> p (h q)"))], [rBT, rXW], rPSs)
                et = ETB[:, hd0:hd0 + HPG, c]
                sy.op("dve", lambda e, et=et, XT2=XT2, BT=BT, WVT=WVT, XW=XW, CBM=CBM, EX=EX, MT=MT, EC=EC, MT2=MT2, CSC=CSC: e.tensor_tensor(out=P32[:], in0=P32[:], in1=et.unsqueeze(2).to_broadcast([128, HPG, 64]), op=ALU.mult),
                      reads=[rP32, rDT], writes=[rP32])
                sy.op("dve", lambda e, XT2=XT2, BT=BT, WVT=WVT, XW=XW, CBM=CBM, EX=EX, MT=MT, EC=EC, MT2=MT2, CSC=CSC: e.tensor_tensor(out=P32[:], in0=P32[:], in1=PSs[:, 0:HPG * 64].rearrange("p (h q) -> p h q", q=64), op=ALU.add),
                      reads=[rP32, rPSs], writes=[rP32])
                sy.op("dve", lambda e, XT2=XT2, BT=BT, WVT=WVT, XW=XW, CBM=CBM, EX=EX, MT=MT, EC=EC, MT2=MT2, CSC=CSC: e.tensor_copy(out=PB[:, 0::2, 0:64], in_=P32[:, 0::2, :]), reads=[rP32], writes=[rPB])
                sy.op("dve", lambda e, XT2=XT2, BT=BT, WVT=WVT, XW=XW, CBM=CBM, EX=EX, MT=MT, EC=EC, MT2=MT2, CSC=CSC: e.tensor_copy(out=PB[:, 1::2, 64:128], in_=P32[:, 1::2, :]), reads=[rP32], writes=[rPB])


            items = [(g, c) for g in range(NG) for c in order]
            stageA(items[0][0], items[0][1], 0)
            for k, (g, c) in enumerate(items):
                if k + 1 < len(items):
                    stageA(items[k + 1][0], items[k + 1][1], (k + 1) % 2)
                stageB(g, c, k % 2, c == order[0])

    def ssd_out(self, l):
        cfg, sy = self.cfg, self.sy
        SIC, NG, TS = cfg.SIC, cfg.NG, cfg.TS2
        CPG = SIC // NG
        GS = cfg.SI // NG
        with Scope(self) as ss:
            YT = [ss.sb("yt", [128, SIC, TS], F32) for i in range(2)]
            ZT = [ss.sb("zt", [128, SIC, TS], BF16) for i in range(2)]
            XS = [ss.sb("xs", [128, SIC, TS], BF16) for i in range(2)]
            rI = [R("soin%d" % i) for i in range(2)]
            SQ = ss.sb("sqb", [128, SIC, TS], BF16)
            rSQ = R("sqb")
            RS = ss.sb("rs", [128, NG, TS], F32)
            rRS = R("rs")
            GNT = [ss.sb("gnt", [128, SIC, TS], BF16) for i in range(2)]
            rGNT = [R("gnt%d" % i) for i in range(2)]
            for ti, (t0, n, s) in enumerate(cfg.tiles2):
                bi = ti % 2
                yt, zt, xs, ri, gnt, rg = YT[bi], ZT[bi], XS[bi], rI[bi], GNT[bi], rGNT[bi]
                sy.dma(yt[:, :, :n], self.Y[:, t0:t0 + n].rearrange("(j p) t -> p j t", p=128), ri, reads=[self.rY], writes=[ri])
                sy.dma(zt[:, :, :n], self.Zs[:, t0:t0 + n].rearrange("(j p) t -> p j t", p=128), ri, reads=[self.rZs], writes=[ri])
                sy.dma(xs[:, :, :n], self.XBC[0:cfg.SI, t0:t0 + n].rearrange("(j p) t -> p j t", p=128), ri, reads=[self.rXBC], writes=[ri])
                for j in range(SIC):
                    sdd = self.vcol("sdd", j, 1)
                    eng = "dve"
                    sy.op(eng, lambda e, j=j, sdd=sdd, yt=yt, xs=xs, n=n: e.scalar_tensor_tensor(
                        out=yt[:, j, :n], in0=xs[:, j, :n], scalar=sdd, in1=yt[:, j, :n], op0=ALU.mult, op1=ALU.add),
                        reads=[ri, self.rVEC], writes=[ri])
                sy.op("dve", lambda e, yt=yt, zt=zt, n=n: e.tensor_tensor(out=yt[:, :, :n], in0=yt[:, :, :n], in1=zt[:, :, :n], op=ALU.mult),
                      reads=[ri], writes=[ri])
                sy.op("act", lambda e, yt=yt, n=n: e.activation(out=SQ[:, :, :n], in_=yt[:, :, :n], func=AF.Square), reads=[ri], writes=[rSQ])
                for g in range(NG):
                    ps, rps = self.nextps()
                    self.mm_group(ps[:, :n], [(self.ones_b[:], SQ[:, g * CPG + q, :n]) for q in range(CPG)], [rSQ, self.rC], rps)
                    sy.op("act", lambda e, ps=ps, g=g, n=n: e.activation(out=RS[:, g, :n], in_=ps[:, :n], func=AF.Sqrt, scale=1.0 / GS, bias=1e-6),
                          reads=[rps], writes=[rRS])
                sy.op("dve", lambda e, n=n: e.reciprocal(out=RS[:, :, :n], in_=RS[:, :, :n]), reads=[rRS], writes=[rRS])
                for j in range(SIC):
                    snw = self.vcol("snw", j, 1)
                    eng = "dve"
                    sy.op(eng, lambda e, j=j, snw=snw, yt=yt, gnt=gnt, n=n: e.scalar_tensor_tensor(
                        out=gnt[:, j, :n], in0=yt[:, j, :n], scalar=snw, in1=RS[:, j // CPG, :n], op0=ALU.mult, op1=ALU.mult),
                        reads=[ri, rRS, self.rVEC], writes=[rg])
                sy.dma(self.GN[:, t0:t0 + n].rearrange("(j p) t -> p j t", p=128), gnt[:, :, :n], rg, reads=[rg], writes=[self.rGN])

    def resid_tiles(self, first, KD):
        pass

    def merge(self, l, first):
        cfg, sy = self.cfg, self.sy
        KD, LWC, SIC, TS, D = cfg.KD, cfg.LWC, cfg.SIC, cfg.TS2, cfg.D
        with Scope(self) as ss:
            WL = ss.sb("wlp", [128, LWC, D], BF16)
            WS = ss.sb("wsp", [128, SIC, D], BF16)
            WO = ss.sb("wop", [128, KD, D], BF16)
            rWL, rWS, rWO = R("wlp"), R("wsp"), R("wop")
            self.load_w(WL, rWL, self.i["lru_proj"][l], LWC, D)
            self.load_w(WS, rWS, self.i["ssd_proj"][l], SIC, D)
            self.load_w(WO, rWO, self.i["w_out"][l], KD, D)
            YA = [ss.sb("ya", [128, LWC, TS], BF16) for i in range(2)]
            GNT = [ss.sb("gn", [128, SIC, TS], BF16) for i in range(2)]
            GTT = [ss.sb("gt", [128, 2 * KD, TS], BF16) for i in range(2)]
            XT = [ss.sb("xr", [128, KD, TS], F32) for i in range(2)]
            rI = [R("mgin%d" % i) for i in range(2)]
            rX = [R("mgx%d" % i) for i in range(2)]
            M1 = ss.sb("m1", [128, TS], F32)
            rM1 = R("m1")
            MB = ss.sb("mb", [128, KD, TS], BF16)
            rMB = R("mb")
            for ti, (t0, n, s) in enumerate(cfg.tiles2):
                bi = ti % 2
                ya, gn, gt, xt, ri, rx = YA[bi], GNT[bi], GTT[bi], XT[bi], rI[bi], rX[bi]
                sy.dma(ya[:, :, :n], self.YAG[:, t0:t0 + n].rearrange("(j p) t -> p j t", p=128), ri, reads=[self.rYAG], writes=[ri])
                sy.dma(gn[:, :, :n], self.GN[:, t0:t0 + n].rearrange("(j p) t -> p j t", p=128), ri, reads=[self.rGN], writes=[ri])
                sy.dma(gt[:, :, :n], self.GT[:, t0:t0 + n].rearrange("(j p) t -> p j t", p=128), ri, reads=[self.rGT], writes=[ri])
                xsrc = (self.i["xr0"] if first else self.XR)[:, t0:t0 + n].rearrange("(kc p) t -> p kc t", p=128)
                sy.dma(xt[:, :, :n], xsrc, rx, reads=([] if first else [self.rXR]), writes=[rx])
                for jo in range(KD):
                    pa, rpa = self.nextps()
                    self.mm_group(pa[:, :n], [(WL[:, k, jo * 128:(jo + 1) * 128], ya[:, k, :n]) for k in range(LWC)], [rWL, ri], rpa)
                    pb, rpb = self.nextps()
                    self.mm_group(pb[:, :n], [(WS[:, k, jo * 128:(jo + 1) * 128], gn[:, k, :n]) for k in range(SIC)], [rWS, ri], rpb)
                    sy.op("dve", lambda e, pa=pa, gt=gt, jo=jo, n=n: e.tensor_tensor(out=M1[:, :n], in0=pa[:, :n], in1=gt[:, jo, :n], op=ALU.mult),
                          reads=[rpa, ri], writes=[rM1])
                    sy.op("dve", lambda e, pb=pb, gt=gt, jo=jo, n=n: e.tensor_tensor(out=MB[:, jo, :n], in0=pb[:, :n], in1=gt[:, KD + jo, :n], op=ALU.mult),
                          reads=[rpb, ri], writes=[rMB])
                    sy.op("pool", lambda e, jo=jo, n=n: e.tensor_tensor(out=MB[:, jo, :n], in0=MB[:, jo, :n], in1=M1[:, :n], op=ALU.add),
                          reads=[rM1, rMB], writes=[rMB])
                for jo in range(KD):
                    po, rpo = self.nextps()
                    self.mm_group(po[:, :n], [(WO[:, k, jo * 128:(jo + 1) * 128], MB[:, k, :n]) for k in range(KD)], [rWO, rMB], rpo)
                    gate = self.mod[:, 2 * KD + jo, s:s + 1]
                    sy.op("dve", lambda e, po=po, xt=xt, jo=jo, n=n, gate=gate: e.scalar_tensor_tensor(
                        out=xt[:, jo, :n], in0=po[:, :n], scalar=gate, in1=xt[:, jo, :n], op0=ALU.mult, op1=ALU.add),
                        reads=[rpo, rx, self.rMOD], writes=[rx])
                sy.dma(self.XR[:, t0:t0 + n].rearrange("(kc p) t -> p kc t", p=128), xt[:, :, :n], rx, reads=[rx], writes=[self.rXR])

    def ffn_up(self, l, H, rH):
        cfg, sy = self.cfg, self.sy
        KD, T, CTX, SEQ, GW, FFC, TS = cfg.KD, cfg.T, cfg.CTX, cfg.SEQ, cfg.GW, cfg.FFC, cfg.TS
        ROWS = SEQ // GW
        wsrc = self.i["ffn_w_up"][l]
        with Scope(self) as ss:
            Ws = [ss.sb("wu", [128, KD, 256], BF16) for i in range(2)]
            rWs = [R("wu%d" % i) for i in range(2)]
            PADX = ss.sb("padx", [128, ROWS + 2, GW + 2], BF16)
            PADC = ss.sb("padc", [128, 3, CTX + 2], BF16)
            rPAD = R("pad")
            VB = ss.sb("vb", [128, T], BF16)
            rVB = R("vb")
            AO = [ss.sb("ao", [128, T], BF16) for i in range(2)]
            rAO = [R("ao%d" % i) for i in range(2)]
            DG = ss.sb("dg9", [128, 9, 128], BF16)
            rDG = R("dg9")
            TMPs = [ss.sb("tmpf", [128, 3, TS], F32) for i in range(2)]
            rTMPs = [R("tmpf%d" % i) for i in range(2)]
            tcount = [0]
            sy.op("dve", lambda e: e.memset(PADX[:], 0.0), writes=[rPAD])
            sy.op("dve", lambda e: e.memset(PADC[:], 0.0), writes=[rPAD])
            for j in range(FFC):
                W, rW = Ws[j % 2], rWs[j % 2]
                ao, rao = AO[j % 2], rAO[j % 2]
                self.load_w(W[:, :, 0:128], rW, wsrc[:, j * 128:(j + 1) * 128], KD, 128)
                self.load_w(W[:, :, 128:256], rW, wsrc[:, cfg.FF + j * 128:cfg.FF + (j + 1) * 128], KD, 128)
                self.make_diag(DG, rDG, "fcw", 9, FFC, j)
                for (t0, n, s) in cfg.tiles:
                    ps, rps = self.proj_tile(W[:, :, 0:128], rW, H, rH, t0, n)
                    if s == 1:
                        dst = PADC[:, 1, 1 + t0:1 + t0 + n]
                        src = ps[:, :n]
                    else:
                        r0 = (t0 - CTX) // GW
                        nr = n // GW
                        dst = PADX[:, 1 + r0:1 + r0 + nr, 1:1 + GW]
                        src = ps[:, :n].rearrange("p (r c) -> p r c", c=GW)
                    sy.op("act", lambda e, dst=dst, src=src: e.activation(out=dst, in_=src, func=AF.Copy), reads=[rps], writes=[rPAD])
                    ps, rps = self.proj_tile(W[:, :, 128:256], rW, H, rH, t0, n)
                    sy.op("act", lambda e, ps=ps, t0=t0, n=n: e.activation(out=VB[:, t0:t0 + n], in_=ps[:, :n], func=AF.Copy), reads=[rps], writes=[rVB])
                fcb = self.vcol("fcb", j, 1)
                for (t0, n, s) in cfg.tiles:
                    ps, rps = self.nextps()
                    pairs = []
                    for kr in range(3):
                        for kc in range(3):
                            if s == 1:
                                rhs = PADC[:, kr, kc + t0:kc + t0 + n]
                            else:
                                r0 = (t0 - CTX) // GW
                                nr = n // GW
                                rhs = PADX[:, r0 + kr:r0 + kr + nr, kc:kc + GW]
                            pairs.append((DG[:, kr * 3 + kc, :], rhs))
                    outp = ps[:, :n] if s == 1 else ps[:, :n].rearrange("p (r c) -> p r c", c=GW)
                    self.mm_group(outp, pairs, [rDG, rPAD], rps)
                    tcount[0] += 1
                    TMP, rTMP = TMPs[tcount[0] % 2], rTMPs[tcount[0] % 2]
                    self.gelu_from(TMP, rTMP, ps[:, :n], rps, ao[:, t0:t0 + n], rao, n, bias=fcb, mul=VB[:, t0:t0 + n], rmul=rVB)
                sy.dma(self.ACTV[j * 128:(j + 1) * 128, :], ao[:], rao, reads=[rao], writes=[self.rACTV])

    def ffn_down(self, l):
        cfg, sy = self.cfg, self.sy
        KD, FFC, TS, D = cfg.KD, cfg.FFC, cfg.TS2, cfg.D
        with Scope(self) as ss:
            WD = ss.sb("wd", [128, FFC, D], BF16)
            rWD = R("wd")
            self.load_w(WD, rWD, self.i["ffn_w_down"][l], FFC, D)
            AT = [ss.sb("at", [128, FFC, TS], BF16) for i in range(2)]
            XT = [ss.sb("xr2", [128, KD, TS], F32) for i in range(2)]
            rI = [R("fdin%d" % i) for i in range(2)]
            rX = [R("fdx%d" % i) for i in range(2)]
            for ti, (t0, n, s) in enumerate(cfg.tiles2):
                bi = ti % 2
                at, xt, ri, rx = AT[bi], XT[bi], rI[bi], rX[bi]
                sy.dma(at[:, :, :n], self.ACTV[:, t0:t0 + n].rearrange("(j p) t -> p j t", p=128), ri, reads=[self.rACTV], writes=[ri])
                sy.dma(xt[:, :, :n], self.XR[:, t0:t0 + n].rearrange("(kc p) t -> p kc t", p=128), rx, reads=[self.rXR], writes=[rx])
                for jo in range(KD):
                    po, rpo = self.nextps()
                    self.mm_group(po[:, :n], [(WD[:, k, jo * 128:(jo + 1) * 128], at[:, k, :n]) for k in range(FFC)], [rWD, ri], rpo)
                    gate = self.mod[:, 5 * KD + jo, s:s + 1]
                    sy.op("dve", lambda e, po=po, xt=xt, jo=jo, n=n, gate=gate: e.scalar_tensor_tensor(
                        out=xt[:, jo, :n], in0=po[:, :n], scalar=gate, in1=xt[:, jo, :n], op0=ALU.mult, op1=ALU.add),
                        reads=[rpo, rx, self.rMOD], writes=[rx])
                sy.dma(self.XR[:, t0:t0 + n].rearrange("(kc p) t -> p kc t", p=128), xt[:, :, :n], rx, reads=[rx], writes=[self.rXR])

    def layer(self, l, first):
        cfg = self.cfg
        on = lambda n: (self.stages is None) or (n in self.stages)
        self.adaln(l)
        with Scope(self) as hs:
            H = hs.sb("H", [128, cfg.KD, cfg.T], BF16)
            rH = R("H")
            if on("norm1"):
                self.norm_mod(first, self.g1, 0, H, rH)
            if on("lru"):
                self.lru(l, H, rH)
            if on("prep"):
                self.ssd_prep(l, H, rH)
        for d in range(2):
            if on("sweep%d" % d):
                self.ssd_sweep(d)
        if on("ssdout"):
            self.ssd_out(l)
        if on("merge"):
            self.merge(l, first)
        with Scope(self) as hs:
            H = hs.sb("H2", [128, cfg.KD, cfg.T], BF16)
            rH = R("H2")
            if on("norm2"):
                self.norm_mod(False, self.g2, 3, H, rH)
            if on("ffnup"):
                self.ffn_up(l, H, rH)
        if on("ffndown"):
            self.ffn_down(l)

    def final_norm(self):
        cfg, sy = self.cfg, self.sy
        KD, TS = cfg.KD, cfg.TS2
        with Scope(self) as ss:
            xts = [ss.sb("xtf", [128, KD, TS], F32) for i in range(2)]
            rxs = [R("xtf%d" % i) for i in range(2)]
            sq = ss.sb("sqf", [128, KD, TS], F32)
            rsq = R("sqf")
            rstd = ss.sb("rstdf", [128, TS], F32)
            rrs = R("rstdf")
            rOUT = R("out")
            ti = 0
            for (t0, n, s) in cfg.tiles2:
                if s == 1:
                    continue
                xt, rx = xts[ti % 2], rxs[ti % 2]
                ti += 1
                sy.dma(xt[:, :, :n], self.XR[:, t0:t0 + n].rearrange("(kc p) t -> p kc t", p=128), rx, reads=[self.rXR], writes=[rx])
                self.rstd_of(xt, rx, n, sq, rsq, rstd, rrs, cfg.D)
                sy.op("dve", lambda e, xt=xt, n=n: e.tensor_tensor(out=xt[:, :, :n], in0=xt[:, :, :n],
                                                                   in1=rstd[:, :n].unsqueeze(1).to_broadcast([128, KD, n]), op=ALU.mult),
                      reads=[rx, rrs], writes=[rx])
                sy.op("dve", lambda e, xt=xt, n=n: e.tensor_tensor(out=xt[:, :, :n], in0=xt[:, :, :n],
                                                                   in1=self.vcol("fnw", 0, KD).unsqueeze(2).to_broadcast([128, KD, n]), op=ALU.mult),
                      reads=[rx, self.rVEC], writes=[rx])
                sy.dma(self.outf[:, t0 - cfg.CTX:t0 - cfg.CTX + n].rearrange("(kc p) t -> p kc t", p=128), xt[:, :, :n], rx, reads=[rx], writes=[rOUT])

    def copy_xr_out(self):
        cfg, sy = self.cfg, self.sy
        KD, TS = cfg.KD, cfg.TS
        with Scope(self) as ss:
            xts = [ss.sb("xtc", [128, KD, TS], F32) for i in range(2)]
            rxs = [R("xtc%d" % i) for i in range(2)]
            rOUT = R("out")
            for ti, (t0, n, s) in enumerate(cfg.tiles):
                xt, rx = xts[ti % 2], rxs[ti % 2]
                sy.dma(xt[:, :, :n], self.XR[:, t0:t0 + n].rearrange("(kc p) t -> p kc t", p=128), rx, reads=[self.rXR], writes=[rx])
                sy.dma(self.out[:, t0:t0 + n].rearrange("(kc p) t -> p kc t", p=128), xt[:, :, :n], rx, reads=[rx], writes=[rOUT])


FUSED = True
CFG_KW = {}
PER_LAYER = ("ada_w", "ada_b", "norm_mix_w", "norm_ffn_w", "w_in", "lru_conv_w", "lru_conv_b", "lru_wa", "lru_ba", "lru_wx",
             "lru_bx", "lru_lambda", "lru_proj", "ssd_conv_w", "ssd_conv_b", "ssd_dt_bias", "ssd_a_log", "ssd_d", "ssd_norm_w",
             "ssd_proj", "w_out", "ffn_w_up", "ffn_conv_w", "ffn_conv_b", "ffn_w_down")


def _maps(cfg, inp, B):
    base = host_prep(cfg, inp, 0)
    maps = [base]
    for b in range(1, B):
        m = dict(base)
        pb = host_prep_core(cfg, inp, b)
        m.update(pb)
        maps.append(m)
    return maps


def host_prep_core(cfg, inp, b):
    m = {}
    m["xr0"] = np.ascontiguousarray(np.concatenate([inp["ctx"][b].T, inp["x"][b].T], axis=1).astype(np.float32))
    cv = np.zeros((128, cfg.KD, 2), np.float32)
    cv[:, :, 0] = colmajor(inp["c"][b])
    cv[:, :, 1] = colmajor(inp["c_ctx"])
    m["cvec"] = cv
    return m


def kernel(**inputs):
    from concourse.bass_utils import run_bass_kernel_spmd
    inp = {k: np.asarray(v) for k, v in inputs.items()}
    B = inp["x"].shape[0]
    L = inp["w_in"].shape[0]
    if FUSED:
        cfg = Cfg(DEPTH=L, **CFG_KW)
        nc = Prog(cfg, layers=list(range(L)), final=True, emit_xr=False).build()
        maps = _maps(cfg, inp, B)
        res = run_bass_kernel_spmd(nc, maps, core_ids=list(range(B)))
        outs = [res.results[b]["out"] for b in range(B)]
    else:
        cfg = Cfg(DEPTH=1, **CFG_KW)
        xr = None
        for l in range(L):
            inpl = {k: (v[l:l + 1] if k in PER_LAYER else v) for k, v in inp.items()}
            maps = _maps(cfg, inpl, B)
            if xr is not None:
                for b in range(B):
                    maps[b]["xr0"] = xr[b]
            nc = Prog(cfg, layers=[0], final=True, emit_xr=True).build()
            res = run_bass_kernel_spmd(nc, maps, core_ids=list(range(B)))
            xr = [np.ascontiguousarray(res.results[b]["outx"]) for b in range(B)]
            outs = [res.results[b]["outf"] for b in range(B)]
    out = np.stack([np.ascontiguousarray(o.T) for o in outs]).astype(np.float32)
    return out
```
